# Optimizing a Trainium2 kernel written in Bass

```python
import math
import jax
import jax.numpy as jnp
from jax import lax
import numpy as np

D_MODEL = 2048
BATCH = 4
SEQ = 2048
DEPTH = 2

NORM_EPS = 1e-6

A_HEADS = 16
A_HEAD = 64
A_WIDTH = A_HEADS * A_HEAD
A_DECAY_LORA = 64
A_ICLR_LORA = 64
A_GATE_LORA = 160
A_VRES_LORA = 32
A_GN_EPS = 64e-5
A_IN = 3 * A_WIDTH + A_DECAY_LORA + A_ICLR_LORA + A_GATE_LORA

B_PAIRS = ((128, 1), (512, 4), (2048, 16))
B_GROUPS = len(B_PAIRS)
B_HEADS_PER_GROUP = 4
B_HEAD = 128
B_WIDTH = B_GROUPS * B_HEADS_PER_GROUP * B_HEAD
B_OUT = B_HEADS_PER_GROUP * B_HEAD
B_IN = 3 * B_WIDTH
B_QBLOCK = 64
ROPE_THETA = 500000.0
ROPE_DIM = B_HEAD // 4

C_HEADS = 8
C_HEAD_K = 128
C_HEAD_V = 128
C_KW = C_HEADS * C_HEAD_K
C_VW = C_HEADS * C_HEAD_V
C_CONV = 4
C_CHUNK = 64
C_IN = 2 * C_KW + C_VW + 2 * C_HEADS + C_VW

G_IN = 3 * D_MODEL
N_IN = A_IN + B_IN + C_IN + G_IN

D_FF = 5632
FFN_CONV = 3

kernel_name = 'hybrid_rwkv7_dilated_gdn_convffn'

F32 = jnp.float32


def rmsnorm(x, g, eps=NORM_EPS):
    xf = x.astype(F32)
    y = xf * lax.rsqrt(jnp.mean(xf * xf, axis=-1, keepdims=True) + eps)
    return (y * g.astype(F32)).astype(x.dtype)


def l2norm(t):
    t = t.astype(F32)
    return t / jnp.maximum(jnp.sqrt(jnp.sum(t * t, axis=-1, keepdims=True)), 1e-12)


def token_shift(x):
    return jnp.pad(x, ((0, 0), (1, 0), (0, 0)))[:, :-1]


def causal_dwconv(x, w):
    K, C = w.shape
    return lax.conv_general_dilated(x, w[:, None, :].astype(x.dtype), window_strides=(1,),
                                    padding=((K - 1, 0),), dimension_numbers=('NWC', 'WIO', 'NWC'),
                                    feature_group_count=C)


def partial_rope(x, pos):
    half = ROPE_DIM // 2
    inv = ROPE_THETA ** (-jnp.arange(half, dtype=F32) / half)
    ang = pos.astype(F32)[:, None] * inv[None, :]
    cos = jnp.cos(ang)[None, :, None, :]
    sin = jnp.sin(ang)[None, :, None, :]
    xr = x[..., :ROPE_DIM].astype(F32)
    x1, x2 = xr[..., :half], xr[..., half:]
    rot = jnp.concatenate([x1 * cos - x2 * sin, x2 * cos + x1 * sin], axis=-1).astype(x.dtype)
    return jnp.concatenate([rot, x[..., ROPE_DIM:]], axis=-1)


def rwkv7_mix(za, mu, w0, w2, a0, a2, g2, k_k, k_a, r_k, ln_w, ln_b, v_first, v_mix):
    Bt, S, _ = za.shape
    z = za.astype(F32)
    z = z + (token_shift(z) - z) * mu
    r, k, v, wd, ad, gd = jnp.split(z, [A_WIDTH, 2 * A_WIDTH, 3 * A_WIDTH, 3 * A_WIDTH + A_DECAY_LORA,
                                        3 * A_WIDTH + A_DECAY_LORA + A_ICLR_LORA], axis=-1)
    logw = -jax.nn.softplus(-(w0 + jnp.tanh(wd) @ w2)) - 0.5
    decay = jnp.exp(-jnp.exp(logw))
    a = jax.nn.sigmoid(a0 + ad @ a2)
    g = jax.nn.sigmoid(gd) @ g2
    heads = lambda t: t.reshape(Bt, S, A_HEADS, A_HEAD)
    kk = l2norm(heads(k * k_k))
    k = k * (1.0 + (a - 1.0) * k_a)
    if v_mix is not None:
        v = v + (v_first.astype(F32) - v) * v_mix
    rh, kh, vh, wh = heads(r), heads(k), heads(v), heads(decay)
    a_vec = -kk
    b_vec = kk * heads(a)
    xs = tuple(jnp.moveaxis(t, 1, 0) for t in (rh, wh, kh, vh, a_vec, b_vec))

    def step(state, inp):
        r_t, w_t, k_t, v_t, a_t, b_t = inp
        sa = jnp.einsum('bhvk,bhk->bhv', state, a_t)
        state = (state * w_t[:, :, None, :] + sa[..., None] * b_t[:, :, None, :]
                 + v_t[..., None] * k_t[:, :, None, :])
        return state, jnp.einsum('bhvk,bhk->bhv', state, r_t)

    s0 = jnp.zeros((Bt, A_HEADS, A_HEAD, A_HEAD), F32)
    _, y = lax.scan(step, s0, xs)
    y = jnp.moveaxis(y, 0, 1)
    mean = jnp.mean(y, axis=-1, keepdims=True)
    var = jnp.mean(jnp.square(y - mean), axis=-1, keepdims=True)
    y = ((y - mean) * lax.rsqrt(var + A_GN_EPS)).reshape(Bt, S, A_WIDTH) * ln_w + ln_b
    bonus = (jnp.sum(rh * kh * r_k, axis=-1, keepdims=True) * vh).reshape(Bt, S, A_WIDTH)
    return ((y + bonus) * g).astype(za.dtype), v.astype(za.dtype)


def dilated_attention(q, k, v):
    Bt, S = q.shape[:2]
    nb = S // B_QBLOCK
    scale = B_HEAD ** -0.5

    def block(bi):
        t0 = bi * B_QBLOCK
        qb = lax.dynamic_slice_in_dim(q, t0, B_QBLOCK, axis=1)
        t = t0 + jnp.arange(B_QBLOCK)
        scores, vals = [], []
        for gi, (win, dil) in enumerate(B_PAIRS):
            dist = dil * jnp.arange(win // dil + 1)
            idx = t[:, None] - dist[None, :]
            valid = idx >= 0
            idx = jnp.maximum(idx, 0)
            kg = jnp.take(k[:, :, gi], idx, axis=1)
            vg = jnp.take(v[:, :, gi], idx, axis=1)
            s = jnp.einsum('bqhd,bqjhd->bhqj', qb[:, :, gi], kg).astype(F32) * scale
            scores.append(jnp.where(valid[None, None], s, -jnp.inf))
            vals.append(vg)
        m = scores[0].max(-1)
        for s in scores[1:]:
            m = jnp.maximum(m, s.max(-1))
        num = jnp.zeros(qb.shape[:2] + qb.shape[3:], F32)
        den = jnp.zeros(m.shape, F32)
        for s, vg in zip(scores, vals):
            p = jnp.exp(s - m[..., None])
            den = den + p.sum(-1)
            num = num + jnp.einsum('bhqj,bqjhd->bqhd', p, vg.astype(F32))
        return (num / jnp.transpose(den, (0, 2, 1))[..., None]).astype(q.dtype)

    out = lax.map(block, jnp.arange(nb))
    return jnp.moveaxis(out, 0, 1).reshape(Bt, S, B_OUT)


def chunk_gated_delta_rule(q, k, v, g, beta):
    Bt, S, H, Dk = q.shape
    Dv = v.shape[-1]
    n = S // C_CHUNK
    ch = lambda t: jnp.transpose(t.reshape(Bt, n, C_CHUNK, H, t.shape[-1]), (0, 1, 3, 2, 4))
    ch2 = lambda t: jnp.transpose(t.reshape(Bt, n, C_CHUNK, H), (0, 1, 3, 2))
    q, k, v = ch(q), ch(k), ch(v)
    g, beta = ch2(g), ch2(beta)
    gam = jnp.cumsum(g, axis=-1)
    causal = jnp.tril(jnp.ones((C_CHUNK, C_CHUNK), bool))
    strict = jnp.tril(jnp.ones((C_CHUNK, C_CHUNK), bool), k=-1)
    decay = jnp.exp(jnp.where(causal, gam[..., :, None] - gam[..., None, :], -jnp.inf))
    Lmat = jnp.where(strict, beta[..., None] * jnp.einsum('bnhid,bnhjd->bnhij', k, k) * decay, 0.0)
    eye = jnp.eye(C_CHUNK, dtype=F32)
    rhs = jnp.concatenate([v * beta[..., None], k * (beta * jnp.exp(gam))[..., None]], axis=-1)
    sol = lax.linalg.triangular_solve(Lmat + eye, rhs, left_side=True, lower=True, unit_diagonal=True)
    u, w = sol[..., :Dv], sol[..., Dv:]
    attn = jnp.where(causal, jnp.einsum('bnhid,bnhjd->bnhij', q, k) * decay, 0.0)
    xs = tuple(jnp.moveaxis(t, 1, 0) for t in (q, k, u, w, attn, gam))

    def step(state, inp):
        qc, kc, uc, wc, ac, gc = inp
        v_new = uc - jnp.einsum('bhck,bhkv->bhcv', wc, state)
        o = (jnp.einsum('bhck,bhkv->bhcv', qc * jnp.exp(gc)[..., None], state)
             + jnp.einsum('bhij,bhjv->bhiv', ac, v_new))
        g_last = gc[..., -1]
        state = (state * jnp.exp(g_last)[..., None, None]
                 + jnp.einsum('bhck,bhcv->bhkv', kc * jnp.exp(g_last[..., None] - gc)[..., None], v_new))
        return state, o

    s0 = jnp.zeros((Bt, H, Dk, Dv), F32)
    _, o = lax.scan(step, s0, xs)
    return jnp.transpose(o, (1, 0, 3, 2, 4)).reshape(Bt, S, H, Dv)


def gated_deltanet(zc, conv_w, A_log, dt_bias, norm_w):
    Bt, S, _ = zc.shape
    q, k, v, beta_in, alpha_in, gate = jnp.split(
        zc, [C_KW, 2 * C_KW, 2 * C_KW + C_VW, 2 * C_KW + C_VW + C_HEADS, 2 * C_KW + C_VW + 2 * C_HEADS], axis=-1)
    qkv = jax.nn.silu(causal_dwconv(jnp.concatenate([q, k, v], axis=-1), conv_w))
    q, k, v = jnp.split(qkv, [C_KW, 2 * C_KW], axis=-1)
    q = l2norm(q.reshape(Bt, S, C_HEADS, C_HEAD_K)) * (C_HEAD_K ** -0.5)
    k = l2norm(k.reshape(Bt, S, C_HEADS, C_HEAD_K))
    v = v.reshape(Bt, S, C_HEADS, C_HEAD_V).astype(F32)
    beta = jax.nn.sigmoid(beta_in.astype(F32))
    g = -jnp.exp(A_log.astype(F32)) * jax.nn.softplus(alpha_in.astype(F32) + dt_bias)
    o = chunk_gated_delta_rule(q, k, v, g, beta)
    o = rmsnorm(o, norm_w) * jax.nn.silu(gate.reshape(Bt, S, C_HEADS, C_HEAD_V).astype(F32))
    return o.reshape(Bt, S, C_VW).astype(zc.dtype)


def setup_inputs(seed: int = 0) -> dict:
    key = jax.random.key(seed)
    ks = iter(jax.random.split(key, 40))
    nrm = lambda shape, scale: jax.random.normal(next(ks), shape, F32) * scale
    uni = lambda shape, lo, hi: jax.random.uniform(next(ks), shape, F32, lo, hi)
    L, D = DEPTH, D_MODEL
    LV = max(DEPTH - 1, 0)
    dt = jnp.exp(uni((L, C_HEADS), math.log(1e-3), math.log(0.1)))
    return {
        'x': nrm((BATCH, SEQ, D), 1.0),
        'attn_norm': 1.0 + nrm((L, D), 0.02),
        'w_in': nrm((L, D, N_IN), D ** -0.5),
        'rwkv_mu': uni((L, A_IN), 0.0, 1.0),
        'rwkv_w0': uni((L, A_WIDTH), -7.0, -2.0),
        'rwkv_w2': nrm((L, A_DECAY_LORA, A_WIDTH), 0.1),
        'rwkv_a0': nrm((L, A_WIDTH), 0.1),
        'rwkv_a2': nrm((L, A_ICLR_LORA, A_WIDTH), 0.5 * A_ICLR_LORA ** -0.5),
        'rwkv_g2': nrm((L, A_GATE_LORA, A_WIDTH), A_GATE_LORA ** -0.5),
        'rwkv_k_k': uni((L, A_WIDTH), 0.7, 1.0),
        'rwkv_k_a': uni((L, A_WIDTH), 0.8, 1.2),
        'rwkv_r_k': nrm((L, A_HEADS, A_HEAD), 0.1),
        'rwkv_ln_w': 1.0 + nrm((L, A_WIDTH), 0.02),
        'rwkv_ln_b': nrm((L, A_WIDTH), 0.02),
        'rwkv_v0': uni((LV, A_WIDTH), 0.0, 1.0),
        'rwkv_v1': nrm((LV, D, A_VRES_LORA), D ** -0.5),
        'rwkv_v2': nrm((LV, A_VRES_LORA, A_WIDTH), A_VRES_LORA ** -0.5),
        'gdn_conv': nrm((L, C_CONV, 2 * C_KW + C_VW), 0.5),
        'gdn_A_log': jnp.log(uni((L, C_HEADS), 1.0, 16.0)),
        'gdn_dt_bias': dt + jnp.log(-jnp.expm1(-dt)),
        'gdn_norm': 1.0 + nrm((L, C_HEAD_V), 0.02),
        'proj_a': nrm((L, A_WIDTH, D), A_WIDTH ** -0.5),
        'proj_b': nrm((L, B_OUT, D), B_OUT ** -0.5),
        'proj_c': nrm((L, C_VW, D), C_VW ** -0.5),
        'w_out': nrm((L, D, D), D ** -0.5),
        'ffn_norm': 1.0 + nrm((L, D), 0.02),
        'ffn_up': nrm((L, D, 2 * D_FF), D ** -0.5),
        'ffn_conv': nrm((L, FFN_CONV, 2 * D_FF), 0.3).at[:, -1].add(1.0),
        'ffn_down': nrm((L, D_FF, D), D_FF ** -0.5),
        'final_norm': 1.0 + nrm((D,), 0.02),
    }


def reference(x, attn_norm, w_in, rwkv_mu, rwkv_w0, rwkv_w2, rwkv_a0, rwkv_a2, rwkv_g2, rwkv_k_k, rwkv_k_a,
              rwkv_r_k, rwkv_ln_w, rwkv_ln_b, rwkv_v0, rwkv_v1, rwkv_v2, gdn_conv, gdn_A_log, gdn_dt_bias,
              gdn_norm, proj_a, proj_b, proj_c, w_out, ffn_norm, ffn_up, ffn_conv, ffn_down, final_norm):
    Bt, S, _ = x.shape
    pos = jnp.arange(S)
    v_first = None
    for l in range(DEPTH):
        h = rmsnorm(x, attn_norm[l])
        z = h @ w_in[l]
        za, zb, zc, zg = jnp.split(z, [A_IN, A_IN + B_IN, A_IN + B_IN + C_IN], axis=-1)
        if l == 0:
            v_mix = None
        else:
            v_mix = jax.nn.sigmoid(rwkv_v0[l - 1] + (h @ rwkv_v1[l - 1]) @ rwkv_v2[l - 1]).astype(F32)
        ya, v_l = rwkv7_mix(za, rwkv_mu[l], rwkv_w0[l], rwkv_w2[l], rwkv_a0[l], rwkv_a2[l], rwkv_g2[l],
                            rwkv_k_k[l], rwkv_k_a[l], rwkv_r_k[l], rwkv_ln_w[l], rwkv_ln_b[l], v_first, v_mix)
        if l == 0:
            v_first = v_l
        qb, kb, vb = jnp.split(zb, 3, axis=-1)
        shp = (Bt, S, B_GROUPS * B_HEADS_PER_GROUP, B_HEAD)
        qb = partial_rope(qb.reshape(shp), pos)
        kb = partial_rope(kb.reshape(shp), pos)
        gshp = (Bt, S, B_GROUPS, B_HEADS_PER_GROUP, B_HEAD)
        yb = dilated_attention(qb.reshape(gshp), kb.reshape(gshp), vb.reshape(gshp))
        yc = gated_deltanet(zc, gdn_conv[l], gdn_A_log[l], gdn_dt_bias[l], gdn_norm[l])
        ga, gb, gc = jnp.split(jax.nn.sigmoid(zg), 3, axis=-1)
        merged = ga * (ya @ proj_a[l]) + gb * (yb @ proj_b[l]) + gc * (yc @ proj_c[l])
        x = x + merged @ w_out[l]
        h = rmsnorm(x, ffn_norm[l])
        u = causal_dwconv(h @ ffn_up[l], ffn_conv[l])
        u_gate, u_val = jnp.split(u, 2, axis=-1)
        x = x + (jax.nn.silu(u_gate) * u_val) @ ffn_down[l]
    return rmsnorm(x, final_norm)
```

```python
import numpy as np
import concourse.bass as bass
import concourse.mybir as mybir
from concourse.bass_utils import run_bass_kernel_spmd
from contextlib import ExitStack

F32 = mybir.dt.float32
BF16 = mybir.dt.bfloat16
ALU = mybir.AluOpType
AF = mybir.ActivationFunctionType
AX = mybir.AxisListType

D = 2048
KC = 16
SEQ = 2048
NTOK_T = 1026
TT = [(0, 2), (2, 514), (514, 1026)]
D_FF = 5632
NFB = 44
EPS = 1e-6

SAME_ENG_SYNC = True


class Dep:
    __slots__ = ("w", "r", "name", "excl")

    def __init__(self, name="", excl=False):
        self.w = None
        self.r = {}
        self.name = name
        self.excl = excl


class Sched:
    ENGS = ["pe", "act", "dve", "pool", "sp"]

    def __init__(self, nc, n_dma=24):
        self.nc = nc
        self.q = {e: [] for e in self.ENGS}
        self.cnt = {e: 0 for e in self.ENGS}
        self.n_dma = n_dma
        self.dma_cnt = [0] * n_dma
        self.dma_rr = 0
        self.waited = {e: {} for e in self.ENGS}
        self.nops = 0
        self.ninst = 0

    def op(self, eng, calls, reads=(), writes=(), dma=False):
        deps = {}

        def add(tok):
            if tok is None:
                return
            k, v = tok
            if deps.get(k, 0) < v:
                deps[k] = v

        for r in reads:
            add(r.w)
            if r.excl:
                for k, v in r.r.items():
                    add((k, v))
        for w in writes:
            add(w.w)
            for k, v in w.r.items():
                add((k, v))
        if dma:
            i = self.dma_rr
            self.dma_rr = (self.dma_rr + 1) % self.n_dma
            add((("dma", i), self.dma_cnt[i]))
            self.dma_cnt[i] += 16
            tok = (("dma", i), self.dma_cnt[i])
        else:
            self.cnt[eng] += 1
            tok = (eng, self.cnt[eng])
        waits = []
        wd = self.waited[eng]
        for k, v in deps.items():
            if v <= 0:
                continue
            if k == eng and (eng == "pe" or not SAME_ENG_SYNC):
                continue
            if wd.get(k, 0) >= v:
                continue
            wd[k] = v
            waits.append((k, v))
        for r in reads:
            if r.r.get(tok[0], 0) < tok[1]:
                r.r[tok[0]] = tok[1]
        for w in writes:
            w.w = tok
            w.r = {}
        self.q[eng].append((waits, calls, tok))
        self.nops += 1
        self.ninst += len(calls)
        return tok

    def emit(self, final_wait_eng="sp"):
        nc = self.nc
        with ExitStack() as es:
            sems = {}
            for e in self.ENGS:
                sems[e] = es.enter_context(nc.semaphore("s_" + e))
            for i in range(self.n_dma):
                sems[("dma", i)] = es.enter_context(nc.semaphore("s_dma%d" % i))
            block = es.enter_context(nc.Block())
            finals = [(("dma", i), self.dma_cnt[i]) for i in range(self.n_dma) if self.dma_cnt[i] > 0]

            def run(engname, engine):
                for waits, calls, tok in self.q[engname]:
                    for k, v in waits:
                        engine.wait_ge(sems[k], v)
                    ins = None
                    for name, kw in calls:
                        ins = getattr(engine, name)(**kw)
                    ins.then_inc(sems[tok[0]], 16 if isinstance(tok[0], tuple) else 1)
                if engname == final_wait_eng:
                    for k, v in finals:
                        engine.wait_ge(sems[k], v)

            @block.tensor
            def _(e):
                run("pe", e)

            @block.scalar
            def _(e):
                run("act", e)

            @block.vector
            def _(e):
                run("dve", e)

            @block.gpsimd
            def _(e):
                run("pool", e)

            @block.sync
            def _(e):
                run("sp", e)


class Arena:
    def __init__(self, ap_f32, words):
        self.base = ap_f32
        self.words = words
        self.top = 0
        self.peak = 0
        self.live = []
        self.dead = []

    def mark(self):
        return self.top

    def release(self, m):
        keep = []
        for ent in self.live:
            if ent[0] >= m:
                self.dead.append(ent)
            else:
                keep.append(ent)
        self.live = keep
        self.top = m

    def alloc(self, nelem, dtype=F32, parts=128, name=""):
        w = nelem if dtype == F32 else (nelem + 1) // 2
        w = (w + 7) // 8 * 8
        assert self.top + w <= self.words, ("SBUF arena overflow", name, self.top, w, self.words)
        lo, hi = self.top, self.top + w
        a = self.base[0:parts, lo:hi]
        self.top = hi
        self.peak = max(self.peak, self.top)
        d = Dep(name)
        nd = []
        for (dlo, dhi, dd) in self.dead:
            if dlo < hi and lo < dhi:
                toks = list(dd.r.items())
                if dd.w is not None:
                    toks.append(dd.w)
                for k, v in toks:
                    if d.r.get(k, 0) < v:
                        d.r[k] = v
                if lo <= dlo and dhi <= hi:
                    continue
            nd.append((dlo, dhi, dd))
        self.dead = nd
        self.live.append((lo, hi, d))
        if dtype != F32:
            a = a.bitcast(dtype)
        return a[:, 0:nelem], d


class T:
    def __init__(self, ap, d):
        self.ap = ap
        self.d = d


class Ctx:
    ARENA_WORDS = 51 * 1024 + 512

    def __init__(self, nc, es):
        self.nc = nc
        self.S = Sched(nc)
        big = es.enter_context(nc.sbuf_tensor("arena", [128, self.ARENA_WORDS], F32))
        self.ar = Arena(big, self.ARENA_WORDS)
        self.psb = []
        for i in range(8):
            p = es.enter_context(nc.psum_tensor("psb%d" % i, [128, 512], F32))
            self.psb.append(T(p, Dep("ps%d" % i, excl=True)))
        self.ps_rr = 0
        self.rings = {}

    def ps(self, pin=False):
        pinned = getattr(self, "pinned", None)
        if pinned is None:
            pinned = self.pinned = set()
        while self.ps_rr in pinned:
            self.ps_rr = (self.ps_rr + 1) % 8
        i = self.ps_rr
        t = self.psb[i]
        self.ps_rr = (self.ps_rr + 1) % 8
        if pin:
            pinned.add(i)
        return t

    def unpin_all(self):
        self.pinned = set()

    def tile(self, nelem, dtype=F32, name="", parts=128):
        ap, d = self.ar.alloc(nelem, dtype, parts, name)
        return T(ap, d)

    def ring(self, key, n, nelem, dtype=F32):
        if key not in self.rings:
            off = self.ar.top
            self.rings[key] = [[self.tile(nelem, dtype, "%s%d" % (key, i)) for i in range(n)], 0, off]
        r = self.rings[key]
        t = r[0][r[1] % len(r[0])]
        r[1] += 1
        return t

    def mark(self):
        return self.ar.mark()

    def release(self, m):
        for k in [k for k, r in self.rings.items() if r[2] >= m]:
            del self.rings[k]
        self.ar.release(m)

    def I(self, eng, name, reads, writes, **kw):
        self.S.op(eng, [(name, kw)], [t.d for t in reads], [t.d for t in writes])

    def dma(self, out, in_, reads=(), writes=(), eng="sp"):
        self.S.op(eng, [("dma_start", dict(out=out, in_=in_))], [t.d for t in reads], [t.d for t in writes], dma=True)

    def act(self, out, in_, func, reads, writes, **kw):
        self.I("act", "activation", reads, writes, out=out, in_=in_, func=func, **kw)

    def tt(self, eng, out, in0, in1, op, reads, writes):
        self.I(eng, "tensor_tensor", reads, writes, out=out, in0=in0, in1=in1, op=op)

    def ts(self, eng, out, in0, s1, op0, reads, writes, s2=None, op1=None):
        kw = dict(out=out, in0=in0, scalar1=s1, scalar2=s2, op0=op0)
        if op1 is not None:
            kw["op1"] = op1
        self.I(eng, "tensor_scalar", reads, writes, **kw)

    def stt(self, out, in0, scalar, in1, op0, op1, reads, writes):
        self.I("dve", "scalar_tensor_tensor", reads, writes, out=out, in0=in0, scalar=scalar, in1=in1, op0=op0, op1=op1)

    def copy(self, eng, out, in_, reads, writes):
        if eng == "act":
            self.act(out, in_, AF.Copy, reads, writes)
        else:
            self.I(eng, "tensor_copy", reads, writes, out=out, in_=in_)

    def mm(self, mms, reads, writes):
        self.S.op("pe", [("matmul", m) for m in mms], [t.d for t in reads], [t.d for t in writes])

    def tr(self, out, in_, ident, reads, writes):
        self.S.op("pe", [("transpose", dict(out=out, in_=in_, identity=ident))], [t.d for t in reads], [t.d for t in writes])


def load_w(cx, dram_ap, kc, cw, key="w"):
    n = kc * cw
    wp = getattr(cx, "wpiece", 2048)
    assert n <= wp
    st = cx.ring(key + "_st", 2, wp, F32)
    wb = cx.ring(key + "_wb", 3, wp, BF16)
    cx.dma(st.ap[:, 0:n], dram_ap.rearrange("p k c -> p (k c)"), [], [st])
    cx.copy("pool", wb.ap[:, 0:n], st.ap[:, 0:n], [st], [wb])
    return wb


def load_w_multi(cx, dram_ap, kc, cw, key="w"):
    per = max(1, getattr(cx, "wpiece", 2048) // cw)
    out = []
    k0 = 0
    while k0 < kc:
        kn = min(per, kc - k0)
        out.append((load_w(cx, dram_ap[:, k0:k0 + kn, :], kn, cw, key), k0, kn))
        k0 += kn
    return out


def load_w_big(cx, dram_ap, kc, cw, name):
    wp = getattr(cx, "wpiece", 2048)
    big = cx.tile(kc * cw, BF16, name)
    per = max(1, wp // cw)
    k0 = 0
    while k0 < kc:
        kn = min(per, kc - k0)
        n = kn * cw
        st = cx.ring("w_st", 2, wp, F32)
        cx.dma(st.ap[:, 0:n], dram_ap[:, k0:k0 + kn, :].rearrange("p k c -> p (k c)"), [], [st])
        cx.copy("pool", big.ap[:, k0 * cw:(k0 + kn) * cw], st.ap[:, 0:n], [st], [big])
        k0 += kn
    return [(big, 0, kc)]


def wslice(pieces, k, cw, m0=0, m1=None):
    m1 = cw if m1 is None else m1
    for wb, k0, kn in pieces:
        if k0 <= k < k0 + kn:
            return wb.ap[:, (k - k0) * cw + m0:(k - k0) * cw + m1]
    raise KeyError(k)


def gemm_block(cx, pieces, cw, act_fn, act_tiles, lo, hi, ks=None, m0=0, m1=None, po=0):
    ps = cx.ps()
    n = hi - lo
    m1 = cw if m1 is None else m1
    if ks is None:
        ks = []
        for wb, k0, kn in pieces:
            ks += list(range(k0, k0 + kn))
    mms = []
    for i, k in enumerate(ks):
        mms.append(dict(out=ps.ap[po:po + m1 - m0, 0:n], lhsT=wslice(pieces, k, cw, m0, m1), rhs=act_fn(k, lo, hi),
                        start=(i == 0), stop=(i == len(ks) - 1)))
    cx.mm(mms, [p[0] for p in pieces] + list(act_tiles), [ps])
    return ps


def rmsnorm_fm(cx, srcs_fn, g_t, out_fn, tiles, ones_bf, nk=KC):
    for (lo, hi) in tiles:
        n = hi - lo
        ps = cx.ps()
        srcs = [srcs_fn(k, lo, hi) for k in range(nk)]
        for k in range(nk):
            sqt = cx.ring("rn_sq", 3, 512, BF16)
            sap, st = srcs[k]
            cx.act(sqt.ap[:, 0:n], sap, AF.Square, [st], [sqt])
            cx.mm([dict(out=ps.ap[:, 0:n], lhsT=ones_bf.ap, rhs=sqt.ap[:, 0:n], start=(k == 0), stop=(k == nk - 1))],
                  [sqt, ones_bf], [ps])
        rinv = cx.ring("rn_rinv", 2, 512, F32)
        cx.act(rinv.ap[:, 0:n], ps.ap[:, 0:n], AF.Ln, [ps], [rinv], scale=1.0 / (nk * 128), bias=EPS)
        cx.act(rinv.ap[:, 0:n], rinv.ap[:, 0:n], AF.Exp, [rinv], [rinv], scale=-0.5)
        for k in range(nk):
            sap, st = srcs[k]
            oap, ot = out_fn(k, lo, hi)
            cx.stt(oap, sap, g_t.ap[:, k:k + 1], rinv.ap[:, 0:n], ALU.mult, ALU.mult, [st, rinv, g_t], [ot])


def load_small(cx, dram_ap, nelem, name, parts=128):
    t = cx.tile(nelem, F32, name, parts)
    cx.dma(t.ap, dram_ap, [], [t])
    return t


def phase_T(cx, io, final, half):
    NT = NTOK_T
    base = cx.mark()
    ones_bf = cx.tile(128, BF16, "ones")
    cx.I("dve", "memset", [], [ones_bf], ap=ones_bf.ap, constant=1.0)
    g1 = load_small(cx, io["g_attn"], 16, "g1")
    g2 = load_small(cx, io["g_ffn"], 16, "g2")
    g3 = load_small(cx, io["g_next"], 16, "g3")
    convw = load_small(cx, io["fconv"], 88 * 3, "convw")
    mg_t = cx.tile(KC * NT, BF16, "merged")
    h_t = cx.tile(KC * NT, BF16, "h")
    mv = mg_t.ap.rearrange("p (k t) -> p k t", k=KC)
    hv = h_t.ap.rearrange("p (k t) -> p k t", k=KC)
    M0 = cx.mark()
    y_t = cx.tile(20 * NT, BF16, "y")
    yv = y_t.ap.rearrange("p (k t) -> p k t", k=20)
    xT = io["xT"].rearrange("(k p) t -> p k t", p=128)
    yT = io["ybuf"].rearrange("(k p) t -> p k t", p=128)
    t0 = 1024 * half
    xdeps = io.get("x_deps", [])
    ydeps = io.get("y_deps", [])

    def load_x(dst_ap, dst_t, k, lo, hi):
        if lo < 2 and half == 0:
            cx.I("dve", "memset", [], [dst_t], ap=dst_ap[:, 0:2 - lo], constant=0.0)
            if hi > 2:
                cx.dma(dst_ap[:, 2 - lo:hi - lo], xT[:, k, 0:hi - 2], xdeps, [dst_t])
        else:
            cx.dma(dst_ap, xT[:, k, t0 - 2 + lo:t0 - 2 + hi], xdeps, [dst_t])

    M1 = cx.mark()
    for k in range(20):
        cx.dma(yv[:, k, 2:NT], yT[:, k, t0:t0 + 1024], ydeps, [y_t])
    if half == 0:
        cx.I("dve", "memset", [], [y_t], ap=yv[:, :, 0:2], constant=0.0)
    else:
        for k in range(20):
            cx.dma(yv[:, k, 0:2], yT[:, k, t0 - 2:t0], ydeps, [y_t])
    for (lo, hi) in TT:
        n = hi - lo
        xt = cx.ring("xtile", 1, KC * 512, F32)
        xtv = xt.ap.rearrange("p (k t) -> p k t", k=KC)
        for k in range(KC):
            load_x(xtv[:, k, 0:n], xt, k, lo, hi)
        rmsnorm_fm(cx, lambda k, l, h_: (xtv[:, k, 0:h_ - l], xt), g1, lambda k, l, h_: (hv[:, k, l:h_], h_t),
                   [(lo, hi)], ones_bf)
    cx.release(M1)
    wg = io["wg"]
    wp = io["wp"]
    ksl = [(0, 8), (8, 12), (12, 20)]
    for j in range(16):
        gts = []
        for br in range(3):
            pcs = load_w_multi(cx, wg[br * 16 + j], KC, 128)
            gt = cx.ring("gate", 4, NT, BF16)
            for (lo, hi) in TT:
                ps = gemm_block(cx, pcs, 128, lambda k, l, h_: hv[:, k, l:h_], [h_t], lo, hi)
                cx.act(gt.ap[:, lo:hi], ps.ap[:, 0:hi - lo], AF.Sigmoid, [ps], [gt])
            gts.append(gt)
        pw = load_w_multi(cx, wp[j], 20, 128)
        for (lo, hi) in TT:
            n = hi - lo
            tmp = cx.ring("mtmp", 2, 512, F32)
            for br in range(3):
                k0, k1 = ksl[br]
                ps = gemm_block(cx, pw, 128, lambda k, l, h_: yv[:, k, l:h_], [y_t], lo, hi, ks=list(range(k0, k1)))
                gt = gts[br]
                if br == 0:
                    cx.tt("dve", tmp.ap[:, 0:n], ps.ap[:, 0:n], gt.ap[:, lo:hi], ALU.mult, [ps, gt], [tmp])
                else:
                    t2 = cx.ring("mtmp2", 2, 512, F32)
                    cx.tt("dve", t2.ap[:, 0:n], ps.ap[:, 0:n], gt.ap[:, lo:hi], ALU.mult, [ps, gt], [t2])
                    if br == 1:
                        cx.tt("pool", tmp.ap[:, 0:n], tmp.ap[:, 0:n], t2.ap[:, 0:n], ALU.add, [tmp, t2], [tmp])
                    else:
                        cx.tt("pool", mv[:, j, lo:hi], tmp.ap[:, 0:n], t2.ap[:, 0:n], ALU.add, [tmp, t2], [mg_t])
    cx.release(M0)
    x_sb = cx.tile(KC * NT, F32, "x_sb")
    xv = x_sb.ap.rearrange("p (k t) -> p k t", k=KC)
    M2 = cx.mark()
    wo = io["wo"]
    for j in range(16):
        pcs = load_w_multi(cx, wo[j], KC, 128)
        xin = cx.ring("xin", 2, NT, F32)
        load_x(xin.ap, xin, j, 0, NT)
        for (lo, hi) in TT:
            ps = gemm_block(cx, pcs, 128, lambda k, l, h_: mv[:, k, l:h_], [mg_t], lo, hi)
            cx.tt("dve", xv[:, j, lo:hi], ps.ap[:, 0:hi - lo], xin.ap[:, lo:hi], ALU.add, [ps, xin], [x_sb])
    cx.release(M2)
    rmsnorm_fm(cx, lambda k, l, h_: (xv[:, k, l:h_], x_sb), g2, lambda k, l, h_: (hv[:, k, l:h_], h_t), TT, ones_bf)
    GRP = 11
    a_t = T(mg_t.ap, mg_t.d)
    av = a_t.ap[:, 0:GRP * 1024].rearrange("p (k t) -> p k t", k=GRP)
    wup = io["wup"]
    wdn = io["wdn"]
    cwv = convw.ap.rearrange("p (b t) -> p b t", t=3)
    for g in range(NFB // GRP):
        for jj in range(GRP):
            jb = g * GRP + jj
            us = []
            for part in range(2):
                blk = part * NFB + jb
                pcs = load_w_multi(cx, wup[blk], KC, 128)
                u = cx.ring("u_f32", 2, NT, F32)
                for (lo, hi) in TT:
                    ps = gemm_block(cx, pcs, 128, lambda k, l, h_: hv[:, k, l:h_], [h_t], lo, hi)
                    cx.copy("act", u.ap[:, lo:hi], ps.ap[:, 0:hi - lo], [ps], [u])
                c = cx.ring("c_f32", 2, 1024, F32)
                cx.ts("dve", c.ap, u.ap[:, 2:NT], cwv[:, blk, 2:3], ALU.mult, [u, convw], [c])
                cx.stt(c.ap, u.ap[:, 1:NT - 1], cwv[:, blk, 1:2], c.ap, ALU.mult, ALU.add, [u, convw, c], [c])
                cx.stt(c.ap, u.ap[:, 0:NT - 2], cwv[:, blk, 0:1], c.ap, ALU.mult, ALU.add, [u, convw, c], [c])
                us.append(c)
            sg = cx.ring("silu", 2, 1024, F32)
            cx.act(sg.ap, us[0].ap, AF.Silu, [us[0]], [sg])
            cx.tt("pool", av[:, jj, :], sg.ap, us[1].ap, ALU.mult, [sg, us[1]], [a_t])
        for j in range(16):
            pcs = load_w_multi(cx, wdn[j][:, g * GRP:(g + 1) * GRP, :], GRP, 128)
            for ti in range(2):
                lo, hi = ti * 512, ti * 512 + 512
                ps = gemm_block(cx, pcs, 128, lambda k, l, h_: av[:, k, l:h_], [a_t], lo, hi)
                cx.tt("dve", xv[:, j, 2 + lo:2 + hi], ps.ap[:, 0:512], xv[:, j, 2 + lo:2 + hi], ALU.add, [ps, x_sb], [x_sb])
    cx.release(M2)
    outT = io["outT"].rearrange("(k p) t -> p k t", p=128)
    odeps = io.get("out_deps", [])
    if not final:
        for k in range(KC):
            cx.dma(outT[:, k, t0:t0 + 1024], xv[:, k, 2:NT], [x_sb], odeps)
    else:
        o_t = T(h_t.ap.bitcast(F32)[:, 0:KC * 512], h_t.d)
        ov = o_t.ap.rearrange("p (k t) -> p k t", k=KC)
        for ti in range(2):
            lo, hi = 2 + ti * 512, 2 + ti * 512 + 512
            rmsnorm_fm(cx, lambda k, l, h_: (xv[:, k, l:h_], x_sb), g3, lambda k, l, h_: (ov[:, k, 0:512], o_t),
                       [(lo, hi)], ones_bf)
            for k in range(KC):
                cx.dma(outT[:, k, t0 + lo - 2:t0 + lo - 2 + 512], ov[:, k, :], [o_t], odeps)
    cx.release(base)


import os
ATT_G = os.environ.get("ATT_G", "012")
ATT_STOP = int(os.environ.get("ATT_STOP", "99"))
ATT_SUB = int(os.environ.get("ATT_SUB", "3"))
RW_DBG = os.environ.get("RW_DBG", "")
ATT_V = os.environ.get("ATT_V", "ab")
NCH = 16
C_ID, C_MUI, C_MUS, C_MLI, C_BO, C_BS, C_S127, C_MLS = 0, 128, 256, 384, 512, 640, 642, 770
C_BD8, C_O16, C_O32, C_O64, C_O128 = 898, 1026, 1154, 1282, 1410
CST_N = 1538


def make_consts():
    c = np.zeros((128, CST_N), np.float32)
    j = np.arange(128)[:, None]
    i = np.arange(128)[None, :]
    c[:, C_ID:C_ID + 128] = (i == j)
    c[:, C_MUI:C_MUI + 128] = (i >= j)
    c[:, C_MUS:C_MUS + 128] = (i > j)
    c[:, C_MLI:C_MLI + 128] = (j >= i)
    c[:, C_BO:C_BO + 128] = ((i // 64) == (j // 64))
    c[:, C_BS:C_BS + 2] = (np.arange(2)[None, :] == (j // 64))
    c[:, C_S127:C_S127 + 128] = (j == 127)
    c[:, C_MLS:C_MLS + 128] = (j > i)
    bd = lambda s_: (i // s_) == (j // s_)
    c[:, C_BD8:C_BD8 + 128] = bd(8)
    c[:, C_O16:C_O16 + 128] = bd(16) & ~bd(8)
    c[:, C_O32:C_O32 + 128] = bd(32) & ~bd(16)
    c[:, C_O64:C_O64 + 128] = bd(64) & ~bd(32)
    c[:, C_O128:C_O128 + 128] = ~bd(64)
    return c


def make_rope():
    half = 16
    inv = 500000.0 ** (-np.arange(half, dtype=np.float32) / half)
    ang = np.arange(SEQ, dtype=np.float32)[None, :] * inv[:, None]
    cos = np.cos(ang).astype(np.float32)
    sin = np.sin(ang).astype(np.float32)
    r = np.zeros((32, 2 * SEQ), np.float32)
    r[0:16, 0:SEQ] = cos
    r[16:32, 0:SEQ] = cos
    r[0:16, SEQ:] = -sin
    r[16:32, SEQ:] = sin
    return r


class Consts:
    pass


def setup_consts(cx, io):
    K = Consts()
    cf = load_small(cx, io["cst"], CST_N, "cst_f32")
    cb = cx.tile(CST_N, BF16, "cst_bf")
    cx.copy("dve", cb.ap, cf.ap, [cf], [cb])
    K.cf, K.cb = cf, cb
    K.ones = cx.tile(128, BF16, "ones")
    cx.I("dve", "memset", [], [K.ones], ap=K.ones.ap, constant=1.0)
    K.m4 = {}
    for nm_, off in (("id", C_ID), ("bd8", C_BD8), ("o16", C_O16), ("o32", C_O32), ("o64", C_O64), ("o128", C_O128)):
        t = cx.tile(512, BF16, "m4" + nm_)
        for q in range(4):
            cx.copy("pool", t.ap[:, q * 128:(q + 1) * 128], cb.ap[:, off:off + 128], [cb], [t])
        K.m4[nm_] = t
    return K


def dplr_head(cx, K, nm, dk, dv, pb, Kg, Bg, Ag, Rg, gtiles, Wst, Win, wtiles, As, Rs, egam, Kn, Bn, V, ntiles, PC, pctiles,
              scale_all, out_cb):
    m0 = cx.mark()
    idb = K.cb.ap[:, C_ID:C_ID + 128]
    neg = Bg is None
    AakT = cx.tile(NCH * 128, BF16, nm + "aak")
    ArkT = cx.tile(NCH * 128, BF16, nm + "ark")
    ArbT = cx.tile(NCH * 128, BF16, nm + "arb")
    TT_ = cx.tile(NCH * 128, BF16, nm + "TT")
    sl = lambda t, c: t.ap[:, c * 128:(c + 1) * 128]
    m1 = cx.mark()
    HC = 8
    tl_ = [cx.tile(HC * 128, BF16, nm + "iv%d" % i) for i in range(12)]
    U, N, Ua, Na, Nb, Ub_, Nc, Uc, P, Q, Z1b, Z2b = tl_
    hs = lambda t, q4: t.ap[:, q4 * 512:(q4 + 1) * 512]
    h1 = lambda t, cl: t.ap[:, cl * 128:(cl + 1) * 128]

    def mm4(dst_ps, lhs_t, rhs_t, q4):
        cx.mm([dict(out=dst_ps.ap[:, q * 128:(q + 1) * 128], lhsT=h1(lhs_t, q4 * 4 + q), rhs=h1(rhs_t, q4 * 4 + q), start=True, stop=True)
               for q in range(4)], [lhs_t, rhs_t], [dst_ps])

    for half in range(NCH // HC):
        for cl in range(HC):
            c = half * HC + cl
            ps = cx.ps()
            mms = [dict(out=ps.ap[:, 0:128], lhsT=Kg(c), rhs=Ag(c), start=True, stop=True),
                   dict(out=ps.ap[:, 128:256], lhsT=Kg(c), rhs=Rg(c), start=True, stop=True)]
            if not neg:
                mms += [dict(out=ps.ap[:, 256:384], lhsT=Bg(c), rhs=Ag(c), start=True, stop=True),
                        dict(out=ps.ap[:, 384:512], lhsT=Bg(c), rhs=Rg(c), start=True, stop=True)]
            cx.mm(mms, gtiles, [ps])
            cx.tt("dve", sl(AakT, c), ps.ap[:, 0:128], Wst(c), ALU.mult, [ps] + wtiles, [AakT])
            cx.tt("dve", sl(ArkT, c), ps.ap[:, 128:256], Win(c), ALU.mult, [ps] + wtiles, [ArkT])
            if neg:
                cx.stt(h1(U, cl), ps.ap[:, 0:128], -1.0, Wst(c), ALU.mult, ALU.mult, [ps] + wtiles, [U])
                cx.stt(sl(ArbT, c), ps.ap[:, 128:256], -1.0, Win(c), ALU.mult, ALU.mult, [ps] + wtiles, [ArbT])
            else:
                cx.tt("dve", h1(U, cl), ps.ap[:, 256:384], Wst(c), ALU.mult, [ps] + wtiles, [U])
                cx.tt("dve", sl(ArbT, c), ps.ap[:, 384:512], Win(c), ALU.mult, [ps] + wtiles, [ArbT])
        for q4 in range(HC // 4):
            ps = cx.ps()
            pv = ps.ap.bitcast(BF16)
            for q in range(4):
                cx.tr(pv[:, q * 128:(q + 1) * 128], h1(U, q4 * 4 + q), idb, [U, K.cb], [ps])
            cx.copy("act", hs(N, q4), pv[:, 0:512], [ps], [N])
        for q4 in range(HC // 4):
            cx.tt("pool", hs(Ua, q4), hs(U, q4), K.m4["bd8"].ap, ALU.mult, [U, K.m4["bd8"]], [Ua])
            cx.tt("pool", hs(Na, q4), hs(N, q4), K.m4["bd8"].ap, ALU.mult, [N, K.m4["bd8"]], [Na])
        for (dn, du, sn, su) in ((Nb, Ub_, Na, Ua), (Nc, Uc, Nb, Ub_)):
            for q4 in range(HC // 4):
                ps = cx.ps()
                mm4(ps, su, sn, q4)
                cx.copy("act", hs(dn, q4), ps.ap[:, 0:512], [ps], [dn])
                ps = cx.ps()
                mm4(ps, sn, su, q4)
                cx.copy("act", hs(du, q4), ps.ap[:, 0:512], [ps], [du])
        for q4 in range(HC // 4):
            cx.tt("pool", hs(P, q4), hs(Ua, q4), K.m4["id"].ap, ALU.add, [Ua, K.m4["id"]], [P])
            cx.tt("pool", hs(Q, q4), hs(Na, q4), K.m4["id"].ap, ALU.add, [Na, K.m4["id"]], [Q])
        for (ln, lu) in ((Nb, Ub_), (Nc, Uc)):
            for q4 in range(HC // 4):
                ps = cx.ps()
                mm4(ps, ln, P, q4)
                cx.tt("dve", hs(P, q4), ps.ap[:, 0:512], hs(P, q4), ALU.add, [ps, P], [P])
                ps = cx.ps()
                mm4(ps, lu, Q, q4)
                cx.tt("dve", hs(Q, q4), ps.ap[:, 0:512], hs(Q, q4), ALU.add, [ps, Q], [Q])
        NO, UO, P2, Q2 = Ua, Na, Nb, Ub_
        curP, curQ, nxtP, nxtQ = P, Q, P2, Q2
        for li, mk in enumerate(("o16", "o32", "o64", "o128")):
            last = (li == 3)
            for q4 in range(HC // 4):
                cx.tt("pool", hs(NO, q4), hs(N, q4), K.m4[mk].ap, ALU.mult, [N, K.m4[mk]], [NO])
                if not last:
                    cx.tt("pool", hs(UO, q4), hs(U, q4), K.m4[mk].ap, ALU.mult, [U, K.m4[mk]], [UO])
            for q4 in range(HC // 4):
                ps = cx.ps()
                mm4(ps, NO, curP, q4)
                cx.copy("act", hs(Z1b, q4), ps.ap[:, 0:512], [ps], [Z1b])
                if not last:
                    ps = cx.ps()
                    mm4(ps, UO, curQ, q4)
                    cx.copy("act", hs(Z2b, q4), ps.ap[:, 0:512], [ps], [Z2b])
            for q4 in range(HC // 4):
                ps = cx.ps()
                mm4(ps, curQ, Z1b, q4)
                if last:
                    c0_ = (half * HC + q4 * 4) * 128
                    cx.tt("dve", TT_.ap[:, c0_:c0_ + 512], ps.ap[:, 0:512], hs(curP, q4), ALU.add, [ps, curP], [TT_])
                else:
                    cx.tt("dve", hs(nxtP, q4), ps.ap[:, 0:512], hs(curP, q4), ALU.add, [ps, curP], [nxtP])
                    ps = cx.ps()
                    mm4(ps, curP, Z2b, q4)
                    cx.tt("dve", hs(nxtQ, q4), ps.ap[:, 0:512], hs(curQ, q4), ALU.add, [ps, curQ], [nxtQ])
            curP, curQ, nxtP, nxtQ = nxtP, nxtQ, curP, curQ
    cx.release(m1)
    AV = None
    if egam is not None:
        AV = cx.tile(NCH * dv, F32, nm + "AV")
        for c in range(NCH):
            ps = cx.ps()
            cx.mm([dict(out=ps.ap[:, 0:dv], lhsT=sl(AakT, c), rhs=V(c), start=True, stop=True)], [AakT] + ntiles, [ps])
            cx.copy("act", AV.ap[:, c * dv:(c + 1) * dv], ps.ap[:, 0:dv], [ps], [AV])
    St = cx.tile(dv, F32, nm + "S")
    Sb = cx.tile(dv, BF16, nm + "Sb")
    cx.I("dve", "memset", [], [St], ap=St.ap, constant=0.0)
    cx.I("dve", "memset", [], [Sb], ap=Sb.ap, constant=0.0)
    rows = slice(pb, pb + dk)
    for c in range(NCH):
        Xb = cx.ring(nm + "Xb", 2, dv, BF16)
        Ub = cx.ring(nm + "Ub", 2, dv, BF16)
        psX = cx.ps()
        if egam is None:
            cx.mm([dict(out=psX.ap[:, 0:dv], lhsT=As(c), rhs=Sb.ap[rows, :], start=True, stop=False),
                   dict(out=psX.ap[:, 0:dv], lhsT=sl(AakT, c), rhs=V(c), start=False, stop=True)],
                  gtiles + [Sb, AakT] + ntiles, [psX])
            cx.copy("act", Xb.ap, psX.ap[:, 0:dv], [psX], [Xb])
        else:
            cx.mm([dict(out=psX.ap[:, 0:dv], lhsT=As(c), rhs=Sb.ap[rows, :], start=True, stop=True)], gtiles + [Sb], [psX])
            cx.stt(Xb.ap, psX.ap[:, 0:dv], egam(c), AV.ap[:, c * dv:(c + 1) * dv], ALU.mult, ALU.add, [psX, AV] + pctiles, [Xb])
        psU = cx.ps()
        cx.mm([dict(out=psU.ap[:, 0:dv], lhsT=sl(TT_, c), rhs=Xb.ap, start=True, stop=True)], [TT_, Xb], [psU])
        cx.copy("act", Ub.ap, psU.ap[:, 0:dv], [psU], [Ub])
        if egam is None:
            psY = cx.ps()
            cx.mm([dict(out=psY.ap[:, 0:dv], lhsT=Rs(c), rhs=Sb.ap[rows, :], start=True, stop=False),
                   dict(out=psY.ap[:, 0:dv], lhsT=sl(ArbT, c), rhs=Ub.ap, start=False, stop=False),
                   dict(out=psY.ap[:, 0:dv], lhsT=sl(ArkT, c), rhs=V(c), start=False, stop=True)],
                  gtiles + [Sb, ArbT, ArkT, Ub] + ntiles, [psY])
            out_cb(c, psY.ap[:, 0:dv], psY)
        else:
            psY1 = cx.ps()
            cx.mm([dict(out=psY1.ap[:, 0:dv], lhsT=Rs(c), rhs=Sb.ap[rows, :], start=True, stop=True)], gtiles + [Sb], [psY1])
            psY2 = cx.ps()
            cx.mm([dict(out=psY2.ap[:, 0:dv], lhsT=sl(ArbT, c), rhs=Ub.ap, start=True, stop=False),
                   dict(out=psY2.ap[:, 0:dv], lhsT=sl(ArkT, c), rhs=V(c), start=False, stop=True)],
                  [ArbT, ArkT, Ub] + ntiles, [psY2])
            y2 = cx.ring(nm + "y2", 2, dv, F32)
            cx.copy("act", y2.ap, psY2.ap[:, 0:dv], [psY2], [y2])
            yo = cx.ring(nm + "yo", 2, dv, F32)
            cx.stt(yo.ap, psY1.ap[:, 0:dv], egam(c), y2.ap, ALU.mult, ALU.add, [psY1, y2] + pctiles, [yo])
            out_cb(c, yo.ap, yo)
        psS = cx.ps()
        cx.mm([dict(out=psS.ap[rows, 0:dv], lhsT=Bn(c), rhs=Ub.ap, start=True, stop=False),
               dict(out=psS.ap[rows, 0:dv], lhsT=Kn(c), rhs=V(c), start=False, stop=True)], ntiles + [Ub], [psS])
        if scale_all:
            cx.tt("dve", St.ap[rows, :], psS.ap[rows, 0:dv], St.ap[rows, :], ALU.add, [psS, St], [St])
            cx.ts("dve", St.ap[rows, :], St.ap[rows, :], PC(c), ALU.mult, [St] + pctiles, [St])
        else:
            cx.stt(St.ap[rows, :], St.ap[rows, :], PC(c), psS.ap[rows, 0:dv], ALU.mult, ALU.add, [St, psS] + pctiles, [St])
        cx.copy("act", Sb.ap[rows, :], St.ap[rows, :], [St], [Sb])
    cx.release(m0)


def ln_exp_rinv(cx, out_ap, in_ap, reads, wt, scale=1.0, bias=1e-24):
    cx.act(out_ap, in_ap, AF.Ln, reads, [wt], scale=scale, bias=bias)
    cx.act(out_ap, out_ap, AF.Exp, [wt], [wt], scale=-0.5)


def mixer_attention(cx, K, io, hT, hv):
    m0 = cx.mark()
    wat = io["wat"]
    wsw = io["wsw"]
    rope = load_small(cx, io["rope"], 2 * SEQ, "rope", parts=32)
    cosv = rope.ap[:, 0:SEQ]
    sinv = rope.ap[:, SEQ:2 * SEQ]
    mdiag = K.cb.ap[:, C_MUI:C_MUI + 128]
    mprev = K.cb.ap[:, C_MLI:C_MLI + 128]
    mcomb = cx.tile(256, BF16, "mcomb")
    cx.copy("dve", mcomb.ap[:, 0:128], mdiag, [K.cb], [mcomb])
    cx.copy("dve", mcomb.ap[:, 128:256], mprev, [K.cb], [mcomb])
    if ATT_STOP <= 1:
        return
    DIL = [1, 4, 16]
    scale = 128.0 ** -0.5
    yout = io["ybuf"].rearrange("(k p) t -> p k t", p=128)
    for hl in range(2):
        m1 = cx.mark()
        qT = [cx.tile(SEQ, BF16, "qT%d" % g) for g in range(3)]
        kT = [cx.tile(SEQ, BF16, "kT%d" % g) for g in range(3)]
        Vt = [cx.tile(NCH * 128, BF16, "V%d" % g) for g in range(3)]
        for g in range(3):
            d = DIL[g]
            for which, dst in ((0, qT[g]), (1, kT[g])):
                pcs = load_w_multi(cx, wat[(hl * 3 + g) * 3 + which], KC, 128)
                pcs_sw = load_w_multi(cx, wsw[(hl * 3 + g) * 2 + which], KC, 32)
                for tb in range(4):
                    lo, hi = tb * 512, tb * 512 + 512
                    ps = gemm_block(cx, pcs, 128, lambda k, l, h_: hv[:, k, l:h_], [hT], lo, hi)
                    if ATT_SUB >= 1:
                        ps2 = gemm_block(cx, pcs_sw, 32, lambda k, l, h_: hv[:, k, l:h_], [hT], lo, hi)
                    t1 = cx.ring("rp1", 2, 512, F32)
                    t2 = cx.ring("rp2", 2, 512, F32)
                    if ATT_SUB >= 2:
                        if "a" in ATT_V:
                            cx.tt("dve", t1.ap[0:32, :], ps.ap[0:32, 0:512], cosv[:, lo:hi], ALU.mult, [ps, rope], [t1])
                        if "b" in ATT_V:
                            cx.tt("dve", t2.ap[0:32, :], ps2.ap[0:32, 0:512], sinv[:, lo:hi], ALU.mult, [ps2, rope], [t2])
                        if "c" in ATT_V:
                            cx.tt("dve", t1.ap[0:32, :], ps.ap[0:32, 0:512], K.cf.ap[0:32, 0:512], ALU.mult, [ps, K.cf], [t1])
                        if "e" in ATT_V:
                            cx.tt("dve", t1.ap[:, :], ps.ap[:, 0:512], K.cf.ap[:, 0:512], ALU.mult, [ps, K.cf], [t1])
                        if "g" in ATT_V:
                            cx.tt("dve", t1.ap[0:32, :], ps.ap[0:32, 0:512], K.cf.ap[0:32, 0:512], ALU.mult, [ps, K.cf], [t1])
                        if "d" in ATT_V:
                            cx.tt("dve", t1.ap[0:32, :], t2.ap[0:32, :], cosv[:, lo:hi], ALU.mult, [t2, rope], [t1])
                    cx.copy("act", dst.ap[:, lo:hi], ps.ap[:, 0:512], [ps, t1] if "s" in ATT_V else [ps], [dst])
                    if ATT_SUB >= 3:
                        cx.tt("pool", dst.ap[0:32, lo:hi], t1.ap[0:32, :], t2.ap[0:32, :], ALU.add, [t1, t2], [dst])
            if ATT_STOP <= 2:
                return
            pcs = load_w_multi(cx, wat[(hl * 3 + g) * 3 + 2], KC, 128)
            nb = NCH // d
            for b4 in range(4):
                ps = cx.ps()
                mms = []
                for q in range(4):
                    blk = b4 * 4 + q
                    r, b = blk // nb, blk % nb
                    t0 = r + d * 128 * b
                    for k in range(KC):
                        mms.append(dict(out=ps.ap[:, q * 128:(q + 1) * 128], lhsT=hv[:, k, t0:t0 + d * 127 + 1:d],
                                        rhs=wslice(pcs, k, 128), start=(k == 0), stop=(k == KC - 1)))
                cx.mm(mms, [p[0] for p in pcs] + [hT], [ps])
                cx.copy("act", Vt[g].ap[:, b4 * 512:(b4 + 1) * 512], ps.ap[:, 0:512], [ps], [Vt[g]])
        if ATT_STOP <= 3:
            return
        for Tb in range(4):
            pso = cx.ps(pin=True)
            psd = cx.ps(pin=True)
            first = [True]

            def unit(g, kslices, qslice, nq, masks):
                pss = cx.ps()
                nk = len(kslices)
                cx.mm([dict(out=pss.ap[:, i * nq:(i + 1) * nq], lhsT=kT[g].ap[:, ks], rhs=qT[g].ap[:, qslice], start=True, stop=True)
                       for i, (ks, vb) in enumerate(kslices)], [kT[g], qT[g]], [pss])
                pe_ = cx.ring("pexp", 3, 256, BF16)
                pm = cx.ring("pmask", 3, 256, BF16)
                cx.act(pe_.ap[:, 0:nk * nq], pss.ap[:, 0:nk * nq], AF.Exp, [pss], [pe_], scale=scale)
                cx.tt("pool", pm.ap[:, 0:nk * nq], pe_.ap[:, 0:nk * nq], masks, ALU.mult, [pe_, mcomb, K.cb], [pm])
                mmo, mmd = [], []
                qs0 = qslice.start - Tb * 512
                st = qslice.step or 1
                ocols = slice(qs0, qs0 + (nq - 1) * st + 1, st)
                for i, (ks, vb) in enumerate(kslices):
                    f = first[0]
                    first[0] = False
                    mmo.append(dict(out=pso.ap[:, ocols], lhsT=Vt[g].ap[:, vb * 128:(vb + 1) * 128], rhs=pm.ap[:, i * nq:(i + 1) * nq],
                                    start=f, stop=False, skip_group_check=True))
                    mmd.append(dict(out=psd.ap[:, ocols], lhsT=K.ones.ap, rhs=pm.ap[:, i * nq:(i + 1) * nq],
                                    start=f, stop=False, skip_group_check=True))
                cx.mm(mmo, [Vt[g], pm], [pso])
                cx.mm(mmd, [K.ones, pm], [psd])

            for qb in range(4 * Tb, 4 * Tb + 4):
                ks = [(slice(qb * 128, qb * 128 + 128), qb)]
                if qb > 0:
                    ks.append((slice((qb - 1) * 128, qb * 128), qb - 1))
                unit(0, ks, slice(qb * 128, qb * 128 + 128), 128, mcomb.ap[:, 0:128 * len(ks)])
            for r in range(4 if "1" in ATT_G else 0):
                tq = r + 4 * 128 * Tb
                ks = [(slice(tq, tq + 4 * 127 + 1, 4), r * 4 + Tb)]
                if Tb > 0:
                    tk = r + 4 * 128 * (Tb - 1)
                    ks.append((slice(tk, tk + 4 * 127 + 1, 4), r * 4 + Tb - 1))
                unit(1, ks, slice(tq, tq + 4 * 127 + 1, 4), 128, mcomb.ap[:, 0:128 * len(ks)])
            for r in range(16 if "2" in ATT_G else 0):
                tq = r + 16 * 32 * Tb
                ks = [(slice(r, r + 16 * 127 + 1, 16), r)]
                unit(2, ks, slice(tq, tq + 16 * 31 + 1, 16), 32, mdiag[:, 32 * Tb:32 * Tb + 32])
            rd = cx.ring("rden", 2, 512, F32)
            cx.I("dve", "reciprocal", [psd], [rd], out=rd.ap, in_=psd.ap[:, 0:512])
            yo = cx.ring("ybo", 2, 512, BF16)
            cx.tt("dve", yo.ap, pso.ap[:, 0:512], rd.ap, ALU.mult, [pso, rd], [yo])
            cx.dma(yout[:, 8 + 2 * io["s"] + hl, Tb * 512:(Tb + 1) * 512], yo.ap, [yo], [io["ydep"]["b"]])
            cx.unpin_all()
            if ATT_STOP <= 4:
                return
        cx.release(m1)
    cx.release(m0)


def mixer_gdn(cx, K, io, hT, hv):
    m0 = cx.mark()
    wc = io["wc"]
    wba = io["wba"]
    gconv = load_small(cx, io["gconv"], 12 * 4, "gconv")
    gcv = gconv.ap.rearrange("p (b t) -> p b t", t=4)
    gpar = load_small(cx, io["gpar"], 8 + 128, "gpar")
    mui = K.cf.ap[:, C_MUI:C_MUI + 128]
    mus = K.cf.ap[:, C_MUS:C_MUS + 128]
    mls = K.cf.ap[:, C_MLS:C_MLS + 128]
    idb = K.cb.ap[:, C_ID:C_ID + 128]
    pcs = load_w_multi(cx, wba, KC, 8)
    ps = cx.ps()
    mms = []
    for c in range(NCH):
        for k in range(KC):
            mms.append(dict(out=ps.ap[:, c * 8:(c + 1) * 8], lhsT=hv[:, k, c * 128:(c + 1) * 128], rhs=wslice(pcs, k, 8),
                            start=(k == 0), stop=(k == KC - 1)))
    cx.mm(mms, [p[0] for p in pcs] + [hT], [ps])
    ba = cx.tile(NCH * 8, F32, "ba")
    cx.copy("act", ba.ap, ps.ap[:, 0:NCH * 8], [ps], [ba])
    bav = ba.ap.rearrange("p (c e) -> p c e", e=8)
    beta = cx.tile(NCH * 4, F32, "beta")
    betav = beta.ap.rearrange("p (c e) -> p c e", e=4)
    cx.act(betav, bav[:, :, 0:4], AF.Sigmoid, [ba], [beta])
    gg = cx.tile(NCH * 4, F32, "gg")
    ggv = gg.ap.rearrange("p (c e) -> p c e", e=4)
    for hh in range(4):
        cx.act(ggv[:, :, hh], bav[:, :, 4 + hh], AF.Exp, [ba, gpar], [gg], bias=gpar.ap[:, 4 + hh:5 + hh])
    cx.act(gg.ap, gg.ap, AF.Ln, [gg], [gg], bias=1.0)
    ea = cx.tile(4, F32, "expA")
    cx.act(ea.ap, gpar.ap[:, 0:4], AF.Exp, [gpar], [ea])
    for hh in range(4):
        cx.ts("dve", ggv[:, :, hh], ggv[:, :, hh], ea.ap[:, hh:hh + 1], ALU.mult, [gg, ea], [gg], s2=-1.0, op1=ALU.mult)
    ps = cx.ps()
    cx.mm([dict(out=ps.ap[:, 0:64], lhsT=mui, rhs=gg.ap, start=True, stop=True)], [K.cf, gg], [ps])
    gam = cx.tile(64, F32, "gam")
    cx.copy("act", gam.ap, ps.ap[:, 0:64], [ps], [gam])
    egam = cx.tile(64, F32, "egam")
    cx.act(egam.ap, gam.ap, AF.Exp, [gam], [egam])
    ps = cx.ps()
    cx.mm([dict(out=ps.ap[:, 0:64], lhsT=K.cf.ap[:, C_S127:C_S127 + 128], rhs=gam.ap, start=True, stop=True)], [K.cf, gam], [ps])
    pcall = cx.tile(64, F32, "pcall")
    cx.act(pcall.ap, ps.ap[:, 0:64], AF.Exp, [ps], [pcall])
    wk = cx.tile(64, F32, "wk")
    cx.tt("dve", wk.ap, ps.ap[:, 0:64], gam.ap, ALU.subtract, [ps, gam], [wk])
    cx.act(wk.ap, wk.ap, AF.Exp, [wk], [wk])
    cx.tt("dve", wk.ap, wk.ap, beta.ap, ALU.mult, [wk, beta], [wk])
    nwk = cx.tile(64, F32, "nwk")
    cx.ts("dve", nwk.ap, wk.ap, -1.0, ALU.mult, [wk], [nwk])
    yout = io["ybuf"].rearrange("(k p) t -> p k t", p=128)
    for hh in range(4):
        m1 = cx.mark()
        qT = cx.tile(SEQ, BF16, "gq")
        kT = cx.tile(SEQ, BF16, "gk")
        Vn = cx.tile(NCH * 128, BF16, "gV")
        Kn = cx.tile(NCH * 128, BF16, "gKn")
        Bn = cx.tile(NCH * 128, BF16, "gBn")
        gate = cx.tile(NCH * 128, BF16, "ggate")
        Wst = cx.tile(NCH * 128, F32, "gWst")
        Win = cx.tile(NCH * 128, F32, "gWin")
        m2 = cx.mark()
        vT = cx.tile(SEQ, BF16, "gvT")
        for which, dst in ((0, qT), (1, kT), (2, vT)):
            pcs = load_w_multi(cx, wc[which * 4 + hh], KC, 128)
            zp = cx.ring("gzp", 1, 3 + SEQ, F32)
            cx.I("dve", "memset", [], [zp], ap=zp.ap[:, 0:3], constant=0.0)
            for tb in range(4):
                lo, hi = tb * 512, tb * 512 + 512
                ps = gemm_block(cx, pcs, 128, lambda k, l, h_: hv[:, k, l:h_], [hT], lo, hi)
                cx.copy("act", zp.ap[:, 3 + lo:3 + hi], ps.ap[:, 0:512], [ps], [zp])
            cv = cx.ring("gcv", 1, SEQ, F32)
            bi = which * 4 + hh
            cx.ts("dve", cv.ap, zp.ap[:, 3:3 + SEQ], gcv[:, bi, 3:4], ALU.mult, [zp, gconv], [cv])
            for tap in range(3):
                cx.stt(cv.ap, zp.ap[:, tap:tap + SEQ], gcv[:, bi, tap:tap + 1], cv.ap, ALU.mult, ALU.add, [zp, gconv, cv], [cv])
            cx.act(cv.ap, cv.ap, AF.Silu, [cv], [cv])
            if which == 2:
                cx.copy("pool", dst.ap, cv.ap, [cv], [dst])
            else:
                for tb in range(4):
                    lo, hi = tb * 512, tb * 512 + 512
                    sq = cx.ring("gsq", 2, 512, BF16)
                    cx.act(sq.ap, cv.ap[:, lo:hi], AF.Square, [cv], [sq])
                    ps = cx.ps()
                    cx.mm([dict(out=ps.ap[:, 0:512], lhsT=K.ones.ap, rhs=sq.ap, start=True, stop=True)], [K.ones, sq], [ps])
                    ri = cx.ring("gri", 2, 512, F32)
                    ln_exp_rinv(cx, ri.ap, ps.ap[:, 0:512], [ps], ri)
                    if which == 0:
                        cx.stt(dst.ap[:, lo:hi], cv.ap[:, lo:hi], 128.0 ** -0.5, ri.ap, ALU.mult, ALU.mult, [cv, ri], [dst])
                    else:
                        cx.tt("dve", dst.ap[:, lo:hi], cv.ap[:, lo:hi], ri.ap, ALU.mult, [cv, ri], [dst])
        for c4 in range(4):
            ps = cx.ps()
            pv = ps.ap.bitcast(BF16)
            for q in range(4):
                c = c4 * 4 + q
                cx.tr(pv[:, q * 128:(q + 1) * 128], vT.ap[:, c * 128:(c + 1) * 128], idb, [vT, K.cb], [ps])
            cx.copy("act", Vn.ap[:, c4 * 512:(c4 + 1) * 512], pv[:, 0:512], [ps], [Vn])
            ps = cx.ps()
            pv = ps.ap.bitcast(BF16)
            for q in range(4):
                c = c4 * 4 + q
                cx.tr(pv[:, q * 128:(q + 1) * 128], kT.ap[:, c * 128:(c + 1) * 128], idb, [kT, K.cb], [ps])
            for q in range(4):
                c = c4 * 4 + q
                col = c * 4 + hh
                cx.ts("dve", Kn.ap[:, c * 128:(c + 1) * 128], pv[:, q * 128:(q + 1) * 128], wk.ap[:, col:col + 1], ALU.mult, [ps, wk], [Kn])
                cx.ts("dve", Bn.ap[:, c * 128:(c + 1) * 128], pv[:, q * 128:(q + 1) * 128], nwk.ap[:, col:col + 1], ALU.mult, [ps, nwk], [Bn])
        pcs = load_w_multi(cx, wc[12 + hh], KC, 128)
        for c4 in range(4):
            ps = cx.ps()
            mms = []
            for q in range(4):
                c = c4 * 4 + q
                for k in range(KC):
                    mms.append(dict(out=ps.ap[:, q * 128:(q + 1) * 128], lhsT=hv[:, k, c * 128:(c + 1) * 128], rhs=wslice(pcs, k, 128),
                                    start=(k == 0), stop=(k == KC - 1)))
            cx.mm(mms, [p[0] for p in pcs] + [hT], [ps])
            cx.act(gate.ap[:, c4 * 512:(c4 + 1) * 512], ps.ap[:, 0:512], AF.Silu, [ps], [gate])
        for c in range(NCH):
            col = c * 4 + hh
            g2 = cx.ring("gG2", 2, 128, F32)
            cx.ts("dve", g2.ap, mui, gg.ap[:, col:col + 1], ALU.mult, [K.cf, gg], [g2])
            ps = cx.ps()
            cx.mm([dict(out=ps.ap[:, 0:128], lhsT=mls, rhs=g2.ap, start=True, stop=True)], [K.cf, g2], [ps])
            ex = cx.ring("gex", 2, 128, F32)
            cx.act(ex.ap, ps.ap[:, 0:128], AF.Exp, [ps], [ex])
            cx.stt(Wst.ap[:, c * 128:(c + 1) * 128], ex.ap, beta.ap[:, col:col + 1], mus, ALU.mult, ALU.mult, [ex, beta, K.cf], [Wst])
            cx.stt(Win.ap[:, c * 128:(c + 1) * 128], ex.ap, beta.ap[:, col:col + 1], mui, ALU.mult, ALU.mult, [ex, beta, K.cf], [Win])
        cx.release(m2)
        sl = lambda t, c: t.ap[:, c * 128:(c + 1) * 128]
        ycs = cx.tile(SEQ, BF16, "ycs")

        def out_cb(c, yap, yt, hh=hh, ycs=ycs):
            junk = cx.ring("gjunk", 2, 128, F32)
            ss = cx.ring("gss", 2, 1, F32)
            cx.act(junk.ap, yap, AF.Square, [yt], [junk, ss], accum_out=ss.ap)
            ln_exp_rinv(cx, ss.ap, ss.ap, [ss], ss, scale=1.0 / 128, bias=EPS)
            o1 = cx.ring("go1", 2, 128, F32)
            cx.stt(o1.ap, yap, ss.ap[:, 0:1], gpar.ap[:, 8:136], ALU.mult, ALU.mult, [yt, ss, gpar], [o1])
            o2 = cx.ring("go2", 2, 128, BF16)
            cx.tt("pool", o2.ap, o1.ap, gate.ap[:, c * 128:(c + 1) * 128], ALU.mult, [o1, gate], [o2])
            pst = cx.ps()
            ptv = pst.ap.bitcast(BF16)
            cx.tr(ptv[:, 0:128], o2.ap, idb, [o2, K.cb], [pst])
            cx.copy("act", ycs.ap[:, c * 128:(c + 1) * 128], ptv[:, 0:128], [pst], [ycs])

        dplr_head(cx, K, "g%d" % hh, 128, 128, 0,
                  Kg=lambda c: sl(kT, c), Bg=None, Ag=lambda c: sl(kT, c), Rg=lambda c: sl(qT, c), gtiles=[kT, qT],
                  Wst=lambda c: sl(Wst, c), Win=lambda c: sl(Win, c), wtiles=[Wst, Win],
                  As=lambda c: sl(kT, c), Rs=lambda c: sl(qT, c), egam=lambda c: egam.ap[:, c * 4 + hh:c * 4 + hh + 1],
                  Kn=lambda c: sl(Kn, c), Bn=lambda c: sl(Bn, c), V=lambda c: sl(Vn, c), ntiles=[Kn, Bn, Vn],
                  PC=lambda c: pcall.ap[:, c * 4 + hh:c * 4 + hh + 1], pctiles=[pcall, egam], scale_all=False, out_cb=out_cb)
        cx.dma(yout[:, 12 + 4 * io["s"] + hh, :], ycs.ap, [ycs], [io["ydep"]["c"]])
        cx.release(m1)
    cx.release(m0)


def mixer_rwkv(cx, K, io, hT, hv, layer1):
    m0 = cx.mark()
    wa = io["wa"]
    wal = io["wal"]
    rp = load_small(cx, io["rpar"], 44, "rpar")
    rl = load_small(cx, io["rlmu"], 4, "rlmu")
    rb = load_small(cx, io["rbc"], 1024, "rbc")
    wab = cx.tile(512, BF16, "wab")
    g2b = cx.tile(1024, BF16, "g2b")
    if layer1:
        v2b_ = cx.tile(512, BF16, "v2b", parts=64)
        v2b = T(v2b_.ap[32:64, :], v2b_.d)
    rmask = cx.tile(SEQ, BF16, "rmask")
    tl = cx.tile(SEQ, BF16, "tl")
    sgd = cx.tile(SEQ, BF16, "sgd")
    sg2h = cx.tile(SEQ, BF16, "sg2h", parts=64)
    sgd2 = T(sg2h.ap[0:32, :], sg2h.d)
    if layer1:
        hv1 = T(sg2h.ap[32:64, :], sg2h.d)
    mt = cx.mark()
    w2a2 = load_small(cx, io["rw2a2"], 512, "rw2a2")
    g2a = load_small(cx, io["rg2"], 1024, "rg2")
    cx.copy("pool", wab.ap, w2a2.ap, [w2a2], [wab])
    cx.copy("pool", g2b.ap, g2a.ap, [g2a], [g2b])
    if layer1:
        v2 = cx.tile(512, F32, "rv2", parts=64)
        cx.dma(v2.ap[32:64, :], io["rv2"], [], [v2])
        cx.copy("pool", v2b.ap, v2.ap[32:64, :], [v2], [v2b])
    idb = K.cb.ap[:, C_ID:C_ID + 128]
    msk_s = K.cb.ap[:, C_MUS:C_MUS + 128]
    msk_i = K.cb.ap[:, C_MUI:C_MUI + 128]
    cx.I("dve", "memset", [], [rmask], ap=rmask.ap, constant=1.0)
    cx.I("dve", "memset", [], [rmask], ap=rmask.ap[:, 0:SEQ:128], constant=0.0)

    def mixed_block(zp, tmp, pcs, cw, c0, m, po, mu_ap, mu_t, dst_ap, dst_t, func=None):
        pr = slice(po, po + m)
        cx.I("dve", "memset", [], [zp], ap=zp.ap[pr, 0:1], constant=0.0)
        for tb in range(4):
            lo, hi = tb * 512, tb * 512 + 512
            ps = cx.ps()
            mms = []
            for k in range(KC):
                mms.append(dict(out=ps.ap[pr, 0:512], lhsT=wslice(pcs, k, cw, c0, c0 + m), rhs=hv[:, k, lo:hi],
                                start=(k == 0), stop=(k == KC - 1)))
            cx.mm(mms, [p[0] for p in pcs] + [hT], [ps])
            cx.copy("act", zp.ap[pr, 1 + lo:1 + hi], ps.ap[pr, 0:512], [ps], [zp])
        cx.tt("dve", tmp.ap[pr, :], zp.ap[pr, 0:SEQ], zp.ap[pr, 1:1 + SEQ], ALU.subtract, [zp], [tmp])
        if func is None:
            cx.stt(dst_ap, tmp.ap[pr, :], mu_ap, zp.ap[pr, 1:1 + SEQ], ALU.mult, ALU.add, [tmp, zp, mu_t], [dst_t])
        else:
            cx.stt(tmp.ap[pr, :], tmp.ap[pr, :], mu_ap, zp.ap[pr, 1:1 + SEQ], ALU.mult, ALU.add, [tmp, zp, mu_t], [tmp])
            cx.act(dst_ap, tmp.ap[pr, :], func, [tmp], [dst_t])

    zp0 = cx.tile(SEQ + 8, F32, "zp0")
    tmp0 = cx.tile(SEQ, F32, "tmp0")
    pcl = load_w_big(cx, wal, KC, 288, "wal_b")
    mixed_block(zp0, tmp0, pcl, 288, 0, 64, 0, rl.ap[0:64, 0:1], rl, tl.ap[0:64, :], tl, func=AF.Tanh)
    mixed_block(zp0, tmp0, pcl, 288, 64, 64, 64, rl.ap[64:128, 0:1], rl, tl.ap[64:128, :], tl, func=AF.Copy)
    mixed_block(zp0, tmp0, pcl, 288, 128, 128, 0, rl.ap[:, 1:2], rl, sgd.ap, sgd, func=AF.Sigmoid)
    mixed_block(zp0, tmp0, pcl, 288, 256, 32, 0, rl.ap[0:32, 2:3], rl, sgd2.ap, sgd2, func=AF.Sigmoid)
    if layer1:
        pc1 = load_w_multi(cx, io["rv1"], KC, 32)
        for tb in range(4):
            lo, hi = tb * 512, tb * 512 + 512
            ps = gemm_block(cx, pc1, 32, lambda k, l, h_: hv[:, k, l:h_], [hT], lo, hi, po=32)
            cx.copy("act", hv1.ap[:, lo:hi], ps.ap[32:64, 0:512], [ps], [hv1])
        vfT = io["vf_in"].rearrange("(k p) t -> p k t", p=128)
    cx.release(mt)
    vout = io["v_out"].rearrange("(k p) t -> p k t", p=128)
    yout = io["ybuf"].rearrange("(k p) t -> p k t", p=128)
    lnw = rb.ap[:, 0:512]
    lnb = rb.ap[:, 512:1024]
    for ct in range(4):
        m1 = cx.mark()
        par = rp.ap[:, 12 + ct * 8:12 + ct * 8 + 8]
        As = cx.tile(SEQ, BF16, "rAs")
        Rs = cx.tile(SEQ, BF16, "rRs")
        Ks = cx.tile(SEQ, BF16, "rKs")
        Bs = cx.tile(SEQ, BF16, "rBs")
        Kn = cx.tile(NCH * 128, BF16, "rKn")
        Bn = cx.tile(NCH * 128, BF16, "rBn")
        Vn = cx.tile(NCH * 128, BF16, "rVn")
        PCt = cx.tile(NCH, F32, "rPC")
        bon = cx.tile(NCH * 2, F32, "rbon")
        m2 = cx.mark()
        zp = cx.tile(SEQ + 8, F32, "zp")
        tmp = cx.tile(SEQ, F32, "tmp")
        ld = cx.tile(SEQ, F32, "ld")
        cum = cx.tile(SEQ, F32, "cum")
        av = cx.tile(SEQ, BF16, "a_sig")
        rm = cx.tile(SEQ, BF16, "r_m")
        km = cx.tile(SEQ, BF16, "k_m")
        vm = cx.tile(SEQ, BF16, "v_m")
        rkr = cx.tile(SEQ, BF16, "rkr")
        for which, dst in ((0, rm), (1, km), (2, vm)):
            pcs = load_w_multi(cx, wa[which * 4 + ct], KC, 128)
            mixed_block(zp, tmp, pcs, 128, 0, 128, 0, rp.ap[:, which * 4 + ct:which * 4 + ct + 1], rp, dst.ap, dst)
        kx = T(zp.ap[:, 0:SEQ], zp.d)
        B1 = tmp
        for tb in range(4):
            lo, hi = tb * 512, tb * 512 + 512
            ps = cx.ps()
            cx.mm([dict(out=ps.ap[:, 0:512], lhsT=wab.ap[0:64, ct * 128:(ct + 1) * 128], rhs=tl.ap[0:64, lo:hi], start=True, stop=True)],
                  [wab, tl], [ps])
            cx.act(ld.ap[:, lo:hi], ps.ap[:, 0:512], AF.Sigmoid, [ps, rp], [ld], bias=par[:, 0:1])
            ps = cx.ps()
            cx.mm([dict(out=ps.ap[:, 0:512], lhsT=wab.ap[64:128, ct * 128:(ct + 1) * 128], rhs=tl.ap[64:128, lo:hi], start=True, stop=True)],
                  [wab, tl], [ps])
            cx.act(av.ap[:, lo:hi], ps.ap[:, 0:512], AF.Sigmoid, [ps, rp], [av], bias=par[:, 1:2])
        cx.ts("dve", ld.ap, ld.ap, -float(np.exp(-0.5)), ALU.mult, [ld], [ld])
        cx.I("dve", "tensor_tensor_scan", [rmask, ld], [cum], out=cum.ap, data0=rmask.ap, data1=ld.ap, initial=0.0,
             op0=ALU.mult, op1=ALU.add)
        if layer1:
            vf = rkr
            cx.dma(vf.ap, vfT[:, ct, :], io["vf_deps"], [vf])
            for tb in range(4):
                lo, hi = tb * 512, tb * 512 + 512
                ps = cx.ps()
                cx.mm([dict(out=ps.ap[:, 0:512], lhsT=v2b.ap[:, ct * 128:(ct + 1) * 128], rhs=hv1.ap[:, lo:hi], start=True, stop=True)],
                      [v2b, hv1], [ps])
                cx.act(B1.ap[:, lo:hi], ps.ap[:, 0:512], AF.Sigmoid, [ps, rp], [B1], bias=par[:, 5:6])
            cx.tt("dve", kx.ap, vf.ap, vm.ap, ALU.subtract, [vf, vm], [kx])
            cx.tt("dve", kx.ap, kx.ap, B1.ap, ALU.mult, [kx, B1], [kx])
            cx.tt("dve", vm.ap, vm.ap, kx.ap, ALU.add, [vm, kx], [vm])
        if io.get("write_v", True):
            cx.dma(vout[:, ct, :], vm.ap, [vm], [io["v_dep"]])
        cx.ts("dve", kx.ap, km.ap, par[:, 2:3], ALU.mult, [km, rp], [kx])
        for tb in range(4):
            lo, hi = tb * 512, tb * 512 + 512
            sq = cx.ring("rsq", 2, 512, BF16)
            cx.act(sq.ap, kx.ap[:, lo:hi], AF.Square, [kx], [sq])
            ps = cx.ps()
            cx.mm([dict(out=ps.ap[:, 0:512], lhsT=K.cb.ap[:, C_BO:C_BO + 128], rhs=sq.ap, start=True, stop=True)], [K.cb, sq], [ps])
            ri = cx.ring("rri", 2, 512, F32)
            ln_exp_rinv(cx, ri.ap, ps.ap[:, 0:512], [ps], ri)
            cx.tt("dve", kx.ap[:, lo:hi], kx.ap[:, lo:hi], ri.ap, ALU.mult, [kx, ri], [kx])
        if RW_DBG and ct == 0:
            dbg = io["dbg"].rearrange("(k p) t -> p k t", p=128)
            cx.dma(dbg[:, 0, :], kx.ap, [kx], [])
            cx.dma(dbg[:, 1, :], ld.ap, [ld], [])
            cx.dma(dbg[:, 2, :], cum.ap, [cum], [])
        cx.ts("dve", B1.ap, av.ap, -1.0, ALU.add, [av, rp], [B1], s2=par[:, 3:4], op1=ALU.mult)
        cx.ts("dve", B1.ap, B1.ap, 1.0, ALU.add, [B1], [B1])
        if RW_DBG and ct == 0:
            cx.dma(dbg[:, 3, :], B1.ap, [B1], [])
        cx.tt("dve", km.ap, km.ap, B1.ap, ALU.mult, [km, B1], [km])
        cx.stt(rkr.ap, rm.ap, par[:, 4:5], km.ap, ALU.mult, ALU.mult, [rm, km, rp], [rkr])
        ps = cx.ps()
        cx.mm([dict(out=ps.ap[:, c * 2:c * 2 + 2], lhsT=rkr.ap[:, c * 128:(c + 1) * 128], rhs=K.cb.ap[:, C_BS:C_BS + 2], start=True, stop=True)
               for c in range(NCH)], [rkr, K.cb], [ps])
        cx.copy("act", bon.ap, ps.ap[:, 0:NCH * 2], [ps], [bon])
        cx.tt("dve", B1.ap, cum.ap, ld.ap, ALU.subtract, [cum, ld], [B1])
        cx.act(B1.ap, B1.ap, AF.Exp, [B1], [B1])
        cx.stt(As.ap, kx.ap, -1.0, B1.ap, ALU.mult, ALU.mult, [kx, B1], [As])
        cx.act(B1.ap, cum.ap, AF.Exp, [cum], [B1])
        cx.tt("dve", Rs.ap, rm.ap, B1.ap, ALU.mult, [rm, B1], [Rs])
        cx.copy("dve", PCt.ap, B1.ap[:, 127:SEQ:128], [B1], [PCt])
        cx.act(B1.ap, cum.ap, AF.Exp, [cum], [B1], scale=-1.0)
        cx.tt("dve", Ks.ap, km.ap, B1.ap, ALU.mult, [km, B1], [Ks])
        cx.tt("dve", kx.ap, kx.ap, av.ap, ALU.mult, [kx, av], [kx])
        cx.tt("dve", Bs.ap, kx.ap, B1.ap, ALU.mult, [kx, B1], [Bs])
        for src, dst in ((Ks, Kn), (Bs, Bn), (vm, Vn)):
            for c4 in range(4):
                ps = cx.ps()
                pv = ps.ap.bitcast(BF16)
                for q in range(4):
                    c = c4 * 4 + q
                    cx.tr(pv[:, q * 128:(q + 1) * 128], src.ap[:, c * 128:(c + 1) * 128], idb, [src, K.cb], [ps])
                cx.copy("act", dst.ap[:, c4 * 512:(c4 + 1) * 512], pv[:, 0:512], [ps], [dst])
        cx.release(m2)
        yas = cx.tile(SEQ, BF16, "yas")
        for hp in range(2):
            pb = hp * 64
            hl = ct * 2 + hp
            pr = slice(pb, pb + 64)

            def out_cb(c, yap, yt, hl=hl, hp=hp, ct=ct, yas=yas, pb=pb):
                st = cx.ring("rst", 2, 6, F32)
                cx.I("dve", "bn_stats", [yt], [st], out=st.ap, in_=yap)
                mv_ = cx.ring("rmv", 2, 2, F32)
                cx.I("dve", "bn_aggr", [st], [mv_], out=mv_.ap, in_=st.ap)
                rs = cx.ring("rrs", 2, 1, F32)
                ln_exp_rinv(cx, rs.ap, mv_.ap[:, 1:2], [mv_], rs, bias=64e-5)
                y1 = cx.ring("ry1", 2, 64, F32)
                cx.ts("dve", y1.ap, yap, mv_.ap[:, 0:1], ALU.subtract, [yt, mv_, rs], [y1], s2=rs.ap[:, 0:1], op1=ALU.mult)
                cx.tt("pool", y1.ap, y1.ap, lnw[:, hl * 64:(hl + 1) * 64], ALU.mult, [y1, rb], [y1])
                cx.tt("pool", y1.ap, y1.ap, lnb[:, hl * 64:(hl + 1) * 64], ALU.add, [y1, rb], [y1])
                y2 = cx.ring("ry2", 2, 64, F32)
                cx.stt(y2.ap, Vn.ap[:, c * 128 + hp * 64:c * 128 + hp * 64 + 64], bon.ap[:, c * 2 + hp:c * 2 + hp + 1], y1.ap,
                       ALU.mult, ALU.add, [Vn, bon, y1], [y2])
                psg = cx.ps()
                cx.mm([dict(out=psg.ap[:, 0:64], lhsT=sgd.ap[:, c * 128:(c + 1) * 128], rhs=g2b.ap[:, hl * 64:(hl + 1) * 64], start=True, stop=False),
                       dict(out=psg.ap[:, 0:64], lhsT=sgd2.ap[:, c * 128:(c + 1) * 128], rhs=g2b.ap[0:32, 512 + hl * 64:512 + (hl + 1) * 64],
                            start=False, stop=True)], [sgd, sgd2, g2b], [psg])
                y3 = cx.ring("ry3", 2, 64, BF16)
                if RW_DBG == "g":
                    cx.copy("dve", y3.ap, psg.ap[:, 0:64], [psg], [y3])
                elif RW_DBG == "y1":
                    cx.copy("dve", y3.ap, y1.ap, [y1, psg], [y3])
                elif RW_DBG == "y2":
                    cx.copy("dve", y3.ap, y2.ap, [y2, psg], [y3])
                elif RW_DBG == "yraw":
                    cx.copy("dve", y3.ap, yap, [yt, psg], [y3])
                else:
                    cx.tt("dve", y3.ap, psg.ap[:, 0:64], y2.ap, ALU.mult, [psg, y2], [y3])
                pst = cx.ps()
                ptv = pst.ap.bitcast(BF16)
                cx.tr(ptv[pb:pb + 64, 0:128], y3.ap, idb, [y3, K.cb], [pst])
                cx.copy("act", yas.ap[pb:pb + 64, c * 128:(c + 1) * 128], ptv[pb:pb + 64, 0:128], [pst], [yas])

            dplr_head(cx, K, "r%d" % hl, 64, 64, pb,
                      Kg=lambda c, pr=pr: Ks.ap[pr, c * 128:(c + 1) * 128], Bg=lambda c, pr=pr: Bs.ap[pr, c * 128:(c + 1) * 128],
                      Ag=lambda c, pr=pr: As.ap[pr, c * 128:(c + 1) * 128], Rg=lambda c, pr=pr: Rs.ap[pr, c * 128:(c + 1) * 128],
                      gtiles=[Ks, Bs, As, Rs],
                      Wst=lambda c: msk_s, Win=lambda c: msk_i, wtiles=[K.cb],
                      As=lambda c, pr=pr: As.ap[pr, c * 128:(c + 1) * 128], Rs=lambda c, pr=pr: Rs.ap[pr, c * 128:(c + 1) * 128], egam=None,
                      Kn=lambda c, pb=pb: Kn.ap[:, c * 128 + pb:c * 128 + pb + 64], Bn=lambda c, pb=pb: Bn.ap[:, c * 128 + pb:c * 128 + pb + 64],
                      V=lambda c, pb=pb: Vn.ap[:, c * 128 + pb:c * 128 + pb + 64], ntiles=[Kn, Bn, Vn],
                      PC=lambda c, pr=pr: PCt.ap[pr, c:c + 1], pctiles=[PCt], scale_all=True, out_cb=out_cb)
        cx.dma(yout[:, 4 * io["s"] + ct, :], yas.ap, [yas], [io["ydep"]["a"]])
        cx.release(m1)
    cx.release(m0)


def phase_M(cx, io, layer1, which=("att", "gdn", "rwkv")):
    base_ = cx.mark()
    K = setup_consts(cx, io)
    g1 = load_small(cx, io["g_attn"], 16, "g1")
    hT = cx.tile(KC * SEQ, BF16, "hT")
    hv = hT.ap.rearrange("p (k t) -> p k t", k=KC)
    xT = io["xT"].rearrange("(k p) t -> p k t", p=128)
    m0 = cx.mark()
    for tb in range(4):
        lo, hi = tb * 512, tb * 512 + 512
        xt = cx.ring("xtile", 1, KC * 512, F32)
        xtv = xt.ap.rearrange("p (k t) -> p k t", k=KC)
        for k in range(KC):
            cx.dma(xtv[:, k, :], xT[:, k, lo:hi], io.get("x_deps", []), [xt])
        rmsnorm_fm(cx, lambda k, l, h_: (xtv[:, k, 0:h_ - l], xt), g1, lambda k, l, h_: (hv[:, k, l:h_], hT),
                   [(lo, hi)], K.ones)
    cx.release(m0)
    if "att" in which:
        mixer_attention(cx, K, io, hT, hv)
    if "gdn" in which:
        mixer_gdn(cx, K, io, hT, hv)
    if "rwkv" in which:
        mixer_rwkv(cx, K, io, hT, hv, layer1)
    cx.release(base_)


M_INPUTS = [("xT", (D, SEQ)), ("g_attn", (128, 16)), ("cst", (128, CST_N)), ("rope", (32, 2 * SEQ)),
            ("wat", (18, 128, 16, 128)), ("wsw", (12, 128, 16, 32)),
            ("wc", (16, 128, 16, 128)), ("wba", (128, 16, 8)), ("gconv", (128, 48)), ("gpar", (128, 136)),
            ("wa", (12, 128, 16, 128)), ("wal", (128, 16, 288)), ("rpar", (128, 44)), ("rlmu", (128, 4)), ("rbc", (128, 1024)),
            ("rw2a2", (128, 512)), ("rg2", (128, 1024))]
M_INPUTS_L1 = [("rv1", (128, 16, 32)), ("rv2", (32, 512)), ("vf_in", (512, SEQ))]
M_OUTPUTS = [("y_fm", (768, SEQ)), ("yc_tm", (SEQ, 512)), ("ya_tm", (SEQ, 512)), ("v_out", (512, SEQ))]


def prep_M_weights(inp, l, s):
    A_IN, B_IN, C_IN = 3360, 4608, 4112
    W = inp["w_in"][l]
    w = {}
    w["g_attn"] = vec_pk(inp["attn_norm"][l])
    w["cst"] = make_consts()
    w["rope"] = make_rope()

    def colblk(cols):
        sub = W[:, cols]
        n = sub.shape[1]
        return np.ascontiguousarray(sub.reshape(16, 128, n // 128, 128).transpose(2, 1, 0, 3))

    def colsmall(cols):
        sub = W[:, cols]
        return np.ascontiguousarray(sub.reshape(16, 128, len(cols)).transpose(1, 0, 2))
    b0 = A_IN
    cols = []
    cols_sw = []
    for hl in range(2):
        hi = 2 * s + hl
        for g in range(3):
            for which in range(3):
                c0 = b0 + which * 1536 + g * 512 + hi * 128
                cols += list(range(c0, c0 + 128))
                if which < 2:
                    cols_sw += list(range(c0 + 16, c0 + 32)) + list(range(c0, c0 + 16))
    w["wat"] = colblk(np.array(cols))
    sw = W[:, np.array(cols_sw)]
    w["wsw"] = np.ascontiguousarray(sw.reshape(16, 128, 12, 32).transpose(2, 1, 0, 3))
    c0 = A_IN + B_IN
    cols = []
    for which in range(3):
        for hh in range(4):
            h = 4 * s + hh
            cols += list(range(c0 + which * 1024 + h * 128, c0 + which * 1024 + (h + 1) * 128))
    for hh in range(4):
        h = 4 * s + hh
        cols += list(range(c0 + 3088 + h * 128, c0 + 3088 + (h + 1) * 128))
    w["wc"] = colblk(np.array(cols))
    w["wba"] = colsmall(np.array([c0 + 3072 + 4 * s + i for i in range(4)] + [c0 + 3080 + 4 * s + i for i in range(4)]))
    gc = inp["gdn_conv"][l]
    gcs = np.zeros((128, 12, 4), np.float32)
    for which in range(3):
        for hh in range(4):
            h = 4 * s + hh
            gcs[:, which * 4 + hh, :] = gc[:, which * 1024 + h * 128: which * 1024 + (h + 1) * 128].T
    w["gconv"] = gcs.reshape(128, 48)
    gp = np.zeros((128, 136), np.float32)
    gp[:, 0:4] = inp["gdn_A_log"][l][4 * s:4 * s + 4][None, :]
    gp[:, 4:8] = inp["gdn_dt_bias"][l][4 * s:4 * s + 4][None, :]
    gp[:, 8:136] = inp["gdn_norm"][l][None, :]
    w["gpar"] = gp
    ch0 = 512 * s
    cols = []
    for which in range(3):
        cols += list(range(which * 1024 + ch0, which * 1024 + ch0 + 512))
    w["wa"] = colblk(np.array(cols))
    w["wal"] = colsmall(np.arange(3072, 3360))
    mu = inp["rwkv_mu"][l]
    rp = np.zeros((128, 44), np.float32)
    for which in range(3):
        for ct in range(4):
            rp[:, which * 4 + ct] = mu[which * 1024 + ch0 + ct * 128: which * 1024 + ch0 + (ct + 1) * 128]
    for ct in range(4):
        sl = slice(ch0 + ct * 128, ch0 + (ct + 1) * 128)
        rp[:, 12 + ct * 8 + 0] = inp["rwkv_w0"][l][sl]
        rp[:, 12 + ct * 8 + 1] = inp["rwkv_a0"][l][sl]
        rp[:, 12 + ct * 8 + 2] = inp["rwkv_k_k"][l][sl]
        rp[:, 12 + ct * 8 + 3] = inp["rwkv_k_a"][l][sl]
        rp[:, 12 + ct * 8 + 4] = inp["rwkv_r_k"][l].reshape(-1)[sl]
        if l > 0:
            rp[:, 12 + ct * 8 + 5] = inp["rwkv_v0"][l - 1][sl]
    w["rpar"] = rp
    rl = np.zeros((128, 4), np.float32)
    rl[0:64, 0] = mu[3072:3136]
    rl[64:128, 0] = mu[3136:3200]
    rl[0:128, 1] = mu[3200:3328]
    rl[0:32, 2] = mu[3328:3360]
    w["rlmu"] = rl
    rb = np.zeros((128, 1024), np.float32)
    rb[:, 0:512] = inp["rwkv_ln_w"][l][ch0:ch0 + 512][None, :]
    rb[:, 512:1024] = inp["rwkv_ln_b"][l][ch0:ch0 + 512][None, :]
    w["rbc"] = rb
    w["rw2a2"] = np.ascontiguousarray(np.concatenate([inp["rwkv_w2"][l][:, ch0:ch0 + 512], inp["rwkv_a2"][l][:, ch0:ch0 + 512]], axis=0))
    g2 = inp["rwkv_g2"][l][:, ch0:ch0 + 512]
    rg = np.zeros((128, 1024), np.float32)
    rg[:, 0:512] = g2[0:128]
    rg[0:32, 512:1024] = g2[128:160]
    w["rg2"] = rg
    if l > 0:
        v1 = inp["rwkv_v1"][l - 1]
        w["rv1"] = np.ascontiguousarray(v1.reshape(16, 128, 32).transpose(1, 0, 2))
        w["rv2"] = np.ascontiguousarray(inp["rwkv_v2"][l - 1][:, ch0:ch0 + 512])
    return w


def tile_w(W, col0, ncols, cw=128):
    K = W.shape[0]
    sub = W[:, col0:col0 + ncols]
    return np.ascontiguousarray(sub.reshape(K // 128, 128, ncols // cw, cw).transpose(2, 1, 0, 3))


def vec_pk(v):
    return np.ascontiguousarray(v.reshape(-1, 128).T)


def prep_T_weights(inp, l):
    A_IN, B_IN, C_IN = 3360, 4608, 4112
    g0 = A_IN + B_IN + C_IN
    w = {}
    w["wg"] = tile_w(inp["w_in"][l], g0, 6144)
    pw = np.concatenate([inp["proj_a"][l], inp["proj_b"][l], inp["proj_c"][l]], axis=0)
    w["wp"] = tile_w(pw, 0, 2048)
    w["wo"] = tile_w(inp["w_out"][l], 0, 2048)
    w["wup"] = tile_w(inp["ffn_up"][l], 0, 11264)
    w["wdn"] = tile_w(inp["ffn_down"][l], 0, 2048)
    w["g_attn"] = vec_pk(inp["attn_norm"][l])
    w["g_ffn"] = vec_pk(inp["ffn_norm"][l])
    w["g_next"] = vec_pk(inp["attn_norm"][l + 1] if l + 1 < inp["attn_norm"].shape[0] else inp["final_norm"])
    fc = inp["ffn_conv"][l]
    w["fconv"] = np.ascontiguousarray(fc.T.reshape(88, 128, 3).transpose(1, 0, 2).reshape(128, 88 * 3))
    return w


class DD:
    def __init__(self, name=""):
        self.d = Dep(name)


M_SHARED = ("xT", "cst", "rope")
T_INPUTS = [("g_attn", (128, 16)), ("g_ffn", (128, 16)), ("g_next", (128, 16)), ("fconv", (128, 88 * 3)),
            ("wg", (48, 128, 16, 128)), ("wp", (16, 128, 20, 128)), ("wo", (16, 128, 16, 128)),
            ("wup", (88, 128, 16, 128)), ("wdn", (16, 128, 44, 128))]


def build_fused(depth=2, do_T=True):
    nc = bass.Bass("TRN2", target_bir_lowering=False)
    ext = {}

    def din(name, shape, dt=F32):
        ext[name] = nc.dram_tensor(name, list(shape), dt, kind="ExternalInput").ap()
        return ext[name]
    xT = din("xT", (D, SEQ))
    cst = din("cst", (128, CST_N))
    rope = din("rope", (32, 2 * SEQ))
    for l in range(depth):
        for s in range(2):
            for name, shape in M_INPUTS + (M_INPUTS_L1[:2] if l == 1 else []):
                if name not in M_SHARED:
                    din("%s_%d%d" % (name, l, s), shape)
        for name, shape in T_INPUTS:
            din("T%s_%d" % (name, l), shape)
    out = nc.dram_tensor("outT", [D, SEQ], F32, kind="ExternalOutput").ap()
    ybuf = [nc.dram_tensor("ybuf%d" % l, [2560, SEQ], BF16).ap() for l in range(depth)]
    x1buf = nc.dram_tensor("x1buf", [D, SEQ], F32).ap()
    vbuf = [nc.dram_tensor("vbuf%d" % s, [512, SEQ], BF16).ap() for s in range(2)]
    with ExitStack() as es:
        cx = Ctx(nc, es)
        cx.wpiece = 1024
        x1dep = DD("x1")
        vdep = [DD("v0"), DD("v1")]
        for l in range(depth):
            last = (l == depth - 1)
            xsrc = xT if l == 0 else x1buf
            xdeps = [] if l == 0 else [x1dep]
            ydep = {"a": DD("ya"), "b": DD("yb"), "c": DD("yc")}
            for s in range(2):
                io = {}
                for name, shape in M_INPUTS + (M_INPUTS_L1[:2] if l == 1 else []):
                    if name not in M_SHARED:
                        io[name] = ext["%s_%d%d" % (name, l, s)]
                io.update(xT=xsrc, x_deps=xdeps, cst=cst, rope=rope, ybuf=ybuf[l], ydep=ydep, s=s,
                          v_out=vbuf[s], v_dep=vdep[s], vf_in=vbuf[s], vf_deps=[vdep[s]], write_v=(l == 0))
                phase_M(cx, io, l == 1)
            if do_T:
                for half in range(2):
                    io = {name: ext["T%s_%d" % (name, l)] for name, shape in T_INPUTS}
                    io.update(xT=xsrc, x_deps=xdeps, ybuf=ybuf[l], y_deps=list(ydep.values()),
                              outT=(out if last else x1buf), out_deps=([] if last else [x1dep]))
                    phase_T(cx, io, last, half)
        cx.S.emit()
        print("fused ops:", cx.S.nops, "insts:", cx.S.ninst, "arena peak KB:", cx.ar.peak * 4 / 1024,
              "eng counts:", cx.S.cnt, "max dma sem:", max(cx.S.dma_cnt))
    return nc


_NC_CACHE = {}


def prep_core_inputs(inp, depth=2):
    m = {"cst": make_consts(), "rope": make_rope()}
    for l in range(depth):
        for s in range(2):
            w = prep_M_weights(inp, l, s)
            for k, v in w.items():
                if k not in M_SHARED:
                    m["%s_%d%d" % (k, l, s)] = v
        w = prep_T_weights(inp, l)
        for k, v in w.items():
            m["T%s_%d" % (k, l)] = v
    return m


def kernel(**inputs):
    inp = {k: np.asarray(v) for k, v in inputs.items()}
    x = inp["x"].astype(np.float32, copy=False)
    B, S, Dm = x.shape
    depth = inp["w_in"].shape[0]
    if "nc" not in _NC_CACHE:
        _NC_CACHE["nc"] = build_fused(depth)
    nc = _NC_CACHE["nc"]
    shared = prep_core_inputs(inp, depth)
    maps = []
    for c in range(8):
        m = dict(shared)
        m["xT"] = np.ascontiguousarray(x[c // 2].T)
        maps.append(m)
    res = run_bass_kernel_spmd(nc, maps, core_ids=list(range(8))).results
    outs = [np.asarray(res[2 * b]["outT"]).T for b in range(B)]
    return np.ascontiguousarray(np.stack(outs, axis=0)).astype(np.float32)
```

```python
import numpy as np
import concourse.bass as bass
import concourse.mybir as mybir
from concourse.bass_utils import run_bass_kernel_spmd
from contextlib import ExitStack

F32 = mybir.dt.float32
BF16 = mybir.dt.bfloat16
ALU = mybir.AluOpType
AF = mybir.ActivationFunctionType
AX = mybir.AxisListType

D = 2048
KC = 16
SEQ = 2048
NTOK_T = 1026
TT = [(0, 2), (2, 514), (514, 1026)]
D_FF = 5632
NFB = 44
EPS = 1e-6

SAME_ENG_SYNC = True


class Dep:
    __slots__ = ("w", "r", "name", "excl")

    def __init__(self, name="", excl=False):
        self.w = None
        self.r = {}
        self.name = name
        self.excl = excl


class Sched:
    ENGS = ["pe", "act", "dve", "pool", "sp"]

    def __init__(self, nc, n_dma=24):
        self.nc = nc
        self.q = {e: [] for e in self.ENGS}
        self.cnt = {e: 0 for e in self.ENGS}
        self.n_dma = n_dma
        self.dma_cnt = [0] * n_dma
        self.dma_rr = 0
        self.waited = {e: {} for e in self.ENGS}
        self.nops = 0
        self.ninst = 0

    def op(self, eng, calls, reads=(), writes=(), dma=False):
        deps = {}

        def add(tok):
            if tok is None:
                return
            k, v = tok
            if deps.get(k, 0) < v:
                deps[k] = v

        for r in reads:
            add(r.w)
            if r.excl:
                for k, v in r.r.items():
                    add((k, v))
        for w in writes:
            add(w.w)
            for k, v in w.r.items():
                add((k, v))
        if dma:
            i = self.dma_rr
            self.dma_rr = (self.dma_rr + 1) % self.n_dma
            add((("dma", i), self.dma_cnt[i]))
            self.dma_cnt[i] += 16
            tok = (("dma", i), self.dma_cnt[i])
        else:
            self.cnt[eng] += 1
            tok = (eng, self.cnt[eng])
        waits = []
        wd = self.waited[eng]
        for k, v in deps.items():
            if v <= 0:
                continue
            if k == eng and (eng == "pe" or not SAME_ENG_SYNC):
                continue
            if wd.get(k, 0) >= v:
                continue
            wd[k] = v
            waits.append((k, v))
        for r in reads:
            if r.r.get(tok[0], 0) < tok[1]:
                r.r[tok[0]] = tok[1]
        for w in writes:
            w.w = tok
            w.r = {}
        self.q[eng].append((waits, calls, tok))
        self.nops += 1
        self.ninst += len(calls)
        return tok

    def emit(self, final_wait_eng="sp"):
        nc = self.nc
        with ExitStack() as es:
            sems = {}
            for e in self.ENGS:
                sems[e] = es.enter_context(nc.semaphore("s_" + e))
            for i in range(self.n_dma):
                sems[("dma", i)] = es.enter_context(nc.semaphore("s_dma%d" % i))
            block = es.enter_context(nc.Block())
            finals = [(("dma", i), self.dma_cnt[i]) for i in range(self.n_dma) if self.dma_cnt[i] > 0]

            def run(engname, engine):
                for waits, calls, tok in self.q[engname]:
                    for k, v in waits:
                        engine.wait_ge(sems[k], v)
                    ins = None
                    for name, kw in calls:
                        ins = getattr(engine, name)(**kw)
                    ins.then_inc(sems[tok[0]], 16 if isinstance(tok[0], tuple) else 1)
                if engname == final_wait_eng:
                    for k, v in finals:
                        engine.wait_ge(sems[k], v)

            @block.tensor
            def _(e):
                run("pe", e)

            @block.scalar
            def _(e):
                run("act", e)

            @block.vector
            def _(e):
                run("dve", e)

            @block.gpsimd
            def _(e):
                run("pool", e)

            @block.sync
            def _(e):
                run("sp", e)


class Arena:
    def __init__(self, ap_f32, words):
        self.base = ap_f32
        self.words = words
        self.top = 0
        self.peak = 0
        self.live = []
        self.dead = []

    def mark(self):
        return self.top

    def release(self, m):
        keep = []
        for ent in self.live:
            if ent[0] >= m:
                self.dead.append(ent)
            else:
                keep.append(ent)
        self.live = keep
        self.top = m

    def alloc(self, nelem, dtype=F32, parts=128, name=""):
        w = nelem if dtype == F32 else (nelem + 1) // 2
        w = (w + 7) // 8 * 8
        assert self.top + w <= self.words, ("SBUF arena overflow", name, self.top, w, self.words)
        lo, hi = self.top, self.top + w
        a = self.base[0:parts, lo:hi]
        self.top = hi
        self.peak = max(self.peak, self.top)
        d = Dep(name)
        nd = []
        for (dlo, dhi, dd) in self.dead:
            if dlo < hi and lo < dhi:
                toks = list(dd.r.items())
                if dd.w is not None:
                    toks.append(dd.w)
                for k, v in toks:
                    if d.r.get(k, 0) < v:
                        d.r[k] = v
                if lo <= dlo and dhi <= hi:
                    continue
            nd.append((dlo, dhi, dd))
        self.dead = nd
        self.live.append((lo, hi, d))
        if dtype != F32:
            a = a.bitcast(dtype)
        return a[:, 0:nelem], d


class T:
    def __init__(self, ap, d):
        self.ap = ap
        self.d = d


class Ctx:
    ARENA_WORDS = 51 * 1024 + 512

    def __init__(self, nc, es):
        self.nc = nc
        self.S = Sched(nc)
        big = es.enter_context(nc.sbuf_tensor("arena", [128, self.ARENA_WORDS], F32))
        self.ar = Arena(big, self.ARENA_WORDS)
        self.psb = []
        for i in range(8):
            p = es.enter_context(nc.psum_tensor("psb%d" % i, [128, 512], F32))
            self.psb.append(T(p, Dep("ps%d" % i, excl=True)))
        self.ps_rr = 0
        self.rings = {}

    def ps(self, pin=False):
        pinned = getattr(self, "pinned", None)
        if pinned is None:
            pinned = self.pinned = set()
        while self.ps_rr in pinned:
            self.ps_rr = (self.ps_rr + 1) % 8
        i = self.ps_rr
        t = self.psb[i]
        self.ps_rr = (self.ps_rr + 1) % 8
        if pin:
            pinned.add(i)
        return t

    def unpin_all(self):
        self.pinned = set()

    def tile(self, nelem, dtype=F32, name="", parts=128):
        ap, d = self.ar.alloc(nelem, dtype, parts, name)
        return T(ap, d)

    def ring(self, key, n, nelem, dtype=F32):
        if key not in self.rings:
            off = self.ar.top
            self.rings[key] = [[self.tile(nelem, dtype, "%s%d" % (key, i)) for i in range(n)], 0, off]
        r = self.rings[key]
        t = r[0][r[1] % len(r[0])]
        r[1] += 1
        return t

    def mark(self):
        return self.ar.mark()

    def release(self, m):
        for k in [k for k, r in self.rings.items() if r[2] >= m]:
            del self.rings[k]
        self.ar.release(m)

    def I(self, eng, name, reads, writes, **kw):
        self.S.op(eng, [(name, kw)], [t.d for t in reads], [t.d for t in writes])

    def dma(self, out, in_, reads=(), writes=(), eng="sp"):
        self.S.op(eng, [("dma_start", dict(out=out, in_=in_))], [t.d for t in reads], [t.d for t in writes], dma=True)

    def act(self, out, in_, func, reads, writes, **kw):
        self.I("act", "activation", reads, writes, out=out, in_=in_, func=func, **kw)

    def tt(self, eng, out, in0, in1, op, reads, writes):
        self.I(eng, "tensor_tensor", reads, writes, out=out, in0=in0, in1=in1, op=op)

    def ts(self, eng, out, in0, s1, op0, reads, writes, s2=None, op1=None):
        kw = dict(out=out, in0=in0, scalar1=s1, scalar2=s2, op0=op0)
        if op1 is not None:
            kw["op1"] = op1
        self.I(eng, "tensor_scalar", reads, writes, **kw)

    def stt(self, out, in0, scalar, in1, op0, op1, reads, writes):
        self.I("dve", "scalar_tensor_tensor", reads, writes, out=out, in0=in0, scalar=scalar, in1=in1, op0=op0, op1=op1)

    def copy(self, eng, out, in_, reads, writes):
        if eng == "act":
            self.act(out, in_, AF.Copy, reads, writes)
        else:
            self.I(eng, "tensor_copy", reads, writes, out=out, in_=in_)

    def mm(self, mms, reads, writes):
        self.S.op("pe", [("matmul", m) for m in mms], [t.d for t in reads], [t.d for t in writes])

    def tr(self, out, in_, ident, reads, writes):
        self.S.op("pe", [("transpose", dict(out=out, in_=in_, identity=ident))], [t.d for t in reads], [t.d for t in writes])


def load_w(cx, dram_ap, kc, cw, key="w"):
    n = kc * cw
    wp = getattr(cx, "wpiece", 2048)
    assert n <= wp
    st = cx.ring(key + "_st", 2, wp, F32)
    wb = cx.ring(key + "_wb", 3, wp, BF16)
    cx.dma(st.ap[:, 0:n], dram_ap.rearrange("p k c -> p (k c)"), [], [st])
    cx.copy(getattr(cx, "cast_eng", "pool"), wb.ap[:, 0:n], st.ap[:, 0:n], [st], [wb])
    return wb


def load_w_multi(cx, dram_ap, kc, cw, key="w"):
    per = max(1, getattr(cx, "wpiece", 2048) // cw)
    out = []
    k0 = 0
    while k0 < kc:
        kn = min(per, kc - k0)
        out.append((load_w(cx, dram_ap[:, k0:k0 + kn, :], kn, cw, key), k0, kn))
        k0 += kn
    return out


def load_w_big(cx, dram_ap, kc, cw, name):
    wp = getattr(cx, "wpiece", 2048)
    big = cx.tile(kc * cw, BF16, name)
    per = max(1, wp // cw)
    k0 = 0
    while k0 < kc:
        kn = min(per, kc - k0)
        n = kn * cw
        st = cx.ring("w_st", 2, wp, F32)
        cx.dma(st.ap[:, 0:n], dram_ap[:, k0:k0 + kn, :].rearrange("p k c -> p (k c)"), [], [st])
        cx.copy("pool", big.ap[:, k0 * cw:(k0 + kn) * cw], st.ap[:, 0:n], [st], [big])
        k0 += kn
    return [(big, 0, kc)]


def wslice(pieces, k, cw, m0=0, m1=None):
    m1 = cw if m1 is None else m1
    for wb, k0, kn in pieces:
        if k0 <= k < k0 + kn:
            return wb.ap[:, (k - k0) * cw + m0:(k - k0) * cw + m1]
    raise KeyError(k)


def gemm_block(cx, pieces, cw, act_fn, act_tiles, lo, hi, ks=None, m0=0, m1=None, po=0):
    ps = cx.ps()
    n = hi - lo
    m1 = cw if m1 is None else m1
    if ks is None:
        ks = []
        for wb, k0, kn in pieces:
            ks += list(range(k0, k0 + kn))
    mms = []
    for i, k in enumerate(ks):
        mms.append(dict(out=ps.ap[po:po + m1 - m0, 0:n], lhsT=wslice(pieces, k, cw, m0, m1), rhs=act_fn(k, lo, hi),
                        start=(i == 0), stop=(i == len(ks) - 1)))
    cx.mm(mms, [p[0] for p in pieces] + list(act_tiles), [ps])
    return ps


def rmsnorm_fm(cx, srcs_fn, g_t, out_fn, tiles, ones_bf, nk=KC):
    for (lo, hi) in tiles:
        n = hi - lo
        ps = cx.ps()
        srcs = [srcs_fn(k, lo, hi) for k in range(nk)]
        for k in range(nk):
            sqt = cx.ring("rn_sq", 3, 512, BF16)
            sap, st = srcs[k]
            cx.act(sqt.ap[:, 0:n], sap, AF.Square, [st], [sqt])
            cx.mm([dict(out=ps.ap[:, 0:n], lhsT=ones_bf.ap, rhs=sqt.ap[:, 0:n], start=(k == 0), stop=(k == nk - 1))],
                  [sqt, ones_bf], [ps])
        rinv = cx.ring("rn_rinv", 2, 512, F32)
        cx.act(rinv.ap[:, 0:n], ps.ap[:, 0:n], AF.Ln, [ps], [rinv], scale=1.0 / (nk * 128), bias=EPS)
        cx.act(rinv.ap[:, 0:n], rinv.ap[:, 0:n], AF.Exp, [rinv], [rinv], scale=-0.5)
        for k in range(nk):
            sap, st = srcs[k]
            oap, ot = out_fn(k, lo, hi)
            cx.stt(oap, sap, g_t.ap[:, k:k + 1], rinv.ap[:, 0:n], ALU.mult, ALU.mult, [st, rinv, g_t], [ot])


def load_small(cx, dram_ap, nelem, name, parts=128):
    t = cx.tile(nelem, F32, name, parts)
    cx.dma(t.ap, dram_ap, [], [t])
    return t


def phase_T(cx, io, final, half):
    NT = NTOK_T
    cx.cast_eng = "act"
    base = cx.mark()
    ones_bf = cx.tile(128, BF16, "ones")
    cx.I("dve", "memset", [], [ones_bf], ap=ones_bf.ap, constant=1.0)
    g1 = load_small(cx, io["g_attn"], 16, "g1")
    g2 = load_small(cx, io["g_ffn"], 16, "g2")
    g3 = load_small(cx, io["g_next"], 16, "g3")
    convw = load_small(cx, io["fconv"], 88 * 3, "convw")
    mg_t = cx.tile(KC * NT, BF16, "merged")
    h_t = cx.tile(KC * NT, BF16, "h")
    mv = mg_t.ap.rearrange("p (k t) -> p k t", k=KC)
    hv = h_t.ap.rearrange("p (k t) -> p k t", k=KC)
    M0 = cx.mark()
    y_t = cx.tile(20 * NT, BF16, "y")
    yv = y_t.ap.rearrange("p (k t) -> p k t", k=20)
    xT = io["xT"].rearrange("(k p) t -> p k t", p=128)
    yT = io["ybuf"].rearrange("(k p) t -> p k t", p=128)
    t0 = 1024 * half
    xdeps = io.get("x_deps", [])
    ydeps = io.get("y_deps", [])

    def load_x(dst_ap, dst_t, k, lo, hi):
        if lo < 2 and half == 0:
            cx.I("dve", "memset", [], [dst_t], ap=dst_ap[:, 0:2 - lo], constant=0.0)
            if hi > 2:
                cx.dma(dst_ap[:, 2 - lo:hi - lo], xT[:, k, 0:hi - 2], xdeps, [dst_t])
        else:
            cx.dma(dst_ap, xT[:, k, t0 - 2 + lo:t0 - 2 + hi], xdeps, [dst_t])

    M1 = cx.mark()
    for k in range(20):
        cx.dma(yv[:, k, 2:NT], yT[:, k, t0:t0 + 1024], ydeps, [y_t])
    if half == 0:
        cx.I("dve", "memset", [], [y_t], ap=yv[:, :, 0:2], constant=0.0)
    else:
        for k in range(20):
            cx.dma(yv[:, k, 0:2], yT[:, k, t0 - 2:t0], ydeps, [y_t])
    for (lo, hi) in TT:
        n = hi - lo
        xt = cx.ring("xtile", 1, KC * 512, F32)
        xtv = xt.ap.rearrange("p (k t) -> p k t", k=KC)
        for k in range(KC):
            load_x(xtv[:, k, 0:n], xt, k, lo, hi)
        rmsnorm_fm(cx, lambda k, l, h_: (xtv[:, k, 0:h_ - l], xt), g1, lambda k, l, h_: (hv[:, k, l:h_], h_t),
                   [(lo, hi)], ones_bf)
    cx.release(M1)
    wg = io["wg"]
    wp = io["wp"]
    ksl = [(0, 8), (8, 12), (12, 20)]
    for j in range(16):
        gts = []
        for br in range(3):
            pcs = load_w_multi(cx, wg[br * 16 + j], KC, 128)
            gt = cx.ring("gate", 4, NT, BF16)
            for (lo, hi) in TT:
                ps = gemm_block(cx, pcs, 128, lambda k, l, h_: hv[:, k, l:h_], [h_t], lo, hi)
                cx.act(gt.ap[:, lo:hi], ps.ap[:, 0:hi - lo], AF.Sigmoid, [ps], [gt])
            gts.append(gt)
        pw = load_w_multi(cx, wp[j], 20, 128)
        for (lo, hi) in TT:
            n = hi - lo
            tmp = cx.ring("mtmp", 2, 512, F32)
            for br in range(3):
                k0, k1 = ksl[br]
                ps = gemm_block(cx, pw, 128, lambda k, l, h_: yv[:, k, l:h_], [y_t], lo, hi, ks=list(range(k0, k1)))
                gt = gts[br]
                if br == 0:
                    cx.tt("dve", tmp.ap[:, 0:n], ps.ap[:, 0:n], gt.ap[:, lo:hi], ALU.mult, [ps, gt], [tmp])
                else:
                    t2 = cx.ring("mtmp2", 2, 512, F32)
                    cx.tt("dve", t2.ap[:, 0:n], ps.ap[:, 0:n], gt.ap[:, lo:hi], ALU.mult, [ps, gt], [t2])
                    if br == 1:
                        cx.tt("pool", tmp.ap[:, 0:n], tmp.ap[:, 0:n], t2.ap[:, 0:n], ALU.add, [tmp, t2], [tmp])
                    else:
                        cx.tt("pool", mv[:, j, lo:hi], tmp.ap[:, 0:n], t2.ap[:, 0:n], ALU.add, [tmp, t2], [mg_t])
    cx.release(M0)
    x_sb = cx.tile(KC * NT, F32, "x_sb")
    xv = x_sb.ap.rearrange("p (k t) -> p k t", k=KC)
    M2 = cx.mark()
    wo = io["wo"]
    for j in range(16):
        pcs = load_w_multi(cx, wo[j], KC, 128)
        xin = cx.ring("xin", 2, NT, F32)
        load_x(xin.ap, xin, j, 0, NT)
        for (lo, hi) in TT:
            ps = gemm_block(cx, pcs, 128, lambda k, l, h_: mv[:, k, l:h_], [mg_t], lo, hi)
            cx.tt("dve", xv[:, j, lo:hi], ps.ap[:, 0:hi - lo], xin.ap[:, lo:hi], ALU.add, [ps, xin], [x_sb])
    cx.release(M2)
    rmsnorm_fm(cx, lambda k, l, h_: (xv[:, k, l:h_], x_sb), g2, lambda k, l, h_: (hv[:, k, l:h_], h_t), TT, ones_bf)
    GRP = 11
    a_t = T(mg_t.ap, mg_t.d)
    av = a_t.ap[:, 0:GRP * 1024].rearrange("p (k t) -> p k t", k=GRP)
    wup = io["wup"]
    wdn = io["wdn"]
    cwv = convw.ap.rearrange("p (b t) -> p b t", t=3)
    for g in range(NFB // GRP):
        for jj in range(GRP):
            jb = g * GRP + jj
            us = []
            for part in range(2):
                blk = part * NFB + jb
                pcs = load_w_multi(cx, wup[blk], KC, 128)
                u = cx.ring("u_f32", 2, NT, F32)
                for (lo, hi) in TT:
                    ps = gemm_block(cx, pcs, 128, lambda k, l, h_: hv[:, k, l:h_], [h_t], lo, hi)
                    cx.copy("act", u.ap[:, lo:hi], ps.ap[:, 0:hi - lo], [ps], [u])
                c = cx.ring("c_f32", 2, 1024, F32)
                cx.ts("dve", c.ap, u.ap[:, 2:NT], cwv[:, blk, 2:3], ALU.mult, [u, convw], [c])
                cx.stt(c.ap, u.ap[:, 1:NT - 1], cwv[:, blk, 1:2], c.ap, ALU.mult, ALU.add, [u, convw, c], [c])
                cx.stt(c.ap, u.ap[:, 0:NT - 2], cwv[:, blk, 0:1], c.ap, ALU.mult, ALU.add, [u, convw, c], [c])
                us.append(c)
            sg = cx.ring("silu", 2, 1024, F32)
            cx.act(sg.ap, us[0].ap, AF.Silu, [us[0]], [sg])
            cx.tt("pool", av[:, jj, :], sg.ap, us[1].ap, ALU.mult, [sg, us[1]], [a_t])
        for j in range(16):
            pcs = load_w_multi(cx, wdn[j][:, g * GRP:(g + 1) * GRP, :], GRP, 128)
            for ti in range(2):
                lo, hi = ti * 512, ti * 512 + 512
                ps = gemm_block(cx, pcs, 128, lambda k, l, h_: av[:, k, l:h_], [a_t], lo, hi)
                cx.tt("dve", xv[:, j, 2 + lo:2 + hi], ps.ap[:, 0:512], xv[:, j, 2 + lo:2 + hi], ALU.add, [ps, x_sb], [x_sb])
    cx.release(M2)
    outT = io["outT"].rearrange("(k p) t -> p k t", p=128)
    odeps = io.get("out_deps", [])
    if not final:
        for k in range(KC):
            cx.dma(outT[:, k, t0:t0 + 1024], xv[:, k, 2:NT], [x_sb], odeps)
    else:
        o_t = T(h_t.ap.bitcast(F32)[:, 0:KC * 512], h_t.d)
        ov = o_t.ap.rearrange("p (k t) -> p k t", k=KC)
        for ti in range(2):
            lo, hi = 2 + ti * 512, 2 + ti * 512 + 512
            rmsnorm_fm(cx, lambda k, l, h_: (xv[:, k, l:h_], x_sb), g3, lambda k, l, h_: (ov[:, k, 0:512], o_t),
                       [(lo, hi)], ones_bf)
            for k in range(KC):
                cx.dma(outT[:, k, t0 + lo - 2:t0 + lo - 2 + 512], ov[:, k, :], [o_t], odeps)
    cx.release(base)


import os
ATT_G = os.environ.get("ATT_G", "012")
ATT_STOP = int(os.environ.get("ATT_STOP", "99"))
ATT_SUB = int(os.environ.get("ATT_SUB", "3"))
RW_DBG = os.environ.get("RW_DBG", "")
ATT_V = os.environ.get("ATT_V", "ab")
NCH = 16
C_ID, C_MUI, C_MUS, C_MLI, C_BO, C_BS, C_S127, C_MLS = 0, 128, 256, 384, 512, 640, 642, 770
C_BD8, C_O16, C_O32, C_O64, C_O128 = 898, 1026, 1154, 1282, 1410
CST_N = 1538


def make_consts():
    c = np.zeros((128, CST_N), np.float32)
    j = np.arange(128)[:, None]
    i = np.arange(128)[None, :]
    c[:, C_ID:C_ID + 128] = (i == j)
    c[:, C_MUI:C_MUI + 128] = (i >= j)
    c[:, C_MUS:C_MUS + 128] = (i > j)
    c[:, C_MLI:C_MLI + 128] = (j >= i)
    c[:, C_BO:C_BO + 128] = ((i // 64) == (j // 64))
    c[:, C_BS:C_BS + 2] = (np.arange(2)[None, :] == (j // 64))
    c[:, C_S127:C_S127 + 128] = (j == 127)
    c[:, C_MLS:C_MLS + 128] = (j > i)
    bd = lambda s_: (i // s_) == (j // s_)
    c[:, C_BD8:C_BD8 + 128] = bd(8)
    c[:, C_O16:C_O16 + 128] = bd(16) & ~bd(8)
    c[:, C_O32:C_O32 + 128] = bd(32) & ~bd(16)
    c[:, C_O64:C_O64 + 128] = bd(64) & ~bd(32)
    c[:, C_O128:C_O128 + 128] = ~bd(64)
    return c


def make_rope():
    half = 16
    inv = 500000.0 ** (-np.arange(half, dtype=np.float32) / half)
    ang = np.arange(SEQ, dtype=np.float32)[None, :] * inv[:, None]
    cos = np.cos(ang).astype(np.float32)
    sin = np.sin(ang).astype(np.float32)
    r = np.zeros((32, 2 * SEQ), np.float32)
    r[0:16, 0:SEQ] = cos
    r[16:32, 0:SEQ] = cos
    r[0:16, SEQ:] = -sin
    r[16:32, SEQ:] = sin
    return r


class Consts:
    pass


def setup_consts(cx, io):
    K = Consts()
    cf = load_small(cx, io["cst"], CST_N, "cst_f32")
    cb = cx.tile(CST_N, BF16, "cst_bf")
    cx.copy("dve", cb.ap, cf.ap, [cf], [cb])
    K.cf, K.cb = cf, cb
    K.ones = cx.tile(128, BF16, "ones")
    cx.I("dve", "memset", [], [K.ones], ap=K.ones.ap, constant=1.0)
    K.m4 = {}
    for nm_, off in (("id", C_ID), ("bd8", C_BD8), ("o16", C_O16), ("o32", C_O32), ("o64", C_O64), ("o128", C_O128)):
        t = cx.tile(512, BF16, "m4" + nm_)
        for q in range(4):
            cx.copy("pool", t.ap[:, q * 128:(q + 1) * 128], cb.ap[:, off:off + 128], [cb], [t])
        K.m4[nm_] = t
    return K


def dplr_head(cx, K, nm, dk, dv, pb, Kg, Bg, Ag, Rg, gtiles, Wst, Win, wtiles, As, Rs, egam, Kn, Bn, V, ntiles, PC, pctiles,
              scale_all, out_cb, defer=False):
    m0 = cx.mark()
    idb = K.cb.ap[:, C_ID:C_ID + 128]
    neg = Bg is None
    AakT = cx.tile(NCH * 128, BF16, nm + "aak")
    ArkT = cx.tile(NCH * 128, BF16, nm + "ark")
    ArbT = cx.tile(NCH * 128, BF16, nm + "arb")
    TT_ = cx.tile(NCH * 128, BF16, nm + "TT")
    sl = lambda t, c: t.ap[:, c * 128:(c + 1) * 128]
    m1 = cx.mark()
    HC = 8
    tl_ = [cx.tile(HC * 128, BF16, nm + "iv%d" % i) for i in range(12)]
    U, N, Ua, Na, Nb, Ub_, Nc, Uc, P, Q, Z1b, Z2b = tl_
    hs = lambda t, q4: t.ap[:, q4 * 512:(q4 + 1) * 512]
    h1 = lambda t, cl: t.ap[:, cl * 128:(cl + 1) * 128]

    def mm4(dst_ps, lhs_t, rhs_t, q4):
        cx.mm([dict(out=dst_ps.ap[:, q * 128:(q + 1) * 128], lhsT=h1(lhs_t, q4 * 4 + q), rhs=h1(rhs_t, q4 * 4 + q), start=True, stop=True)
               for q in range(4)], [lhs_t, rhs_t], [dst_ps])

    for half in range(NCH // HC):
        for cl in range(HC):
            c = half * HC + cl
            ps = cx.ps()
            mms = [dict(out=ps.ap[:, 0:128], lhsT=Kg(c), rhs=Ag(c), start=True, stop=True),
                   dict(out=ps.ap[:, 128:256], lhsT=Kg(c), rhs=Rg(c), start=True, stop=True)]
            if not neg:
                mms += [dict(out=ps.ap[:, 256:384], lhsT=Bg(c), rhs=Ag(c), start=True, stop=True),
                        dict(out=ps.ap[:, 384:512], lhsT=Bg(c), rhs=Rg(c), start=True, stop=True)]
            cx.mm(mms, gtiles, [ps])
            cx.tt("dve", sl(AakT, c), ps.ap[:, 0:128], Wst(c), ALU.mult, [ps] + wtiles, [AakT])
            cx.tt("dve", sl(ArkT, c), ps.ap[:, 128:256], Win(c), ALU.mult, [ps] + wtiles, [ArkT])
            if neg:
                cx.stt(h1(U, cl), ps.ap[:, 0:128], -1.0, Wst(c), ALU.mult, ALU.mult, [ps] + wtiles, [U])
                cx.stt(sl(ArbT, c), ps.ap[:, 128:256], -1.0, Win(c), ALU.mult, ALU.mult, [ps] + wtiles, [ArbT])
            else:
                cx.tt("dve", h1(U, cl), ps.ap[:, 256:384], Wst(c), ALU.mult, [ps] + wtiles, [U])
                cx.tt("dve", sl(ArbT, c), ps.ap[:, 384:512], Win(c), ALU.mult, [ps] + wtiles, [ArbT])
        for q4 in range(HC // 4):
            ps = cx.ps()
            pv = ps.ap.bitcast(BF16)
            for q in range(4):
                cx.tr(pv[:, q * 128:(q + 1) * 128], h1(U, q4 * 4 + q), idb, [U, K.cb], [ps])
            cx.copy("act", hs(N, q4), pv[:, 0:512], [ps], [N])
        for q4 in range(HC // 4):
            cx.tt("pool", hs(Ua, q4), hs(U, q4), K.m4["bd8"].ap, ALU.mult, [U, K.m4["bd8"]], [Ua])
            cx.tt("pool", hs(Na, q4), hs(N, q4), K.m4["bd8"].ap, ALU.mult, [N, K.m4["bd8"]], [Na])
        for (dn, du, sn, su) in ((Nb, Ub_, Na, Ua), (Nc, Uc, Nb, Ub_)):
            for q4 in range(HC // 4):
                ps = cx.ps()
                mm4(ps, su, sn, q4)
                cx.copy("act", hs(dn, q4), ps.ap[:, 0:512], [ps], [dn])
                ps = cx.ps()
                mm4(ps, sn, su, q4)
                cx.copy("act", hs(du, q4), ps.ap[:, 0:512], [ps], [du])
        for q4 in range(HC // 4):
            cx.tt("pool", hs(P, q4), hs(Ua, q4), K.m4["id"].ap, ALU.add, [Ua, K.m4["id"]], [P])
            cx.tt("pool", hs(Q, q4), hs(Na, q4), K.m4["id"].ap, ALU.add, [Na, K.m4["id"]], [Q])
        for (ln, lu) in ((Nb, Ub_), (Nc, Uc)):
            for q4 in range(HC // 4):
                ps = cx.ps()
                mm4(ps, ln, P, q4)
                cx.tt("dve", hs(P, q4), ps.ap[:, 0:512], hs(P, q4), ALU.add, [ps, P], [P])
                ps = cx.ps()
                mm4(ps, lu, Q, q4)
                cx.tt("dve", hs(Q, q4), ps.ap[:, 0:512], hs(Q, q4), ALU.add, [ps, Q], [Q])
        NO, UO, P2, Q2 = Ua, Na, Nb, Ub_
        curP, curQ, nxtP, nxtQ = P, Q, P2, Q2
        for li, mk in enumerate(("o16", "o32", "o64", "o128")):
            last = (li == 3)
            for q4 in range(HC // 4):
                cx.tt("pool", hs(NO, q4), hs(N, q4), K.m4[mk].ap, ALU.mult, [N, K.m4[mk]], [NO])
                if not last:
                    cx.tt("pool", hs(UO, q4), hs(U, q4), K.m4[mk].ap, ALU.mult, [U, K.m4[mk]], [UO])
            for q4 in range(HC // 4):
                ps = cx.ps()
                mm4(ps, NO, curP, q4)
                cx.copy("act", hs(Z1b, q4), ps.ap[:, 0:512], [ps], [Z1b])
                if not last:
                    ps = cx.ps()
                    mm4(ps, UO, curQ, q4)
                    cx.copy("act", hs(Z2b, q4), ps.ap[:, 0:512], [ps], [Z2b])
            for q4 in range(HC // 4):
                ps = cx.ps()
                mm4(ps, curQ, Z1b, q4)
                if last:
                    c0_ = (half * HC + q4 * 4) * 128
                    cx.tt("dve", TT_.ap[:, c0_:c0_ + 512], ps.ap[:, 0:512], hs(curP, q4), ALU.add, [ps, curP], [TT_])
                else:
                    cx.tt("dve", hs(nxtP, q4), ps.ap[:, 0:512], hs(curP, q4), ALU.add, [ps, curP], [nxtP])
                    ps = cx.ps()
                    mm4(ps, curP, Z2b, q4)
                    cx.tt("dve", hs(nxtQ, q4), ps.ap[:, 0:512], hs(curQ, q4), ALU.add, [ps, curQ], [nxtQ])
            curP, curQ, nxtP, nxtQ = nxtP, nxtQ, curP, curQ
    cx.release(m1)
    AV = None
    if egam is not None:
        AV = cx.tile(NCH * dv, F32, nm + "AV")
        for c in range(NCH):
            ps = cx.ps()
            cx.mm([dict(out=ps.ap[:, 0:dv], lhsT=sl(AakT, c), rhs=V(c), start=True, stop=True)], [AakT] + ntiles, [ps])
            cx.copy("act", AV.ap[:, c * dv:(c + 1) * dv], ps.ap[:, 0:dv], [ps], [AV])
    St = cx.tile(dv, F32, nm + "S")
    Sb = cx.tile(dv, BF16, nm + "Sb")
    cx.I("dve", "memset", [], [St], ap=St.ap, constant=0.0)
    cx.I("dve", "memset", [], [Sb], ap=Sb.ap, constant=0.0)
    rows = slice(pb, pb + dk)

    def step(c):
        Xb = cx.ring(nm + "Xb", 2, dv, BF16)
        Ub = cx.ring(nm + "Ub", 2, dv, BF16)
        psX = cx.ps()
        if egam is None:
            cx.mm([dict(out=psX.ap[:, 0:dv], lhsT=As(c), rhs=Sb.ap[rows, :], start=True, stop=False),
                   dict(out=psX.ap[:, 0:dv], lhsT=sl(AakT, c), rhs=V(c), start=False, stop=True)],
                  gtiles + [Sb, AakT] + ntiles, [psX])
            cx.copy("act", Xb.ap, psX.ap[:, 0:dv], [psX], [Xb])
        else:
            cx.mm([dict(out=psX.ap[:, 0:dv], lhsT=As(c), rhs=Sb.ap[rows, :], start=True, stop=True)], gtiles + [Sb], [psX])
            cx.stt(Xb.ap, psX.ap[:, 0:dv], egam(c), AV.ap[:, c * dv:(c + 1) * dv], ALU.mult, ALU.add, [psX, AV] + pctiles, [Xb])
        psU = cx.ps()
        cx.mm([dict(out=psU.ap[:, 0:dv], lhsT=sl(TT_, c), rhs=Xb.ap, start=True, stop=True)], [TT_, Xb], [psU])
        cx.copy("act", Ub.ap, psU.ap[:, 0:dv], [psU], [Ub])
        if egam is None:
            psY = cx.ps()
            cx.mm([dict(out=psY.ap[:, 0:dv], lhsT=Rs(c), rhs=Sb.ap[rows, :], start=True, stop=False),
                   dict(out=psY.ap[:, 0:dv], lhsT=sl(ArbT, c), rhs=Ub.ap, start=False, stop=False),
                   dict(out=psY.ap[:, 0:dv], lhsT=sl(ArkT, c), rhs=V(c), start=False, stop=True)],
                  gtiles + [Sb, ArbT, ArkT, Ub] + ntiles, [psY])
            out_cb(c, psY.ap[:, 0:dv], psY)
        else:
            psY1 = cx.ps()
            cx.mm([dict(out=psY1.ap[:, 0:dv], lhsT=Rs(c), rhs=Sb.ap[rows, :], start=True, stop=True)], gtiles + [Sb], [psY1])
            psY2 = cx.ps()
            cx.mm([dict(out=psY2.ap[:, 0:dv], lhsT=sl(ArbT, c), rhs=Ub.ap, start=True, stop=False),
                   dict(out=psY2.ap[:, 0:dv], lhsT=sl(ArkT, c), rhs=V(c), start=False, stop=True)],
                  [ArbT, ArkT, Ub] + ntiles, [psY2])
            y2 = cx.ring(nm + "y2", 2, dv, F32)
            cx.copy("act", y2.ap, psY2.ap[:, 0:dv], [psY2], [y2])
            yo = cx.ring(nm + "yo", 2, dv, F32)
            cx.stt(yo.ap, psY1.ap[:, 0:dv], egam(c), y2.ap, ALU.mult, ALU.add, [psY1, y2] + pctiles, [yo])
            out_cb(c, yo.ap, yo)
        psS = cx.ps()
        cx.mm([dict(out=psS.ap[rows, 0:dv], lhsT=Bn(c), rhs=Ub.ap, start=True, stop=False),
               dict(out=psS.ap[rows, 0:dv], lhsT=Kn(c), rhs=V(c), start=False, stop=True)], ntiles + [Ub], [psS])
        if scale_all:
            cx.tt("dve", St.ap[rows, :], psS.ap[rows, 0:dv], St.ap[rows, :], ALU.add, [psS, St], [St])
            cx.ts("dve", St.ap[rows, :], St.ap[rows, :], PC(c), ALU.mult, [St] + pctiles, [St])
        else:
            cx.stt(St.ap[rows, :], St.ap[rows, :], PC(c), psS.ap[rows, 0:dv], ALU.mult, ALU.add, [St, psS] + pctiles, [St])
        cx.copy("act", Sb.ap[rows, :], St.ap[rows, :], [St], [Sb])

    def finish():
        cx.release(m0)
    if defer:
        return step, finish
    for c in range(NCH):
        step(c)
    finish()


def ln_exp_rinv(cx, out_ap, in_ap, reads, wt, scale=1.0, bias=1e-24):
    cx.act(out_ap, in_ap, AF.Ln, reads, [wt], scale=scale, bias=bias)
    cx.act(out_ap, out_ap, AF.Exp, [wt], [wt], scale=-0.5)


def mixer_attention(cx, K, io, hT, hv):
    m0 = cx.mark()
    wat = io["wat"]
    wsw = io["wsw"]
    rope = load_small(cx, io["rope"], 2 * SEQ, "rope", parts=32)
    cosv = rope.ap[:, 0:SEQ]
    sinv = rope.ap[:, SEQ:2 * SEQ]
    mdiag = K.cb.ap[:, C_MUI:C_MUI + 128]
    mprev = K.cb.ap[:, C_MLI:C_MLI + 128]
    mcomb = cx.tile(256, BF16, "mcomb")
    cx.copy("dve", mcomb.ap[:, 0:128], mdiag, [K.cb], [mcomb])
    cx.copy("dve", mcomb.ap[:, 128:256], mprev, [K.cb], [mcomb])
    if ATT_STOP <= 1:
        return
    DIL = [1, 4, 16]
    scale = 128.0 ** -0.5
    yout = io["ybuf"].rearrange("(k p) t -> p k t", p=128)
    for hl in range(2):
        m1 = cx.mark()
        qT = [cx.tile(SEQ, BF16, "qT%d" % g) for g in range(3)]
        kT = [cx.tile(SEQ, BF16, "kT%d" % g) for g in range(3)]
        Vt = [cx.tile(NCH * 128, BF16, "V%d" % g) for g in range(3)]
        for g in range(3):
            d = DIL[g]
            for which, dst in ((0, qT[g]), (1, kT[g])):
                pcs = load_w_multi(cx, wat[(hl * 3 + g) * 3 + which], KC, 128)
                pcs_sw = load_w_multi(cx, wsw[(hl * 3 + g) * 2 + which], KC, 32)
                for tb in range(4):
                    lo, hi = tb * 512, tb * 512 + 512
                    ps = gemm_block(cx, pcs, 128, lambda k, l, h_: hv[:, k, l:h_], [hT], lo, hi)
                    if ATT_SUB >= 1:
                        ps2 = gemm_block(cx, pcs_sw, 32, lambda k, l, h_: hv[:, k, l:h_], [hT], lo, hi)
                    t1 = cx.ring("rp1", 2, 512, F32)
                    t2 = cx.ring("rp2", 2, 512, F32)
                    if ATT_SUB >= 2:
                        if "a" in ATT_V:
                            cx.tt("dve", t1.ap[0:32, :], ps.ap[0:32, 0:512], cosv[:, lo:hi], ALU.mult, [ps, rope], [t1])
                        if "b" in ATT_V:
                            cx.tt("dve", t2.ap[0:32, :], ps2.ap[0:32, 0:512], sinv[:, lo:hi], ALU.mult, [ps2, rope], [t2])
                        if "c" in ATT_V:
                            cx.tt("dve", t1.ap[0:32, :], ps.ap[0:32, 0:512], K.cf.ap[0:32, 0:512], ALU.mult, [ps, K.cf], [t1])
                        if "e" in ATT_V:
                            cx.tt("dve", t1.ap[:, :], ps.ap[:, 0:512], K.cf.ap[:, 0:512], ALU.mult, [ps, K.cf], [t1])
                        if "g" in ATT_V:
                            cx.tt("dve", t1.ap[0:32, :], ps.ap[0:32, 0:512], K.cf.ap[0:32, 0:512], ALU.mult, [ps, K.cf], [t1])
                        if "d" in ATT_V:
                            cx.tt("dve", t1.ap[0:32, :], t2.ap[0:32, :], cosv[:, lo:hi], ALU.mult, [t2, rope], [t1])
                    cx.copy("act", dst.ap[:, lo:hi], ps.ap[:, 0:512], [ps, t1] if "s" in ATT_V else [ps], [dst])
                    if ATT_SUB >= 3:
                        cx.tt("pool", dst.ap[0:32, lo:hi], t1.ap[0:32, :], t2.ap[0:32, :], ALU.add, [t1, t2], [dst])
            if ATT_STOP <= 2:
                return
            pcs = load_w_multi(cx, wat[(hl * 3 + g) * 3 + 2], KC, 128)
            nb = NCH // d
            for b4 in range(4):
                ps = cx.ps()
                mms = []
                for q in range(4):
                    blk = b4 * 4 + q
                    r, b = blk // nb, blk % nb
                    t0 = r + d * 128 * b
                    for k in range(KC):
                        mms.append(dict(out=ps.ap[:, q * 128:(q + 1) * 128], lhsT=hv[:, k, t0:t0 + d * 127 + 1:d],
                                        rhs=wslice(pcs, k, 128), start=(k == 0), stop=(k == KC - 1)))
                cx.mm(mms, [p[0] for p in pcs] + [hT], [ps])
                cx.copy("act", Vt[g].ap[:, b4 * 512:(b4 + 1) * 512], ps.ap[:, 0:512], [ps], [Vt[g]])
        if ATT_STOP <= 3:
            return
        for Tb in range(4):
            pso = cx.ps(pin=True)
            psd = cx.ps(pin=True)
            first = [True]

            def unit(g, kslices, qslice, nq, masks):
                pss = cx.ps()
                nk = len(kslices)
                cx.mm([dict(out=pss.ap[:, i * nq:(i + 1) * nq], lhsT=kT[g].ap[:, ks], rhs=qT[g].ap[:, qslice], start=True, stop=True)
                       for i, (ks, vb) in enumerate(kslices)], [kT[g], qT[g]], [pss])
                pe_ = cx.ring("pexp", 3, 256, BF16)
                pm = cx.ring("pmask", 3, 256, BF16)
                cx.act(pe_.ap[:, 0:nk * nq], pss.ap[:, 0:nk * nq], AF.Exp, [pss], [pe_], scale=scale)
                cx.tt("pool", pm.ap[:, 0:nk * nq], pe_.ap[:, 0:nk * nq], masks, ALU.mult, [pe_, mcomb, K.cb], [pm])
                mmo, mmd = [], []
                qs0 = qslice.start - Tb * 512
                st = qslice.step or 1
                ocols = slice(qs0, qs0 + (nq - 1) * st + 1, st)
                for i, (ks, vb) in enumerate(kslices):
                    f = first[0]
                    first[0] = False
                    mmo.append(dict(out=pso.ap[:, ocols], lhsT=Vt[g].ap[:, vb * 128:(vb + 1) * 128], rhs=pm.ap[:, i * nq:(i + 1) * nq],
                                    start=f, stop=False, skip_group_check=True))
                    mmd.append(dict(out=psd.ap[:, ocols], lhsT=K.ones.ap, rhs=pm.ap[:, i * nq:(i + 1) * nq],
                                    start=f, stop=False, skip_group_check=True))
                cx.mm(mmo, [Vt[g], pm], [pso])
                cx.mm(mmd, [K.ones, pm], [psd])

            for qb in range(4 * Tb, 4 * Tb + 4):
                ks = [(slice(qb * 128, qb * 128 + 128), qb)]
                if qb > 0:
                    ks.append((slice((qb - 1) * 128, qb * 128), qb - 1))
                unit(0, ks, slice(qb * 128, qb * 128 + 128), 128, mcomb.ap[:, 0:128 * len(ks)])
            for r in range(4 if "1" in ATT_G else 0):
                tq = r + 4 * 128 * Tb
                ks = [(slice(tq, tq + 4 * 127 + 1, 4), r * 4 + Tb)]
                if Tb > 0:
                    tk = r + 4 * 128 * (Tb - 1)
                    ks.append((slice(tk, tk + 4 * 127 + 1, 4), r * 4 + Tb - 1))
                unit(1, ks, slice(tq, tq + 4 * 127 + 1, 4), 128, mcomb.ap[:, 0:128 * len(ks)])
            for r in range(16 if "2" in ATT_G else 0):
                tq = r + 16 * 32 * Tb
                ks = [(slice(r, r + 16 * 127 + 1, 16), r)]
                unit(2, ks, slice(tq, tq + 16 * 31 + 1, 16), 32, mdiag[:, 32 * Tb:32 * Tb + 32])
            rd = cx.ring("rden", 2, 512, F32)
            cx.I("dve", "reciprocal", [psd], [rd], out=rd.ap, in_=psd.ap[:, 0:512])
            yo = cx.ring("ybo", 2, 512, BF16)
            cx.tt("dve", yo.ap, pso.ap[:, 0:512], rd.ap, ALU.mult, [pso, rd], [yo])
            cx.dma(yout[:, 8 + 2 * io["s"] + hl, Tb * 512:(Tb + 1) * 512], yo.ap, [yo], [io["ydep"]["b"]])
            cx.unpin_all()
            if ATT_STOP <= 4:
                return
        cx.release(m1)
    cx.release(m0)


def mixer_gdn(cx, K, io, hT, hv):
    m0 = cx.mark()
    wc = io["wc"]
    wba = io["wba"]
    gconv = load_small(cx, io["gconv"], 12 * 4, "gconv")
    gcv = gconv.ap.rearrange("p (b t) -> p b t", t=4)
    gpar = load_small(cx, io["gpar"], 8 + 128, "gpar")
    mui = K.cf.ap[:, C_MUI:C_MUI + 128]
    mus = K.cf.ap[:, C_MUS:C_MUS + 128]
    mls = K.cf.ap[:, C_MLS:C_MLS + 128]
    idb = K.cb.ap[:, C_ID:C_ID + 128]
    pcs = load_w_multi(cx, wba, KC, 8)
    ps = cx.ps()
    mms = []
    for c in range(NCH):
        for k in range(KC):
            mms.append(dict(out=ps.ap[:, c * 8:(c + 1) * 8], lhsT=hv[:, k, c * 128:(c + 1) * 128], rhs=wslice(pcs, k, 8),
                            start=(k == 0), stop=(k == KC - 1)))
    cx.mm(mms, [p[0] for p in pcs] + [hT], [ps])
    ba = cx.tile(NCH * 8, F32, "ba")
    cx.copy("act", ba.ap, ps.ap[:, 0:NCH * 8], [ps], [ba])
    bav = ba.ap.rearrange("p (c e) -> p c e", e=8)
    beta = cx.tile(NCH * 4, F32, "beta")
    betav = beta.ap.rearrange("p (c e) -> p c e", e=4)
    cx.act(betav, bav[:, :, 0:4], AF.Sigmoid, [ba], [beta])
    gg = cx.tile(NCH * 4, F32, "gg")
    ggv = gg.ap.rearrange("p (c e) -> p c e", e=4)
    for hh in range(4):
        cx.act(ggv[:, :, hh], bav[:, :, 4 + hh], AF.Exp, [ba, gpar], [gg], bias=gpar.ap[:, 4 + hh:5 + hh])
    cx.act(gg.ap, gg.ap, AF.Ln, [gg], [gg], bias=1.0)
    ea = cx.tile(4, F32, "expA")
    cx.act(ea.ap, gpar.ap[:, 0:4], AF.Exp, [gpar], [ea])
    for hh in range(4):
        cx.ts("dve", ggv[:, :, hh], ggv[:, :, hh], ea.ap[:, hh:hh + 1], ALU.mult, [gg, ea], [gg], s2=-1.0, op1=ALU.mult)
    ps = cx.ps()
    cx.mm([dict(out=ps.ap[:, 0:64], lhsT=mui, rhs=gg.ap, start=True, stop=True)], [K.cf, gg], [ps])
    gam = cx.tile(64, F32, "gam")
    cx.copy("act", gam.ap, ps.ap[:, 0:64], [ps], [gam])
    egam = cx.tile(64, F32, "egam")
    cx.act(egam.ap, gam.ap, AF.Exp, [gam], [egam])
    ps = cx.ps()
    cx.mm([dict(out=ps.ap[:, 0:64], lhsT=K.cf.ap[:, C_S127:C_S127 + 128], rhs=gam.ap, start=True, stop=True)], [K.cf, gam], [ps])
    pcall = cx.tile(64, F32, "pcall")
    cx.act(pcall.ap, ps.ap[:, 0:64], AF.Exp, [ps], [pcall])
    wk = cx.tile(64, F32, "wk")
    cx.tt("dve", wk.ap, ps.ap[:, 0:64], gam.ap, ALU.subtract, [ps, gam], [wk])
    cx.act(wk.ap, wk.ap, AF.Exp, [wk], [wk])
    cx.tt("dve", wk.ap, wk.ap, beta.ap, ALU.mult, [wk, beta], [wk])
    nwk = cx.tile(64, F32, "nwk")
    cx.ts("dve", nwk.ap, wk.ap, -1.0, ALU.mult, [wk], [nwk])
    yout = io["ybuf"].rearrange("(k p) t -> p k t", p=128)
    for hh in range(4):
        m1 = cx.mark()
        qT = cx.tile(SEQ, BF16, "gq")
        kT = cx.tile(SEQ, BF16, "gk")
        Vn = cx.tile(NCH * 128, BF16, "gV")
        Kn = cx.tile(NCH * 128, BF16, "gKn")
        Bn = cx.tile(NCH * 128, BF16, "gBn")
        gate = cx.tile(NCH * 128, BF16, "ggate")
        Wst = cx.tile(NCH * 128, F32, "gWst")
        Win = cx.tile(NCH * 128, F32, "gWin")
        m2 = cx.mark()
        vT = cx.tile(SEQ, BF16, "gvT")
        for which, dst in ((0, qT), (1, kT), (2, vT)):
            pcs = load_w_multi(cx, wc[which * 4 + hh], KC, 128)
            zp = cx.ring("gzp", 1, 3 + SEQ, F32)
            cx.I("dve", "memset", [], [zp], ap=zp.ap[:, 0:3], constant=0.0)
            for tb in range(4):
                lo, hi = tb * 512, tb * 512 + 512
                ps = gemm_block(cx, pcs, 128, lambda k, l, h_: hv[:, k, l:h_], [hT], lo, hi)
                cx.copy("act", zp.ap[:, 3 + lo:3 + hi], ps.ap[:, 0:512], [ps], [zp])
            cv = cx.ring("gcv", 1, SEQ, F32)
            bi = which * 4 + hh
            cx.ts("dve", cv.ap, zp.ap[:, 3:3 + SEQ], gcv[:, bi, 3:4], ALU.mult, [zp, gconv], [cv])
            for tap in range(3):
                cx.stt(cv.ap, zp.ap[:, tap:tap + SEQ], gcv[:, bi, tap:tap + 1], cv.ap, ALU.mult, ALU.add, [zp, gconv, cv], [cv])
            cx.act(cv.ap, cv.ap, AF.Silu, [cv], [cv])
            if which == 2:
                cx.copy("pool", dst.ap, cv.ap, [cv], [dst])
            else:
                for tb in range(4):
                    lo, hi = tb * 512, tb * 512 + 512
                    sq = cx.ring("gsq", 2, 512, BF16)
                    cx.act(sq.ap, cv.ap[:, lo:hi], AF.Square, [cv], [sq])
                    ps = cx.ps()
                    cx.mm([dict(out=ps.ap[:, 0:512], lhsT=K.ones.ap, rhs=sq.ap, start=True, stop=True)], [K.ones, sq], [ps])
                    ri = cx.ring("gri", 2, 512, F32)
                    ln_exp_rinv(cx, ri.ap, ps.ap[:, 0:512], [ps], ri)
                    if which == 0:
                        cx.stt(dst.ap[:, lo:hi], cv.ap[:, lo:hi], 128.0 ** -0.5, ri.ap, ALU.mult, ALU.mult, [cv, ri], [dst])
                    else:
                        cx.tt("dve", dst.ap[:, lo:hi], cv.ap[:, lo:hi], ri.ap, ALU.mult, [cv, ri], [dst])
        for c4 in range(4):
            ps = cx.ps()
            pv = ps.ap.bitcast(BF16)
            for q in range(4):
                c = c4 * 4 + q
                cx.tr(pv[:, q * 128:(q + 1) * 128], vT.ap[:, c * 128:(c + 1) * 128], idb, [vT, K.cb], [ps])
            cx.copy("act", Vn.ap[:, c4 * 512:(c4 + 1) * 512], pv[:, 0:512], [ps], [Vn])
            ps = cx.ps()
            pv = ps.ap.bitcast(BF16)
            for q in range(4):
                c = c4 * 4 + q
                cx.tr(pv[:, q * 128:(q + 1) * 128], kT.ap[:, c * 128:(c + 1) * 128], idb, [kT, K.cb], [ps])
            for q in range(4):
                c = c4 * 4 + q
                col = c * 4 + hh
                cx.ts("dve", Kn.ap[:, c * 128:(c + 1) * 128], pv[:, q * 128:(q + 1) * 128], wk.ap[:, col:col + 1], ALU.mult, [ps, wk], [Kn])
                cx.ts("dve", Bn.ap[:, c * 128:(c + 1) * 128], pv[:, q * 128:(q + 1) * 128], nwk.ap[:, col:col + 1], ALU.mult, [ps, nwk], [Bn])
        pcs = load_w_multi(cx, wc[12 + hh], KC, 128)
        for c4 in range(4):
            ps = cx.ps()
            mms = []
            for q in range(4):
                c = c4 * 4 + q
                for k in range(KC):
                    mms.append(dict(out=ps.ap[:, q * 128:(q + 1) * 128], lhsT=hv[:, k, c * 128:(c + 1) * 128], rhs=wslice(pcs, k, 128),
                                    start=(k == 0), stop=(k == KC - 1)))
            cx.mm(mms, [p[0] for p in pcs] + [hT], [ps])
            cx.act(gate.ap[:, c4 * 512:(c4 + 1) * 512], ps.ap[:, 0:512], AF.Silu, [ps], [gate])
        for c in range(NCH):
            col = c * 4 + hh
            g2 = cx.ring("gG2", 2, 128, F32)
            cx.ts("dve", g2.ap, mui, gg.ap[:, col:col + 1], ALU.mult, [K.cf, gg], [g2])
            ps = cx.ps()
            cx.mm([dict(out=ps.ap[:, 0:128], lhsT=mls, rhs=g2.ap, start=True, stop=True)], [K.cf, g2], [ps])
            ex = cx.ring("gex", 2, 128, F32)
            cx.act(ex.ap, ps.ap[:, 0:128], AF.Exp, [ps], [ex])
            cx.stt(Wst.ap[:, c * 128:(c + 1) * 128], ex.ap, beta.ap[:, col:col + 1], mus, ALU.mult, ALU.mult, [ex, beta, K.cf], [Wst])
            cx.stt(Win.ap[:, c * 128:(c + 1) * 128], ex.ap, beta.ap[:, col:col + 1], mui, ALU.mult, ALU.mult, [ex, beta, K.cf], [Win])
        cx.release(m2)
        sl = lambda t, c: t.ap[:, c * 128:(c + 1) * 128]
        ycs = cx.tile(SEQ, BF16, "ycs")

        def out_cb(c, yap, yt, hh=hh, ycs=ycs):
            junk = cx.ring("gjunk", 2, 128, F32)
            ss = cx.ring("gss", 2, 1, F32)
            cx.act(junk.ap, yap, AF.Square, [yt], [junk, ss], accum_out=ss.ap)
            ln_exp_rinv(cx, ss.ap, ss.ap, [ss], ss, scale=1.0 / 128, bias=EPS)
            o1 = cx.ring("go1", 2, 128, F32)
            cx.stt(o1.ap, yap, ss.ap[:, 0:1], gpar.ap[:, 8:136], ALU.mult, ALU.mult, [yt, ss, gpar], [o1])
            o2 = cx.ring("go2", 2, 128, BF16)
            cx.tt("pool", o2.ap, o1.ap, gate.ap[:, c * 128:(c + 1) * 128], ALU.mult, [o1, gate], [o2])
            pst = cx.ps()
            ptv = pst.ap.bitcast(BF16)
            cx.tr(ptv[:, 0:128], o2.ap, idb, [o2, K.cb], [pst])
            cx.copy("act", ycs.ap[:, c * 128:(c + 1) * 128], ptv[:, 0:128], [pst], [ycs])

        dplr_head(cx, K, "g%d" % hh, 128, 128, 0,
                  Kg=lambda c: sl(kT, c), Bg=None, Ag=lambda c: sl(kT, c), Rg=lambda c: sl(qT, c), gtiles=[kT, qT],
                  Wst=lambda c: sl(Wst, c), Win=lambda c: sl(Win, c), wtiles=[Wst, Win],
                  As=lambda c: sl(kT, c), Rs=lambda c: sl(qT, c), egam=lambda c: egam.ap[:, c * 4 + hh:c * 4 + hh + 1],
                  Kn=lambda c: sl(Kn, c), Bn=lambda c: sl(Bn, c), V=lambda c: sl(Vn, c), ntiles=[Kn, Bn, Vn],
                  PC=lambda c: pcall.ap[:, c * 4 + hh:c * 4 + hh + 1], pctiles=[pcall, egam], scale_all=False, out_cb=out_cb)
        cx.dma(yout[:, 12 + 4 * io["s"] + hh, :], ycs.ap, [ycs], [io["ydep"]["c"]])
        cx.release(m1)
    cx.release(m0)


def mixer_rwkv(cx, K, io, hT, hv, layer1):
    m0 = cx.mark()
    wa = io["wa"]
    wal = io["wal"]
    rp = load_small(cx, io["rpar"], 44, "rpar")
    rl = load_small(cx, io["rlmu"], 4, "rlmu")
    rb = load_small(cx, io["rbc"], 1024, "rbc")
    wab = cx.tile(512, BF16, "wab")
    g2b = cx.tile(1024, BF16, "g2b")
    if layer1:
        v2b_ = cx.tile(512, BF16, "v2b", parts=64)
        v2b = T(v2b_.ap[32:64, :], v2b_.d)
    rmask = cx.tile(SEQ, BF16, "rmask")
    tl = cx.tile(SEQ, BF16, "tl")
    sgd = cx.tile(SEQ, BF16, "sgd")
    sg2h = cx.tile(SEQ, BF16, "sg2h", parts=64)
    sgd2 = T(sg2h.ap[0:32, :], sg2h.d)
    if layer1:
        hv1 = T(sg2h.ap[32:64, :], sg2h.d)
    mt = cx.mark()
    w2a2 = load_small(cx, io["rw2a2"], 512, "rw2a2")
    g2a = load_small(cx, io["rg2"], 1024, "rg2")
    cx.copy("pool", wab.ap, w2a2.ap, [w2a2], [wab])
    cx.copy("pool", g2b.ap, g2a.ap, [g2a], [g2b])
    if layer1:
        v2 = cx.tile(512, F32, "rv2", parts=64)
        cx.dma(v2.ap[32:64, :], io["rv2"], [], [v2])
        cx.copy("pool", v2b.ap, v2.ap[32:64, :], [v2], [v2b])
    idb = K.cb.ap[:, C_ID:C_ID + 128]
    msk_s = K.cb.ap[:, C_MUS:C_MUS + 128]
    msk_i = K.cb.ap[:, C_MUI:C_MUI + 128]
    cx.I("dve", "memset", [], [rmask], ap=rmask.ap, constant=1.0)
    cx.I("dve", "memset", [], [rmask], ap=rmask.ap[:, 0:SEQ:128], constant=0.0)

    def mixed_block(zp, tmp, pcs, cw, c0, m, po, mu_ap, mu_t, dst_ap, dst_t, func=None):
        pr = slice(po, po + m)
        cx.I("dve", "memset", [], [zp], ap=zp.ap[pr, 0:1], constant=0.0)
        for tb in range(4):
            lo, hi = tb * 512, tb * 512 + 512
            ps = cx.ps()
            mms = []
            for k in range(KC):
                mms.append(dict(out=ps.ap[pr, 0:512], lhsT=wslice(pcs, k, cw, c0, c0 + m), rhs=hv[:, k, lo:hi],
                                start=(k == 0), stop=(k == KC - 1)))
            cx.mm(mms, [p[0] for p in pcs] + [hT], [ps])
            cx.copy("act", zp.ap[pr, 1 + lo:1 + hi], ps.ap[pr, 0:512], [ps], [zp])
        cx.tt("dve", tmp.ap[pr, :], zp.ap[pr, 0:SEQ], zp.ap[pr, 1:1 + SEQ], ALU.subtract, [zp], [tmp])
        if func is None:
            cx.stt(dst_ap, tmp.ap[pr, :], mu_ap, zp.ap[pr, 1:1 + SEQ], ALU.mult, ALU.add, [tmp, zp, mu_t], [dst_t])
        else:
            cx.stt(tmp.ap[pr, :], tmp.ap[pr, :], mu_ap, zp.ap[pr, 1:1 + SEQ], ALU.mult, ALU.add, [tmp, zp, mu_t], [tmp])
            cx.act(dst_ap, tmp.ap[pr, :], func, [tmp], [dst_t])

    zp0 = cx.tile(SEQ + 8, F32, "zp0")
    tmp0 = cx.tile(SEQ, F32, "tmp0")
    pcl = load_w_big(cx, wal, KC, 288, "wal_b")
    mixed_block(zp0, tmp0, pcl, 288, 0, 64, 0, rl.ap[0:64, 0:1], rl, tl.ap[0:64, :], tl, func=AF.Tanh)
    mixed_block(zp0, tmp0, pcl, 288, 64, 64, 64, rl.ap[64:128, 0:1], rl, tl.ap[64:128, :], tl, func=AF.Copy)
    mixed_block(zp0, tmp0, pcl, 288, 128, 128, 0, rl.ap[:, 1:2], rl, sgd.ap, sgd, func=AF.Sigmoid)
    mixed_block(zp0, tmp0, pcl, 288, 256, 32, 0, rl.ap[0:32, 2:3], rl, sgd2.ap, sgd2, func=AF.Sigmoid)
    if layer1:
        pc1 = load_w_multi(cx, io["rv1"], KC, 32)
        for tb in range(4):
            lo, hi = tb * 512, tb * 512 + 512
            ps = gemm_block(cx, pc1, 32, lambda k, l, h_: hv[:, k, l:h_], [hT], lo, hi, po=32)
            cx.copy("act", hv1.ap[:, lo:hi], ps.ap[32:64, 0:512], [ps], [hv1])
        vfT = io["vf_in"].rearrange("(k p) t -> p k t", p=128)
    cx.release(mt)
    vout = io["v_out"].rearrange("(k p) t -> p k t", p=128)
    yout = io["ybuf"].rearrange("(k p) t -> p k t", p=128)
    lnw = rb.ap[:, 0:512]
    lnb = rb.ap[:, 512:1024]
    for ct in range(4):
        m1 = cx.mark()
        par = rp.ap[:, 12 + ct * 8:12 + ct * 8 + 8]
        As = cx.tile(SEQ, BF16, "rAs")
        Rs = cx.tile(SEQ, BF16, "rRs")
        Ks = cx.tile(SEQ, BF16, "rKs")
        Bs = cx.tile(SEQ, BF16, "rBs")
        Kn = cx.tile(NCH * 128, BF16, "rKn")
        Bn = cx.tile(NCH * 128, BF16, "rBn")
        Vn = cx.tile(NCH * 128, BF16, "rVn")
        PCt = cx.tile(NCH, F32, "rPC")
        bon = cx.tile(NCH * 2, F32, "rbon")
        m2 = cx.mark()
        zp = cx.tile(SEQ + 8, F32, "zp")
        tmp = cx.tile(SEQ, F32, "tmp")
        ld = cx.tile(SEQ, F32, "ld")
        cum = cx.tile(SEQ, F32, "cum")
        av = cx.tile(SEQ, BF16, "a_sig")
        rm = cx.tile(SEQ, BF16, "r_m")
        km = cx.tile(SEQ, BF16, "k_m")
        vm = cx.tile(SEQ, BF16, "v_m")
        rkr = cx.tile(SEQ, BF16, "rkr")
        for which, dst in ((0, rm), (1, km), (2, vm)):
            pcs = load_w_multi(cx, wa[which * 4 + ct], KC, 128)
            mixed_block(zp, tmp, pcs, 128, 0, 128, 0, rp.ap[:, which * 4 + ct:which * 4 + ct + 1], rp, dst.ap, dst)
        kx = T(zp.ap[:, 0:SEQ], zp.d)
        B1 = tmp
        for tb in range(4):
            lo, hi = tb * 512, tb * 512 + 512
            ps = cx.ps()
            cx.mm([dict(out=ps.ap[:, 0:512], lhsT=wab.ap[0:64, ct * 128:(ct + 1) * 128], rhs=tl.ap[0:64, lo:hi], start=True, stop=True)],
                  [wab, tl], [ps])
            cx.act(ld.ap[:, lo:hi], ps.ap[:, 0:512], AF.Sigmoid, [ps, rp], [ld], bias=par[:, 0:1])
            ps = cx.ps()
            cx.mm([dict(out=ps.ap[:, 0:512], lhsT=wab.ap[64:128, ct * 128:(ct + 1) * 128], rhs=tl.ap[64:128, lo:hi], start=True, stop=True)],
                  [wab, tl], [ps])
            cx.act(av.ap[:, lo:hi], ps.ap[:, 0:512], AF.Sigmoid, [ps, rp], [av], bias=par[:, 1:2])
        cx.ts("dve", ld.ap, ld.ap, -float(np.exp(-0.5)), ALU.mult, [ld], [ld])
        cx.I("dve", "tensor_tensor_scan", [rmask, ld], [cum], out=cum.ap, data0=rmask.ap, data1=ld.ap, initial=0.0,
             op0=ALU.mult, op1=ALU.add)
        if layer1:
            vf = rkr
            cx.dma(vf.ap, vfT[:, ct, :], io["vf_deps"], [vf])
            for tb in range(4):
                lo, hi = tb * 512, tb * 512 + 512
                ps = cx.ps()
                cx.mm([dict(out=ps.ap[:, 0:512], lhsT=v2b.ap[:, ct * 128:(ct + 1) * 128], rhs=hv1.ap[:, lo:hi], start=True, stop=True)],
                      [v2b, hv1], [ps])
                cx.act(B1.ap[:, lo:hi], ps.ap[:, 0:512], AF.Sigmoid, [ps, rp], [B1], bias=par[:, 5:6])
            cx.tt("dve", kx.ap, vf.ap, vm.ap, ALU.subtract, [vf, vm], [kx])
            cx.tt("dve", kx.ap, kx.ap, B1.ap, ALU.mult, [kx, B1], [kx])
            cx.tt("dve", vm.ap, vm.ap, kx.ap, ALU.add, [vm, kx], [vm])
        if io.get("write_v", True):
            cx.dma(vout[:, ct, :], vm.ap, [vm], [io["v_dep"]])
        cx.ts("dve", kx.ap, km.ap, par[:, 2:3], ALU.mult, [km, rp], [kx])
        for tb in range(4):
            lo, hi = tb * 512, tb * 512 + 512
            sq = cx.ring("rsq", 2, 512, BF16)
            cx.act(sq.ap, kx.ap[:, lo:hi], AF.Square, [kx], [sq])
            ps = cx.ps()
            cx.mm([dict(out=ps.ap[:, 0:512], lhsT=K.cb.ap[:, C_BO:C_BO + 128], rhs=sq.ap, start=True, stop=True)], [K.cb, sq], [ps])
            ri = cx.ring("rri", 2, 512, F32)
            ln_exp_rinv(cx, ri.ap, ps.ap[:, 0:512], [ps], ri)
            cx.tt("dve", kx.ap[:, lo:hi], kx.ap[:, lo:hi], ri.ap, ALU.mult, [kx, ri], [kx])
        if RW_DBG and ct == 0:
            dbg = io["dbg"].rearrange("(k p) t -> p k t", p=128)
            cx.dma(dbg[:, 0, :], kx.ap, [kx], [])
            cx.dma(dbg[:, 1, :], ld.ap, [ld], [])
            cx.dma(dbg[:, 2, :], cum.ap, [cum], [])
        cx.ts("dve", B1.ap, av.ap, -1.0, ALU.add, [av, rp], [B1], s2=par[:, 3:4], op1=ALU.mult)
        cx.ts("dve", B1.ap, B1.ap, 1.0, ALU.add, [B1], [B1])
        if RW_DBG and ct == 0:
            cx.dma(dbg[:, 3, :], B1.ap, [B1], [])
        cx.tt("dve", km.ap, km.ap, B1.ap, ALU.mult, [km, B1], [km])
        cx.stt(rkr.ap, rm.ap, par[:, 4:5], km.ap, ALU.mult, ALU.mult, [rm, km, rp], [rkr])
        ps = cx.ps()
        cx.mm([dict(out=ps.ap[:, c * 2:c * 2 + 2], lhsT=rkr.ap[:, c * 128:(c + 1) * 128], rhs=K.cb.ap[:, C_BS:C_BS + 2], start=True, stop=True)
               for c in range(NCH)], [rkr, K.cb], [ps])
        cx.copy("act", bon.ap, ps.ap[:, 0:NCH * 2], [ps], [bon])
        cx.tt("dve", B1.ap, cum.ap, ld.ap, ALU.subtract, [cum, ld], [B1])
        cx.act(B1.ap, B1.ap, AF.Exp, [B1], [B1])
        cx.stt(As.ap, kx.ap, -1.0, B1.ap, ALU.mult, ALU.mult, [kx, B1], [As])
        cx.act(B1.ap, cum.ap, AF.Exp, [cum], [B1])
        cx.tt("dve", Rs.ap, rm.ap, B1.ap, ALU.mult, [rm, B1], [Rs])
        cx.copy("dve", PCt.ap, B1.ap[:, 127:SEQ:128], [B1], [PCt])
        cx.act(B1.ap, cum.ap, AF.Exp, [cum], [B1], scale=-1.0)
        cx.tt("dve", Ks.ap, km.ap, B1.ap, ALU.mult, [km, B1], [Ks])
        cx.tt("dve", kx.ap, kx.ap, av.ap, ALU.mult, [kx, av], [kx])
        cx.tt("dve", Bs.ap, kx.ap, B1.ap, ALU.mult, [kx, B1], [Bs])
        for src, dst in ((Ks, Kn), (Bs, Bn), (vm, Vn)):
            for c4 in range(4):
                ps = cx.ps()
                pv = ps.ap.bitcast(BF16)
                for q in range(4):
                    c = c4 * 4 + q
                    cx.tr(pv[:, q * 128:(q + 1) * 128], src.ap[:, c * 128:(c + 1) * 128], idb, [src, K.cb], [ps])
                cx.copy("act", dst.ap[:, c4 * 512:(c4 + 1) * 512], pv[:, 0:512], [ps], [dst])
        cx.release(m2)
        yas = cx.tile(SEQ, BF16, "yas")
        sfs = []
        for hp in range(2):
            pb = hp * 64
            hl = ct * 2 + hp
            pr = slice(pb, pb + 64)

            def out_cb(c, yap, yt, hl=hl, hp=hp, ct=ct, yas=yas, pb=pb):
                st = cx.ring("rst%d" % hp, 2, 6, F32)
                cx.I("dve", "bn_stats", [yt], [st], out=st.ap, in_=yap)
                mv_ = cx.ring("rmv%d" % hp, 2, 2, F32)
                cx.I("dve", "bn_aggr", [st], [mv_], out=mv_.ap, in_=st.ap)
                rs = cx.ring("rrs%d" % hp, 2, 1, F32)
                ln_exp_rinv(cx, rs.ap, mv_.ap[:, 1:2], [mv_], rs, bias=64e-5)
                y1 = cx.ring("ry1%d" % hp, 2, 64, F32)
                cx.ts("dve", y1.ap, yap, mv_.ap[:, 0:1], ALU.subtract, [yt, mv_, rs], [y1], s2=rs.ap[:, 0:1], op1=ALU.mult)
                cx.tt("pool", y1.ap, y1.ap, lnw[:, hl * 64:(hl + 1) * 64], ALU.mult, [y1, rb], [y1])
                cx.tt("pool", y1.ap, y1.ap, lnb[:, hl * 64:(hl + 1) * 64], ALU.add, [y1, rb], [y1])
                y2 = cx.ring("ry2%d" % hp, 2, 64, F32)
                cx.stt(y2.ap, Vn.ap[:, c * 128 + hp * 64:c * 128 + hp * 64 + 64], bon.ap[:, c * 2 + hp:c * 2 + hp + 1], y1.ap,
                       ALU.mult, ALU.add, [Vn, bon, y1], [y2])
                psg = cx.ps()
                cx.mm([dict(out=psg.ap[:, 0:64], lhsT=sgd.ap[:, c * 128:(c + 1) * 128], rhs=g2b.ap[:, hl * 64:(hl + 1) * 64], start=True, stop=False),
                       dict(out=psg.ap[:, 0:64], lhsT=sgd2.ap[:, c * 128:(c + 1) * 128], rhs=g2b.ap[0:32, 512 + hl * 64:512 + (hl + 1) * 64],
                            start=False, stop=True)], [sgd, sgd2, g2b], [psg])
                y3 = cx.ring("ry3%d" % hp, 2, 64, BF16)
                if RW_DBG == "g":
                    cx.copy("dve", y3.ap, psg.ap[:, 0:64], [psg], [y3])
                elif RW_DBG == "y1":
                    cx.copy("dve", y3.ap, y1.ap, [y1, psg], [y3])
                elif RW_DBG == "y2":
                    cx.copy("dve", y3.ap, y2.ap, [y2, psg], [y3])
                elif RW_DBG == "yraw":
                    cx.copy("dve", y3.ap, yap, [yt, psg], [y3])
                else:
                    cx.tt("dve", y3.ap, psg.ap[:, 0:64], y2.ap, ALU.mult, [psg, y2], [y3])
                pst = cx.ps()
                ptv = pst.ap.bitcast(BF16)
                cx.tr(ptv[pb:pb + 64, 0:128], y3.ap, idb, [y3, K.cb], [pst])
                cx.copy("act", yas.ap[pb:pb + 64, c * 128:(c + 1) * 128], ptv[pb:pb + 64, 0:128], [pst], [yas])

            sf = dplr_head(cx, K, "r%d" % hl, 64, 64, pb, defer=True,
                      Kg=lambda c, pr=pr: Ks.ap[pr, c * 128:(c + 1) * 128], Bg=lambda c, pr=pr: Bs.ap[pr, c * 128:(c + 1) * 128],
                      Ag=lambda c, pr=pr: As.ap[pr, c * 128:(c + 1) * 128], Rg=lambda c, pr=pr: Rs.ap[pr, c * 128:(c + 1) * 128],
                      gtiles=[Ks, Bs, As, Rs],
                      Wst=lambda c: msk_s, Win=lambda c: msk_i, wtiles=[K.cb],
                      As=lambda c, pr=pr: As.ap[pr, c * 128:(c + 1) * 128], Rs=lambda c, pr=pr: Rs.ap[pr, c * 128:(c + 1) * 128], egam=None,
                      Kn=lambda c, pb=pb: Kn.ap[:, c * 128 + pb:c * 128 + pb + 64], Bn=lambda c, pb=pb: Bn.ap[:, c * 128 + pb:c * 128 + pb + 64],
                      V=lambda c, pb=pb: Vn.ap[:, c * 128 + pb:c * 128 + pb + 64], ntiles=[Kn, Bn, Vn],
                      PC=lambda c, pr=pr: PCt.ap[pr, c:c + 1], pctiles=[PCt], scale_all=True, out_cb=out_cb)
            sfs.append(sf)
        for c in range(NCH):
            for st_, fn_ in sfs:
                st_(c)
        for st_, fn_ in reversed(sfs):
            fn_()
        cx.dma(yout[:, 4 * io["s"] + ct, :], yas.ap, [yas], [io["ydep"]["a"]])
        cx.release(m1)
    cx.release(m0)


def phase_M(cx, io, layer1, which=("att", "gdn", "rwkv")):
    base_ = cx.mark()
    cx.cast_eng = "pool"
    K = setup_consts(cx, io)
    g1 = load_small(cx, io["g_attn"], 16, "g1")
    hT = cx.tile(KC * SEQ, BF16, "hT")
    hv = hT.ap.rearrange("p (k t) -> p k t", k=KC)
    xT = io["xT"].rearrange("(k p) t -> p k t", p=128)
    m0 = cx.mark()
    for tb in range(4):
        lo, hi = tb * 512, tb * 512 + 512
        xt = cx.ring("xtile", 1, KC * 512, F32)
        xtv = xt.ap.rearrange("p (k t) -> p k t", k=KC)
        for k in range(KC):
            cx.dma(xtv[:, k, :], xT[:, k, lo:hi], io.get("x_deps", []), [xt])
        rmsnorm_fm(cx, lambda k, l, h_: (xtv[:, k, 0:h_ - l], xt), g1, lambda k, l, h_: (hv[:, k, l:h_], hT),
                   [(lo, hi)], K.ones)
    cx.release(m0)
    if "att" in which:
        mixer_attention(cx, K, io, hT, hv)
    if "gdn" in which:
        mixer_gdn(cx, K, io, hT, hv)
    if "rwkv" in which:
        mixer_rwkv(cx, K, io, hT, hv, layer1)
    cx.release(base_)


M_INPUTS = [("xT", (D, SEQ)), ("g_attn", (128, 16)), ("cst", (128, CST_N)), ("rope", (32, 2 * SEQ)),
            ("wat", (18, 128, 16, 128)), ("wsw", (12, 128, 16, 32)),
            ("wc", (16, 128, 16, 128)), ("wba", (128, 16, 8)), ("gconv", (128, 48)), ("gpar", (128, 136)),
            ("wa", (12, 128, 16, 128)), ("wal", (128, 16, 288)), ("rpar", (128, 44)), ("rlmu", (128, 4)), ("rbc", (128, 1024)),
            ("rw2a2", (128, 512)), ("rg2", (128, 1024))]
M_INPUTS_L1 = [("rv1", (128, 16, 32)), ("rv2", (32, 512)), ("vf_in", (512, SEQ))]
M_OUTPUTS = [("y_fm", (768, SEQ)), ("yc_tm", (SEQ, 512)), ("ya_tm", (SEQ, 512)), ("v_out", (512, SEQ))]


def prep_M_weights(inp, l, s):
    A_IN, B_IN, C_IN = 3360, 4608, 4112
    W = inp["w_in"][l]
    w = {}
    w["g_attn"] = vec_pk(inp["attn_norm"][l])
    w["cst"] = make_consts()
    w["rope"] = make_rope()

    def colblk(cols):
        sub = W[:, cols]
        n = sub.shape[1]
        return np.ascontiguousarray(sub.reshape(16, 128, n // 128, 128).transpose(2, 1, 0, 3))

    def colsmall(cols):
        sub = W[:, cols]
        return np.ascontiguousarray(sub.reshape(16, 128, len(cols)).transpose(1, 0, 2))
    b0 = A_IN
    cols = []
    cols_sw = []
    for hl in range(2):
        hi = 2 * s + hl
        for g in range(3):
            for which in range(3):
                c0 = b0 + which * 1536 + g * 512 + hi * 128
                cols += list(range(c0, c0 + 128))
                if which < 2:
                    cols_sw += list(range(c0 + 16, c0 + 32)) + list(range(c0, c0 + 16))
    w["wat"] = colblk(np.array(cols))
    sw = W[:, np.array(cols_sw)]
    w["wsw"] = np.ascontiguousarray(sw.reshape(16, 128, 12, 32).transpose(2, 1, 0, 3))
    c0 = A_IN + B_IN
    cols = []
    for which in range(3):
        for hh in range(4):
            h = 4 * s + hh
            cols += list(range(c0 + which * 1024 + h * 128, c0 + which * 1024 + (h + 1) * 128))
    for hh in range(4):
        h = 4 * s + hh
        cols += list(range(c0 + 3088 + h * 128, c0 + 3088 + (h + 1) * 128))
    w["wc"] = colblk(np.array(cols))
    w["wba"] = colsmall(np.array([c0 + 3072 + 4 * s + i for i in range(4)] + [c0 + 3080 + 4 * s + i for i in range(4)]))
    gc = inp["gdn_conv"][l]
    gcs = np.zeros((128, 12, 4), np.float32)
    for which in range(3):
        for hh in range(4):
            h = 4 * s + hh
            gcs[:, which * 4 + hh, :] = gc[:, which * 1024 + h * 128: which * 1024 + (h + 1) * 128].T
    w["gconv"] = gcs.reshape(128, 48)
    gp = np.zeros((128, 136), np.float32)
    gp[:, 0:4] = inp["gdn_A_log"][l][4 * s:4 * s + 4][None, :]
    gp[:, 4:8] = inp["gdn_dt_bias"][l][4 * s:4 * s + 4][None, :]
    gp[:, 8:136] = inp["gdn_norm"][l][None, :]
    w["gpar"] = gp
    ch0 = 512 * s
    cols = []
    for which in range(3):
        cols += list(range(which * 1024 + ch0, which * 1024 + ch0 + 512))
    w["wa"] = colblk(np.array(cols))
    w["wal"] = colsmall(np.arange(3072, 3360))
    mu = inp["rwkv_mu"][l]
    rp = np.zeros((128, 44), np.float32)
    for which in range(3):
        for ct in range(4):
            rp[:, which * 4 + ct] = mu[which * 1024 + ch0 + ct * 128: which * 1024 + ch0 + (ct + 1) * 128]
    for ct in range(4):
        sl = slice(ch0 + ct * 128, ch0 + (ct + 1) * 128)
        rp[:, 12 + ct * 8 + 0] = inp["rwkv_w0"][l][sl]
        rp[:, 12 + ct * 8 + 1] = inp["rwkv_a0"][l][sl]
        rp[:, 12 + ct * 8 + 2] = inp["rwkv_k_k"][l][sl]
        rp[:, 12 + ct * 8 + 3] = inp["rwkv_k_a"][l][sl]
        rp[:, 12 + ct * 8 + 4] = inp["rwkv_r_k"][l].reshape(-1)[sl]
        if l > 0:
            rp[:, 12 + ct * 8 + 5] = inp["rwkv_v0"][l - 1][sl]
    w["rpar"] = rp
    rl = np.zeros((128, 4), np.float32)
    rl[0:64, 0] = mu[3072:3136]
    rl[64:128, 0] = mu[3136:3200]
    rl[0:128, 1] = mu[3200:3328]
    rl[0:32, 2] = mu[3328:3360]
    w["rlmu"] = rl
    rb = np.zeros((128, 1024), np.float32)
    rb[:, 0:512] = inp["rwkv_ln_w"][l][ch0:ch0 + 512][None, :]
    rb[:, 512:1024] = inp["rwkv_ln_b"][l][ch0:ch0 + 512][None, :]
    w["rbc"] = rb
    w["rw2a2"] = np.ascontiguousarray(np.concatenate([inp["rwkv_w2"][l][:, ch0:ch0 + 512], inp["rwkv_a2"][l][:, ch0:ch0 + 512]], axis=0))
    g2 = inp["rwkv_g2"][l][:, ch0:ch0 + 512]
    rg = np.zeros((128, 1024), np.float32)
    rg[:, 0:512] = g2[0:128]
    rg[0:32, 512:1024] = g2[128:160]
    w["rg2"] = rg
    if l > 0:
        v1 = inp["rwkv_v1"][l - 1]
        w["rv1"] = np.ascontiguousarray(v1.reshape(16, 128, 32).transpose(1, 0, 2))
        w["rv2"] = np.ascontiguousarray(inp["rwkv_v2"][l - 1][:, ch0:ch0 + 512])
    return w


def tile_w(W, col0, ncols, cw=128):
    K = W.shape[0]
    sub = W[:, col0:col0 + ncols]
    return np.ascontiguousarray(sub.reshape(K // 128, 128, ncols // cw, cw).transpose(2, 1, 0, 3))


def vec_pk(v):
    return np.ascontiguousarray(v.reshape(-1, 128).T)


def prep_T_weights(inp, l):
    A_IN, B_IN, C_IN = 3360, 4608, 4112
    g0 = A_IN + B_IN + C_IN
    w = {}
    w["wg"] = tile_w(inp["w_in"][l], g0, 6144)
    pw = np.concatenate([inp["proj_a"][l], inp["proj_b"][l], inp["proj_c"][l]], axis=0)
    w["wp"] = tile_w(pw, 0, 2048)
    w["wo"] = tile_w(inp["w_out"][l], 0, 2048)
    w["wup"] = tile_w(inp["ffn_up"][l], 0, 11264)
    w["wdn"] = tile_w(inp["ffn_down"][l], 0, 2048)
    w["g_attn"] = vec_pk(inp["attn_norm"][l])
    w["g_ffn"] = vec_pk(inp["ffn_norm"][l])
    w["g_next"] = vec_pk(inp["attn_norm"][l + 1] if l + 1 < inp["attn_norm"].shape[0] else inp["final_norm"])
    fc = inp["ffn_conv"][l]
    w["fconv"] = np.ascontiguousarray(fc.T.reshape(88, 128, 3).transpose(1, 0, 2).reshape(128, 88 * 3))
    return w


class DD:
    def __init__(self, name=""):
        self.d = Dep(name)


M_SHARED = ("xT", "cst", "rope")
T_INPUTS = [("g_attn", (128, 16)), ("g_ffn", (128, 16)), ("g_next", (128, 16)), ("fconv", (128, 88 * 3)),
            ("wg", (48, 128, 16, 128)), ("wp", (16, 128, 20, 128)), ("wo", (16, 128, 16, 128)),
            ("wup", (88, 128, 16, 128)), ("wdn", (16, 128, 44, 128))]


def build_fused(depth=2, do_T=True):
    nc = bass.Bass("TRN2", target_bir_lowering=False)
    ext = {}

    def din(name, shape, dt=F32):
        ext[name] = nc.dram_tensor(name, list(shape), dt, kind="ExternalInput").ap()
        return ext[name]
    xT = din("xT", (D, SEQ))
    cst = din("cst", (128, CST_N))
    rope = din("rope", (32, 2 * SEQ))
    for l in range(depth):
        for s in range(2):
            for name, shape in M_INPUTS + (M_INPUTS_L1[:2] if l == 1 else []):
                if name not in M_SHARED:
                    din("%s_%d%d" % (name, l, s), shape)
        for name, shape in T_INPUTS:
            din("T%s_%d" % (name, l), shape)
    out = nc.dram_tensor("outT", [D, SEQ], F32, kind="ExternalOutput").ap()
    ybuf = [nc.dram_tensor("ybuf%d" % l, [2560, SEQ], BF16).ap() for l in range(depth)]
    x1buf = nc.dram_tensor("x1buf", [D, SEQ], F32).ap()
    vbuf = [nc.dram_tensor("vbuf%d" % s, [512, SEQ], BF16).ap() for s in range(2)]
    with ExitStack() as es:
        cx = Ctx(nc, es)
        cx.wpiece = 1024
        x1dep = DD("x1")
        vdep = [DD("v0"), DD("v1")]
        for l in range(depth):
            last = (l == depth - 1)
            xsrc = xT if l == 0 else x1buf
            xdeps = [] if l == 0 else [x1dep]
            ydep = {"a": DD("ya"), "b": DD("yb"), "c": DD("yc")}
            for s in range(2):
                io = {}
                for name, shape in M_INPUTS + (M_INPUTS_L1[:2] if l == 1 else []):
                    if name not in M_SHARED:
                        io[name] = ext["%s_%d%d" % (name, l, s)]
                io.update(xT=xsrc, x_deps=xdeps, cst=cst, rope=rope, ybuf=ybuf[l], ydep=ydep, s=s,
                          v_out=vbuf[s], v_dep=vdep[s], vf_in=vbuf[s], vf_deps=[vdep[s]], write_v=(l == 0))
                phase_M(cx, io, l == 1)
            if do_T:
                for half in range(2):
                    io = {name: ext["T%s_%d" % (name, l)] for name, shape in T_INPUTS}
                    io.update(xT=xsrc, x_deps=xdeps, ybuf=ybuf[l], y_deps=list(ydep.values()),
                              outT=(out if last else x1buf), out_deps=([] if last else [x1dep]))
                    phase_T(cx, io, last, half)
        cx.S.emit()
        print("fused ops:", cx.S.nops, "insts:", cx.S.ninst, "arena peak KB:", cx.ar.peak * 4 / 1024,
              "eng counts:", cx.S.cnt, "max dma sem:", max(cx.S.dma_cnt))
    return nc


_NC_CACHE = {}


def prep_core_inputs(inp, depth=2):
    m = {"cst": make_consts(), "rope": make_rope()}
    for l in range(depth):
        for s in range(2):
            w = prep_M_weights(inp, l, s)
            for k, v in w.items():
                if k not in M_SHARED:
                    m["%s_%d%d" % (k, l, s)] = v
        w = prep_T_weights(inp, l)
        for k, v in w.items():
            m["T%s_%d" % (k, l)] = v
    return m


def kernel(**inputs):
    inp = {k: np.asarray(v) for k, v in inputs.items()}
    x = inp["x"].astype(np.float32, copy=False)
    B, S, Dm = x.shape
    depth = inp["w_in"].shape[0]
    if "nc" not in _NC_CACHE:
        _NC_CACHE["nc"] = build_fused(depth)
    nc = _NC_CACHE["nc"]
    shared = prep_core_inputs(inp, depth)
    maps = []
    for c in range(8):
        m = dict(shared)
        m["xT"] = np.ascontiguousarray(x[c // 2].T)
        maps.append(m)
    res = run_bass_kernel_spmd(nc, maps, core_ids=list(range(8))).results
    outs = [np.asarray(res[2 * b]["outT"]).T for b in range(B)]
    return np.ascontiguousarray(np.stack(outs, axis=0)).astype(np.float32)
```

```python
import os
import numpy as np
import concourse.bass as bass
import concourse.mybir as mybir
from concourse.bass_utils import run_bass_kernel_spmd
from contextlib import ExitStack

F32 = mybir.dt.float32
BF16 = mybir.dt.bfloat16
ALU = mybir.AluOpType
AF = mybir.ActivationFunctionType
AX = mybir.AxisListType

D = 2048
KC = 16
SEQ = 2048
NTOK_T = 1026
TT = [(0, 2), (2, 514), (514, 1026)]
D_FF = 5632
NFB = 44
EPS = 1e-6

SAME_ENG_SYNC = True


class Dep:
    __slots__ = ("w", "r", "name", "excl")

    def __init__(self, name="", excl=False):
        self.w = None
        self.r = {}
        self.name = name
        self.excl = excl


class Sched:
    ENGS = ["pe", "act", "dve", "pool", "sp"]

    def __init__(self, nc, n_dma=24):
        self.nc = nc
        self.q = {e: [] for e in self.ENGS}
        self.cnt = {e: 0 for e in self.ENGS}
        self.n_dma = n_dma
        self.dma_cnt = [0] * n_dma
        self.dma_rr = 0
        self.waited = {e: {} for e in self.ENGS}
        self.nops = 0
        self.ninst = 0
        self.tag = ""
        self.taglog = [] if os.environ.get("BASS_TAGLOG") else None

    def op(self, eng, calls, reads=(), writes=(), dma=False):
        deps = {}

        def add(tok):
            if tok is None:
                return
            k, v = tok
            if deps.get(k, 0) < v:
                deps[k] = v

        for r in reads:
            add(r.w)
            if r.excl:
                for k, v in r.r.items():
                    add((k, v))
        for w in writes:
            add(w.w)
            for k, v in w.r.items():
                add((k, v))
        if dma:
            i = self.dma_rr
            self.dma_rr = (self.dma_rr + 1) % self.n_dma
            add((("dma", i), self.dma_cnt[i]))
            self.dma_cnt[i] += 16
            tok = (("dma", i), self.dma_cnt[i])
        else:
            self.cnt[eng] += 1
            tok = (eng, self.cnt[eng])
        waits = []
        wd = self.waited[eng]
        for k, v in deps.items():
            if v <= 0:
                continue
            if k == eng and (eng == "pe" or not SAME_ENG_SYNC):
                continue
            if wd.get(k, 0) >= v:
                continue
            wd[k] = v
            waits.append((k, v))
        for r in reads:
            if r.r.get(tok[0], 0) < tok[1]:
                r.r[tok[0]] = tok[1]
        for w in writes:
            w.w = tok
            w.r = {}
        self.q[eng].append((waits, calls, tok))
        if self.taglog is not None:
            self.taglog.append((eng, self.tag, len(calls)))
        self.nops += 1
        self.ninst += len(calls)
        return tok

    def emit(self, final_wait_eng="sp"):
        nc = self.nc
        with ExitStack() as es:
            sems = {}
            for e in self.ENGS:
                sems[e] = es.enter_context(nc.semaphore("s_" + e))
            for i in range(self.n_dma):
                sems[("dma", i)] = es.enter_context(nc.semaphore("s_dma%d" % i))
            block = es.enter_context(nc.Block())
            finals = [(("dma", i), self.dma_cnt[i]) for i in range(self.n_dma) if self.dma_cnt[i] > 0]

            def run(engname, engine):
                for waits, calls, tok in self.q[engname]:
                    for k, v in waits:
                        engine.wait_ge(sems[k], v)
                    ins = None
                    for name, kw in calls:
                        ins = getattr(engine, name)(**kw)
                    ins.then_inc(sems[tok[0]], 16 if isinstance(tok[0], tuple) else 1)
                if engname == final_wait_eng:
                    for k, v in finals:
                        engine.wait_ge(sems[k], v)

            @block.tensor
            def _(e):
                run("pe", e)

            @block.scalar
            def _(e):
                run("act", e)

            @block.vector
            def _(e):
                run("dve", e)

            @block.gpsimd
            def _(e):
                run("pool", e)

            @block.sync
            def _(e):
                run("sp", e)


class Arena:
    def __init__(self, ap_f32, words):
        self.base = ap_f32
        self.words = words
        self.top = 0
        self.peak = 0
        self.live = []
        self.dead = []

    def mark(self):
        return self.top

    def release(self, m):
        keep = []
        for ent in self.live:
            if ent[0] >= m:
                self.dead.append(ent)
            else:
                keep.append(ent)
        self.live = keep
        self.top = m

    def alloc(self, nelem, dtype=F32, parts=128, name=""):
        w = nelem if dtype == F32 else (nelem + 1) // 2
        w = (w + 7) // 8 * 8
        assert self.top + w <= self.words, ("SBUF arena overflow", name, self.top, w, self.words)
        lo, hi = self.top, self.top + w
        a = self.base[0:parts, lo:hi]
        self.top = hi
        self.peak = max(self.peak, self.top)
        d = Dep(name)
        nd = []
        for (dlo, dhi, dd) in self.dead:
            if dlo < hi and lo < dhi:
                toks = list(dd.r.items())
                if dd.w is not None:
                    toks.append(dd.w)
                for k, v in toks:
                    if d.r.get(k, 0) < v:
                        d.r[k] = v
                if lo <= dlo and dhi <= hi:
                    continue
            nd.append((dlo, dhi, dd))
        self.dead = nd
        self.live.append((lo, hi, d))
        if dtype != F32:
            a = a.bitcast(dtype)
        return a[:, 0:nelem], d


class T:
    def __init__(self, ap, d):
        self.ap = ap
        self.d = d


class Ctx:
    ARENA_WORDS = 51 * 1024 + 512

    def __init__(self, nc, es):
        self.nc = nc
        self.S = Sched(nc)
        big = es.enter_context(nc.sbuf_tensor("arena", [128, self.ARENA_WORDS], F32))
        self.ar = Arena(big, self.ARENA_WORDS)
        self.psb = []
        for i in range(8):
            p = es.enter_context(nc.psum_tensor("psb%d" % i, [128, 512], F32))
            self.psb.append(T(p, Dep("ps%d" % i, excl=True)))
        self.ps_rr = 0
        self.rings = {}

    def ps(self, pin=False):
        pinned = getattr(self, "pinned", None)
        if pinned is None:
            pinned = self.pinned = set()
        while self.ps_rr in pinned:
            self.ps_rr = (self.ps_rr + 1) % 8
        i = self.ps_rr
        t = self.psb[i]
        self.ps_rr = (self.ps_rr + 1) % 8
        if pin:
            pinned.add(i)
        return t

    def unpin_all(self):
        self.pinned = set()

    def tile(self, nelem, dtype=F32, name="", parts=128):
        ap, d = self.ar.alloc(nelem, dtype, parts, name)
        return T(ap, d)

    def ring(self, key, n, nelem, dtype=F32):
        if key not in self.rings:
            off = self.ar.top
            self.rings[key] = [[self.tile(nelem, dtype, "%s%d" % (key, i)) for i in range(n)], 0, off]
        r = self.rings[key]
        t = r[0][r[1] % len(r[0])]
        r[1] += 1
        return t

    def mark(self):
        return self.ar.mark()

    def release(self, m):
        for k in [k for k, r in self.rings.items() if r[2] >= m]:
            del self.rings[k]
        self.ar.release(m)

    def I(self, eng, name, reads, writes, **kw):
        self.S.op(eng, [(name, kw)], [t.d for t in reads], [t.d for t in writes])

    def dma(self, out, in_, reads=(), writes=(), eng="sp"):
        self.S.op(eng, [("dma_start", dict(out=out, in_=in_))], [t.d for t in reads], [t.d for t in writes], dma=True)

    def act(self, out, in_, func, reads, writes, **kw):
        self.I("act", "activation", reads, writes, out=out, in_=in_, func=func, **kw)

    def tt(self, eng, out, in0, in1, op, reads, writes):
        self.I(eng, "tensor_tensor", reads, writes, out=out, in0=in0, in1=in1, op=op)

    def ts(self, eng, out, in0, s1, op0, reads, writes, s2=None, op1=None):
        kw = dict(out=out, in0=in0, scalar1=s1, scalar2=s2, op0=op0)
        if op1 is not None:
            kw["op1"] = op1
        self.I(eng, "tensor_scalar", reads, writes, **kw)

    def stt(self, out, in0, scalar, in1, op0, op1, reads, writes):
        self.I("dve", "scalar_tensor_tensor", reads, writes, out=out, in0=in0, scalar=scalar, in1=in1, op0=op0, op1=op1)

    def copy(self, eng, out, in_, reads, writes):
        if eng == "act":
            self.act(out, in_, AF.Copy, reads, writes)
        else:
            self.I(eng, "tensor_copy", reads, writes, out=out, in_=in_)

    def mm(self, mms, reads, writes):
        self.S.op("pe", [("matmul", m) for m in mms], [t.d for t in reads], [t.d for t in writes])

    def tr(self, out, in_, ident, reads, writes):
        self.S.op("pe", [("transpose", dict(out=out, in_=in_, identity=ident))], [t.d for t in reads], [t.d for t in writes])


def load_w(cx, dram_ap, kc, cw, key="w"):
    n = kc * cw
    wp = getattr(cx, "wpiece", 2048)
    assert n <= wp
    st = cx.ring(key + "_st", 2, wp, F32)
    wb = cx.ring(key + "_wb", 3, wp, BF16)
    cx.dma(st.ap[:, 0:n], dram_ap.rearrange("p k c -> p (k c)"), [], [st])
    cx.copy(getattr(cx, "cast_eng", "pool"), wb.ap[:, 0:n], st.ap[:, 0:n], [st], [wb])
    return wb


def load_w_multi(cx, dram_ap, kc, cw, key="w"):
    per = max(1, getattr(cx, "wpiece", 2048) // cw)
    out = []
    k0 = 0
    while k0 < kc:
        kn = min(per, kc - k0)
        out.append((load_w(cx, dram_ap[:, k0:k0 + kn, :], kn, cw, key), k0, kn))
        k0 += kn
    return out


def load_w_big(cx, dram_ap, kc, cw, name):
    wp = getattr(cx, "wpiece", 2048)
    big = cx.tile(kc * cw, BF16, name)
    per = max(1, wp // cw)
    k0 = 0
    while k0 < kc:
        kn = min(per, kc - k0)
        n = kn * cw
        st = cx.ring("w_st", 2, wp, F32)
        cx.dma(st.ap[:, 0:n], dram_ap[:, k0:k0 + kn, :].rearrange("p k c -> p (k c)"), [], [st])
        cx.copy("pool", big.ap[:, k0 * cw:(k0 + kn) * cw], st.ap[:, 0:n], [st], [big])
        k0 += kn
    return [(big, 0, kc)]


def wslice(pieces, k, cw, m0=0, m1=None):
    m1 = cw if m1 is None else m1
    for wb, k0, kn in pieces:
        if k0 <= k < k0 + kn:
            return wb.ap[:, (k - k0) * cw + m0:(k - k0) * cw + m1]
    raise KeyError(k)


def gemm_block(cx, pieces, cw, act_fn, act_tiles, lo, hi, ks=None, m0=0, m1=None, po=0):
    ps = cx.ps()
    n = hi - lo
    m1 = cw if m1 is None else m1
    if ks is None:
        ks = []
        for wb, k0, kn in pieces:
            ks += list(range(k0, k0 + kn))
    mms = []
    for i, k in enumerate(ks):
        mms.append(dict(out=ps.ap[po:po + m1 - m0, 0:n], lhsT=wslice(pieces, k, cw, m0, m1), rhs=act_fn(k, lo, hi),
                        start=(i == 0), stop=(i == len(ks) - 1)))
    cx.mm(mms, [p[0] for p in pieces] + list(act_tiles), [ps])
    return ps


def rmsnorm_fm(cx, srcs_fn, g_t, out_fn, tiles, ones_bf, nk=KC):
    for (lo, hi) in tiles:
        n = hi - lo
        ps = cx.ps()
        srcs = [srcs_fn(k, lo, hi) for k in range(nk)]
        for k in range(nk):
            sqt = cx.ring("rn_sq", 3, 512, BF16)
            sap, st = srcs[k]
            cx.act(sqt.ap[:, 0:n], sap, AF.Square, [st], [sqt])
            cx.mm([dict(out=ps.ap[:, 0:n], lhsT=ones_bf.ap, rhs=sqt.ap[:, 0:n], start=(k == 0), stop=(k == nk - 1))],
                  [sqt, ones_bf], [ps])
        rinv = cx.ring("rn_rinv", 2, 512, F32)
        cx.act(rinv.ap[:, 0:n], ps.ap[:, 0:n], AF.Ln, [ps], [rinv], scale=1.0 / (nk * 128), bias=EPS)
        cx.act(rinv.ap[:, 0:n], rinv.ap[:, 0:n], AF.Exp, [rinv], [rinv], scale=-0.5)
        for k in range(nk):
            sap, st = srcs[k]
            oap, ot = out_fn(k, lo, hi)
            cx.stt(oap, sap, g_t.ap[:, k:k + 1], rinv.ap[:, 0:n], ALU.mult, ALU.mult, [st, rinv, g_t], [ot])


def load_small(cx, dram_ap, nelem, name, parts=128):
    t = cx.tile(nelem, F32, name, parts)
    cx.dma(t.ap, dram_ap, [], [t])
    return t


def phase_T(cx, io, final, half):
    NT = NTOK_T
    cx.cast_eng = "act"
    base = cx.mark()
    ones_bf = cx.tile(128, BF16, "ones")
    cx.I("dve", "memset", [], [ones_bf], ap=ones_bf.ap, constant=1.0)
    g1 = load_small(cx, io["g_attn"], 16, "g1")
    g2 = load_small(cx, io["g_ffn"], 16, "g2")
    g3 = load_small(cx, io["g_next"], 16, "g3")
    convw = load_small(cx, io["fconv"], 88 * 3, "convw")
    mg_t = cx.tile(KC * NT, BF16, "merged")
    h_t = cx.tile(KC * NT, BF16, "h")
    mv = mg_t.ap.rearrange("p (k t) -> p k t", k=KC)
    hv = h_t.ap.rearrange("p (k t) -> p k t", k=KC)
    M0 = cx.mark()
    cx.S.tag = "T/A1"
    y_t = cx.tile(20 * NT, BF16, "y")
    yv = y_t.ap.rearrange("p (k t) -> p k t", k=20)
    xT = io["xT"].rearrange("(k p) t -> p k t", p=128)
    yT = io["ybuf"].rearrange("(k p) t -> p k t", p=128)
    t0 = 1024 * half
    xdeps = io.get("x_deps", [])
    ydeps = io.get("y_deps", [])

    def load_x(dst_ap, dst_t, k, lo, hi):
        if lo < 2 and half == 0:
            cx.I("dve", "memset", [], [dst_t], ap=dst_ap[:, 0:2 - lo], constant=0.0)
            if hi > 2:
                cx.dma(dst_ap[:, 2 - lo:hi - lo], xT[:, k, 0:hi - 2], xdeps, [dst_t])
        else:
            cx.dma(dst_ap, xT[:, k, t0 - 2 + lo:t0 - 2 + hi], xdeps, [dst_t])

    M1 = cx.mark()
    for k in range(20):
        cx.dma(yv[:, k, 2:NT], yT[:, k, t0:t0 + 1024], ydeps, [y_t])
    if half == 0:
        cx.I("dve", "memset", [], [y_t], ap=yv[:, :, 0:2], constant=0.0)
    else:
        for k in range(20):
            cx.dma(yv[:, k, 0:2], yT[:, k, t0 - 2:t0], ydeps, [y_t])
    for (lo, hi) in TT:
        n = hi - lo
        xt = cx.ring("xtile", 1, KC * 512, F32)
        xtv = xt.ap.rearrange("p (k t) -> p k t", k=KC)
        for k in range(KC):
            load_x(xtv[:, k, 0:n], xt, k, lo, hi)
        rmsnorm_fm(cx, lambda k, l, h_: (xtv[:, k, 0:h_ - l], xt), g1, lambda k, l, h_: (hv[:, k, l:h_], h_t),
                   [(lo, hi)], ones_bf)
    cx.release(M1)
    wg = io["wg"]
    wp = io["wp"]
    ksl = [(0, 8), (8, 12), (12, 20)]
    for j in range(16):
        gts = []
        for br in range(3):
            pcs = load_w_multi(cx, wg[br * 16 + j], KC, 128)
            gt = cx.ring("gate", 4, NT, BF16)
            for (lo, hi) in TT:
                ps = gemm_block(cx, pcs, 128, lambda k, l, h_: hv[:, k, l:h_], [h_t], lo, hi)
                cx.act(gt.ap[:, lo:hi], ps.ap[:, 0:hi - lo], AF.Sigmoid, [ps], [gt])
            gts.append(gt)
        pw = load_w_multi(cx, wp[j], 20, 128)
        for (lo, hi) in TT:
            n = hi - lo
            tmp = cx.ring("mtmp", 2, 512, F32)
            for br in range(3):
                k0, k1 = ksl[br]
                ps = gemm_block(cx, pw, 128, lambda k, l, h_: yv[:, k, l:h_], [y_t], lo, hi, ks=list(range(k0, k1)))
                gt = gts[br]
                if br == 0:
                    cx.tt("dve", tmp.ap[:, 0:n], ps.ap[:, 0:n], gt.ap[:, lo:hi], ALU.mult, [ps, gt], [tmp])
                else:
                    t2 = cx.ring("mtmp2", 2, 512, F32)
                    cx.tt("dve", t2.ap[:, 0:n], ps.ap[:, 0:n], gt.ap[:, lo:hi], ALU.mult, [ps, gt], [t2])
                    if br == 1:
                        cx.tt("pool", tmp.ap[:, 0:n], tmp.ap[:, 0:n], t2.ap[:, 0:n], ALU.add, [tmp, t2], [tmp])
                    else:
                        cx.tt("pool", mv[:, j, lo:hi], tmp.ap[:, 0:n], t2.ap[:, 0:n], ALU.add, [tmp, t2], [mg_t])
    cx.release(M0)
    cx.S.tag = "T/A2"
    x_sb = cx.tile(KC * NT, F32, "x_sb")
    xv = x_sb.ap.rearrange("p (k t) -> p k t", k=KC)
    M2 = cx.mark()
    wo = io["wo"]
    for j in range(16):
        pcs = load_w_multi(cx, wo[j], KC, 128)
        xin = cx.ring("xin", 2, NT, F32)
        load_x(xin.ap, xin, j, 0, NT)
        for (lo, hi) in TT:
            ps = gemm_block(cx, pcs, 128, lambda k, l, h_: mv[:, k, l:h_], [mg_t], lo, hi)
            cx.tt("dve", xv[:, j, lo:hi], ps.ap[:, 0:hi - lo], xin.ap[:, lo:hi], ALU.add, [ps, xin], [x_sb])
    cx.release(M2)
    cx.S.tag = "T/F"
    rmsnorm_fm(cx, lambda k, l, h_: (xv[:, k, l:h_], x_sb), g2, lambda k, l, h_: (hv[:, k, l:h_], h_t), TT, ones_bf)
    GRP = 11
    a_t = T(mg_t.ap, mg_t.d)
    av = a_t.ap[:, 0:GRP * 1024].rearrange("p (k t) -> p k t", k=GRP)
    wup = io["wup"]
    wdn = io["wdn"]
    cwv = convw.ap.rearrange("p (b t) -> p b t", t=3)
    for g in range(NFB // GRP):
        for jj in range(GRP):
            jb = g * GRP + jj
            us = []
            for part in range(2):
                blk = part * NFB + jb
                pcs = load_w_multi(cx, wup[blk], KC, 128)
                u = cx.ring("u_f32", 2, NT, F32)
                for (lo, hi) in TT:
                    ps = gemm_block(cx, pcs, 128, lambda k, l, h_: hv[:, k, l:h_], [h_t], lo, hi)
                    cx.copy("act", u.ap[:, lo:hi], ps.ap[:, 0:hi - lo], [ps], [u])
                c = cx.ring("c_f32", 2, 1024, F32)
                cx.ts("dve", c.ap, u.ap[:, 2:NT], cwv[:, blk, 2:3], ALU.mult, [u, convw], [c])
                cx.stt(c.ap, u.ap[:, 1:NT - 1], cwv[:, blk, 1:2], c.ap, ALU.mult, ALU.add, [u, convw, c], [c])
                cx.stt(c.ap, u.ap[:, 0:NT - 2], cwv[:, blk, 0:1], c.ap, ALU.mult, ALU.add, [u, convw, c], [c])
                us.append(c)
            sg = cx.ring("silu", 2, 1024, F32)
            cx.act(sg.ap, us[0].ap, AF.Silu, [us[0]], [sg])
            cx.tt("pool", av[:, jj, :], sg.ap, us[1].ap, ALU.mult, [sg, us[1]], [a_t])
        for j in range(16):
            pcs = load_w_multi(cx, wdn[j][:, g * GRP:(g + 1) * GRP, :], GRP, 128)
            for ti in range(2):
                lo, hi = ti * 512, ti * 512 + 512
                ps = gemm_block(cx, pcs, 128, lambda k, l, h_: av[:, k, l:h_], [a_t], lo, hi)
                cx.tt("dve", xv[:, j, 2 + lo:2 + hi], ps.ap[:, 0:512], xv[:, j, 2 + lo:2 + hi], ALU.add, [ps, x_sb], [x_sb])
    cx.release(M2)
    outT = io["outT"].rearrange("(k p) t -> p k t", p=128)
    odeps = io.get("out_deps", [])
    if not final:
        for k in range(KC):
            cx.dma(outT[:, k, t0:t0 + 1024], xv[:, k, 2:NT], [x_sb], odeps)
    else:
        o_t = T(h_t.ap.bitcast(F32)[:, 0:KC * 512], h_t.d)
        ov = o_t.ap.rearrange("p (k t) -> p k t", k=KC)
        for ti in range(2):
            lo, hi = 2 + ti * 512, 2 + ti * 512 + 512
            rmsnorm_fm(cx, lambda k, l, h_: (xv[:, k, l:h_], x_sb), g3, lambda k, l, h_: (ov[:, k, 0:512], o_t),
                       [(lo, hi)], ones_bf)
            for k in range(KC):
                cx.dma(outT[:, k, t0 + lo - 2:t0 + lo - 2 + 512], ov[:, k, :], [o_t], odeps)
    cx.release(base)


import os
ATT_G = os.environ.get("ATT_G", "012")
ATT_STOP = int(os.environ.get("ATT_STOP", "99"))
ATT_SUB = int(os.environ.get("ATT_SUB", "3"))
RW_DBG = os.environ.get("RW_DBG", "")
ATT_V = os.environ.get("ATT_V", "ab")
NCH = 16
C_ID, C_MUI, C_MUS, C_MLI, C_BO, C_BS, C_S127, C_MLS = 0, 128, 256, 384, 512, 640, 642, 770
C_BD8, C_O16, C_O32, C_O64, C_O128 = 898, 1026, 1154, 1282, 1410
CST_N = 1538


def make_consts():
    c = np.zeros((128, CST_N), np.float32)
    j = np.arange(128)[:, None]
    i = np.arange(128)[None, :]
    c[:, C_ID:C_ID + 128] = (i == j)
    c[:, C_MUI:C_MUI + 128] = (i >= j)
    c[:, C_MUS:C_MUS + 128] = (i > j)
    c[:, C_MLI:C_MLI + 128] = (j >= i)
    c[:, C_BO:C_BO + 128] = ((i // 64) == (j // 64))
    c[:, C_BS:C_BS + 2] = (np.arange(2)[None, :] == (j // 64))
    c[:, C_S127:C_S127 + 128] = (j == 127)
    c[:, C_MLS:C_MLS + 128] = (j > i)
    bd = lambda s_: (i // s_) == (j // s_)
    c[:, C_BD8:C_BD8 + 128] = bd(8)
    c[:, C_O16:C_O16 + 128] = bd(16) & ~bd(8)
    c[:, C_O32:C_O32 + 128] = bd(32) & ~bd(16)
    c[:, C_O64:C_O64 + 128] = bd(64) & ~bd(32)
    c[:, C_O128:C_O128 + 128] = ~bd(64)
    return c


def make_rope():
    half = 16
    inv = 500000.0 ** (-np.arange(half, dtype=np.float32) / half)
    ang = np.arange(SEQ, dtype=np.float32)[None, :] * inv[:, None]
    cos = np.cos(ang).astype(np.float32)
    sin = np.sin(ang).astype(np.float32)
    r = np.zeros((32, 2 * SEQ), np.float32)
    r[0:16, 0:SEQ] = cos
    r[16:32, 0:SEQ] = cos
    r[0:16, SEQ:] = -sin
    r[16:32, SEQ:] = sin
    return r


class Consts:
    pass


def setup_consts(cx, io):
    K = Consts()
    cf = load_small(cx, io["cst"], CST_N, "cst_f32")
    cb = cx.tile(CST_N, BF16, "cst_bf")
    cx.copy("dve", cb.ap, cf.ap, [cf], [cb])
    K.cf, K.cb = cf, cb
    K.ones = cx.tile(128, BF16, "ones")
    cx.I("dve", "memset", [], [K.ones], ap=K.ones.ap, constant=1.0)
    K.m4 = {}
    for nm_, off in (("id", C_ID), ("bd8", C_BD8), ("o16", C_O16), ("o32", C_O32), ("o64", C_O64), ("o128", C_O128)):
        t = cx.tile(512, BF16, "m4" + nm_)
        for q in range(4):
            cx.copy("pool", t.ap[:, q * 128:(q + 1) * 128], cb.ap[:, off:off + 128], [cb], [t])
        K.m4[nm_] = t
    return K


def dplr_head(cx, K, nm, dk, dv, pb, Kg, Bg, Ag, Rg, gtiles, Wst, Win, wtiles, As, Rs, egam, Kn, Bn, V, ntiles, PC, pctiles,
              scale_all, out_cb, defer=False):
    m0 = cx.mark()
    tag0 = cx.S.tag
    cx.S.tag = tag0 + "/dplr_prep"
    idb = K.cb.ap[:, C_ID:C_ID + 128]
    neg = Bg is None
    AakT = cx.tile(NCH * 128, BF16, nm + "aak")
    ArkT = cx.tile(NCH * 128, BF16, nm + "ark")
    ArbT = cx.tile(NCH * 128, BF16, nm + "arb")
    TT_ = cx.tile(NCH * 128, BF16, nm + "TT")
    sl = lambda t, c: t.ap[:, c * 128:(c + 1) * 128]
    m1 = cx.mark()
    HC = 8
    tl_ = [cx.tile(HC * 128, BF16, nm + "iv%d" % i) for i in range(12)]
    U, N, Ua, Na, Nb, Ub_, Nc, Uc, P, Q, Z1b, Z2b = tl_
    hs = lambda t, q4: t.ap[:, q4 * 512:(q4 + 1) * 512]
    h1 = lambda t, cl: t.ap[:, cl * 128:(cl + 1) * 128]

    def mm4(dst_ps, lhs_t, rhs_t, q4):
        cx.mm([dict(out=dst_ps.ap[:, q * 128:(q + 1) * 128], lhsT=h1(lhs_t, q4 * 4 + q), rhs=h1(rhs_t, q4 * 4 + q), start=True, stop=True)
               for q in range(4)], [lhs_t, rhs_t], [dst_ps])

    for half in range(NCH // HC):
        for cl in range(HC):
            c = half * HC + cl
            ps = cx.ps()
            mms = [dict(out=ps.ap[:, 0:128], lhsT=Kg(c), rhs=Ag(c), start=True, stop=True),
                   dict(out=ps.ap[:, 128:256], lhsT=Kg(c), rhs=Rg(c), start=True, stop=True)]
            if not neg:
                mms += [dict(out=ps.ap[:, 256:384], lhsT=Bg(c), rhs=Ag(c), start=True, stop=True),
                        dict(out=ps.ap[:, 384:512], lhsT=Bg(c), rhs=Rg(c), start=True, stop=True)]
            cx.mm(mms, gtiles, [ps])
            cx.tt("dve", sl(AakT, c), ps.ap[:, 0:128], Wst(c), ALU.mult, [ps] + wtiles, [AakT])
            cx.tt("dve", sl(ArkT, c), ps.ap[:, 128:256], Win(c), ALU.mult, [ps] + wtiles, [ArkT])
            if neg:
                cx.stt(h1(U, cl), ps.ap[:, 0:128], -1.0, Wst(c), ALU.mult, ALU.mult, [ps] + wtiles, [U])
                cx.stt(sl(ArbT, c), ps.ap[:, 128:256], -1.0, Win(c), ALU.mult, ALU.mult, [ps] + wtiles, [ArbT])
            else:
                cx.tt("dve", h1(U, cl), ps.ap[:, 256:384], Wst(c), ALU.mult, [ps] + wtiles, [U])
                cx.tt("dve", sl(ArbT, c), ps.ap[:, 384:512], Win(c), ALU.mult, [ps] + wtiles, [ArbT])
        for q4 in range(HC // 4):
            ps = cx.ps()
            pv = ps.ap.bitcast(BF16)
            for q in range(4):
                cx.tr(pv[:, q * 128:(q + 1) * 128], h1(U, q4 * 4 + q), idb, [U, K.cb], [ps])
            cx.copy("act", hs(N, q4), pv[:, 0:512], [ps], [N])
        for q4 in range(HC // 4):
            cx.tt("pool", hs(Ua, q4), hs(U, q4), K.m4["bd8"].ap, ALU.mult, [U, K.m4["bd8"]], [Ua])
            cx.tt("pool", hs(Na, q4), hs(N, q4), K.m4["bd8"].ap, ALU.mult, [N, K.m4["bd8"]], [Na])
        for (dn, du, sn, su) in ((Nb, Ub_, Na, Ua), (Nc, Uc, Nb, Ub_)):
            for q4 in range(HC // 4):
                ps = cx.ps()
                mm4(ps, su, sn, q4)
                cx.copy("act", hs(dn, q4), ps.ap[:, 0:512], [ps], [dn])
                ps = cx.ps()
                mm4(ps, sn, su, q4)
                cx.copy("act", hs(du, q4), ps.ap[:, 0:512], [ps], [du])
        for q4 in range(HC // 4):
            cx.tt("pool", hs(P, q4), hs(Ua, q4), K.m4["id"].ap, ALU.add, [Ua, K.m4["id"]], [P])
            cx.tt("pool", hs(Q, q4), hs(Na, q4), K.m4["id"].ap, ALU.add, [Na, K.m4["id"]], [Q])
        for (ln, lu) in ((Nb, Ub_), (Nc, Uc)):
            for q4 in range(HC // 4):
                ps = cx.ps()
                mm4(ps, ln, P, q4)
                cx.tt("dve", hs(P, q4), ps.ap[:, 0:512], hs(P, q4), ALU.add, [ps, P], [P])
                ps = cx.ps()
                mm4(ps, lu, Q, q4)
                cx.tt("dve", hs(Q, q4), ps.ap[:, 0:512], hs(Q, q4), ALU.add, [ps, Q], [Q])
        NO, UO, P2, Q2 = Ua, Na, Nb, Ub_
        curP, curQ, nxtP, nxtQ = P, Q, P2, Q2
        for li, mk in enumerate(("o16", "o32", "o64", "o128")):
            last = (li == 3)
            for q4 in range(HC // 4):
                cx.tt("pool", hs(NO, q4), hs(N, q4), K.m4[mk].ap, ALU.mult, [N, K.m4[mk]], [NO])
                if not last:
                    cx.tt("pool", hs(UO, q4), hs(U, q4), K.m4[mk].ap, ALU.mult, [U, K.m4[mk]], [UO])
            for q4 in range(HC // 4):
                ps = cx.ps()
                mm4(ps, NO, curP, q4)
                cx.copy("act", hs(Z1b, q4), ps.ap[:, 0:512], [ps], [Z1b])
                if not last:
                    ps = cx.ps()
                    mm4(ps, UO, curQ, q4)
                    cx.copy("act", hs(Z2b, q4), ps.ap[:, 0:512], [ps], [Z2b])
            for q4 in range(HC // 4):
                ps = cx.ps()
                mm4(ps, curQ, Z1b, q4)
                if last:
                    c0_ = (half * HC + q4 * 4) * 128
                    cx.tt("dve", TT_.ap[:, c0_:c0_ + 512], ps.ap[:, 0:512], hs(curP, q4), ALU.add, [ps, curP], [TT_])
                else:
                    cx.tt("dve", hs(nxtP, q4), ps.ap[:, 0:512], hs(curP, q4), ALU.add, [ps, curP], [nxtP])
                    ps = cx.ps()
                    mm4(ps, curP, Z2b, q4)
                    cx.tt("dve", hs(nxtQ, q4), ps.ap[:, 0:512], hs(curQ, q4), ALU.add, [ps, curQ], [nxtQ])
            curP, curQ, nxtP, nxtQ = nxtP, nxtQ, curP, curQ
    cx.release(m1)
    AV = None
    if egam is not None:
        AV = cx.tile(NCH * dv, F32, nm + "AV")
        for c in range(NCH):
            ps = cx.ps()
            cx.mm([dict(out=ps.ap[:, 0:dv], lhsT=sl(AakT, c), rhs=V(c), start=True, stop=True)], [AakT] + ntiles, [ps])
            cx.copy("act", AV.ap[:, c * dv:(c + 1) * dv], ps.ap[:, 0:dv], [ps], [AV])
    St = cx.tile(dv, F32, nm + "S")
    Sb = cx.tile(dv, BF16, nm + "Sb")
    cx.I("dve", "memset", [], [St], ap=St.ap, constant=0.0)
    cx.I("dve", "memset", [], [Sb], ap=Sb.ap, constant=0.0)
    rows = slice(pb, pb + dk)

    def step(c):
        cx.S.tag = tag0 + "/dplr_seq"
        Xb = cx.ring(nm + "Xb", 2, dv, BF16)
        Ub = cx.ring(nm + "Ub", 2, dv, BF16)
        psX = cx.ps()
        if egam is None:
            cx.mm([dict(out=psX.ap[:, 0:dv], lhsT=As(c), rhs=Sb.ap[rows, :], start=True, stop=False),
                   dict(out=psX.ap[:, 0:dv], lhsT=sl(AakT, c), rhs=V(c), start=False, stop=True)],
                  gtiles + [Sb, AakT] + ntiles, [psX])
            cx.copy("act", Xb.ap, psX.ap[:, 0:dv], [psX], [Xb])
        else:
            cx.mm([dict(out=psX.ap[:, 0:dv], lhsT=As(c), rhs=Sb.ap[rows, :], start=True, stop=True)], gtiles + [Sb], [psX])
            cx.stt(Xb.ap, psX.ap[:, 0:dv], egam(c), AV.ap[:, c * dv:(c + 1) * dv], ALU.mult, ALU.add, [psX, AV] + pctiles, [Xb])
        psU = cx.ps()
        cx.mm([dict(out=psU.ap[:, 0:dv], lhsT=sl(TT_, c), rhs=Xb.ap, start=True, stop=True)], [TT_, Xb], [psU])
        cx.copy("act", Ub.ap, psU.ap[:, 0:dv], [psU], [Ub])
        if egam is None:
            psY = cx.ps()
            cx.mm([dict(out=psY.ap[:, 0:dv], lhsT=Rs(c), rhs=Sb.ap[rows, :], start=True, stop=False),
                   dict(out=psY.ap[:, 0:dv], lhsT=sl(ArbT, c), rhs=Ub.ap, start=False, stop=False),
                   dict(out=psY.ap[:, 0:dv], lhsT=sl(ArkT, c), rhs=V(c), start=False, stop=True)],
                  gtiles + [Sb, ArbT, ArkT, Ub] + ntiles, [psY])
            out_cb(c, psY.ap[:, 0:dv], psY)
        else:
            psY1 = cx.ps()
            cx.mm([dict(out=psY1.ap[:, 0:dv], lhsT=Rs(c), rhs=Sb.ap[rows, :], start=True, stop=True)], gtiles + [Sb], [psY1])
            psY2 = cx.ps()
            cx.mm([dict(out=psY2.ap[:, 0:dv], lhsT=sl(ArbT, c), rhs=Ub.ap, start=True, stop=False),
                   dict(out=psY2.ap[:, 0:dv], lhsT=sl(ArkT, c), rhs=V(c), start=False, stop=True)],
                  [ArbT, ArkT, Ub] + ntiles, [psY2])
            y2 = cx.ring(nm + "y2", 2, dv, F32)
            cx.copy("act", y2.ap, psY2.ap[:, 0:dv], [psY2], [y2])
            yo = cx.ring(nm + "yo", 2, dv, F32)
            cx.stt(yo.ap, psY1.ap[:, 0:dv], egam(c), y2.ap, ALU.mult, ALU.add, [psY1, y2] + pctiles, [yo])
            out_cb(c, yo.ap, yo)
        psS = cx.ps()
        cx.mm([dict(out=psS.ap[rows, 0:dv], lhsT=Bn(c), rhs=Ub.ap, start=True, stop=False),
               dict(out=psS.ap[rows, 0:dv], lhsT=Kn(c), rhs=V(c), start=False, stop=True)], ntiles + [Ub], [psS])
        if scale_all:
            cx.tt("dve", St.ap[rows, :], psS.ap[rows, 0:dv], St.ap[rows, :], ALU.add, [psS, St], [St])
            cx.ts("dve", St.ap[rows, :], St.ap[rows, :], PC(c), ALU.mult, [St] + pctiles, [St])
        else:
            cx.stt(St.ap[rows, :], St.ap[rows, :], PC(c), psS.ap[rows, 0:dv], ALU.mult, ALU.add, [St, psS] + pctiles, [St])
        cx.copy("act", Sb.ap[rows, :], St.ap[rows, :], [St], [Sb])

    def finish():
        cx.release(m0)
    if defer:
        return step, finish
    for c in range(NCH):
        step(c)
    finish()


def ln_exp_rinv(cx, out_ap, in_ap, reads, wt, scale=1.0, bias=1e-24):
    cx.act(out_ap, in_ap, AF.Ln, reads, [wt], scale=scale, bias=bias)
    cx.act(out_ap, out_ap, AF.Exp, [wt], [wt], scale=-0.5)


def mixer_attention(cx, K, io, hT, hv):
    m0 = cx.mark()
    wat = io["wat"]
    wsw = io["wsw"]
    rope = load_small(cx, io["rope"], 2 * SEQ, "rope", parts=32)
    cosv = rope.ap[:, 0:SEQ]
    sinv = rope.ap[:, SEQ:2 * SEQ]
    mdiag = K.cb.ap[:, C_MUI:C_MUI + 128]
    mprev = K.cb.ap[:, C_MLI:C_MLI + 128]
    mcomb = cx.tile(256, BF16, "mcomb")
    cx.copy("dve", mcomb.ap[:, 0:128], mdiag, [K.cb], [mcomb])
    cx.copy("dve", mcomb.ap[:, 128:256], mprev, [K.cb], [mcomb])
    if ATT_STOP <= 1:
        return
    DIL = [1, 4, 16]
    scale = 128.0 ** -0.5
    yout = io["ybuf"].rearrange("(k p) t -> p k t", p=128)
    for hl in range(2):
        m1 = cx.mark()
        cx.S.tag = "M/att/proj"
        qT = [cx.tile(SEQ, BF16, "qT%d" % g) for g in range(3)]
        kT = [cx.tile(SEQ, BF16, "kT%d" % g) for g in range(3)]
        Vt = [cx.tile(NCH * 128, BF16, "V%d" % g) for g in range(3)]
        for g in range(3):
            d = DIL[g]
            for which, dst in ((0, qT[g]), (1, kT[g])):
                pcs = load_w_multi(cx, wat[(hl * 3 + g) * 3 + which], KC, 128)
                pcs_sw = load_w_multi(cx, wsw[(hl * 3 + g) * 2 + which], KC, 32)
                for tb in range(4):
                    lo, hi = tb * 512, tb * 512 + 512
                    ps = gemm_block(cx, pcs, 128, lambda k, l, h_: hv[:, k, l:h_], [hT], lo, hi)
                    if ATT_SUB >= 1:
                        ps2 = gemm_block(cx, pcs_sw, 32, lambda k, l, h_: hv[:, k, l:h_], [hT], lo, hi)
                    t1 = cx.ring("rp1", 2, 512, F32)
                    t2 = cx.ring("rp2", 2, 512, F32)
                    if ATT_SUB >= 2:
                        if "a" in ATT_V:
                            cx.tt("dve", t1.ap[0:32, :], ps.ap[0:32, 0:512], cosv[:, lo:hi], ALU.mult, [ps, rope], [t1])
                        if "b" in ATT_V:
                            cx.tt("dve", t2.ap[0:32, :], ps2.ap[0:32, 0:512], sinv[:, lo:hi], ALU.mult, [ps2, rope], [t2])
                        if "c" in ATT_V:
                            cx.tt("dve", t1.ap[0:32, :], ps.ap[0:32, 0:512], K.cf.ap[0:32, 0:512], ALU.mult, [ps, K.cf], [t1])
                        if "e" in ATT_V:
                            cx.tt("dve", t1.ap[:, :], ps.ap[:, 0:512], K.cf.ap[:, 0:512], ALU.mult, [ps, K.cf], [t1])
                        if "g" in ATT_V:
                            cx.tt("dve", t1.ap[0:32, :], ps.ap[0:32, 0:512], K.cf.ap[0:32, 0:512], ALU.mult, [ps, K.cf], [t1])
                        if "d" in ATT_V:
                            cx.tt("dve", t1.ap[0:32, :], t2.ap[0:32, :], cosv[:, lo:hi], ALU.mult, [t2, rope], [t1])
                    cx.copy("act", dst.ap[:, lo:hi], ps.ap[:, 0:512], [ps, t1] if "s" in ATT_V else [ps], [dst])
                    if ATT_SUB >= 3:
                        cx.tt("pool", dst.ap[0:32, lo:hi], t1.ap[0:32, :], t2.ap[0:32, :], ALU.add, [t1, t2], [dst])
            if ATT_STOP <= 2:
                return
            pcs = load_w_multi(cx, wat[(hl * 3 + g) * 3 + 2], KC, 128)
            nb = NCH // d
            for b4 in range(4):
                ps = cx.ps()
                mms = []
                for q in range(4):
                    blk = b4 * 4 + q
                    r, b = blk // nb, blk % nb
                    t0 = r + d * 128 * b
                    for k in range(KC):
                        mms.append(dict(out=ps.ap[:, q * 128:(q + 1) * 128], lhsT=hv[:, k, t0:t0 + d * 127 + 1:d],
                                        rhs=wslice(pcs, k, 128), start=(k == 0), stop=(k == KC - 1)))
                cx.mm(mms, [p[0] for p in pcs] + [hT], [ps])
                cx.copy("act", Vt[g].ap[:, b4 * 512:(b4 + 1) * 512], ps.ap[:, 0:512], [ps], [Vt[g]])
        if ATT_STOP <= 3:
            return
        cx.S.tag = "M/att/core"
        for Tb in range(4):
            pso = cx.ps(pin=True)
            psd = cx.ps(pin=True)
            first = [True]

            def unit(g, kslices, qslice, nq, masks):
                pss = cx.ps()
                nk = len(kslices)
                cx.mm([dict(out=pss.ap[:, i * nq:(i + 1) * nq], lhsT=kT[g].ap[:, ks], rhs=qT[g].ap[:, qslice], start=True, stop=True)
                       for i, (ks, vb) in enumerate(kslices)], [kT[g], qT[g]], [pss])
                pe_ = cx.ring("pexp", 3, 256, BF16)
                pm = cx.ring("pmask", 3, 256, BF16)
                cx.act(pe_.ap[:, 0:nk * nq], pss.ap[:, 0:nk * nq], AF.Exp, [pss], [pe_], scale=scale)
                cx.tt("pool", pm.ap[:, 0:nk * nq], pe_.ap[:, 0:nk * nq], masks, ALU.mult, [pe_, mcomb, K.cb], [pm])
                mmo, mmd = [], []
                qs0 = qslice.start - Tb * 512
                st = qslice.step or 1
                ocols = slice(qs0, qs0 + (nq - 1) * st + 1, st)
                for i, (ks, vb) in enumerate(kslices):
                    f = first[0]
                    first[0] = False
                    mmo.append(dict(out=pso.ap[:, ocols], lhsT=Vt[g].ap[:, vb * 128:(vb + 1) * 128], rhs=pm.ap[:, i * nq:(i + 1) * nq],
                                    start=f, stop=False, skip_group_check=True))
                    mmd.append(dict(out=psd.ap[:, ocols], lhsT=K.ones.ap, rhs=pm.ap[:, i * nq:(i + 1) * nq],
                                    start=f, stop=False, skip_group_check=True))
                cx.mm(mmo, [Vt[g], pm], [pso])
                cx.mm(mmd, [K.ones, pm], [psd])

            for qb in range(4 * Tb, 4 * Tb + 4):
                ks = [(slice(qb * 128, qb * 128 + 128), qb)]
                if qb > 0:
                    ks.append((slice((qb - 1) * 128, qb * 128), qb - 1))
                unit(0, ks, slice(qb * 128, qb * 128 + 128), 128, mcomb.ap[:, 0:128 * len(ks)])
            for r in range(4 if "1" in ATT_G else 0):
                tq = r + 4 * 128 * Tb
                ks = [(slice(tq, tq + 4 * 127 + 1, 4), r * 4 + Tb)]
                if Tb > 0:
                    tk = r + 4 * 128 * (Tb - 1)
                    ks.append((slice(tk, tk + 4 * 127 + 1, 4), r * 4 + Tb - 1))
                unit(1, ks, slice(tq, tq + 4 * 127 + 1, 4), 128, mcomb.ap[:, 0:128 * len(ks)])
            for r in range(16 if "2" in ATT_G else 0):
                tq = r + 16 * 32 * Tb
                ks = [(slice(r, r + 16 * 127 + 1, 16), r)]
                unit(2, ks, slice(tq, tq + 16 * 31 + 1, 16), 32, mdiag[:, 32 * Tb:32 * Tb + 32])
            rd = cx.ring("rden", 2, 512, F32)
            cx.I("dve", "reciprocal", [psd], [rd], out=rd.ap, in_=psd.ap[:, 0:512])
            yo = cx.ring("ybo", 2, 512, BF16)
            cx.tt("dve", yo.ap, pso.ap[:, 0:512], rd.ap, ALU.mult, [pso, rd], [yo])
            cx.dma(yout[:, 8 + 2 * io["s"] + hl, Tb * 512:(Tb + 1) * 512], yo.ap, [yo], [io["ydep"]["b"]])
            cx.unpin_all()
            if ATT_STOP <= 4:
                return
        cx.release(m1)
    cx.release(m0)


def mixer_gdn(cx, K, io, hT, hv):
    m0 = cx.mark()
    wc = io["wc"]
    wba = io["wba"]
    gconv = load_small(cx, io["gconv"], 12 * 4, "gconv")
    gcv = gconv.ap.rearrange("p (b t) -> p b t", t=4)
    gpar = load_small(cx, io["gpar"], 8 + 128, "gpar")
    mui = K.cf.ap[:, C_MUI:C_MUI + 128]
    mus = K.cf.ap[:, C_MUS:C_MUS + 128]
    mls = K.cf.ap[:, C_MLS:C_MLS + 128]
    idb = K.cb.ap[:, C_ID:C_ID + 128]
    pcs = load_w_multi(cx, wba, KC, 8)
    ps = cx.ps()
    mms = []
    for c in range(NCH):
        for k in range(KC):
            mms.append(dict(out=ps.ap[:, c * 8:(c + 1) * 8], lhsT=hv[:, k, c * 128:(c + 1) * 128], rhs=wslice(pcs, k, 8),
                            start=(k == 0), stop=(k == KC - 1)))
    cx.mm(mms, [p[0] for p in pcs] + [hT], [ps])
    ba = cx.tile(NCH * 8, F32, "ba")
    cx.copy("act", ba.ap, ps.ap[:, 0:NCH * 8], [ps], [ba])
    bav = ba.ap.rearrange("p (c e) -> p c e", e=8)
    beta = cx.tile(NCH * 4, F32, "beta")
    betav = beta.ap.rearrange("p (c e) -> p c e", e=4)
    cx.act(betav, bav[:, :, 0:4], AF.Sigmoid, [ba], [beta])
    gg = cx.tile(NCH * 4, F32, "gg")
    ggv = gg.ap.rearrange("p (c e) -> p c e", e=4)
    for hh in range(4):
        cx.act(ggv[:, :, hh], bav[:, :, 4 + hh], AF.Exp, [ba, gpar], [gg], bias=gpar.ap[:, 4 + hh:5 + hh])
    cx.act(gg.ap, gg.ap, AF.Ln, [gg], [gg], bias=1.0)
    ea = cx.tile(4, F32, "expA")
    cx.act(ea.ap, gpar.ap[:, 0:4], AF.Exp, [gpar], [ea])
    for hh in range(4):
        cx.ts("dve", ggv[:, :, hh], ggv[:, :, hh], ea.ap[:, hh:hh + 1], ALU.mult, [gg, ea], [gg], s2=-1.0, op1=ALU.mult)
    ps = cx.ps()
    cx.mm([dict(out=ps.ap[:, 0:64], lhsT=mui, rhs=gg.ap, start=True, stop=True)], [K.cf, gg], [ps])
    gam = cx.tile(64, F32, "gam")
    cx.copy("act", gam.ap, ps.ap[:, 0:64], [ps], [gam])
    egam = cx.tile(64, F32, "egam")
    cx.act(egam.ap, gam.ap, AF.Exp, [gam], [egam])
    ps = cx.ps()
    cx.mm([dict(out=ps.ap[:, 0:64], lhsT=K.cf.ap[:, C_S127:C_S127 + 128], rhs=gam.ap, start=True, stop=True)], [K.cf, gam], [ps])
    pcall = cx.tile(64, F32, "pcall")
    cx.act(pcall.ap, ps.ap[:, 0:64], AF.Exp, [ps], [pcall])
    wk = cx.tile(64, F32, "wk")
    cx.tt("dve", wk.ap, ps.ap[:, 0:64], gam.ap, ALU.subtract, [ps, gam], [wk])
    cx.act(wk.ap, wk.ap, AF.Exp, [wk], [wk])
    cx.tt("dve", wk.ap, wk.ap, beta.ap, ALU.mult, [wk, beta], [wk])
    nwk = cx.tile(64, F32, "nwk")
    cx.ts("dve", nwk.ap, wk.ap, -1.0, ALU.mult, [wk], [nwk])
    yout = io["ybuf"].rearrange("(k p) t -> p k t", p=128)
    for hh in range(4):
        m1 = cx.mark()
        cx.S.tag = "M/gdn/prep"
        qT = cx.tile(SEQ, BF16, "gq")
        kT = cx.tile(SEQ, BF16, "gk")
        Vn = cx.tile(NCH * 128, BF16, "gV")
        Kn = cx.tile(NCH * 128, BF16, "gKn")
        Bn = cx.tile(NCH * 128, BF16, "gBn")
        gate = cx.tile(NCH * 128, BF16, "ggate")
        Wst = cx.tile(NCH * 128, F32, "gWst")
        Win = cx.tile(NCH * 128, F32, "gWin")
        m2 = cx.mark()
        vT = cx.tile(SEQ, BF16, "gvT")
        for which, dst in ((0, qT), (1, kT), (2, vT)):
            pcs = load_w_multi(cx, wc[which * 4 + hh], KC, 128)
            zp = cx.ring("gzp", 2, 3 + SEQ, F32)
            cx.I("dve", "memset", [], [zp], ap=zp.ap[:, 0:3], constant=0.0)
            for tb in range(4):
                lo, hi = tb * 512, tb * 512 + 512
                ps = gemm_block(cx, pcs, 128, lambda k, l, h_: hv[:, k, l:h_], [hT], lo, hi)
                cx.copy("act", zp.ap[:, 3 + lo:3 + hi], ps.ap[:, 0:512], [ps], [zp])
            cv = cx.ring("gcv", 2, SEQ, F32)
            bi = which * 4 + hh
            cx.ts("dve", cv.ap, zp.ap[:, 3:3 + SEQ], gcv[:, bi, 3:4], ALU.mult, [zp, gconv], [cv])
            for tap in range(3):
                cx.stt(cv.ap, zp.ap[:, tap:tap + SEQ], gcv[:, bi, tap:tap + 1], cv.ap, ALU.mult, ALU.add, [zp, gconv, cv], [cv])
            cx.act(cv.ap, cv.ap, AF.Silu, [cv], [cv])
            if which == 2:
                cx.copy("pool", dst.ap, cv.ap, [cv], [dst])
            else:
                for tb in range(4):
                    lo, hi = tb * 512, tb * 512 + 512
                    sq = cx.ring("gsq", 2, 512, BF16)
                    cx.act(sq.ap, cv.ap[:, lo:hi], AF.Square, [cv], [sq])
                    ps = cx.ps()
                    cx.mm([dict(out=ps.ap[:, 0:512], lhsT=K.ones.ap, rhs=sq.ap, start=True, stop=True)], [K.ones, sq], [ps])
                    ri = cx.ring("gri", 2, 512, F32)
                    ln_exp_rinv(cx, ri.ap, ps.ap[:, 0:512], [ps], ri)
                    if which == 0:
                        cx.stt(dst.ap[:, lo:hi], cv.ap[:, lo:hi], 128.0 ** -0.5, ri.ap, ALU.mult, ALU.mult, [cv, ri], [dst])
                    else:
                        cx.tt("dve", dst.ap[:, lo:hi], cv.ap[:, lo:hi], ri.ap, ALU.mult, [cv, ri], [dst])
        for c4 in range(4):
            ps = cx.ps()
            pv = ps.ap.bitcast(BF16)
            for q in range(4):
                c = c4 * 4 + q
                cx.tr(pv[:, q * 128:(q + 1) * 128], vT.ap[:, c * 128:(c + 1) * 128], idb, [vT, K.cb], [ps])
            cx.copy("act", Vn.ap[:, c4 * 512:(c4 + 1) * 512], pv[:, 0:512], [ps], [Vn])
            ps = cx.ps()
            pv = ps.ap.bitcast(BF16)
            for q in range(4):
                c = c4 * 4 + q
                cx.tr(pv[:, q * 128:(q + 1) * 128], kT.ap[:, c * 128:(c + 1) * 128], idb, [kT, K.cb], [ps])
            for q in range(4):
                c = c4 * 4 + q
                col = c * 4 + hh
                cx.ts("dve", Kn.ap[:, c * 128:(c + 1) * 128], pv[:, q * 128:(q + 1) * 128], wk.ap[:, col:col + 1], ALU.mult, [ps, wk], [Kn])
                cx.ts("dve", Bn.ap[:, c * 128:(c + 1) * 128], pv[:, q * 128:(q + 1) * 128], nwk.ap[:, col:col + 1], ALU.mult, [ps, nwk], [Bn])
        pcs = load_w_multi(cx, wc[12 + hh], KC, 128)
        for c4 in range(4):
            ps = cx.ps()
            mms = []
            for q in range(4):
                c = c4 * 4 + q
                for k in range(KC):
                    mms.append(dict(out=ps.ap[:, q * 128:(q + 1) * 128], lhsT=hv[:, k, c * 128:(c + 1) * 128], rhs=wslice(pcs, k, 128),
                                    start=(k == 0), stop=(k == KC - 1)))
            cx.mm(mms, [p[0] for p in pcs] + [hT], [ps])
            cx.act(gate.ap[:, c4 * 512:(c4 + 1) * 512], ps.ap[:, 0:512], AF.Silu, [ps], [gate])
        for c in range(NCH):
            col = c * 4 + hh
            g2 = cx.ring("gG2", 2, 128, F32)
            cx.ts("dve", g2.ap, mui, gg.ap[:, col:col + 1], ALU.mult, [K.cf, gg], [g2])
            ps = cx.ps()
            cx.mm([dict(out=ps.ap[:, 0:128], lhsT=mls, rhs=g2.ap, start=True, stop=True)], [K.cf, g2], [ps])
            ex = cx.ring("gex", 2, 128, F32)
            cx.act(ex.ap, ps.ap[:, 0:128], AF.Exp, [ps], [ex])
            cx.stt(Wst.ap[:, c * 128:(c + 1) * 128], ex.ap, beta.ap[:, col:col + 1], mus, ALU.mult, ALU.mult, [ex, beta, K.cf], [Wst])
            cx.stt(Win.ap[:, c * 128:(c + 1) * 128], ex.ap, beta.ap[:, col:col + 1], mui, ALU.mult, ALU.mult, [ex, beta, K.cf], [Win])
        cx.release(m2)
        cx.S.tag = "M/gdn"
        sl = lambda t, c: t.ap[:, c * 128:(c + 1) * 128]
        ycs = cx.tile(SEQ, BF16, "ycs")

        def out_cb(c, yap, yt, hh=hh, ycs=ycs):
            junk = cx.ring("gjunk", 2, 128, F32)
            ss = cx.ring("gss", 2, 1, F32)
            cx.act(junk.ap, yap, AF.Square, [yt], [junk, ss], accum_out=ss.ap)
            ln_exp_rinv(cx, ss.ap, ss.ap, [ss], ss, scale=1.0 / 128, bias=EPS)
            o1 = cx.ring("go1", 2, 128, F32)
            cx.stt(o1.ap, yap, ss.ap[:, 0:1], gpar.ap[:, 8:136], ALU.mult, ALU.mult, [yt, ss, gpar], [o1])
            o2 = cx.ring("go2", 2, 128, BF16)
            cx.tt("pool", o2.ap, o1.ap, gate.ap[:, c * 128:(c + 1) * 128], ALU.mult, [o1, gate], [o2])
            pst = cx.ps()
            ptv = pst.ap.bitcast(BF16)
            cx.tr(ptv[:, 0:128], o2.ap, idb, [o2, K.cb], [pst])
            cx.copy("act", ycs.ap[:, c * 128:(c + 1) * 128], ptv[:, 0:128], [pst], [ycs])

        dplr_head(cx, K, "g%d" % hh, 128, 128, 0,
                  Kg=lambda c: sl(kT, c), Bg=None, Ag=lambda c: sl(kT, c), Rg=lambda c: sl(qT, c), gtiles=[kT, qT],
                  Wst=lambda c: sl(Wst, c), Win=lambda c: sl(Win, c), wtiles=[Wst, Win],
                  As=lambda c: sl(kT, c), Rs=lambda c: sl(qT, c), egam=lambda c: egam.ap[:, c * 4 + hh:c * 4 + hh + 1],
                  Kn=lambda c: sl(Kn, c), Bn=lambda c: sl(Bn, c), V=lambda c: sl(Vn, c), ntiles=[Kn, Bn, Vn],
                  PC=lambda c: pcall.ap[:, c * 4 + hh:c * 4 + hh + 1], pctiles=[pcall, egam], scale_all=False, out_cb=out_cb)
        cx.dma(yout[:, 12 + 4 * io["s"] + hh, :], ycs.ap, [ycs], [io["ydep"]["c"]])
        cx.release(m1)
    cx.release(m0)


def mixer_rwkv(cx, K, io, hT, hv, layer1):
    m0 = cx.mark()
    wa = io["wa"]
    wal = io["wal"]
    rp = load_small(cx, io["rpar"], 44, "rpar")
    rl = load_small(cx, io["rlmu"], 4, "rlmu")
    rb = load_small(cx, io["rbc"], 1024, "rbc")
    wab = cx.tile(512, BF16, "wab")
    g2b = cx.tile(1024, BF16, "g2b")
    if layer1:
        v2b_ = cx.tile(512, BF16, "v2b", parts=64)
        v2b = T(v2b_.ap[32:64, :], v2b_.d)
    rmask = cx.tile(SEQ, BF16, "rmask")
    tl = cx.tile(SEQ, BF16, "tl")
    sgd = cx.tile(SEQ, BF16, "sgd")
    sg2h = cx.tile(SEQ, BF16, "sg2h", parts=64)
    sgd2 = T(sg2h.ap[0:32, :], sg2h.d)
    if layer1:
        hv1 = T(sg2h.ap[32:64, :], sg2h.d)
    mt = cx.mark()
    w2a2 = load_small(cx, io["rw2a2"], 512, "rw2a2")
    g2a = load_small(cx, io["rg2"], 1024, "rg2")
    cx.copy("pool", wab.ap, w2a2.ap, [w2a2], [wab])
    cx.copy("pool", g2b.ap, g2a.ap, [g2a], [g2b])
    if layer1:
        v2 = cx.tile(512, F32, "rv2", parts=64)
        cx.dma(v2.ap[32:64, :], io["rv2"], [], [v2])
        cx.copy("pool", v2b.ap, v2.ap[32:64, :], [v2], [v2b])
    idb = K.cb.ap[:, C_ID:C_ID + 128]
    msk_s = K.cb.ap[:, C_MUS:C_MUS + 128]
    msk_i = K.cb.ap[:, C_MUI:C_MUI + 128]
    cx.I("dve", "memset", [], [rmask], ap=rmask.ap, constant=1.0)
    cx.I("dve", "memset", [], [rmask], ap=rmask.ap[:, 0:SEQ:128], constant=0.0)

    def mixed_block(zp, tmp, pcs, cw, c0, m, po, mu_ap, mu_t, dst_ap, dst_t, func=None):
        pr = slice(po, po + m)
        cx.I("dve", "memset", [], [zp], ap=zp.ap[pr, 0:1], constant=0.0)
        for tb in range(4):
            lo, hi = tb * 512, tb * 512 + 512
            ps = cx.ps()
            mms = []
            for k in range(KC):
                mms.append(dict(out=ps.ap[pr, 0:512], lhsT=wslice(pcs, k, cw, c0, c0 + m), rhs=hv[:, k, lo:hi],
                                start=(k == 0), stop=(k == KC - 1)))
            cx.mm(mms, [p[0] for p in pcs] + [hT], [ps])
            cx.copy("act", zp.ap[pr, 1 + lo:1 + hi], ps.ap[pr, 0:512], [ps], [zp])
        cx.tt("dve", tmp.ap[pr, :], zp.ap[pr, 0:SEQ], zp.ap[pr, 1:1 + SEQ], ALU.subtract, [zp], [tmp])
        if func is None:
            cx.stt(dst_ap, tmp.ap[pr, :], mu_ap, zp.ap[pr, 1:1 + SEQ], ALU.mult, ALU.add, [tmp, zp, mu_t], [dst_t])
        else:
            cx.stt(tmp.ap[pr, :], tmp.ap[pr, :], mu_ap, zp.ap[pr, 1:1 + SEQ], ALU.mult, ALU.add, [tmp, zp, mu_t], [tmp])
            cx.act(dst_ap, tmp.ap[pr, :], func, [tmp], [dst_t])

    zp0 = cx.tile(SEQ + 8, F32, "zp0")
    tmp0 = cx.tile(SEQ, F32, "tmp0")
    pcl = load_w_big(cx, wal, KC, 288, "wal_b")
    mixed_block(zp0, tmp0, pcl, 288, 0, 64, 0, rl.ap[0:64, 0:1], rl, tl.ap[0:64, :], tl, func=AF.Tanh)
    mixed_block(zp0, tmp0, pcl, 288, 64, 64, 64, rl.ap[64:128, 0:1], rl, tl.ap[64:128, :], tl, func=AF.Copy)
    mixed_block(zp0, tmp0, pcl, 288, 128, 128, 0, rl.ap[:, 1:2], rl, sgd.ap, sgd, func=AF.Sigmoid)
    mixed_block(zp0, tmp0, pcl, 288, 256, 32, 0, rl.ap[0:32, 2:3], rl, sgd2.ap, sgd2, func=AF.Sigmoid)
    if layer1:
        pc1 = load_w_multi(cx, io["rv1"], KC, 32)
        for tb in range(4):
            lo, hi = tb * 512, tb * 512 + 512
            ps = gemm_block(cx, pc1, 32, lambda k, l, h_: hv[:, k, l:h_], [hT], lo, hi, po=32)
            cx.copy("act", hv1.ap[:, lo:hi], ps.ap[32:64, 0:512], [ps], [hv1])
        vfT = io["vf_in"].rearrange("(k p) t -> p k t", p=128)
    cx.release(mt)
    vout = io["v_out"].rearrange("(k p) t -> p k t", p=128)
    yout = io["ybuf"].rearrange("(k p) t -> p k t", p=128)
    lnw = rb.ap[:, 0:512]
    lnb = rb.ap[:, 512:1024]
    for ct in range(4):
        m1 = cx.mark()
        par = rp.ap[:, 12 + ct * 8:12 + ct * 8 + 8]
        cx.S.tag = "M/rwkv/prep"
        As = cx.tile(SEQ, BF16, "rAs")
        Rs = cx.tile(SEQ, BF16, "rRs")
        Ks = cx.tile(SEQ, BF16, "rKs")
        Bs = cx.tile(SEQ, BF16, "rBs")
        Kn = cx.tile(NCH * 128, BF16, "rKn")
        Bn = cx.tile(NCH * 128, BF16, "rBn")
        Vn = cx.tile(NCH * 128, BF16, "rVn")
        PCt = cx.tile(NCH, F32, "rPC")
        bon = cx.tile(NCH * 2, F32, "rbon")
        m2 = cx.mark()
        zp = cx.tile(SEQ + 8, F32, "zp")
        tmp = cx.tile(SEQ, F32, "tmp")
        ld = cx.tile(SEQ, F32, "ld")
        cum = cx.tile(SEQ, F32, "cum")
        av = cx.tile(SEQ, BF16, "a_sig")
        rm = cx.tile(SEQ, BF16, "r_m")
        km = cx.tile(SEQ, BF16, "k_m")
        vm = cx.tile(SEQ, BF16, "v_m")
        rkr = cx.tile(SEQ, BF16, "rkr")
        for which, dst in ((0, rm), (1, km), (2, vm)):
            pcs = load_w_multi(cx, wa[which * 4 + ct], KC, 128)
            mixed_block(zp, tmp, pcs, 128, 0, 128, 0, rp.ap[:, which * 4 + ct:which * 4 + ct + 1], rp, dst.ap, dst)
        kx = T(zp.ap[:, 0:SEQ], zp.d)
        B1 = tmp
        for tb in range(4):
            lo, hi = tb * 512, tb * 512 + 512
            ps = cx.ps()
            cx.mm([dict(out=ps.ap[:, 0:512], lhsT=wab.ap[0:64, ct * 128:(ct + 1) * 128], rhs=tl.ap[0:64, lo:hi], start=True, stop=True)],
                  [wab, tl], [ps])
            cx.act(ld.ap[:, lo:hi], ps.ap[:, 0:512], AF.Sigmoid, [ps, rp], [ld], bias=par[:, 0:1])
            ps = cx.ps()
            cx.mm([dict(out=ps.ap[:, 0:512], lhsT=wab.ap[64:128, ct * 128:(ct + 1) * 128], rhs=tl.ap[64:128, lo:hi], start=True, stop=True)],
                  [wab, tl], [ps])
            cx.act(av.ap[:, lo:hi], ps.ap[:, 0:512], AF.Sigmoid, [ps, rp], [av], bias=par[:, 1:2])
        cx.ts("dve", ld.ap, ld.ap, -float(np.exp(-0.5)), ALU.mult, [ld], [ld])
        cx.I("dve", "tensor_tensor_scan", [rmask, ld], [cum], out=cum.ap, data0=rmask.ap, data1=ld.ap, initial=0.0,
             op0=ALU.mult, op1=ALU.add)
        if layer1:
            vf = rkr
            cx.dma(vf.ap, vfT[:, ct, :], io["vf_deps"], [vf])
            for tb in range(4):
                lo, hi = tb * 512, tb * 512 + 512
                ps = cx.ps()
                cx.mm([dict(out=ps.ap[:, 0:512], lhsT=v2b.ap[:, ct * 128:(ct + 1) * 128], rhs=hv1.ap[:, lo:hi], start=True, stop=True)],
                      [v2b, hv1], [ps])
                cx.act(B1.ap[:, lo:hi], ps.ap[:, 0:512], AF.Sigmoid, [ps, rp], [B1], bias=par[:, 5:6])
            cx.tt("dve", kx.ap, vf.ap, vm.ap, ALU.subtract, [vf, vm], [kx])
            cx.tt("dve", kx.ap, kx.ap, B1.ap, ALU.mult, [kx, B1], [kx])
            cx.tt("dve", vm.ap, vm.ap, kx.ap, ALU.add, [vm, kx], [vm])
        if io.get("write_v", True):
            cx.dma(vout[:, ct, :], vm.ap, [vm], [io["v_dep"]])
        cx.ts("dve", kx.ap, km.ap, par[:, 2:3], ALU.mult, [km, rp], [kx])
        for tb in range(4):
            lo, hi = tb * 512, tb * 512 + 512
            sq = cx.ring("rsq", 2, 512, BF16)
            cx.act(sq.ap, kx.ap[:, lo:hi], AF.Square, [kx], [sq])
            ps = cx.ps()
            cx.mm([dict(out=ps.ap[:, 0:512], lhsT=K.cb.ap[:, C_BO:C_BO + 128], rhs=sq.ap, start=True, stop=True)], [K.cb, sq], [ps])
            ri = cx.ring("rri", 2, 512, F32)
            ln_exp_rinv(cx, ri.ap, ps.ap[:, 0:512], [ps], ri)
            cx.tt("dve", kx.ap[:, lo:hi], kx.ap[:, lo:hi], ri.ap, ALU.mult, [kx, ri], [kx])
        if RW_DBG and ct == 0:
            dbg = io["dbg"].rearrange("(k p) t -> p k t", p=128)
            cx.dma(dbg[:, 0, :], kx.ap, [kx], [])
            cx.dma(dbg[:, 1, :], ld.ap, [ld], [])
            cx.dma(dbg[:, 2, :], cum.ap, [cum], [])
        cx.ts("dve", B1.ap, av.ap, -1.0, ALU.add, [av, rp], [B1], s2=par[:, 3:4], op1=ALU.mult)
        cx.ts("dve", B1.ap, B1.ap, 1.0, ALU.add, [B1], [B1])
        if RW_DBG and ct == 0:
            cx.dma(dbg[:, 3, :], B1.ap, [B1], [])
        cx.tt("dve", km.ap, km.ap, B1.ap, ALU.mult, [km, B1], [km])
        cx.stt(rkr.ap, rm.ap, par[:, 4:5], km.ap, ALU.mult, ALU.mult, [rm, km, rp], [rkr])
        ps = cx.ps()
        cx.mm([dict(out=ps.ap[:, c * 2:c * 2 + 2], lhsT=rkr.ap[:, c * 128:(c + 1) * 128], rhs=K.cb.ap[:, C_BS:C_BS + 2], start=True, stop=True)
               for c in range(NCH)], [rkr, K.cb], [ps])
        cx.copy("act", bon.ap, ps.ap[:, 0:NCH * 2], [ps], [bon])
        cx.tt("dve", B1.ap, cum.ap, ld.ap, ALU.subtract, [cum, ld], [B1])
        cx.act(B1.ap, B1.ap, AF.Exp, [B1], [B1])
        cx.stt(As.ap, kx.ap, -1.0, B1.ap, ALU.mult, ALU.mult, [kx, B1], [As])
        cx.act(B1.ap, cum.ap, AF.Exp, [cum], [B1])
        cx.tt("dve", Rs.ap, rm.ap, B1.ap, ALU.mult, [rm, B1], [Rs])
        cx.copy("dve", PCt.ap, B1.ap[:, 127:SEQ:128], [B1], [PCt])
        cx.act(B1.ap, cum.ap, AF.Exp, [cum], [B1], scale=-1.0)
        cx.tt("dve", Ks.ap, km.ap, B1.ap, ALU.mult, [km, B1], [Ks])
        cx.tt("dve", kx.ap, kx.ap, av.ap, ALU.mult, [kx, av], [kx])
        cx.tt("dve", Bs.ap, kx.ap, B1.ap, ALU.mult, [kx, B1], [Bs])
        for src, dst in ((Ks, Kn), (Bs, Bn), (vm, Vn)):
            for c4 in range(4):
                ps = cx.ps()
                pv = ps.ap.bitcast(BF16)
                for q in range(4):
                    c = c4 * 4 + q
                    cx.tr(pv[:, q * 128:(q + 1) * 128], src.ap[:, c * 128:(c + 1) * 128], idb, [src, K.cb], [ps])
                cx.copy("act", dst.ap[:, c4 * 512:(c4 + 1) * 512], pv[:, 0:512], [ps], [dst])
        cx.release(m2)
        yas = cx.tile(SEQ, BF16, "yas")
        cx.S.tag = "M/rwkv"
        sfs = []
        for hp in range(2):
            pb = hp * 64
            hl = ct * 2 + hp
            pr = slice(pb, pb + 64)

            def out_cb(c, yap, yt, hl=hl, hp=hp, ct=ct, yas=yas, pb=pb):
                st = cx.ring("rst%d" % hp, 2, 6, F32)
                cx.I("dve", "bn_stats", [yt], [st], out=st.ap, in_=yap)
                mv_ = cx.ring("rmv%d" % hp, 2, 2, F32)
                cx.I("dve", "bn_aggr", [st], [mv_], out=mv_.ap, in_=st.ap)
                rs = cx.ring("rrs%d" % hp, 2, 1, F32)
                ln_exp_rinv(cx, rs.ap, mv_.ap[:, 1:2], [mv_], rs, bias=64e-5)
                y1 = cx.ring("ry1%d" % hp, 2, 64, F32)
                cx.ts("dve", y1.ap, yap, mv_.ap[:, 0:1], ALU.subtract, [yt, mv_, rs], [y1], s2=rs.ap[:, 0:1], op1=ALU.mult)
                cx.tt("pool", y1.ap, y1.ap, lnw[:, hl * 64:(hl + 1) * 64], ALU.mult, [y1, rb], [y1])
                cx.tt("pool", y1.ap, y1.ap, lnb[:, hl * 64:(hl + 1) * 64], ALU.add, [y1, rb], [y1])
                y2 = cx.ring("ry2%d" % hp, 2, 64, F32)
                cx.stt(y2.ap, Vn.ap[:, c * 128 + hp * 64:c * 128 + hp * 64 + 64], bon.ap[:, c * 2 + hp:c * 2 + hp + 1], y1.ap,
                       ALU.mult, ALU.add, [Vn, bon, y1], [y2])
                psg = cx.ps()
                cx.mm([dict(out=psg.ap[:, 0:64], lhsT=sgd.ap[:, c * 128:(c + 1) * 128], rhs=g2b.ap[:, hl * 64:(hl + 1) * 64], start=True, stop=False),
                       dict(out=psg.ap[:, 0:64], lhsT=sgd2.ap[:, c * 128:(c + 1) * 128], rhs=g2b.ap[0:32, 512 + hl * 64:512 + (hl + 1) * 64],
                            start=False, stop=True)], [sgd, sgd2, g2b], [psg])
                y3 = cx.ring("ry3%d" % hp, 2, 64, BF16)
                if RW_DBG == "g":
                    cx.copy("dve", y3.ap, psg.ap[:, 0:64], [psg], [y3])
                elif RW_DBG == "y1":
                    cx.copy("dve", y3.ap, y1.ap, [y1, psg], [y3])
                elif RW_DBG == "y2":
                    cx.copy("dve", y3.ap, y2.ap, [y2, psg], [y3])
                elif RW_DBG == "yraw":
                    cx.copy("dve", y3.ap, yap, [yt, psg], [y3])
                else:
                    cx.tt("dve", y3.ap, psg.ap[:, 0:64], y2.ap, ALU.mult, [psg, y2], [y3])
                pst = cx.ps()
                ptv = pst.ap.bitcast(BF16)
                cx.tr(ptv[pb:pb + 64, 0:128], y3.ap, idb, [y3, K.cb], [pst])
                cx.copy("act", yas.ap[pb:pb + 64, c * 128:(c + 1) * 128], ptv[pb:pb + 64, 0:128], [pst], [yas])

            sf = dplr_head(cx, K, "r%d" % hl, 64, 64, pb, defer=True,
                      Kg=lambda c, pr=pr: Ks.ap[pr, c * 128:(c + 1) * 128], Bg=lambda c, pr=pr: Bs.ap[pr, c * 128:(c + 1) * 128],
                      Ag=lambda c, pr=pr: As.ap[pr, c * 128:(c + 1) * 128], Rg=lambda c, pr=pr: Rs.ap[pr, c * 128:(c + 1) * 128],
                      gtiles=[Ks, Bs, As, Rs],
                      Wst=lambda c: msk_s, Win=lambda c: msk_i, wtiles=[K.cb],
                      As=lambda c, pr=pr: As.ap[pr, c * 128:(c + 1) * 128], Rs=lambda c, pr=pr: Rs.ap[pr, c * 128:(c + 1) * 128], egam=None,
                      Kn=lambda c, pb=pb: Kn.ap[:, c * 128 + pb:c * 128 + pb + 64], Bn=lambda c, pb=pb: Bn.ap[:, c * 128 + pb:c * 128 + pb + 64],
                      V=lambda c, pb=pb: Vn.ap[:, c * 128 + pb:c * 128 + pb + 64], ntiles=[Kn, Bn, Vn],
                      PC=lambda c, pr=pr: PCt.ap[pr, c:c + 1], pctiles=[PCt], scale_all=True, out_cb=out_cb)
            sfs.append(sf)
        for c in range(NCH):
            for st_, fn_ in sfs:
                st_(c)
        for st_, fn_ in reversed(sfs):
            fn_()
        cx.dma(yout[:, 4 * io["s"] + ct, :], yas.ap, [yas], [io["ydep"]["a"]])
        cx.release(m1)
    cx.release(m0)


def phase_M(cx, ios, layer1, which=("att", "gdn", "rwkv")):
    io = ios[0]
    base_ = cx.mark()
    cx.S.tag = "M/pro"
    cx.cast_eng = "pool"
    K = setup_consts(cx, io)
    g1 = load_small(cx, io["g_attn"], 16, "g1")
    hT = cx.tile(KC * SEQ, BF16, "hT")
    hv = hT.ap.rearrange("p (k t) -> p k t", k=KC)
    xT = io["xT"].rearrange("(k p) t -> p k t", p=128)
    m0 = cx.mark()
    for tb in range(4):
        lo, hi = tb * 512, tb * 512 + 512
        xt = cx.ring("xtile", 1, KC * 512, F32)
        xtv = xt.ap.rearrange("p (k t) -> p k t", k=KC)
        for k in range(KC):
            cx.dma(xtv[:, k, :], xT[:, k, lo:hi], io.get("x_deps", []), [xt])
        rmsnorm_fm(cx, lambda k, l, h_: (xtv[:, k, 0:h_ - l], xt), g1, lambda k, l, h_: (hv[:, k, l:h_], hT),
                   [(lo, hi)], K.ones)
    cx.release(m0)
    for io in ios:
        if "att" in which:
            cx.S.tag = "M/att"
            mixer_attention(cx, K, io, hT, hv)
        if "gdn" in which:
            cx.S.tag = "M/gdn"
            mixer_gdn(cx, K, io, hT, hv)
        if "rwkv" in which:
            cx.S.tag = "M/rwkv"
            mixer_rwkv(cx, K, io, hT, hv, layer1)
    cx.release(base_)


M_INPUTS = [("xT", (D, SEQ)), ("g_attn", (128, 16)), ("cst", (128, CST_N)), ("rope", (32, 2 * SEQ)),
            ("wat", (18, 128, 16, 128)), ("wsw", (12, 128, 16, 32)),
            ("wc", (16, 128, 16, 128)), ("wba", (128, 16, 8)), ("gconv", (128, 48)), ("gpar", (128, 136)),
            ("wa", (12, 128, 16, 128)), ("wal", (128, 16, 288)), ("rpar", (128, 44)), ("rlmu", (128, 4)), ("rbc", (128, 1024)),
            ("rw2a2", (128, 512)), ("rg2", (128, 1024))]
M_INPUTS_L1 = [("rv1", (128, 16, 32)), ("rv2", (32, 512)), ("vf_in", (512, SEQ))]
M_OUTPUTS = [("y_fm", (768, SEQ)), ("yc_tm", (SEQ, 512)), ("ya_tm", (SEQ, 512)), ("v_out", (512, SEQ))]


def prep_M_weights(inp, l, s):
    A_IN, B_IN, C_IN = 3360, 4608, 4112
    W = inp["w_in"][l]
    w = {}
    w["g_attn"] = vec_pk(inp["attn_norm"][l])
    w["cst"] = make_consts()
    w["rope"] = make_rope()

    def colblk(cols):
        sub = W[:, cols]
        n = sub.shape[1]
        return np.ascontiguousarray(sub.reshape(16, 128, n // 128, 128).transpose(2, 1, 0, 3))

    def colsmall(cols):
        sub = W[:, cols]
        return np.ascontiguousarray(sub.reshape(16, 128, len(cols)).transpose(1, 0, 2))
    b0 = A_IN
    cols = []
    cols_sw = []
    for hl in range(2):
        hi = 2 * s + hl
        for g in range(3):
            for which in range(3):
                c0 = b0 + which * 1536 + g * 512 + hi * 128
                cols += list(range(c0, c0 + 128))
                if which < 2:
                    cols_sw += list(range(c0 + 16, c0 + 32)) + list(range(c0, c0 + 16))
    w["wat"] = colblk(np.array(cols))
    sw = W[:, np.array(cols_sw)]
    w["wsw"] = np.ascontiguousarray(sw.reshape(16, 128, 12, 32).transpose(2, 1, 0, 3))
    c0 = A_IN + B_IN
    cols = []
    for which in range(3):
        for hh in range(4):
            h = 4 * s + hh
            cols += list(range(c0 + which * 1024 + h * 128, c0 + which * 1024 + (h + 1) * 128))
    for hh in range(4):
        h = 4 * s + hh
        cols += list(range(c0 + 3088 + h * 128, c0 + 3088 + (h + 1) * 128))
    w["wc"] = colblk(np.array(cols))
    w["wba"] = colsmall(np.array([c0 + 3072 + 4 * s + i for i in range(4)] + [c0 + 3080 + 4 * s + i for i in range(4)]))
    gc = inp["gdn_conv"][l]
    gcs = np.zeros((128, 12, 4), np.float32)
    for which in range(3):
        for hh in range(4):
            h = 4 * s + hh
            gcs[:, which * 4 + hh, :] = gc[:, which * 1024 + h * 128: which * 1024 + (h + 1) * 128].T
    w["gconv"] = gcs.reshape(128, 48)
    gp = np.zeros((128, 136), np.float32)
    gp[:, 0:4] = inp["gdn_A_log"][l][4 * s:4 * s + 4][None, :]
    gp[:, 4:8] = inp["gdn_dt_bias"][l][4 * s:4 * s + 4][None, :]
    gp[:, 8:136] = inp["gdn_norm"][l][None, :]
    w["gpar"] = gp
    ch0 = 512 * s
    cols = []
    for which in range(3):
        cols += list(range(which * 1024 + ch0, which * 1024 + ch0 + 512))
    w["wa"] = colblk(np.array(cols))
    w["wal"] = colsmall(np.arange(3072, 3360))
    mu = inp["rwkv_mu"][l]
    rp = np.zeros((128, 44), np.float32)
    for which in range(3):
        for ct in range(4):
            rp[:, which * 4 + ct] = mu[which * 1024 + ch0 + ct * 128: which * 1024 + ch0 + (ct + 1) * 128]
    for ct in range(4):
        sl = slice(ch0 + ct * 128, ch0 + (ct + 1) * 128)
        rp[:, 12 + ct * 8 + 0] = inp["rwkv_w0"][l][sl]
        rp[:, 12 + ct * 8 + 1] = inp["rwkv_a0"][l][sl]
        rp[:, 12 + ct * 8 + 2] = inp["rwkv_k_k"][l][sl]
        rp[:, 12 + ct * 8 + 3] = inp["rwkv_k_a"][l][sl]
        rp[:, 12 + ct * 8 + 4] = inp["rwkv_r_k"][l].reshape(-1)[sl]
        if l > 0:
            rp[:, 12 + ct * 8 + 5] = inp["rwkv_v0"][l - 1][sl]
    w["rpar"] = rp
    rl = np.zeros((128, 4), np.float32)
    rl[0:64, 0] = mu[3072:3136]
    rl[64:128, 0] = mu[3136:3200]
    rl[0:128, 1] = mu[3200:3328]
    rl[0:32, 2] = mu[3328:3360]
    w["rlmu"] = rl
    rb = np.zeros((128, 1024), np.float32)
    rb[:, 0:512] = inp["rwkv_ln_w"][l][ch0:ch0 + 512][None, :]
    rb[:, 512:1024] = inp["rwkv_ln_b"][l][ch0:ch0 + 512][None, :]
    w["rbc"] = rb
    w["rw2a2"] = np.ascontiguousarray(np.concatenate([inp["rwkv_w2"][l][:, ch0:ch0 + 512], inp["rwkv_a2"][l][:, ch0:ch0 + 512]], axis=0))
    g2 = inp["rwkv_g2"][l][:, ch0:ch0 + 512]
    rg = np.zeros((128, 1024), np.float32)
    rg[:, 0:512] = g2[0:128]
    rg[0:32, 512:1024] = g2[128:160]
    w["rg2"] = rg
    if l > 0:
        v1 = inp["rwkv_v1"][l - 1]
        w["rv1"] = np.ascontiguousarray(v1.reshape(16, 128, 32).transpose(1, 0, 2))
        w["rv2"] = np.ascontiguousarray(inp["rwkv_v2"][l - 1][:, ch0:ch0 + 512])
    return w


def tile_w(W, col0, ncols, cw=128):
    K = W.shape[0]
    sub = W[:, col0:col0 + ncols]
    return np.ascontiguousarray(sub.reshape(K // 128, 128, ncols // cw, cw).transpose(2, 1, 0, 3))


def vec_pk(v):
    return np.ascontiguousarray(v.reshape(-1, 128).T)


def prep_T_weights(inp, l):
    A_IN, B_IN, C_IN = 3360, 4608, 4112
    g0 = A_IN + B_IN + C_IN
    w = {}
    w["wg"] = tile_w(inp["w_in"][l], g0, 6144)
    pw = np.concatenate([inp["proj_a"][l], inp["proj_b"][l], inp["proj_c"][l]], axis=0)
    w["wp"] = tile_w(pw, 0, 2048)
    w["wo"] = tile_w(inp["w_out"][l], 0, 2048)
    w["wup"] = tile_w(inp["ffn_up"][l], 0, 11264)
    w["wdn"] = tile_w(inp["ffn_down"][l], 0, 2048)
    w["g_attn"] = vec_pk(inp["attn_norm"][l])
    w["g_ffn"] = vec_pk(inp["ffn_norm"][l])
    w["g_next"] = vec_pk(inp["attn_norm"][l + 1] if l + 1 < inp["attn_norm"].shape[0] else inp["final_norm"])
    fc = inp["ffn_conv"][l]
    w["fconv"] = np.ascontiguousarray(fc.T.reshape(88, 128, 3).transpose(1, 0, 2).reshape(128, 88 * 3))
    return w


class DD:
    def __init__(self, name=""):
        self.d = Dep(name)


M_SHARED = ("xT", "cst", "rope")
T_INPUTS = [("g_attn", (128, 16)), ("g_ffn", (128, 16)), ("g_next", (128, 16)), ("fconv", (128, 88 * 3)),
            ("wg", (48, 128, 16, 128)), ("wp", (16, 128, 20, 128)), ("wo", (16, 128, 16, 128)),
            ("wup", (88, 128, 16, 128)), ("wdn", (16, 128, 44, 128))]


def build_fused(depth=2, do_T=True):
    nc = bass.Bass("TRN2", target_bir_lowering=False)
    ext = {}

    def din(name, shape, dt=F32):
        ext[name] = nc.dram_tensor(name, list(shape), dt, kind="ExternalInput").ap()
        return ext[name]
    xT = din("xT", (D, SEQ))
    cst = din("cst", (128, CST_N))
    rope = din("rope", (32, 2 * SEQ))
    for l in range(depth):
        for s in range(2):
            for name, shape in M_INPUTS + (M_INPUTS_L1[:2] if l == 1 else []):
                if name not in M_SHARED:
                    din("%s_%d%d" % (name, l, s), shape)
        for name, shape in T_INPUTS:
            din("T%s_%d" % (name, l), shape)
    out = nc.dram_tensor("outT", [D, SEQ], F32, kind="ExternalOutput").ap()
    ybuf = [nc.dram_tensor("ybuf%d" % l, [2560, SEQ], BF16).ap() for l in range(depth)]
    x1buf = nc.dram_tensor("x1buf", [D, SEQ], F32).ap()
    vbuf = [nc.dram_tensor("vbuf%d" % s, [512, SEQ], BF16).ap() for s in range(2)]
    with ExitStack() as es:
        cx = Ctx(nc, es)
        cx.wpiece = 1024
        x1dep = DD("x1")
        vdep = [DD("v0"), DD("v1")]
        for l in range(depth):
            last = (l == depth - 1)
            xsrc = xT if l == 0 else x1buf
            xdeps = [] if l == 0 else [x1dep]
            ydep = {"a": DD("ya"), "b": DD("yb"), "c": DD("yc")}
            ios = []
            for s in range(2):
                io = {}
                for name, shape in M_INPUTS + (M_INPUTS_L1[:2] if l == 1 else []):
                    if name not in M_SHARED:
                        io[name] = ext["%s_%d%d" % (name, l, s)]
                io.update(xT=xsrc, x_deps=xdeps, cst=cst, rope=rope, ybuf=ybuf[l], ydep=ydep, s=s,
                          v_out=vbuf[s], v_dep=vdep[s], vf_in=vbuf[s], vf_deps=[vdep[s]], write_v=(l == 0))
                ios.append(io)
            phase_M(cx, ios, l == 1)
            if do_T:
                for half in range(2):
                    io = {name: ext["T%s_%d" % (name, l)] for name, shape in T_INPUTS}
                    io.update(xT=xsrc, x_deps=xdeps, ybuf=ybuf[l], y_deps=list(ydep.values()),
                              outT=(out if last else x1buf), out_deps=([] if last else [x1dep]))
                    phase_T(cx, io, last, half)
        if cx.S.taglog is not None:
            import json
            json.dump(cx.S.taglog, open(os.environ["BASS_TAGLOG"], "w"))
        cx.S.emit()
        print("fused ops:", cx.S.nops, "insts:", cx.S.ninst, "arena peak KB:", cx.ar.peak * 4 / 1024,
              "eng counts:", cx.S.cnt, "max dma sem:", max(cx.S.dma_cnt))
    return nc


_NC_CACHE = {}


def prep_core_inputs(inp, depth=2):
    m = {"cst": make_consts(), "rope": make_rope()}
    for l in range(depth):
        for s in range(2):
            w = prep_M_weights(inp, l, s)
            for k, v in w.items():
                if k not in M_SHARED:
                    m["%s_%d%d" % (k, l, s)] = v
        w = prep_T_weights(inp, l)
        for k, v in w.items():
            m["T%s_%d" % (k, l)] = v
    return m


def kernel(**inputs):
    inp = {k: np.asarray(v) for k, v in inputs.items()}
    x = inp["x"].astype(np.float32, copy=False)
    B, S, Dm = x.shape
    depth = inp["w_in"].shape[0]
    if "nc" not in _NC_CACHE:
        _NC_CACHE["nc"] = build_fused(depth)
    nc = _NC_CACHE["nc"]
    shared = prep_core_inputs(inp, depth)
    maps = []
    for c in range(8):
        m = dict(shared)
        m["xT"] = np.ascontiguousarray(x[c // 2].T)
        maps.append(m)
    res = run_bass_kernel_spmd(nc, maps, core_ids=list(range(8))).results
    outs = [np.asarray(res[2 * b]["outT"]).T for b in range(B)]
    return np.ascontiguousarray(np.stack(outs, axis=0)).astype(np.float32)
```

```python
import os
import numpy as np
import concourse.bass as bass
import concourse.mybir as mybir
from concourse.bass_utils import run_bass_kernel_spmd
from contextlib import ExitStack

F32 = mybir.dt.float32
BF16 = mybir.dt.bfloat16
ALU = mybir.AluOpType
AF = mybir.ActivationFunctionType
AX = mybir.AxisListType

D = 2048
KC = 16
SEQ = 2048
NTOK_T = 1026
TT = [(0, 2), (2, 514), (514, 1026)]
D_FF = 5632
NFB = 44
EPS = 1e-6

SAME_ENG_SYNC = True


class Dep:
    __slots__ = ("w", "r", "name", "excl")

    def __init__(self, name="", excl=False):
        self.w = None
        self.r = {}
        self.name = name
        self.excl = excl


class Sched:
    ENGS = ["pe", "act", "dve", "pool", "sp"]

    def __init__(self, nc, n_dma=24):
        self.nc = nc
        self.q = {e: [] for e in self.ENGS}
        self.cnt = {e: 0 for e in self.ENGS}
        self.n_dma = n_dma
        self.dma_cnt = [0] * n_dma
        self.dma_rr = 0
        self.waited = {e: {} for e in self.ENGS}
        self.nops = 0
        self.ninst = 0
        self.tag = ""
        self.cc_cnt = 0
        self.taglog = [] if os.environ.get("BASS_TAGLOG") else None

    def op(self, eng, calls, reads=(), writes=(), dma=False):
        deps = {}

        def add(tok):
            if tok is None:
                return
            k, v = tok
            if deps.get(k, 0) < v:
                deps[k] = v

        for r in reads:
            add(r.w)
            if r.excl:
                for k, v in r.r.items():
                    add((k, v))
        for w in writes:
            add(w.w)
            for k, v in w.r.items():
                add((k, v))
        if dma == "cc":
            self.cc_cnt += 1
            tok = (("cc", 0), self.cc_cnt)
        elif dma:
            i = self.dma_rr
            self.dma_rr = (self.dma_rr + 1) % self.n_dma
            add((("dma", i), self.dma_cnt[i]))
            self.dma_cnt[i] += 16
            tok = (("dma", i), self.dma_cnt[i])
        else:
            self.cnt[eng] += 1
            tok = (eng, self.cnt[eng])
        waits = []
        wd = self.waited[eng]
        for k, v in deps.items():
            if v <= 0:
                continue
            if k == eng and (eng == "pe" or not SAME_ENG_SYNC):
                continue
            if wd.get(k, 0) >= v:
                continue
            wd[k] = v
            waits.append((k, v))
        for r in reads:
            if r.r.get(tok[0], 0) < tok[1]:
                r.r[tok[0]] = tok[1]
        for w in writes:
            w.w = tok
            w.r = {}
        self.q[eng].append((waits, calls, tok))
        if self.taglog is not None:
            self.taglog.append((eng, self.tag, len(calls)))
        self.nops += 1
        self.ninst += len(calls)
        return tok

    def emit(self, final_wait_eng="sp"):
        nc = self.nc
        with ExitStack() as es:
            sems = {}
            for e in self.ENGS:
                sems[e] = es.enter_context(nc.semaphore("s_" + e))
            for i in range(self.n_dma):
                sems[("dma", i)] = es.enter_context(nc.semaphore("s_dma%d" % i))
            sems[("cc", 0)] = es.enter_context(nc.semaphore("s_cc"))
            block = es.enter_context(nc.Block())
            finals = [(("dma", i), self.dma_cnt[i]) for i in range(self.n_dma) if self.dma_cnt[i] > 0]

            def run(engname, engine):
                for waits, calls, tok in self.q[engname]:
                    for k, v in waits:
                        engine.wait_ge(sems[k], v)
                    ins = None
                    for name, kw in calls:
                        ins = getattr(engine, name)(**kw)
                    ins.then_inc(sems[tok[0]], 16 if (isinstance(tok[0], tuple) and tok[0][0] == "dma") else 1)
                if engname == final_wait_eng:
                    for k, v in finals:
                        engine.wait_ge(sems[k], v)

            @block.tensor
            def _(e):
                run("pe", e)

            @block.scalar
            def _(e):
                run("act", e)

            @block.vector
            def _(e):
                run("dve", e)

            @block.gpsimd
            def _(e):
                run("pool", e)

            @block.sync
            def _(e):
                run("sp", e)


class Arena:
    def __init__(self, ap_f32, words):
        self.base = ap_f32
        self.words = words
        self.top = 0
        self.peak = 0
        self.live = []
        self.dead = []

    def mark(self):
        return self.top

    def release(self, m):
        keep = []
        for ent in self.live:
            if ent[0] >= m:
                self.dead.append(ent)
            else:
                keep.append(ent)
        self.live = keep
        self.top = m

    def alloc(self, nelem, dtype=F32, parts=128, name=""):
        w = nelem if dtype == F32 else (nelem + 1) // 2
        w = (w + 7) // 8 * 8
        assert self.top + w <= self.words, ("SBUF arena overflow", name, self.top, w, self.words)
        lo, hi = self.top, self.top + w
        a = self.base[0:parts, lo:hi]
        self.top = hi
        self.peak = max(self.peak, self.top)
        d = Dep(name)
        nd = []
        for (dlo, dhi, dd) in self.dead:
            if dlo < hi and lo < dhi:
                toks = list(dd.r.items())
                if dd.w is not None:
                    toks.append(dd.w)
                for k, v in toks:
                    if d.r.get(k, 0) < v:
                        d.r[k] = v
                if lo <= dlo and dhi <= hi:
                    continue
            nd.append((dlo, dhi, dd))
        self.dead = nd
        self.live.append((lo, hi, d))
        if dtype != F32:
            a = a.bitcast(dtype)
        return a[:, 0:nelem], d


class T:
    def __init__(self, ap, d):
        self.ap = ap
        self.d = d


class Ctx:
    ARENA_WORDS = 51 * 1024 + 512

    def __init__(self, nc, es):
        self.nc = nc
        self.S = Sched(nc)
        big = es.enter_context(nc.sbuf_tensor("arena", [128, self.ARENA_WORDS], F32))
        self.ar = Arena(big, self.ARENA_WORDS)
        self.psb = []
        for i in range(8):
            p = es.enter_context(nc.psum_tensor("psb%d" % i, [128, 512], F32))
            self.psb.append(T(p, Dep("ps%d" % i, excl=True)))
        self.ps_rr = 0
        self.rings = {}

    def ps(self, pin=False):
        pinned = getattr(self, "pinned", None)
        if pinned is None:
            pinned = self.pinned = set()
        while self.ps_rr in pinned:
            self.ps_rr = (self.ps_rr + 1) % 8
        i = self.ps_rr
        t = self.psb[i]
        self.ps_rr = (self.ps_rr + 1) % 8
        if pin:
            pinned.add(i)
        return t

    def unpin_all(self):
        self.pinned = set()

    def tile(self, nelem, dtype=F32, name="", parts=128):
        ap, d = self.ar.alloc(nelem, dtype, parts, name)
        return T(ap, d)

    def ring(self, key, n, nelem, dtype=F32):
        if key not in self.rings:
            off = self.ar.top
            self.rings[key] = [[self.tile(nelem, dtype, "%s%d" % (key, i)) for i in range(n)], 0, off]
        r = self.rings[key]
        t = r[0][r[1] % len(r[0])]
        r[1] += 1
        return t

    def mark(self):
        return self.ar.mark()

    def release(self, m):
        for k in [k for k, r in self.rings.items() if r[2] >= m]:
            del self.rings[k]
        self.ar.release(m)

    def I(self, eng, name, reads, writes, **kw):
        self.S.op(eng, [(name, kw)], [t.d for t in reads], [t.d for t in writes])

    def dma(self, out, in_, reads=(), writes=(), eng="sp"):
        self.S.op(eng, [("dma_start", dict(out=out, in_=in_))], [t.d for t in reads], [t.d for t in writes], dma=True)

    def act(self, out, in_, func, reads, writes, **kw):
        self.I("act", "activation", reads, writes, out=out, in_=in_, func=func, **kw)

    def tt(self, eng, out, in0, in1, op, reads, writes):
        self.I(eng, "tensor_tensor", reads, writes, out=out, in0=in0, in1=in1, op=op)

    def ts(self, eng, out, in0, s1, op0, reads, writes, s2=None, op1=None):
        kw = dict(out=out, in0=in0, scalar1=s1, scalar2=s2, op0=op0)
        if op1 is not None:
            kw["op1"] = op1
        self.I(eng, "tensor_scalar", reads, writes, **kw)

    def stt(self, out, in0, scalar, in1, op0, op1, reads, writes):
        self.I("dve", "scalar_tensor_tensor", reads, writes, out=out, in0=in0, scalar=scalar, in1=in1, op0=op0, op1=op1)

    def copy(self, eng, out, in_, reads, writes):
        if eng == "act":
            self.act(out, in_, AF.Copy, reads, writes)
        else:
            self.I(eng, "tensor_copy", reads, writes, out=out, in_=in_)

    def mm(self, mms, reads, writes):
        self.S.op("pe", [("matmul", m) for m in mms], [t.d for t in reads], [t.d for t in writes])

    def tr(self, out, in_, ident, reads, writes):
        self.S.op("pe", [("transpose", dict(out=out, in_=in_, identity=ident))], [t.d for t in reads], [t.d for t in writes])


def load_w(cx, dram_ap, kc, cw, key="w"):
    n = kc * cw
    wp = getattr(cx, "wpiece", 2048)
    assert n <= wp
    st = cx.ring(key + "_st", 2, wp, F32)
    wb = cx.ring(key + "_wb", 3, wp, BF16)
    cx.dma(st.ap[:, 0:n], dram_ap.rearrange("p k c -> p (k c)"), [], [st])
    cx.copy(getattr(cx, "cast_eng", "pool"), wb.ap[:, 0:n], st.ap[:, 0:n], [st], [wb])
    return wb


def load_w_multi(cx, dram_ap, kc, cw, key="w"):
    per = max(1, getattr(cx, "wpiece", 2048) // cw)
    out = []
    k0 = 0
    while k0 < kc:
        kn = min(per, kc - k0)
        out.append((load_w(cx, dram_ap[:, k0:k0 + kn, :], kn, cw, key), k0, kn))
        k0 += kn
    return out


def load_w_big(cx, dram_ap, kc, cw, name):
    wp = getattr(cx, "wpiece", 2048)
    big = cx.tile(kc * cw, BF16, name)
    per = max(1, wp // cw)
    k0 = 0
    while k0 < kc:
        kn = min(per, kc - k0)
        n = kn * cw
        st = cx.ring("w_st", 2, wp, F32)
        cx.dma(st.ap[:, 0:n], dram_ap[:, k0:k0 + kn, :].rearrange("p k c -> p (k c)"), [], [st])
        cx.copy("pool", big.ap[:, k0 * cw:(k0 + kn) * cw], st.ap[:, 0:n], [st], [big])
        k0 += kn
    return [(big, 0, kc)]


def wslice(pieces, k, cw, m0=0, m1=None):
    m1 = cw if m1 is None else m1
    for wb, k0, kn in pieces:
        if k0 <= k < k0 + kn:
            return wb.ap[:, (k - k0) * cw + m0:(k - k0) * cw + m1]
    raise KeyError(k)


def gemm_block(cx, pieces, cw, act_fn, act_tiles, lo, hi, ks=None, m0=0, m1=None, po=0):
    ps = cx.ps()
    n = hi - lo
    m1 = cw if m1 is None else m1
    if ks is None:
        ks = []
        for wb, k0, kn in pieces:
            ks += list(range(k0, k0 + kn))
    mms = []
    for i, k in enumerate(ks):
        mms.append(dict(out=ps.ap[po:po + m1 - m0, 0:n], lhsT=wslice(pieces, k, cw, m0, m1), rhs=act_fn(k, lo, hi),
                        start=(i == 0), stop=(i == len(ks) - 1)))
    cx.mm(mms, [p[0] for p in pieces] + list(act_tiles), [ps])
    return ps


def rmsnorm_fm(cx, srcs_fn, g_t, out_fn, tiles, ones_bf, nk=KC):
    for (lo, hi) in tiles:
        n = hi - lo
        ps = cx.ps()
        srcs = [srcs_fn(k, lo, hi) for k in range(nk)]
        for k in range(nk):
            sqt = cx.ring("rn_sq", 3, 512, BF16)
            sap, st = srcs[k]
            cx.act(sqt.ap[:, 0:n], sap, AF.Square, [st], [sqt])
            cx.mm([dict(out=ps.ap[:, 0:n], lhsT=ones_bf.ap, rhs=sqt.ap[:, 0:n], start=(k == 0), stop=(k == nk - 1))],
                  [sqt, ones_bf], [ps])
        rinv = cx.ring("rn_rinv", 2, 512, F32)
        cx.act(rinv.ap[:, 0:n], ps.ap[:, 0:n], AF.Ln, [ps], [rinv], scale=1.0 / (nk * 128), bias=EPS)
        cx.act(rinv.ap[:, 0:n], rinv.ap[:, 0:n], AF.Exp, [rinv], [rinv], scale=-0.5)
        for k in range(nk):
            sap, st = srcs[k]
            oap, ot = out_fn(k, lo, hi)
            cx.stt(oap, sap, g_t.ap[:, k:k + 1], rinv.ap[:, 0:n], ALU.mult, ALU.mult, [st, rinv, g_t], [ot])


def load_small(cx, dram_ap, nelem, name, parts=128):
    t = cx.tile(nelem, F32, name, parts)
    cx.dma(t.ap, dram_ap, [], [t])
    return t


def phase_T(cx, io, final, half):
    NT = NTOK_T
    cx.cast_eng = "act"
    base = cx.mark()
    ones_bf = cx.tile(128, BF16, "ones")
    cx.I("dve", "memset", [], [ones_bf], ap=ones_bf.ap, constant=1.0)
    g1 = load_small(cx, io["g_attn"], 16, "g1")
    g2 = load_small(cx, io["g_ffn"], 16, "g2")
    g3 = load_small(cx, io["g_next"], 16, "g3")
    convw = load_small(cx, io["fconv"], 88 * 3, "convw")
    hm = load_small(cx, io["hmask"], 2, "hmask")
    mg_t = cx.tile(KC * NT, BF16, "merged")
    h_t = cx.tile(KC * NT, BF16, "h")
    mv = mg_t.ap.rearrange("p (k t) -> p k t", k=KC)
    hv = h_t.ap.rearrange("p (k t) -> p k t", k=KC)
    M0 = cx.mark()
    cx.S.tag = "T/A1"
    y_t = cx.tile(20 * NT, BF16, "y")
    yv = y_t.ap.rearrange("p (k t) -> p k t", k=20)
    t0 = 0
    xdeps = io.get("x_deps", [])
    ydeps = io.get("y_deps", [])
    if io["x_mode"] == "input":
        xTT = io["xTT"].rearrange("(k p) t -> p k t", p=128)

        def load_x(dst_ap, dst_t, k, lo, hi):
            cx.dma(dst_ap, xTT[:, k, lo:hi], [], [dst_t])
    else:
        xloc_read = io["xloc_read"]
        xhalo = io["xhalo"]

        def load_x(dst_ap, dst_t, k, lo, hi):
            a = lo
            if lo < 2:
                cx.dma(dst_ap[:, 0:2 - lo], xhalo(k)[:, lo:2], xdeps, [dst_t])
                cx.ts("dve", dst_ap[:, 0:2 - lo], dst_ap[:, 0:2 - lo], hm.ap[:, 1:2], ALU.mult, [dst_t, hm], [dst_t])
                a = 2
            if hi > a:
                for src, off, n in xloc_read(k, a - 2, hi - 2):
                    cx.dma(dst_ap[:, a - lo + off:a - lo + off + n], src, xdeps, [dst_t])

    yread = io["yread"]
    kmap = [(0, b) for b in range(4)] + [(1, b) for b in range(4)] + [(0, 4), (0, 5), (1, 4), (1, 5)] + \
           [(0, b) for b in range(6, 10)] + [(1, b) for b in range(6, 10)]
    M1 = cx.mark()
    for k in range(20):
        r_, b_ = kmap[k]
        ta = cx.ring("yh0", 2, 1024, BF16)
        tb_ = cx.ring("yh1", 2, 1024, BF16)
        for q in range(2):
            cx.dma(ta.ap[:, q * 512:(q + 1) * 512], yread(r_, b_, q), ydeps, [ta])
            cx.dma(tb_.ap[:, q * 512:(q + 1) * 512], yread(r_, b_, 2 + q), ydeps, [tb_])
        cx.ts("pool", yv[:, k, 0:2], ta.ap[:, 1022:1024], hm.ap[:, 1:2], ALU.mult, [ta, hm], [y_t])
        cx.ts("pool", tb_.ap, tb_.ap, hm.ap[:, 1:2], ALU.mult, [tb_, hm], [tb_])
        cx.stt(yv[:, k, 2:NT], ta.ap, hm.ap[:, 0:1], tb_.ap, ALU.mult, ALU.add, [ta, tb_, hm], [y_t])
    for (lo, hi) in TT:
        n = hi - lo
        xt = cx.ring("xtile", 1, KC * 512, F32)
        xtv = xt.ap.rearrange("p (k t) -> p k t", k=KC)
        for k in range(KC):
            load_x(xtv[:, k, 0:n], xt, k, lo, hi)
        rmsnorm_fm(cx, lambda k, l, h_: (xtv[:, k, 0:h_ - l], xt), g1, lambda k, l, h_: (hv[:, k, l:h_], h_t),
                   [(lo, hi)], ones_bf)
    cx.release(M1)
    wg = io["wg"]
    wp = io["wp"]
    ksl = [(0, 8), (8, 12), (12, 20)]
    for j in range(16):
        gts = []
        for br in range(3):
            pcs = load_w_multi(cx, wg[br * 16 + j], KC, 128)
            gt = cx.ring("gate", 4, NT, BF16)
            for (lo, hi) in TT:
                ps = gemm_block(cx, pcs, 128, lambda k, l, h_: hv[:, k, l:h_], [h_t], lo, hi)
                cx.act(gt.ap[:, lo:hi], ps.ap[:, 0:hi - lo], AF.Sigmoid, [ps], [gt])
            gts.append(gt)
        pw = load_w_multi(cx, wp[j], 20, 128)
        for (lo, hi) in TT:
            n = hi - lo
            tmp = cx.ring("mtmp", 2, 512, F32)
            for br in range(3):
                k0, k1 = ksl[br]
                ps = gemm_block(cx, pw, 128, lambda k, l, h_: yv[:, k, l:h_], [y_t], lo, hi, ks=list(range(k0, k1)))
                gt = gts[br]
                if br == 0:
                    cx.tt("dve", tmp.ap[:, 0:n], ps.ap[:, 0:n], gt.ap[:, lo:hi], ALU.mult, [ps, gt], [tmp])
                else:
                    t2 = cx.ring("mtmp2", 2, 512, F32)
                    cx.tt("dve", t2.ap[:, 0:n], ps.ap[:, 0:n], gt.ap[:, lo:hi], ALU.mult, [ps, gt], [t2])
                    if br == 1:
                        cx.tt("pool", tmp.ap[:, 0:n], tmp.ap[:, 0:n], t2.ap[:, 0:n], ALU.add, [tmp, t2], [tmp])
                    else:
                        cx.tt("pool", mv[:, j, lo:hi], tmp.ap[:, 0:n], t2.ap[:, 0:n], ALU.add, [tmp, t2], [mg_t])
    cx.release(M0)
    cx.S.tag = "T/A2"
    x_sb = cx.tile(KC * NT, F32, "x_sb")
    xv = x_sb.ap.rearrange("p (k t) -> p k t", k=KC)
    M2 = cx.mark()
    wo = io["wo"]
    for j in range(16):
        pcs = load_w_multi(cx, wo[j], KC, 128)
        xin = cx.ring("xin", 2, NT, F32)
        load_x(xin.ap, xin, j, 0, NT)
        for (lo, hi) in TT:
            ps = gemm_block(cx, pcs, 128, lambda k, l, h_: mv[:, k, l:h_], [mg_t], lo, hi)
            cx.tt("dve", xv[:, j, lo:hi], ps.ap[:, 0:hi - lo], xin.ap[:, lo:hi], ALU.add, [ps, xin], [x_sb])
    cx.release(M2)
    cx.S.tag = "T/F"
    rmsnorm_fm(cx, lambda k, l, h_: (xv[:, k, l:h_], x_sb), g2, lambda k, l, h_: (hv[:, k, l:h_], h_t), TT, ones_bf)
    GRP = 11
    a_t = T(mg_t.ap, mg_t.d)
    av = a_t.ap[:, 0:GRP * 1024].rearrange("p (k t) -> p k t", k=GRP)
    wup = io["wup"]
    wdn = io["wdn"]
    cwv = convw.ap.rearrange("p (b t) -> p b t", t=3)
    for g in range(NFB // GRP):
        for jj in range(GRP):
            jb = g * GRP + jj
            us = []
            for part in range(2):
                blk = part * NFB + jb
                pcs = load_w_multi(cx, wup[blk], KC, 128)
                u = cx.ring("u_f32", 2, NT, F32)
                for (lo, hi) in TT:
                    ps = gemm_block(cx, pcs, 128, lambda k, l, h_: hv[:, k, l:h_], [h_t], lo, hi)
                    cx.copy("act", u.ap[:, lo:hi], ps.ap[:, 0:hi - lo], [ps], [u])
                c = cx.ring("c_f32", 2, 1024, F32)
                cx.ts("dve", c.ap, u.ap[:, 2:NT], cwv[:, blk, 2:3], ALU.mult, [u, convw], [c])
                cx.stt(c.ap, u.ap[:, 1:NT - 1], cwv[:, blk, 1:2], c.ap, ALU.mult, ALU.add, [u, convw, c], [c])
                cx.stt(c.ap, u.ap[:, 0:NT - 2], cwv[:, blk, 0:1], c.ap, ALU.mult, ALU.add, [u, convw, c], [c])
                us.append(c)
            sg = cx.ring("silu", 2, 1024, F32)
            cx.act(sg.ap, us[0].ap, AF.Silu, [us[0]], [sg])
            cx.tt("pool", av[:, jj, :], sg.ap, us[1].ap, ALU.mult, [sg, us[1]], [a_t])
        for j in range(16):
            pcs = load_w_multi(cx, wdn[j][:, g * GRP:(g + 1) * GRP, :], GRP, 128)
            for ti in range(2):
                lo, hi = ti * 512, ti * 512 + 512
                ps = gemm_block(cx, pcs, 128, lambda k, l, h_: av[:, k, l:h_], [a_t], lo, hi)
                cx.tt("dve", xv[:, j, 2 + lo:2 + hi], ps.ap[:, 0:512], xv[:, j, 2 + lo:2 + hi], ALU.add, [ps, x_sb], [x_sb])
    cx.release(M2)
    outT = io["outT"].rearrange("(k p) t -> p k t", p=128) if final else None
    odeps = io.get("out_deps", [])
    if not final:
        for k in range(KC):
            io["xwrite"](cx, k, xv[:, k, 2:NT], [x_sb])
    else:
        o_t = T(h_t.ap.bitcast(F32)[:, 0:KC * 512], h_t.d)
        ov = o_t.ap.rearrange("p (k t) -> p k t", k=KC)
        for ti in range(2):
            lo, hi = 2 + ti * 512, 2 + ti * 512 + 512
            rmsnorm_fm(cx, lambda k, l, h_: (xv[:, k, l:h_], x_sb), g3, lambda k, l, h_: (ov[:, k, 0:512], o_t),
                       [(lo, hi)], ones_bf)
            for k in range(KC):
                cx.dma(outT[:, k, t0 + lo - 2:t0 + lo - 2 + 512], ov[:, k, :], [o_t], odeps)
    cx.release(base)


import os
ATT_G = os.environ.get("ATT_G", "012")
ATT_STOP = int(os.environ.get("ATT_STOP", "99"))
ATT_SUB = int(os.environ.get("ATT_SUB", "3"))
RW_DBG = os.environ.get("RW_DBG", "")
ATT_V = os.environ.get("ATT_V", "ab")
NCH = 16
C_ID, C_MUI, C_MUS, C_MLI, C_BO, C_BS, C_S127, C_MLS = 0, 128, 256, 384, 512, 640, 642, 770
C_BD8, C_O16, C_O32, C_O64, C_O128 = 898, 1026, 1154, 1282, 1410
CST_N = 1538


def make_consts():
    c = np.zeros((128, CST_N), np.float32)
    j = np.arange(128)[:, None]
    i = np.arange(128)[None, :]
    c[:, C_ID:C_ID + 128] = (i == j)
    c[:, C_MUI:C_MUI + 128] = (i >= j)
    c[:, C_MUS:C_MUS + 128] = (i > j)
    c[:, C_MLI:C_MLI + 128] = (j >= i)
    c[:, C_BO:C_BO + 128] = ((i // 64) == (j // 64))
    c[:, C_BS:C_BS + 2] = (np.arange(2)[None, :] == (j // 64))
    c[:, C_S127:C_S127 + 128] = (j == 127)
    c[:, C_MLS:C_MLS + 128] = (j > i)
    bd = lambda s_: (i // s_) == (j // s_)
    c[:, C_BD8:C_BD8 + 128] = bd(8)
    c[:, C_O16:C_O16 + 128] = bd(16) & ~bd(8)
    c[:, C_O32:C_O32 + 128] = bd(32) & ~bd(16)
    c[:, C_O64:C_O64 + 128] = bd(64) & ~bd(32)
    c[:, C_O128:C_O128 + 128] = ~bd(64)
    return c


def make_rope():
    half = 16
    inv = 500000.0 ** (-np.arange(half, dtype=np.float32) / half)
    ang = np.arange(SEQ, dtype=np.float32)[None, :] * inv[:, None]
    cos = np.cos(ang).astype(np.float32)
    sin = np.sin(ang).astype(np.float32)
    r = np.zeros((32, 2 * SEQ), np.float32)
    r[0:16, 0:SEQ] = cos
    r[16:32, 0:SEQ] = cos
    r[0:16, SEQ:] = -sin
    r[16:32, SEQ:] = sin
    return r


class Consts:
    pass


def setup_consts(cx, io):
    K = Consts()
    cf = load_small(cx, io["cst"], CST_N, "cst_f32")
    cb = cx.tile(CST_N, BF16, "cst_bf")
    cx.copy("dve", cb.ap, cf.ap, [cf], [cb])
    K.cf, K.cb = cf, cb
    K.ones = cx.tile(128, BF16, "ones")
    cx.I("dve", "memset", [], [K.ones], ap=K.ones.ap, constant=1.0)
    K.m4 = {}
    for nm_, off in (("id", C_ID), ("bd8", C_BD8), ("o16", C_O16), ("o32", C_O32), ("o64", C_O64), ("o128", C_O128)):
        t = cx.tile(512, BF16, "m4" + nm_)
        for q in range(4):
            cx.copy("pool", t.ap[:, q * 128:(q + 1) * 128], cb.ap[:, off:off + 128], [cb], [t])
        K.m4[nm_] = t
    return K


def dplr_head(cx, K, nm, dk, dv, pb, Kg, Bg, Ag, Rg, gtiles, Wst, Win, wtiles, As, Rs, egam, Kn, Bn, V, ntiles, PC, pctiles,
              scale_all, out_cb, defer=False):
    m0 = cx.mark()
    tag0 = cx.S.tag
    cx.S.tag = tag0 + "/dplr_prep"
    idb = K.cb.ap[:, C_ID:C_ID + 128]
    neg = Bg is None
    AakT = cx.tile(NCH * 128, BF16, nm + "aak")
    ArkT = cx.tile(NCH * 128, BF16, nm + "ark")
    ArbT = cx.tile(NCH * 128, BF16, nm + "arb")
    TT_ = cx.tile(NCH * 128, BF16, nm + "TT")
    sl = lambda t, c: t.ap[:, c * 128:(c + 1) * 128]
    m1 = cx.mark()
    HC = 8
    tl_ = [cx.tile(HC * 128, BF16, nm + "iv%d" % i) for i in range(12)]
    U, N, Ua, Na, Nb, Ub_, Nc, Uc, P, Q, Z1b, Z2b = tl_
    hs = lambda t, q4: t.ap[:, q4 * 512:(q4 + 1) * 512]
    h1 = lambda t, cl: t.ap[:, cl * 128:(cl + 1) * 128]

    def mm4(dst_ps, lhs_t, rhs_t, q4):
        cx.mm([dict(out=dst_ps.ap[:, q * 128:(q + 1) * 128], lhsT=h1(lhs_t, q4 * 4 + q), rhs=h1(rhs_t, q4 * 4 + q), start=True, stop=True)
               for q in range(4)], [lhs_t, rhs_t], [dst_ps])

    for half in range(NCH // HC):
        for cl in range(HC):
            c = half * HC + cl
            ps = cx.ps()
            mms = [dict(out=ps.ap[:, 0:128], lhsT=Kg(c), rhs=Ag(c), start=True, stop=True),
                   dict(out=ps.ap[:, 128:256], lhsT=Kg(c), rhs=Rg(c), start=True, stop=True)]
            if not neg:
                mms += [dict(out=ps.ap[:, 256:384], lhsT=Bg(c), rhs=Ag(c), start=True, stop=True),
                        dict(out=ps.ap[:, 384:512], lhsT=Bg(c), rhs=Rg(c), start=True, stop=True)]
            cx.mm(mms, gtiles, [ps])
            cx.tt("dve", sl(AakT, c), ps.ap[:, 0:128], Wst(c), ALU.mult, [ps] + wtiles, [AakT])
            cx.tt("dve", sl(ArkT, c), ps.ap[:, 128:256], Win(c), ALU.mult, [ps] + wtiles, [ArkT])
            if neg:
                cx.stt(h1(U, cl), ps.ap[:, 0:128], -1.0, Wst(c), ALU.mult, ALU.mult, [ps] + wtiles, [U])
                cx.stt(sl(ArbT, c), ps.ap[:, 128:256], -1.0, Win(c), ALU.mult, ALU.mult, [ps] + wtiles, [ArbT])
            else:
                cx.tt("dve", h1(U, cl), ps.ap[:, 256:384], Wst(c), ALU.mult, [ps] + wtiles, [U])
                cx.tt("dve", sl(ArbT, c), ps.ap[:, 384:512], Win(c), ALU.mult, [ps] + wtiles, [ArbT])
        for q4 in range(HC // 4):
            ps = cx.ps()
            pv = ps.ap.bitcast(BF16)
            for q in range(4):
                cx.tr(pv[:, q * 128:(q + 1) * 128], h1(U, q4 * 4 + q), idb, [U, K.cb], [ps])
            cx.copy("act", hs(N, q4), pv[:, 0:512], [ps], [N])
        for q4 in range(HC // 4):
            cx.tt("pool", hs(Ua, q4), hs(U, q4), K.m4["bd8"].ap, ALU.mult, [U, K.m4["bd8"]], [Ua])
            cx.tt("pool", hs(Na, q4), hs(N, q4), K.m4["bd8"].ap, ALU.mult, [N, K.m4["bd8"]], [Na])
        for (dn, du, sn, su) in ((Nb, Ub_, Na, Ua), (Nc, Uc, Nb, Ub_)):
            for q4 in range(HC // 4):
                ps = cx.ps()
                mm4(ps, su, sn, q4)
                cx.copy("act", hs(dn, q4), ps.ap[:, 0:512], [ps], [dn])
                ps = cx.ps()
                mm4(ps, sn, su, q4)
                cx.copy("act", hs(du, q4), ps.ap[:, 0:512], [ps], [du])
        for q4 in range(HC // 4):
            cx.tt("pool", hs(P, q4), hs(Ua, q4), K.m4["id"].ap, ALU.add, [Ua, K.m4["id"]], [P])
            cx.tt("pool", hs(Q, q4), hs(Na, q4), K.m4["id"].ap, ALU.add, [Na, K.m4["id"]], [Q])
        for (ln, lu) in ((Nb, Ub_), (Nc, Uc)):
            for q4 in range(HC // 4):
                ps = cx.ps()
                mm4(ps, ln, P, q4)
                cx.tt("dve", hs(P, q4), ps.ap[:, 0:512], hs(P, q4), ALU.add, [ps, P], [P])
                ps = cx.ps()
                mm4(ps, lu, Q, q4)
                cx.tt("dve", hs(Q, q4), ps.ap[:, 0:512], hs(Q, q4), ALU.add, [ps, Q], [Q])
        NO, UO, P2, Q2 = Ua, Na, Nb, Ub_
        curP, curQ, nxtP, nxtQ = P, Q, P2, Q2
        for li, mk in enumerate(("o16", "o32", "o64", "o128")):
            last = (li == 3)
            for q4 in range(HC // 4):
                cx.tt("pool", hs(NO, q4), hs(N, q4), K.m4[mk].ap, ALU.mult, [N, K.m4[mk]], [NO])
                if not last:
                    cx.tt("pool", hs(UO, q4), hs(U, q4), K.m4[mk].ap, ALU.mult, [U, K.m4[mk]], [UO])
            for q4 in range(HC // 4):
                ps = cx.ps()
                mm4(ps, NO, curP, q4)
                cx.copy("act", hs(Z1b, q4), ps.ap[:, 0:512], [ps], [Z1b])
                if not last:
                    ps = cx.ps()
                    mm4(ps, UO, curQ, q4)
                    cx.copy("act", hs(Z2b, q4), ps.ap[:, 0:512], [ps], [Z2b])
            for q4 in range(HC // 4):
                ps = cx.ps()
                mm4(ps, curQ, Z1b, q4)
                if last:
                    c0_ = (half * HC + q4 * 4) * 128
                    cx.tt("dve", TT_.ap[:, c0_:c0_ + 512], ps.ap[:, 0:512], hs(curP, q4), ALU.add, [ps, curP], [TT_])
                else:
                    cx.tt("dve", hs(nxtP, q4), ps.ap[:, 0:512], hs(curP, q4), ALU.add, [ps, curP], [nxtP])
                    ps = cx.ps()
                    mm4(ps, curP, Z2b, q4)
                    cx.tt("dve", hs(nxtQ, q4), ps.ap[:, 0:512], hs(curQ, q4), ALU.add, [ps, curQ], [nxtQ])
            curP, curQ, nxtP, nxtQ = nxtP, nxtQ, curP, curQ
    cx.release(m1)
    AV = None
    if egam is not None:
        AV = cx.tile(NCH * dv, F32, nm + "AV")
        for c in range(NCH):
            ps = cx.ps()
            cx.mm([dict(out=ps.ap[:, 0:dv], lhsT=sl(AakT, c), rhs=V(c), start=True, stop=True)], [AakT] + ntiles, [ps])
            cx.copy("act", AV.ap[:, c * dv:(c + 1) * dv], ps.ap[:, 0:dv], [ps], [AV])
    St = cx.tile(dv, F32, nm + "S")
    Sb = cx.tile(dv, BF16, nm + "Sb")
    cx.I("dve", "memset", [], [St], ap=St.ap, constant=0.0)
    cx.I("dve", "memset", [], [Sb], ap=Sb.ap, constant=0.0)
    rows = slice(pb, pb + dk)

    def step(c):
        cx.S.tag = tag0 + "/dplr_seq"
        Xb = cx.ring(nm + "Xb", 2, dv, BF16)
        Ub = cx.ring(nm + "Ub", 2, dv, BF16)
        psX = cx.ps()
        if egam is None:
            cx.mm([dict(out=psX.ap[:, 0:dv], lhsT=As(c), rhs=Sb.ap[rows, :], start=True, stop=False),
                   dict(out=psX.ap[:, 0:dv], lhsT=sl(AakT, c), rhs=V(c), start=False, stop=True)],
                  gtiles + [Sb, AakT] + ntiles, [psX])
            cx.copy("act", Xb.ap, psX.ap[:, 0:dv], [psX], [Xb])
        else:
            cx.mm([dict(out=psX.ap[:, 0:dv], lhsT=As(c), rhs=Sb.ap[rows, :], start=True, stop=True)], gtiles + [Sb], [psX])
            cx.stt(Xb.ap, psX.ap[:, 0:dv], egam(c), AV.ap[:, c * dv:(c + 1) * dv], ALU.mult, ALU.add, [psX, AV] + pctiles, [Xb])
        psU = cx.ps()
        cx.mm([dict(out=psU.ap[:, 0:dv], lhsT=sl(TT_, c), rhs=Xb.ap, start=True, stop=True)], [TT_, Xb], [psU])
        cx.copy("act", Ub.ap, psU.ap[:, 0:dv], [psU], [Ub])
        if egam is None:
            psY = cx.ps()
            cx.mm([dict(out=psY.ap[:, 0:dv], lhsT=Rs(c), rhs=Sb.ap[rows, :], start=True, stop=False),
                   dict(out=psY.ap[:, 0:dv], lhsT=sl(ArbT, c), rhs=Ub.ap, start=False, stop=False),
                   dict(out=psY.ap[:, 0:dv], lhsT=sl(ArkT, c), rhs=V(c), start=False, stop=True)],
                  gtiles + [Sb, ArbT, ArkT, Ub] + ntiles, [psY])
            out_cb(c, psY.ap[:, 0:dv], psY)
        else:
            psY1 = cx.ps()
            cx.mm([dict(out=psY1.ap[:, 0:dv], lhsT=Rs(c), rhs=Sb.ap[rows, :], start=True, stop=True)], gtiles + [Sb], [psY1])
            psY2 = cx.ps()
            cx.mm([dict(out=psY2.ap[:, 0:dv], lhsT=sl(ArbT, c), rhs=Ub.ap, start=True, stop=False),
                   dict(out=psY2.ap[:, 0:dv], lhsT=sl(ArkT, c), rhs=V(c), start=False, stop=True)],
                  [ArbT, ArkT, Ub] + ntiles, [psY2])
            y2 = cx.ring(nm + "y2", 2, dv, F32)
            cx.copy("act", y2.ap, psY2.ap[:, 0:dv], [psY2], [y2])
            yo = cx.ring(nm + "yo", 2, dv, F32)
            cx.stt(yo.ap, psY1.ap[:, 0:dv], egam(c), y2.ap, ALU.mult, ALU.add, [psY1, y2] + pctiles, [yo])
            out_cb(c, yo.ap, yo)
        psS = cx.ps()
        cx.mm([dict(out=psS.ap[rows, 0:dv], lhsT=Bn(c), rhs=Ub.ap, start=True, stop=False),
               dict(out=psS.ap[rows, 0:dv], lhsT=Kn(c), rhs=V(c), start=False, stop=True)], ntiles + [Ub], [psS])
        if scale_all:
            cx.tt("dve", St.ap[rows, :], psS.ap[rows, 0:dv], St.ap[rows, :], ALU.add, [psS, St], [St])
            cx.ts("dve", St.ap[rows, :], St.ap[rows, :], PC(c), ALU.mult, [St] + pctiles, [St])
        else:
            cx.stt(St.ap[rows, :], St.ap[rows, :], PC(c), psS.ap[rows, 0:dv], ALU.mult, ALU.add, [St, psS] + pctiles, [St])
        cx.copy("act", Sb.ap[rows, :], St.ap[rows, :], [St], [Sb])

    def finish():
        cx.release(m0)
    if defer:
        return step, finish
    for c in range(NCH):
        step(c)
    finish()


def ln_exp_rinv(cx, out_ap, in_ap, reads, wt, scale=1.0, bias=1e-24):
    cx.act(out_ap, in_ap, AF.Ln, reads, [wt], scale=scale, bias=bias)
    cx.act(out_ap, out_ap, AF.Exp, [wt], [wt], scale=-0.5)


def mixer_attention(cx, K, io, hT, hv):
    m0 = cx.mark()
    wat = io["wat"]
    wsw = io["wsw"]
    rope = load_small(cx, io["rope"], 2 * SEQ, "rope", parts=32)
    cosv = rope.ap[:, 0:SEQ]
    sinv = rope.ap[:, SEQ:2 * SEQ]
    mdiag = K.cb.ap[:, C_MUI:C_MUI + 128]
    mprev = K.cb.ap[:, C_MLI:C_MLI + 128]
    mcomb = cx.tile(256, BF16, "mcomb")
    cx.copy("dve", mcomb.ap[:, 0:128], mdiag, [K.cb], [mcomb])
    cx.copy("dve", mcomb.ap[:, 128:256], mprev, [K.cb], [mcomb])
    if ATT_STOP <= 1:
        return
    DIL = [1, 4, 16]
    scale = 128.0 ** -0.5
    for hl in range(2):
        m1 = cx.mark()
        cx.S.tag = "M/att/proj"
        qT = [cx.tile(SEQ, BF16, "qT%d" % g) for g in range(3)]
        kT = [cx.tile(SEQ, BF16, "kT%d" % g) for g in range(3)]
        Vt = [cx.tile(NCH * 128, BF16, "V%d" % g) for g in range(3)]
        for g in range(3):
            d = DIL[g]
            for which, dst in ((0, qT[g]), (1, kT[g])):
                pcs = load_w_multi(cx, wat[(hl * 3 + g) * 3 + which], KC, 128)
                pcs_sw = load_w_multi(cx, wsw[(hl * 3 + g) * 2 + which], KC, 32)
                for tb in range(4):
                    lo, hi = tb * 512, tb * 512 + 512
                    ps = gemm_block(cx, pcs, 128, lambda k, l, h_: hv[:, k, l:h_], [hT], lo, hi)
                    if ATT_SUB >= 1:
                        ps2 = gemm_block(cx, pcs_sw, 32, lambda k, l, h_: hv[:, k, l:h_], [hT], lo, hi)
                    t1 = cx.ring("rp1", 2, 512, F32)
                    t2 = cx.ring("rp2", 2, 512, F32)
                    if ATT_SUB >= 2:
                        if "a" in ATT_V:
                            cx.tt("dve", t1.ap[0:32, :], ps.ap[0:32, 0:512], cosv[:, lo:hi], ALU.mult, [ps, rope], [t1])
                        if "b" in ATT_V:
                            cx.tt("dve", t2.ap[0:32, :], ps2.ap[0:32, 0:512], sinv[:, lo:hi], ALU.mult, [ps2, rope], [t2])
                        if "c" in ATT_V:
                            cx.tt("dve", t1.ap[0:32, :], ps.ap[0:32, 0:512], K.cf.ap[0:32, 0:512], ALU.mult, [ps, K.cf], [t1])
                        if "e" in ATT_V:
                            cx.tt("dve", t1.ap[:, :], ps.ap[:, 0:512], K.cf.ap[:, 0:512], ALU.mult, [ps, K.cf], [t1])
                        if "g" in ATT_V:
                            cx.tt("dve", t1.ap[0:32, :], ps.ap[0:32, 0:512], K.cf.ap[0:32, 0:512], ALU.mult, [ps, K.cf], [t1])
                        if "d" in ATT_V:
                            cx.tt("dve", t1.ap[0:32, :], t2.ap[0:32, :], cosv[:, lo:hi], ALU.mult, [t2, rope], [t1])
                    cx.copy("act", dst.ap[:, lo:hi], ps.ap[:, 0:512], [ps, t1] if "s" in ATT_V else [ps], [dst])
                    if ATT_SUB >= 3:
                        cx.tt("pool", dst.ap[0:32, lo:hi], t1.ap[0:32, :], t2.ap[0:32, :], ALU.add, [t1, t2], [dst])
            if ATT_STOP <= 2:
                return
            pcs = load_w_multi(cx, wat[(hl * 3 + g) * 3 + 2], KC, 128)
            nb = NCH // d
            for b4 in range(4):
                ps = cx.ps()
                mms = []
                for q in range(4):
                    blk = b4 * 4 + q
                    r, b = blk // nb, blk % nb
                    t0 = r + d * 128 * b
                    for k in range(KC):
                        mms.append(dict(out=ps.ap[:, q * 128:(q + 1) * 128], lhsT=hv[:, k, t0:t0 + d * 127 + 1:d],
                                        rhs=wslice(pcs, k, 128), start=(k == 0), stop=(k == KC - 1)))
                cx.mm(mms, [p[0] for p in pcs] + [hT], [ps])
                cx.copy("act", Vt[g].ap[:, b4 * 512:(b4 + 1) * 512], ps.ap[:, 0:512], [ps], [Vt[g]])
        if ATT_STOP <= 3:
            return
        cx.S.tag = "M/att/core"
        for Tb in range(4):
            pso = cx.ps(pin=True)
            psd = cx.ps(pin=True)
            first = [True]

            def unit(g, kslices, qslice, nq, masks):
                pss = cx.ps()
                nk = len(kslices)
                cx.mm([dict(out=pss.ap[:, i * nq:(i + 1) * nq], lhsT=kT[g].ap[:, ks], rhs=qT[g].ap[:, qslice], start=True, stop=True)
                       for i, (ks, vb) in enumerate(kslices)], [kT[g], qT[g]], [pss])
                pe_ = cx.ring("pexp", 3, 256, BF16)
                pm = cx.ring("pmask", 3, 256, BF16)
                cx.act(pe_.ap[:, 0:nk * nq], pss.ap[:, 0:nk * nq], AF.Exp, [pss], [pe_], scale=scale)
                cx.tt("pool", pm.ap[:, 0:nk * nq], pe_.ap[:, 0:nk * nq], masks, ALU.mult, [pe_, mcomb, K.cb], [pm])
                mmo, mmd = [], []
                qs0 = qslice.start - Tb * 512
                st = qslice.step or 1
                ocols = slice(qs0, qs0 + (nq - 1) * st + 1, st)
                for i, (ks, vb) in enumerate(kslices):
                    f = first[0]
                    first[0] = False
                    mmo.append(dict(out=pso.ap[:, ocols], lhsT=Vt[g].ap[:, vb * 128:(vb + 1) * 128], rhs=pm.ap[:, i * nq:(i + 1) * nq],
                                    start=f, stop=False, skip_group_check=True))
                    mmd.append(dict(out=psd.ap[:, ocols], lhsT=K.ones.ap, rhs=pm.ap[:, i * nq:(i + 1) * nq],
                                    start=f, stop=False, skip_group_check=True))
                cx.mm(mmo, [Vt[g], pm], [pso])
                cx.mm(mmd, [K.ones, pm], [psd])

            for qb in range(4 * Tb, 4 * Tb + 4):
                ks = [(slice(qb * 128, qb * 128 + 128), qb)]
                if qb > 0:
                    ks.append((slice((qb - 1) * 128, qb * 128), qb - 1))
                unit(0, ks, slice(qb * 128, qb * 128 + 128), 128, mcomb.ap[:, 0:128 * len(ks)])
            for r in range(4 if "1" in ATT_G else 0):
                tq = r + 4 * 128 * Tb
                ks = [(slice(tq, tq + 4 * 127 + 1, 4), r * 4 + Tb)]
                if Tb > 0:
                    tk = r + 4 * 128 * (Tb - 1)
                    ks.append((slice(tk, tk + 4 * 127 + 1, 4), r * 4 + Tb - 1))
                unit(1, ks, slice(tq, tq + 4 * 127 + 1, 4), 128, mcomb.ap[:, 0:128 * len(ks)])
            for r in range(16 if "2" in ATT_G else 0):
                tq = r + 16 * 32 * Tb
                ks = [(slice(r, r + 16 * 127 + 1, 16), r)]
                unit(2, ks, slice(tq, tq + 16 * 31 + 1, 16), 32, mdiag[:, 32 * Tb:32 * Tb + 32])
            rd = cx.ring("rden", 2, 512, F32)
            cx.I("dve", "reciprocal", [psd], [rd], out=rd.ap, in_=psd.ap[:, 0:512])
            yo = cx.ring("ybo", 2, 512, BF16)
            cx.tt("dve", yo.ap, pso.ap[:, 0:512], rd.ap, ALU.mult, [pso, rd], [yo])
            io["ywrite"](cx, io["rb"] + hl, Tb * 512, Tb * 512 + 512, yo.ap, [yo], io["ydep"]["b"])
            cx.unpin_all()
            if ATT_STOP <= 4:
                return
        cx.release(m1)
    cx.release(m0)


def mixer_gdn(cx, K, io, hT, hv):
    m0 = cx.mark()
    wc = io["wc"]
    wba = io["wba"]
    gconv = load_small(cx, io["gconv"], 12 * 4, "gconv")
    gcv = gconv.ap.rearrange("p (b t) -> p b t", t=4)
    gpar = load_small(cx, io["gpar"], 8 + 128, "gpar")
    mui = K.cf.ap[:, C_MUI:C_MUI + 128]
    mus = K.cf.ap[:, C_MUS:C_MUS + 128]
    mls = K.cf.ap[:, C_MLS:C_MLS + 128]
    idb = K.cb.ap[:, C_ID:C_ID + 128]
    pcs = load_w_multi(cx, wba, KC, 8)
    ps = cx.ps()
    mms = []
    for c in range(NCH):
        for k in range(KC):
            mms.append(dict(out=ps.ap[:, c * 8:(c + 1) * 8], lhsT=hv[:, k, c * 128:(c + 1) * 128], rhs=wslice(pcs, k, 8),
                            start=(k == 0), stop=(k == KC - 1)))
    cx.mm(mms, [p[0] for p in pcs] + [hT], [ps])
    ba = cx.tile(NCH * 8, F32, "ba")
    cx.copy("act", ba.ap, ps.ap[:, 0:NCH * 8], [ps], [ba])
    bav = ba.ap.rearrange("p (c e) -> p c e", e=8)
    beta = cx.tile(NCH * 4, F32, "beta")
    betav = beta.ap.rearrange("p (c e) -> p c e", e=4)
    cx.act(betav, bav[:, :, 0:4], AF.Sigmoid, [ba], [beta])
    gg = cx.tile(NCH * 4, F32, "gg")
    ggv = gg.ap.rearrange("p (c e) -> p c e", e=4)
    for hh in range(4):
        cx.act(ggv[:, :, hh], bav[:, :, 4 + hh], AF.Exp, [ba, gpar], [gg], bias=gpar.ap[:, 4 + hh:5 + hh])
    cx.act(gg.ap, gg.ap, AF.Ln, [gg], [gg], bias=1.0)
    ea = cx.tile(4, F32, "expA")
    cx.act(ea.ap, gpar.ap[:, 0:4], AF.Exp, [gpar], [ea])
    for hh in range(4):
        cx.ts("dve", ggv[:, :, hh], ggv[:, :, hh], ea.ap[:, hh:hh + 1], ALU.mult, [gg, ea], [gg], s2=-1.0, op1=ALU.mult)
    ps = cx.ps()
    cx.mm([dict(out=ps.ap[:, 0:64], lhsT=mui, rhs=gg.ap, start=True, stop=True)], [K.cf, gg], [ps])
    gam = cx.tile(64, F32, "gam")
    cx.copy("act", gam.ap, ps.ap[:, 0:64], [ps], [gam])
    egam = cx.tile(64, F32, "egam")
    cx.act(egam.ap, gam.ap, AF.Exp, [gam], [egam])
    ps = cx.ps()
    cx.mm([dict(out=ps.ap[:, 0:64], lhsT=K.cf.ap[:, C_S127:C_S127 + 128], rhs=gam.ap, start=True, stop=True)], [K.cf, gam], [ps])
    pcall = cx.tile(64, F32, "pcall")
    cx.act(pcall.ap, ps.ap[:, 0:64], AF.Exp, [ps], [pcall])
    wk = cx.tile(64, F32, "wk")
    cx.tt("dve", wk.ap, ps.ap[:, 0:64], gam.ap, ALU.subtract, [ps, gam], [wk])
    cx.act(wk.ap, wk.ap, AF.Exp, [wk], [wk])
    cx.tt("dve", wk.ap, wk.ap, beta.ap, ALU.mult, [wk, beta], [wk])
    nwk = cx.tile(64, F32, "nwk")
    cx.ts("dve", nwk.ap, wk.ap, -1.0, ALU.mult, [wk], [nwk])
    for hh in range(4):
        m1 = cx.mark()
        cx.S.tag = "M/gdn/prep"
        qT = cx.tile(SEQ, BF16, "gq")
        kT = cx.tile(SEQ, BF16, "gk")
        Vn = cx.tile(NCH * 128, BF16, "gV")
        Kn = cx.tile(NCH * 128, BF16, "gKn")
        Bn = cx.tile(NCH * 128, BF16, "gBn")
        gate = cx.tile(NCH * 128, BF16, "ggate")
        Wst = cx.tile(NCH * 128, F32, "gWst")
        Win = cx.tile(NCH * 128, F32, "gWin")
        m2 = cx.mark()
        vT = cx.tile(SEQ, BF16, "gvT")
        for which, dst in ((0, qT), (1, kT), (2, vT)):
            pcs = load_w_multi(cx, wc[which * 4 + hh], KC, 128)
            zp = cx.ring("gzp", 2, 3 + SEQ, F32)
            cx.I("dve", "memset", [], [zp], ap=zp.ap[:, 0:3], constant=0.0)
            for tb in range(4):
                lo, hi = tb * 512, tb * 512 + 512
                ps = gemm_block(cx, pcs, 128, lambda k, l, h_: hv[:, k, l:h_], [hT], lo, hi)
                cx.copy("act", zp.ap[:, 3 + lo:3 + hi], ps.ap[:, 0:512], [ps], [zp])
            cv = cx.ring("gcv", 2, SEQ, F32)
            bi = which * 4 + hh
            cx.ts("dve", cv.ap, zp.ap[:, 3:3 + SEQ], gcv[:, bi, 3:4], ALU.mult, [zp, gconv], [cv])
            for tap in range(3):
                cx.stt(cv.ap, zp.ap[:, tap:tap + SEQ], gcv[:, bi, tap:tap + 1], cv.ap, ALU.mult, ALU.add, [zp, gconv, cv], [cv])
            cx.act(cv.ap, cv.ap, AF.Silu, [cv], [cv])
            if which == 2:
                cx.copy("pool", dst.ap, cv.ap, [cv], [dst])
            else:
                for tb in range(4):
                    lo, hi = tb * 512, tb * 512 + 512
                    sq = cx.ring("gsq", 2, 512, BF16)
                    cx.act(sq.ap, cv.ap[:, lo:hi], AF.Square, [cv], [sq])
                    ps = cx.ps()
                    cx.mm([dict(out=ps.ap[:, 0:512], lhsT=K.ones.ap, rhs=sq.ap, start=True, stop=True)], [K.ones, sq], [ps])
                    ri = cx.ring("gri", 2, 512, F32)
                    ln_exp_rinv(cx, ri.ap, ps.ap[:, 0:512], [ps], ri)
                    if which == 0:
                        cx.stt(dst.ap[:, lo:hi], cv.ap[:, lo:hi], 128.0 ** -0.5, ri.ap, ALU.mult, ALU.mult, [cv, ri], [dst])
                    else:
                        cx.tt("dve", dst.ap[:, lo:hi], cv.ap[:, lo:hi], ri.ap, ALU.mult, [cv, ri], [dst])
        for c4 in range(4):
            ps = cx.ps()
            pv = ps.ap.bitcast(BF16)
            for q in range(4):
                c = c4 * 4 + q
                cx.tr(pv[:, q * 128:(q + 1) * 128], vT.ap[:, c * 128:(c + 1) * 128], idb, [vT, K.cb], [ps])
            cx.copy("act", Vn.ap[:, c4 * 512:(c4 + 1) * 512], pv[:, 0:512], [ps], [Vn])
            ps = cx.ps()
            pv = ps.ap.bitcast(BF16)
            for q in range(4):
                c = c4 * 4 + q
                cx.tr(pv[:, q * 128:(q + 1) * 128], kT.ap[:, c * 128:(c + 1) * 128], idb, [kT, K.cb], [ps])
            for q in range(4):
                c = c4 * 4 + q
                col = c * 4 + hh
                cx.ts("dve", Kn.ap[:, c * 128:(c + 1) * 128], pv[:, q * 128:(q + 1) * 128], wk.ap[:, col:col + 1], ALU.mult, [ps, wk], [Kn])
                cx.ts("dve", Bn.ap[:, c * 128:(c + 1) * 128], pv[:, q * 128:(q + 1) * 128], nwk.ap[:, col:col + 1], ALU.mult, [ps, nwk], [Bn])
        pcs = load_w_multi(cx, wc[12 + hh], KC, 128)
        for c4 in range(4):
            ps = cx.ps()
            mms = []
            for q in range(4):
                c = c4 * 4 + q
                for k in range(KC):
                    mms.append(dict(out=ps.ap[:, q * 128:(q + 1) * 128], lhsT=hv[:, k, c * 128:(c + 1) * 128], rhs=wslice(pcs, k, 128),
                                    start=(k == 0), stop=(k == KC - 1)))
            cx.mm(mms, [p[0] for p in pcs] + [hT], [ps])
            cx.act(gate.ap[:, c4 * 512:(c4 + 1) * 512], ps.ap[:, 0:512], AF.Silu, [ps], [gate])
        for c in range(NCH):
            col = c * 4 + hh
            g2 = cx.ring("gG2", 2, 128, F32)
            cx.ts("dve", g2.ap, mui, gg.ap[:, col:col + 1], ALU.mult, [K.cf, gg], [g2])
            ps = cx.ps()
            cx.mm([dict(out=ps.ap[:, 0:128], lhsT=mls, rhs=g2.ap, start=True, stop=True)], [K.cf, g2], [ps])
            ex = cx.ring("gex", 2, 128, F32)
            cx.act(ex.ap, ps.ap[:, 0:128], AF.Exp, [ps], [ex])
            cx.stt(Wst.ap[:, c * 128:(c + 1) * 128], ex.ap, beta.ap[:, col:col + 1], mus, ALU.mult, ALU.mult, [ex, beta, K.cf], [Wst])
            cx.stt(Win.ap[:, c * 128:(c + 1) * 128], ex.ap, beta.ap[:, col:col + 1], mui, ALU.mult, ALU.mult, [ex, beta, K.cf], [Win])
        cx.release(m2)
        cx.S.tag = "M/gdn"
        sl = lambda t, c: t.ap[:, c * 128:(c + 1) * 128]
        ycs = cx.tile(SEQ, BF16, "ycs")

        def out_cb(c, yap, yt, hh=hh, ycs=ycs):
            junk = cx.ring("gjunk", 2, 128, F32)
            ss = cx.ring("gss", 2, 1, F32)
            cx.act(junk.ap, yap, AF.Square, [yt], [junk, ss], accum_out=ss.ap)
            ln_exp_rinv(cx, ss.ap, ss.ap, [ss], ss, scale=1.0 / 128, bias=EPS)
            o1 = cx.ring("go1", 2, 128, F32)
            cx.stt(o1.ap, yap, ss.ap[:, 0:1], gpar.ap[:, 8:136], ALU.mult, ALU.mult, [yt, ss, gpar], [o1])
            o2 = cx.ring("go2", 2, 128, BF16)
            cx.tt("pool", o2.ap, o1.ap, gate.ap[:, c * 128:(c + 1) * 128], ALU.mult, [o1, gate], [o2])
            pst = cx.ps()
            ptv = pst.ap.bitcast(BF16)
            cx.tr(ptv[:, 0:128], o2.ap, idb, [o2, K.cb], [pst])
            cx.copy("act", ycs.ap[:, c * 128:(c + 1) * 128], ptv[:, 0:128], [pst], [ycs])

        dplr_head(cx, K, "g%d" % hh, 128, 128, 0,
                  Kg=lambda c: sl(kT, c), Bg=None, Ag=lambda c: sl(kT, c), Rg=lambda c: sl(qT, c), gtiles=[kT, qT],
                  Wst=lambda c: sl(Wst, c), Win=lambda c: sl(Win, c), wtiles=[Wst, Win],
                  As=lambda c: sl(kT, c), Rs=lambda c: sl(qT, c), egam=lambda c: egam.ap[:, c * 4 + hh:c * 4 + hh + 1],
                  Kn=lambda c: sl(Kn, c), Bn=lambda c: sl(Bn, c), V=lambda c: sl(Vn, c), ntiles=[Kn, Bn, Vn],
                  PC=lambda c: pcall.ap[:, c * 4 + hh:c * 4 + hh + 1], pctiles=[pcall, egam], scale_all=False, out_cb=out_cb)
        io["ywrite"](cx, io["rc"] + hh, 0, SEQ, ycs.ap, [ycs], io["ydep"]["c"])
        cx.release(m1)
    cx.release(m0)


def mixer_rwkv(cx, K, io, hT, hv, layer1):
    m0 = cx.mark()
    wa = io["wa"]
    wal = io["wal"]
    rp = load_small(cx, io["rpar"], 44, "rpar")
    rl = load_small(cx, io["rlmu"], 4, "rlmu")
    rb = load_small(cx, io["rbc"], 1024, "rbc")
    wab = cx.tile(512, BF16, "wab")
    g2b = cx.tile(1024, BF16, "g2b")
    if layer1:
        v2b_ = cx.tile(512, BF16, "v2b", parts=64)
        v2b = T(v2b_.ap[32:64, :], v2b_.d)
    rmask = cx.tile(SEQ, BF16, "rmask")
    tl = cx.tile(SEQ, BF16, "tl")
    sgd = cx.tile(SEQ, BF16, "sgd")
    sg2h = cx.tile(SEQ, BF16, "sg2h", parts=64)
    sgd2 = T(sg2h.ap[0:32, :], sg2h.d)
    if layer1:
        hv1 = T(sg2h.ap[32:64, :], sg2h.d)
    mt = cx.mark()
    w2a2 = load_small(cx, io["rw2a2"], 512, "rw2a2")
    g2a = load_small(cx, io["rg2"], 1024, "rg2")
    cx.copy("pool", wab.ap, w2a2.ap, [w2a2], [wab])
    cx.copy("pool", g2b.ap, g2a.ap, [g2a], [g2b])
    if layer1:
        v2 = cx.tile(512, F32, "rv2", parts=64)
        cx.dma(v2.ap[32:64, :], io["rv2"], [], [v2])
        cx.copy("pool", v2b.ap, v2.ap[32:64, :], [v2], [v2b])
    idb = K.cb.ap[:, C_ID:C_ID + 128]
    msk_s = K.cb.ap[:, C_MUS:C_MUS + 128]
    msk_i = K.cb.ap[:, C_MUI:C_MUI + 128]
    cx.I("dve", "memset", [], [rmask], ap=rmask.ap, constant=1.0)
    cx.I("dve", "memset", [], [rmask], ap=rmask.ap[:, 0:SEQ:128], constant=0.0)

    def mixed_block(zp, tmp, pcs, cw, c0, m, po, mu_ap, mu_t, dst_ap, dst_t, func=None):
        pr = slice(po, po + m)
        cx.I("dve", "memset", [], [zp], ap=zp.ap[pr, 0:1], constant=0.0)
        for tb in range(4):
            lo, hi = tb * 512, tb * 512 + 512
            ps = cx.ps()
            mms = []
            for k in range(KC):
                mms.append(dict(out=ps.ap[pr, 0:512], lhsT=wslice(pcs, k, cw, c0, c0 + m), rhs=hv[:, k, lo:hi],
                                start=(k == 0), stop=(k == KC - 1)))
            cx.mm(mms, [p[0] for p in pcs] + [hT], [ps])
            cx.copy("act", zp.ap[pr, 1 + lo:1 + hi], ps.ap[pr, 0:512], [ps], [zp])
        cx.tt("dve", tmp.ap[pr, :], zp.ap[pr, 0:SEQ], zp.ap[pr, 1:1 + SEQ], ALU.subtract, [zp], [tmp])
        if func is None:
            cx.stt(dst_ap, tmp.ap[pr, :], mu_ap, zp.ap[pr, 1:1 + SEQ], ALU.mult, ALU.add, [tmp, zp, mu_t], [dst_t])
        else:
            cx.stt(tmp.ap[pr, :], tmp.ap[pr, :], mu_ap, zp.ap[pr, 1:1 + SEQ], ALU.mult, ALU.add, [tmp, zp, mu_t], [tmp])
            cx.act(dst_ap, tmp.ap[pr, :], func, [tmp], [dst_t])

    zp0 = cx.tile(SEQ + 8, F32, "zp0")
    tmp0 = cx.tile(SEQ, F32, "tmp0")
    pcl = load_w_big(cx, wal, KC, 288, "wal_b")
    mixed_block(zp0, tmp0, pcl, 288, 0, 64, 0, rl.ap[0:64, 0:1], rl, tl.ap[0:64, :], tl, func=AF.Tanh)
    mixed_block(zp0, tmp0, pcl, 288, 64, 64, 64, rl.ap[64:128, 0:1], rl, tl.ap[64:128, :], tl, func=AF.Copy)
    mixed_block(zp0, tmp0, pcl, 288, 128, 128, 0, rl.ap[:, 1:2], rl, sgd.ap, sgd, func=AF.Sigmoid)
    mixed_block(zp0, tmp0, pcl, 288, 256, 32, 0, rl.ap[0:32, 2:3], rl, sgd2.ap, sgd2, func=AF.Sigmoid)
    if layer1:
        pc1 = load_w_multi(cx, io["rv1"], KC, 32)
        for tb in range(4):
            lo, hi = tb * 512, tb * 512 + 512
            ps = gemm_block(cx, pc1, 32, lambda k, l, h_: hv[:, k, l:h_], [hT], lo, hi, po=32)
            cx.copy("act", hv1.ap[:, lo:hi], ps.ap[32:64, 0:512], [ps], [hv1])
        vfT = io["vf_in"].rearrange("(k p) t -> p k t", p=128)
    cx.release(mt)
    vout = io["v_out"].rearrange("(k p) t -> p k t", p=128)
    lnw = rb.ap[:, 0:512]
    lnb = rb.ap[:, 512:1024]
    for ct in range(4):
        m1 = cx.mark()
        par = rp.ap[:, 12 + ct * 8:12 + ct * 8 + 8]
        cx.S.tag = "M/rwkv/prep"
        As = cx.tile(SEQ, BF16, "rAs")
        Rs = cx.tile(SEQ, BF16, "rRs")
        Ks = cx.tile(SEQ, BF16, "rKs")
        Bs = cx.tile(SEQ, BF16, "rBs")
        Kn = cx.tile(NCH * 128, BF16, "rKn")
        Bn = cx.tile(NCH * 128, BF16, "rBn")
        Vn = cx.tile(NCH * 128, BF16, "rVn")
        PCt = cx.tile(NCH, F32, "rPC")
        bon = cx.tile(NCH * 2, F32, "rbon")
        m2 = cx.mark()
        zp = cx.tile(SEQ + 8, F32, "zp")
        tmp = cx.tile(SEQ, F32, "tmp")
        ld = cx.tile(SEQ, F32, "ld")
        cum = cx.tile(SEQ, F32, "cum")
        av = cx.tile(SEQ, BF16, "a_sig")
        rm = cx.tile(SEQ, BF16, "r_m")
        km = cx.tile(SEQ, BF16, "k_m")
        vm = cx.tile(SEQ, BF16, "v_m")
        rkr = cx.tile(SEQ, BF16, "rkr")
        for which, dst in ((0, rm), (1, km), (2, vm)):
            pcs = load_w_multi(cx, wa[which * 4 + ct], KC, 128)
            mixed_block(zp, tmp, pcs, 128, 0, 128, 0, rp.ap[:, which * 4 + ct:which * 4 + ct + 1], rp, dst.ap, dst)
        kx = T(zp.ap[:, 0:SEQ], zp.d)
        B1 = tmp
        for tb in range(4):
            lo, hi = tb * 512, tb * 512 + 512
            ps = cx.ps()
            cx.mm([dict(out=ps.ap[:, 0:512], lhsT=wab.ap[0:64, ct * 128:(ct + 1) * 128], rhs=tl.ap[0:64, lo:hi], start=True, stop=True)],
                  [wab, tl], [ps])
            cx.act(ld.ap[:, lo:hi], ps.ap[:, 0:512], AF.Sigmoid, [ps, rp], [ld], bias=par[:, 0:1])
            ps = cx.ps()
            cx.mm([dict(out=ps.ap[:, 0:512], lhsT=wab.ap[64:128, ct * 128:(ct + 1) * 128], rhs=tl.ap[64:128, lo:hi], start=True, stop=True)],
                  [wab, tl], [ps])
            cx.act(av.ap[:, lo:hi], ps.ap[:, 0:512], AF.Sigmoid, [ps, rp], [av], bias=par[:, 1:2])
        cx.ts("dve", ld.ap, ld.ap, -float(np.exp(-0.5)), ALU.mult, [ld], [ld])
        cx.I("dve", "tensor_tensor_scan", [rmask, ld], [cum], out=cum.ap, data0=rmask.ap, data1=ld.ap, initial=0.0,
             op0=ALU.mult, op1=ALU.add)
        if layer1:
            vf = rkr
            cx.dma(vf.ap, vfT[:, ct, :], io["vf_deps"], [vf])
            for tb in range(4):
                lo, hi = tb * 512, tb * 512 + 512
                ps = cx.ps()
                cx.mm([dict(out=ps.ap[:, 0:512], lhsT=v2b.ap[:, ct * 128:(ct + 1) * 128], rhs=hv1.ap[:, lo:hi], start=True, stop=True)],
                      [v2b, hv1], [ps])
                cx.act(B1.ap[:, lo:hi], ps.ap[:, 0:512], AF.Sigmoid, [ps, rp], [B1], bias=par[:, 5:6])
            cx.tt("dve", kx.ap, vf.ap, vm.ap, ALU.subtract, [vf, vm], [kx])
            cx.tt("dve", kx.ap, kx.ap, B1.ap, ALU.mult, [kx, B1], [kx])
            cx.tt("dve", vm.ap, vm.ap, kx.ap, ALU.add, [vm, kx], [vm])
        if io.get("write_v", True):
            cx.dma(vout[:, ct, :], vm.ap, [vm], [io["v_dep"]])
        cx.ts("dve", kx.ap, km.ap, par[:, 2:3], ALU.mult, [km, rp], [kx])
        for tb in range(4):
            lo, hi = tb * 512, tb * 512 + 512
            sq = cx.ring("rsq", 2, 512, BF16)
            cx.act(sq.ap, kx.ap[:, lo:hi], AF.Square, [kx], [sq])
            ps = cx.ps()
            cx.mm([dict(out=ps.ap[:, 0:512], lhsT=K.cb.ap[:, C_BO:C_BO + 128], rhs=sq.ap, start=True, stop=True)], [K.cb, sq], [ps])
            ri = cx.ring("rri", 2, 512, F32)
            ln_exp_rinv(cx, ri.ap, ps.ap[:, 0:512], [ps], ri)
            cx.tt("dve", kx.ap[:, lo:hi], kx.ap[:, lo:hi], ri.ap, ALU.mult, [kx, ri], [kx])
        if RW_DBG and ct == 0:
            dbg = io["dbg"].rearrange("(k p) t -> p k t", p=128)
            cx.dma(dbg[:, 0, :], kx.ap, [kx], [])
            cx.dma(dbg[:, 1, :], ld.ap, [ld], [])
            cx.dma(dbg[:, 2, :], cum.ap, [cum], [])
        cx.ts("dve", B1.ap, av.ap, -1.0, ALU.add, [av, rp], [B1], s2=par[:, 3:4], op1=ALU.mult)
        cx.ts("dve", B1.ap, B1.ap, 1.0, ALU.add, [B1], [B1])
        if RW_DBG and ct == 0:
            cx.dma(dbg[:, 3, :], B1.ap, [B1], [])
        cx.tt("dve", km.ap, km.ap, B1.ap, ALU.mult, [km, B1], [km])
        cx.stt(rkr.ap, rm.ap, par[:, 4:5], km.ap, ALU.mult, ALU.mult, [rm, km, rp], [rkr])
        ps = cx.ps()
        cx.mm([dict(out=ps.ap[:, c * 2:c * 2 + 2], lhsT=rkr.ap[:, c * 128:(c + 1) * 128], rhs=K.cb.ap[:, C_BS:C_BS + 2], start=True, stop=True)
               for c in range(NCH)], [rkr, K.cb], [ps])
        cx.copy("act", bon.ap, ps.ap[:, 0:NCH * 2], [ps], [bon])
        cx.tt("dve", B1.ap, cum.ap, ld.ap, ALU.subtract, [cum, ld], [B1])
        cx.act(B1.ap, B1.ap, AF.Exp, [B1], [B1])
        cx.stt(As.ap, kx.ap, -1.0, B1.ap, ALU.mult, ALU.mult, [kx, B1], [As])
        cx.act(B1.ap, cum.ap, AF.Exp, [cum], [B1])
        cx.tt("dve", Rs.ap, rm.ap, B1.ap, ALU.mult, [rm, B1], [Rs])
        cx.copy("dve", PCt.ap, B1.ap[:, 127:SEQ:128], [B1], [PCt])
        cx.act(B1.ap, cum.ap, AF.Exp, [cum], [B1], scale=-1.0)
        cx.tt("dve", Ks.ap, km.ap, B1.ap, ALU.mult, [km, B1], [Ks])
        cx.tt("dve", kx.ap, kx.ap, av.ap, ALU.mult, [kx, av], [kx])
        cx.tt("dve", Bs.ap, kx.ap, B1.ap, ALU.mult, [kx, B1], [Bs])
        for src, dst in ((Ks, Kn), (Bs, Bn), (vm, Vn)):
            for c4 in range(4):
                ps = cx.ps()
                pv = ps.ap.bitcast(BF16)
                for q in range(4):
                    c = c4 * 4 + q
                    cx.tr(pv[:, q * 128:(q + 1) * 128], src.ap[:, c * 128:(c + 1) * 128], idb, [src, K.cb], [ps])
                cx.copy("act", dst.ap[:, c4 * 512:(c4 + 1) * 512], pv[:, 0:512], [ps], [dst])
        cx.release(m2)
        yas = cx.tile(SEQ, BF16, "yas")
        cx.S.tag = "M/rwkv"
        sfs = []
        for hp in range(2):
            pb = hp * 64
            hl = ct * 2 + hp
            pr = slice(pb, pb + 64)

            def out_cb(c, yap, yt, hl=hl, hp=hp, ct=ct, yas=yas, pb=pb):
                st = cx.ring("rst%d" % hp, 2, 6, F32)
                cx.I("dve", "bn_stats", [yt], [st], out=st.ap, in_=yap)
                mv_ = cx.ring("rmv%d" % hp, 2, 2, F32)
                cx.I("dve", "bn_aggr", [st], [mv_], out=mv_.ap, in_=st.ap)
                rs = cx.ring("rrs%d" % hp, 2, 1, F32)
                ln_exp_rinv(cx, rs.ap, mv_.ap[:, 1:2], [mv_], rs, bias=64e-5)
                y1 = cx.ring("ry1%d" % hp, 2, 64, F32)
                cx.ts("dve", y1.ap, yap, mv_.ap[:, 0:1], ALU.subtract, [yt, mv_, rs], [y1], s2=rs.ap[:, 0:1], op1=ALU.mult)
                cx.tt("pool", y1.ap, y1.ap, lnw[:, hl * 64:(hl + 1) * 64], ALU.mult, [y1, rb], [y1])
                cx.tt("pool", y1.ap, y1.ap, lnb[:, hl * 64:(hl + 1) * 64], ALU.add, [y1, rb], [y1])
                y2 = cx.ring("ry2%d" % hp, 2, 64, F32)
                cx.stt(y2.ap, Vn.ap[:, c * 128 + hp * 64:c * 128 + hp * 64 + 64], bon.ap[:, c * 2 + hp:c * 2 + hp + 1], y1.ap,
                       ALU.mult, ALU.add, [Vn, bon, y1], [y2])
                psg = cx.ps()
                cx.mm([dict(out=psg.ap[:, 0:64], lhsT=sgd.ap[:, c * 128:(c + 1) * 128], rhs=g2b.ap[:, hl * 64:(hl + 1) * 64], start=True, stop=False),
                       dict(out=psg.ap[:, 0:64], lhsT=sgd2.ap[:, c * 128:(c + 1) * 128], rhs=g2b.ap[0:32, 512 + hl * 64:512 + (hl + 1) * 64],
                            start=False, stop=True)], [sgd, sgd2, g2b], [psg])
                y3 = cx.ring("ry3%d" % hp, 2, 64, BF16)
                if RW_DBG == "g":
                    cx.copy("dve", y3.ap, psg.ap[:, 0:64], [psg], [y3])
                elif RW_DBG == "y1":
                    cx.copy("dve", y3.ap, y1.ap, [y1, psg], [y3])
                elif RW_DBG == "y2":
                    cx.copy("dve", y3.ap, y2.ap, [y2, psg], [y3])
                elif RW_DBG == "yraw":
                    cx.copy("dve", y3.ap, yap, [yt, psg], [y3])
                else:
                    cx.tt("dve", y3.ap, psg.ap[:, 0:64], y2.ap, ALU.mult, [psg, y2], [y3])
                pst = cx.ps()
                ptv = pst.ap.bitcast(BF16)
                cx.tr(ptv[pb:pb + 64, 0:128], y3.ap, idb, [y3, K.cb], [pst])
                cx.copy("act", yas.ap[pb:pb + 64, c * 128:(c + 1) * 128], ptv[pb:pb + 64, 0:128], [pst], [yas])

            sf = dplr_head(cx, K, "r%d" % hl, 64, 64, pb, defer=True,
                      Kg=lambda c, pr=pr: Ks.ap[pr, c * 128:(c + 1) * 128], Bg=lambda c, pr=pr: Bs.ap[pr, c * 128:(c + 1) * 128],
                      Ag=lambda c, pr=pr: As.ap[pr, c * 128:(c + 1) * 128], Rg=lambda c, pr=pr: Rs.ap[pr, c * 128:(c + 1) * 128],
                      gtiles=[Ks, Bs, As, Rs],
                      Wst=lambda c: msk_s, Win=lambda c: msk_i, wtiles=[K.cb],
                      As=lambda c, pr=pr: As.ap[pr, c * 128:(c + 1) * 128], Rs=lambda c, pr=pr: Rs.ap[pr, c * 128:(c + 1) * 128], egam=None,
                      Kn=lambda c, pb=pb: Kn.ap[:, c * 128 + pb:c * 128 + pb + 64], Bn=lambda c, pb=pb: Bn.ap[:, c * 128 + pb:c * 128 + pb + 64],
                      V=lambda c, pb=pb: Vn.ap[:, c * 128 + pb:c * 128 + pb + 64], ntiles=[Kn, Bn, Vn],
                      PC=lambda c, pr=pr: PCt.ap[pr, c:c + 1], pctiles=[PCt], scale_all=True, out_cb=out_cb)
            sfs.append(sf)
        for c in range(NCH):
            for st_, fn_ in sfs:
                st_(c)
        for st_, fn_ in reversed(sfs):
            fn_()
        io["ywrite"](cx, io["ra"] + ct, 0, SEQ, yas.ap, [yas], io["ydep"]["a"])
        cx.release(m1)
    cx.release(m0)


def phase_M(cx, ios, layer1, which=("att", "gdn", "rwkv")):
    io = ios[0]
    base_ = cx.mark()
    cx.S.tag = "M/pro"
    cx.cast_eng = "pool"
    K = setup_consts(cx, io)
    g1 = load_small(cx, io["g_attn"], 16, "g1")
    hT = cx.tile(KC * SEQ, BF16, "hT")
    hv = hT.ap.rearrange("p (k t) -> p k t", k=KC)
    if "x_view" in io:
        xview = io["x_view"]
    else:
        xT = io["xT"].rearrange("(k p) t -> p k t", p=128)
        xview = lambda k, lo, hi: xT[:, k, lo:hi]
    m0 = cx.mark()
    for tb in range(4):
        lo, hi = tb * 512, tb * 512 + 512
        xt = cx.ring("xtile", 1, KC * 512, F32)
        xtv = xt.ap.rearrange("p (k t) -> p k t", k=KC)
        for k in range(KC):
            cx.dma(xtv[:, k, :], xview(k, lo, hi), io.get("x_deps", []), [xt])
        rmsnorm_fm(cx, lambda k, l, h_: (xtv[:, k, 0:h_ - l], xt), g1, lambda k, l, h_: (hv[:, k, l:h_], hT),
                   [(lo, hi)], K.ones)
    cx.release(m0)
    for io in ios:
        if "att" in which:
            cx.S.tag = "M/att"
            mixer_attention(cx, K, io, hT, hv)
        if "gdn" in which:
            cx.S.tag = "M/gdn"
            mixer_gdn(cx, K, io, hT, hv)
        if "rwkv" in which:
            cx.S.tag = "M/rwkv"
            mixer_rwkv(cx, K, io, hT, hv, layer1)
    cx.release(base_)


M_INPUTS = [("xT", (D, SEQ)), ("g_attn", (128, 16)), ("cst", (128, CST_N)), ("rope", (32, 2 * SEQ)),
            ("wat", (18, 128, 16, 128)), ("wsw", (12, 128, 16, 32)),
            ("wc", (16, 128, 16, 128)), ("wba", (128, 16, 8)), ("gconv", (128, 48)), ("gpar", (128, 136)),
            ("wa", (12, 128, 16, 128)), ("wal", (128, 16, 288)), ("rpar", (128, 44)), ("rlmu", (128, 4)), ("rbc", (128, 1024)),
            ("rw2a2", (128, 512)), ("rg2", (128, 1024))]
M_INPUTS_L1 = [("rv1", (128, 16, 32)), ("rv2", (32, 512)), ("vf_in", (512, SEQ))]
M_OUTPUTS = [("y_fm", (768, SEQ)), ("yc_tm", (SEQ, 512)), ("ya_tm", (SEQ, 512)), ("v_out", (512, SEQ))]


def prep_M_weights(inp, l, s):
    A_IN, B_IN, C_IN = 3360, 4608, 4112
    W = inp["w_in"][l]
    w = {}
    w["g_attn"] = vec_pk(inp["attn_norm"][l])
    w["cst"] = make_consts()
    w["rope"] = make_rope()

    def colblk(cols):
        sub = W[:, cols]
        n = sub.shape[1]
        return np.ascontiguousarray(sub.reshape(16, 128, n // 128, 128).transpose(2, 1, 0, 3))

    def colsmall(cols):
        sub = W[:, cols]
        return np.ascontiguousarray(sub.reshape(16, 128, len(cols)).transpose(1, 0, 2))
    b0 = A_IN
    cols = []
    cols_sw = []
    for hl in range(2):
        hi = 2 * s + hl
        for g in range(3):
            for which in range(3):
                c0 = b0 + which * 1536 + g * 512 + hi * 128
                cols += list(range(c0, c0 + 128))
                if which < 2:
                    cols_sw += list(range(c0 + 16, c0 + 32)) + list(range(c0, c0 + 16))
    w["wat"] = colblk(np.array(cols))
    sw = W[:, np.array(cols_sw)]
    w["wsw"] = np.ascontiguousarray(sw.reshape(16, 128, 12, 32).transpose(2, 1, 0, 3))
    c0 = A_IN + B_IN
    cols = []
    for which in range(3):
        for hh in range(4):
            h = 4 * s + hh
            cols += list(range(c0 + which * 1024 + h * 128, c0 + which * 1024 + (h + 1) * 128))
    for hh in range(4):
        h = 4 * s + hh
        cols += list(range(c0 + 3088 + h * 128, c0 + 3088 + (h + 1) * 128))
    w["wc"] = colblk(np.array(cols))
    w["wba"] = colsmall(np.array([c0 + 3072 + 4 * s + i for i in range(4)] + [c0 + 3080 + 4 * s + i for i in range(4)]))
    gc = inp["gdn_conv"][l]
    gcs = np.zeros((128, 12, 4), np.float32)
    for which in range(3):
        for hh in range(4):
            h = 4 * s + hh
            gcs[:, which * 4 + hh, :] = gc[:, which * 1024 + h * 128: which * 1024 + (h + 1) * 128].T
    w["gconv"] = gcs.reshape(128, 48)
    gp = np.zeros((128, 136), np.float32)
    gp[:, 0:4] = inp["gdn_A_log"][l][4 * s:4 * s + 4][None, :]
    gp[:, 4:8] = inp["gdn_dt_bias"][l][4 * s:4 * s + 4][None, :]
    gp[:, 8:136] = inp["gdn_norm"][l][None, :]
    w["gpar"] = gp
    ch0 = 512 * s
    cols = []
    for which in range(3):
        cols += list(range(which * 1024 + ch0, which * 1024 + ch0 + 512))
    w["wa"] = colblk(np.array(cols))
    w["wal"] = colsmall(np.arange(3072, 3360))
    mu = inp["rwkv_mu"][l]
    rp = np.zeros((128, 44), np.float32)
    for which in range(3):
        for ct in range(4):
            rp[:, which * 4 + ct] = mu[which * 1024 + ch0 + ct * 128: which * 1024 + ch0 + (ct + 1) * 128]
    for ct in range(4):
        sl = slice(ch0 + ct * 128, ch0 + (ct + 1) * 128)
        rp[:, 12 + ct * 8 + 0] = inp["rwkv_w0"][l][sl]
        rp[:, 12 + ct * 8 + 1] = inp["rwkv_a0"][l][sl]
        rp[:, 12 + ct * 8 + 2] = inp["rwkv_k_k"][l][sl]
        rp[:, 12 + ct * 8 + 3] = inp["rwkv_k_a"][l][sl]
        rp[:, 12 + ct * 8 + 4] = inp["rwkv_r_k"][l].reshape(-1)[sl]
        if l > 0:
            rp[:, 12 + ct * 8 + 5] = inp["rwkv_v0"][l - 1][sl]
    w["rpar"] = rp
    rl = np.zeros((128, 4), np.float32)
    rl[0:64, 0] = mu[3072:3136]
    rl[64:128, 0] = mu[3136:3200]
    rl[0:128, 1] = mu[3200:3328]
    rl[0:32, 2] = mu[3328:3360]
    w["rlmu"] = rl
    rb = np.zeros((128, 1024), np.float32)
    rb[:, 0:512] = inp["rwkv_ln_w"][l][ch0:ch0 + 512][None, :]
    rb[:, 512:1024] = inp["rwkv_ln_b"][l][ch0:ch0 + 512][None, :]
    w["rbc"] = rb
    w["rw2a2"] = np.ascontiguousarray(np.concatenate([inp["rwkv_w2"][l][:, ch0:ch0 + 512], inp["rwkv_a2"][l][:, ch0:ch0 + 512]], axis=0))
    g2 = inp["rwkv_g2"][l][:, ch0:ch0 + 512]
    rg = np.zeros((128, 1024), np.float32)
    rg[:, 0:512] = g2[0:128]
    rg[0:32, 512:1024] = g2[128:160]
    w["rg2"] = rg
    if l > 0:
        v1 = inp["rwkv_v1"][l - 1]
        w["rv1"] = np.ascontiguousarray(v1.reshape(16, 128, 32).transpose(1, 0, 2))
        w["rv2"] = np.ascontiguousarray(inp["rwkv_v2"][l - 1][:, ch0:ch0 + 512])
    return w


def tile_w(W, col0, ncols, cw=128):
    K = W.shape[0]
    sub = W[:, col0:col0 + ncols]
    return np.ascontiguousarray(sub.reshape(K // 128, 128, ncols // cw, cw).transpose(2, 1, 0, 3))


def vec_pk(v):
    return np.ascontiguousarray(v.reshape(-1, 128).T)


def prep_T_weights(inp, l):
    A_IN, B_IN, C_IN = 3360, 4608, 4112
    g0 = A_IN + B_IN + C_IN
    w = {}
    w["wg"] = tile_w(inp["w_in"][l], g0, 6144)
    pw = np.concatenate([inp["proj_a"][l], inp["proj_b"][l], inp["proj_c"][l]], axis=0)
    w["wp"] = tile_w(pw, 0, 2048)
    w["wo"] = tile_w(inp["w_out"][l], 0, 2048)
    w["wup"] = tile_w(inp["ffn_up"][l], 0, 11264)
    w["wdn"] = tile_w(inp["ffn_down"][l], 0, 2048)
    w["g_attn"] = vec_pk(inp["attn_norm"][l])
    w["g_ffn"] = vec_pk(inp["ffn_norm"][l])
    w["g_next"] = vec_pk(inp["attn_norm"][l + 1] if l + 1 < inp["attn_norm"].shape[0] else inp["final_norm"])
    fc = inp["ffn_conv"][l]
    w["fconv"] = np.ascontiguousarray(fc.T.reshape(88, 128, 3).transpose(1, 0, 2).reshape(128, 88 * 3))
    return w


class DD:
    def __init__(self, name=""):
        self.d = Dep(name)


M_SHARED = ("xT", "cst", "rope")
T_INPUTS = [("g_attn", (128, 16)), ("g_ffn", (128, 16)), ("g_next", (128, 16)), ("fconv", (128, 88 * 3)),
            ("wg", (48, 128, 16, 128)), ("wp", (16, 128, 20, 128)), ("wo", (16, 128, 16, 128)),
            ("wup", (88, 128, 16, 128)), ("wdn", (16, 128, 44, 128))]


PAIRS = [[0, 1], [2, 3], [4, 5], [6, 7]]


def build_fused(depth=2):
    nc = bass.Bass("TRN2", target_bir_lowering=False)
    ext = {}

    def din(name, shape, dt=F32):
        ext[name] = nc.dram_tensor(name, list(shape), dt, kind="ExternalInput").ap()
        return ext[name]
    xT = din("xT", (D, SEQ))
    xTT = din("xTT", (D, NTOK_T))
    hmask = din("hmask", (128, 2))
    cst = din("cst", (128, CST_N))
    rope = din("rope", (32, 2 * SEQ))
    for l in range(depth):
        for name, shape in M_INPUTS + (M_INPUTS_L1[:2] if l == 1 else []):
            if name not in M_SHARED:
                din("%s_%d" % (name, l), shape)
        for name, shape in T_INPUTS:
            din("T%s_%d" % (name, l), shape)
    out = nc.dram_tensor("outT", [D, 1024], F32, kind="ExternalOutput").ap()
    yloc = [[nc.dram_tensor("yloc%d_%d" % (l, q), [1280, 512], BF16).ap() for q in range(4)] for l in range(depth)]
    yg = [[nc.dram_tensor("yg%d_%d" % (l, q), [2560, 512], BF16).ap() for q in range(4)] for l in range(depth)]
    xloc = [[nc.dram_tensor("xloc%d_%d" % (rh, ch), [1024, 512], F32).ap() for ch in range(2)] for rh in range(2)]
    xg = [[nc.dram_tensor("xg%d_%d" % (rh, ch), [2048, 512], F32).ap() for ch in range(2)] for rh in range(2)]
    vbuf = nc.dram_tensor("vbuf", [512, SEQ], BF16).ap()
    with ExitStack() as es:
        cx = Ctx(nc, es)
        cx.wpiece = 1024
        xloc_d = [[DD("xloc") for ch in range(2)] for rh in range(2)]
        xg_d = [[DD("xg") for ch in range(2)] for rh in range(2)]
        all_xloc_d = [d for r_ in xloc_d for d in r_]
        all_xg_d = [d for r_ in xg_d for d in r_]
        v_d = DD("v")
        xloc_v = [[xloc[rh][ch].rearrange("(k p) t -> p k t", p=128) for ch in range(2)] for rh in range(2)]
        xg_v = [[xg[rh][ch].rearrange("(r k p) t -> p r k t", r=2, p=128) for ch in range(2)] for rh in range(2)]

        def ag(in_ap, out_ap, rd, wr):
            cx.S.op("pool", [("collective_compute", dict(kind="AllGather", op=ALU.bypass, replica_groups=PAIRS,
                                                        ins=[in_ap], outs=[out_ap]))], [d.d for d in rd], [d.d for d in wr], dma="cc")

        for l in range(depth):
            last = (l == depth - 1)
            yl_d = [{"a": DD("ya"), "b": DD("yb"), "c": DD("yc")} for q in range(4)]
            yg_d = [DD("yg") for q in range(4)]
            yloc_v = [yloc[l][q].rearrange("(b p) t -> p b t", p=128) for q in range(4)]
            yg_v = [yg[l][q].rearrange("(r b p) t -> p r b t", r=2, p=128) for q in range(4)]

            def ywrite(cx_, blk, t_lo, t_hi, src_ap, src_tiles, key, yl_d=yl_d, yloc_v=yloc_v):
                for q in range(t_lo // 512, t_hi // 512):
                    cx_.dma(yloc_v[q][:, blk, :], src_ap[:, q * 512 - t_lo:q * 512 - t_lo + 512], src_tiles, [yl_d[q][key]])

            io = {}
            for name, shape in M_INPUTS + (M_INPUTS_L1[:2] if l == 1 else []):
                if name not in M_SHARED:
                    io[name] = ext["%s_%d" % (name, l)]
            io.update(cst=cst, rope=rope, ydep={"a": "a", "b": "b", "c": "c"}, ywrite=ywrite, ra=0, rb=4, rc=6,
                      v_out=vbuf, v_dep=v_d, vf_in=vbuf, vf_deps=[v_d], write_v=(l == 0))
            if l == 0:
                io.update(xT=xT, x_deps=[])
            else:
                io.update(x_view=(lambda k, lo, hi: xg_v[k // 8][(lo % 1024) // 512][:, lo // 1024, k % 8, lo % 512:lo % 512 + (hi - lo)]),
                          x_deps=all_xg_d)
            phase_M(cx, [io], l == 1)
            cx.S.tag = "AG/y"
            for q in range(4):
                ag(yloc[l][q], yg[l][q], list(yl_d[q].values()), [yg_d[q]])
            io = {name: ext["T%s_%d" % (name, l)] for name, shape in T_INPUTS}
            io.update(hmask=hmask, yread=(lambda r_, b_, q, yg_v=yg_v: yg_v[q][:, r_, b_, :]), y_deps=yg_d,
                      x_deps=all_xloc_d + all_xg_d)
            if l == 0:
                io.update(x_mode="input", xTT=xTT)
            else:
                def xloc_read(k, t_lo, t_hi):
                    out_ = []
                    t = t_lo
                    while t < t_hi:
                        ch = t // 512
                        e = min(t_hi, (ch + 1) * 512)
                        out_.append((xloc_v[k // 8][ch][:, k % 8, t - ch * 512:e - ch * 512], t - t_lo, e - t))
                        t = e
                    return out_
                io.update(x_mode="exchange", xloc_read=xloc_read,
                          xhalo=(lambda k: xg_v[k // 8][1][:, 0, k % 8, 510:512]))
            if last:
                io.update(outT=out, out_deps=[])
            else:
                def xwrite(cx_, k, src_ap, src_tiles):
                    for ch in range(2):
                        cx_.dma(xloc_v[k // 8][ch][:, k % 8, :], src_ap[:, ch * 512:(ch + 1) * 512], src_tiles, [xloc_d[k // 8][ch]])
                io.update(outT=None, xwrite=xwrite)
            phase_T(cx, io, last, 0)
            if not last:
                cx.S.tag = "AG/x"
                for rh in range(2):
                    for ch in range(2):
                        ag(xloc[rh][ch], xg[rh][ch], [xloc_d[rh][ch]], [xg_d[rh][ch]])
        if cx.S.taglog is not None:
            import json
            json.dump(cx.S.taglog, open(os.environ["BASS_TAGLOG"], "w"))
        cx.S.emit()
        print("fused ops:", cx.S.nops, "insts:", cx.S.ninst, "arena peak KB:", cx.ar.peak * 4 / 1024,
              "eng counts:", cx.S.cnt, "max dma sem:", max(cx.S.dma_cnt))
    return nc


_NC_CACHE = {}


def prep_core_inputs(inp, depth, s):
    m = {"cst": make_consts(), "rope": make_rope()}
    hm = np.zeros((128, 2), np.float32)
    hm[:, s] = 1.0
    m["hmask"] = hm
    for l in range(depth):
        w = prep_M_weights(inp, l, s)
        for k, v in w.items():
            if k not in M_SHARED:
                m["%s_%d" % (k, l)] = v
    return m


def kernel(**inputs):
    inp = {k: np.asarray(v) for k, v in inputs.items()}
    x = inp["x"].astype(np.float32, copy=False)
    B, S, Dm = x.shape
    depth = inp["w_in"].shape[0]
    if "nc" not in _NC_CACHE:
        _NC_CACHE["nc"] = build_fused(depth)
    nc = _NC_CACHE["nc"]
    per_s = [prep_core_inputs(inp, depth, s) for s in range(2)]
    tw = {}
    for l in range(depth):
        for k, v in prep_T_weights(inp, l).items():
            tw["T%s_%d" % (k, l)] = v
    maps = []
    for c in range(8):
        b, s = c // 2, c % 2
        m = dict(per_s[s])
        m.update(tw)
        m["xT"] = np.ascontiguousarray(x[b].T)
        lo = 1024 * s - 2
        xs = np.zeros((NTOK_T, Dm), np.float32)
        a = max(lo, 0)
        xs[a - lo:] = x[b][a:lo + NTOK_T]
        m["xTT"] = np.ascontiguousarray(xs.T)
        maps.append(m)
    res = run_bass_kernel_spmd(nc, maps, core_ids=list(range(8))).results
    outs = [np.concatenate([np.asarray(res[2 * b + s]["outT"]).T for s in range(2)], axis=0) for b in range(B)]
    return np.ascontiguousarray(np.stack(outs, axis=0)).astype(np.float32)
```

```python
import os
import numpy as np
import concourse.bass as bass
import concourse.mybir as mybir
from concourse.bass_utils import run_bass_kernel_spmd
from contextlib import ExitStack

F32 = mybir.dt.float32
BF16 = mybir.dt.bfloat16
ALU = mybir.AluOpType
AF = mybir.ActivationFunctionType
AX = mybir.AxisListType

D = 2048
KC = 16
SEQ = 2048
NTOK_T = 1026
TT = [(0, 2), (2, 514), (514, 1026)]
D_FF = 5632
NFB = 44
EPS = 1e-6

SAME_ENG_SYNC = True


class Dep:
    __slots__ = ("w", "r", "name", "excl")

    def __init__(self, name="", excl=False):
        self.w = None
        self.r = {}
        self.name = name
        self.excl = excl


class Sched:
    ENGS = ["pe", "act", "dve", "pool", "sp"]

    def __init__(self, nc, n_dma=24):
        self.nc = nc
        self.q = {e: [] for e in self.ENGS}
        self.cnt = {e: 0 for e in self.ENGS}
        self.n_dma = n_dma
        self.dma_cnt = [0] * n_dma
        self.dma_rr = 0
        self.waited = {e: {} for e in self.ENGS}
        self.nops = 0
        self.ninst = 0
        self.tag = ""
        self.cc_cnt = 0
        self.taglog = [] if os.environ.get("BASS_TAGLOG") else None

    def op(self, eng, calls, reads=(), writes=(), dma=False):
        deps = {}

        def add(tok):
            if tok is None:
                return
            k, v = tok
            if deps.get(k, 0) < v:
                deps[k] = v

        for r in reads:
            add(r.w)
            if r.excl:
                for k, v in r.r.items():
                    add((k, v))
        for w in writes:
            add(w.w)
            for k, v in w.r.items():
                add((k, v))
        if dma == "cc":
            self.cc_cnt += 1
            tok = (("cc", 0), self.cc_cnt)
        elif dma:
            i = self.dma_rr
            self.dma_rr = (self.dma_rr + 1) % self.n_dma
            add((("dma", i), self.dma_cnt[i]))
            self.dma_cnt[i] += 16
            tok = (("dma", i), self.dma_cnt[i])
        else:
            self.cnt[eng] += 1
            tok = (eng, self.cnt[eng])
        waits = []
        wd = self.waited[eng]
        for k, v in deps.items():
            if v <= 0:
                continue
            if k == eng and (eng == "pe" or not SAME_ENG_SYNC):
                continue
            if wd.get(k, 0) >= v:
                continue
            wd[k] = v
            waits.append((k, v))
        for r in reads:
            if r.r.get(tok[0], 0) < tok[1]:
                r.r[tok[0]] = tok[1]
        for w in writes:
            w.w = tok
            w.r = {}
        self.q[eng].append((waits, calls, tok))
        if self.taglog is not None:
            self.taglog.append((eng, self.tag, len(calls)))
        self.nops += 1
        self.ninst += len(calls)
        return tok

    def emit(self, final_wait_eng="sp"):
        nc = self.nc
        with ExitStack() as es:
            sems = {}
            for e in self.ENGS:
                sems[e] = es.enter_context(nc.semaphore("s_" + e))
            for i in range(self.n_dma):
                sems[("dma", i)] = es.enter_context(nc.semaphore("s_dma%d" % i))
            sems[("cc", 0)] = es.enter_context(nc.semaphore("s_cc"))
            block = es.enter_context(nc.Block())
            finals = [(("dma", i), self.dma_cnt[i]) for i in range(self.n_dma) if self.dma_cnt[i] > 0]

            def run(engname, engine):
                for waits, calls, tok in self.q[engname]:
                    for k, v in waits:
                        engine.wait_ge(sems[k], v)
                    ins = None
                    for name, kw in calls:
                        ins = getattr(engine, name)(**kw)
                    ins.then_inc(sems[tok[0]], 16 if (isinstance(tok[0], tuple) and tok[0][0] == "dma") else 1)
                if engname == final_wait_eng:
                    for k, v in finals:
                        engine.wait_ge(sems[k], v)

            @block.tensor
            def _(e):
                run("pe", e)

            @block.scalar
            def _(e):
                run("act", e)

            @block.vector
            def _(e):
                run("dve", e)

            @block.gpsimd
            def _(e):
                run("pool", e)

            @block.sync
            def _(e):
                run("sp", e)


class Arena:
    def __init__(self, ap_f32, words):
        self.base = ap_f32
        self.words = words
        self.top = 0
        self.peak = 0
        self.live = []
        self.dead = []

    def mark(self):
        return self.top

    def release(self, m):
        keep = []
        for ent in self.live:
            if ent[0] >= m:
                self.dead.append(ent)
            else:
                keep.append(ent)
        self.live = keep
        self.top = m

    def alloc(self, nelem, dtype=F32, parts=128, name=""):
        w = nelem if dtype == F32 else (nelem + 1) // 2
        w = (w + 7) // 8 * 8
        assert self.top + w <= self.words, ("SBUF arena overflow", name, self.top, w, self.words)
        lo, hi = self.top, self.top + w
        a = self.base[0:parts, lo:hi]
        self.top = hi
        self.peak = max(self.peak, self.top)
        d = Dep(name)
        nd = []
        for (dlo, dhi, dd) in self.dead:
            if dlo < hi and lo < dhi:
                toks = list(dd.r.items())
                if dd.w is not None:
                    toks.append(dd.w)
                for k, v in toks:
                    if d.r.get(k, 0) < v:
                        d.r[k] = v
                if lo <= dlo and dhi <= hi:
                    continue
            nd.append((dlo, dhi, dd))
        self.dead = nd
        self.live.append((lo, hi, d))
        if dtype != F32:
            a = a.bitcast(dtype)
        return a[:, 0:nelem], d


class T:
    def __init__(self, ap, d):
        self.ap = ap
        self.d = d


class Ctx:
    ARENA_WORDS = 51 * 1024 + 512

    def __init__(self, nc, es):
        self.nc = nc
        self.S = Sched(nc)
        big = es.enter_context(nc.sbuf_tensor("arena", [128, self.ARENA_WORDS], F32))
        self.ar = Arena(big, self.ARENA_WORDS)
        self.psb = []
        for i in range(8):
            p = es.enter_context(nc.psum_tensor("psb%d" % i, [128, 512], F32))
            self.psb.append(T(p, Dep("ps%d" % i, excl=True)))
        self.ps_rr = 0
        self.rings = {}

    def ps(self, pin=False):
        pinned = getattr(self, "pinned", None)
        if pinned is None:
            pinned = self.pinned = set()
        while self.ps_rr in pinned:
            self.ps_rr = (self.ps_rr + 1) % 8
        i = self.ps_rr
        t = self.psb[i]
        self.ps_rr = (self.ps_rr + 1) % 8
        if pin:
            pinned.add(i)
        return t

    def unpin_all(self):
        self.pinned = set()

    def tile(self, nelem, dtype=F32, name="", parts=128):
        ap, d = self.ar.alloc(nelem, dtype, parts, name)
        return T(ap, d)

    def ring(self, key, n, nelem, dtype=F32):
        if key not in self.rings:
            off = self.ar.top
            self.rings[key] = [[self.tile(nelem, dtype, "%s%d" % (key, i)) for i in range(n)], 0, off]
        r = self.rings[key]
        t = r[0][r[1] % len(r[0])]
        r[1] += 1
        return t

    def mark(self):
        return self.ar.mark()

    def release(self, m):
        for k in [k for k, r in self.rings.items() if r[2] >= m]:
            del self.rings[k]
        self.ar.release(m)

    def I(self, eng, name, reads, writes, **kw):
        self.S.op(eng, [(name, kw)], [t.d for t in reads], [t.d for t in writes])

    def dma(self, out, in_, reads=(), writes=(), eng="sp"):
        self.S.op(eng, [("dma_start", dict(out=out, in_=in_))], [t.d for t in reads], [t.d for t in writes], dma=True)

    def act(self, out, in_, func, reads, writes, **kw):
        self.I("act", "activation", reads, writes, out=out, in_=in_, func=func, **kw)

    def tt(self, eng, out, in0, in1, op, reads, writes):
        self.I(eng, "tensor_tensor", reads, writes, out=out, in0=in0, in1=in1, op=op)

    def ts(self, eng, out, in0, s1, op0, reads, writes, s2=None, op1=None):
        kw = dict(out=out, in0=in0, scalar1=s1, scalar2=s2, op0=op0)
        if op1 is not None:
            kw["op1"] = op1
        self.I(eng, "tensor_scalar", reads, writes, **kw)

    def stt(self, out, in0, scalar, in1, op0, op1, reads, writes):
        self.I("dve", "scalar_tensor_tensor", reads, writes, out=out, in0=in0, scalar=scalar, in1=in1, op0=op0, op1=op1)

    def copy(self, eng, out, in_, reads, writes):
        if eng == "act":
            self.act(out, in_, AF.Copy, reads, writes)
        else:
            self.I(eng, "tensor_copy", reads, writes, out=out, in_=in_)

    def mm(self, mms, reads, writes):
        self.S.op("pe", [("matmul", m) for m in mms], [t.d for t in reads], [t.d for t in writes])

    def tr(self, out, in_, ident, reads, writes):
        self.S.op("pe", [("transpose", dict(out=out, in_=in_, identity=ident))], [t.d for t in reads], [t.d for t in writes])


def load_w(cx, dram_ap, kc, cw, key="w"):
    n = kc * cw
    wp = getattr(cx, "wpiece", 2048)
    assert n <= wp
    st = cx.ring(key + "_st", 2, wp, F32)
    wb = cx.ring(key + "_wb", 3, wp, BF16)
    cx.dma(st.ap[:, 0:n], dram_ap.rearrange("p k c -> p (k c)"), [], [st])
    cx.copy(getattr(cx, "cast_eng", "pool"), wb.ap[:, 0:n], st.ap[:, 0:n], [st], [wb])
    return wb


def load_w_multi(cx, dram_ap, kc, cw, key="w"):
    per = max(1, getattr(cx, "wpiece", 2048) // cw)
    out = []
    k0 = 0
    while k0 < kc:
        kn = min(per, kc - k0)
        out.append((load_w(cx, dram_ap[:, k0:k0 + kn, :], kn, cw, key), k0, kn))
        k0 += kn
    return out


def load_w_big(cx, dram_ap, kc, cw, name):
    wp = getattr(cx, "wpiece", 2048)
    big = cx.tile(kc * cw, BF16, name)
    per = max(1, wp // cw)
    k0 = 0
    while k0 < kc:
        kn = min(per, kc - k0)
        n = kn * cw
        st = cx.ring("w_st", 2, wp, F32)
        cx.dma(st.ap[:, 0:n], dram_ap[:, k0:k0 + kn, :].rearrange("p k c -> p (k c)"), [], [st])
        cx.copy("pool", big.ap[:, k0 * cw:(k0 + kn) * cw], st.ap[:, 0:n], [st], [big])
        k0 += kn
    return [(big, 0, kc)]


def wslice(pieces, k, cw, m0=0, m1=None):
    m1 = cw if m1 is None else m1
    for wb, k0, kn in pieces:
        if k0 <= k < k0 + kn:
            return wb.ap[:, (k - k0) * cw + m0:(k - k0) * cw + m1]
    raise KeyError(k)


def gemm_block(cx, pieces, cw, act_fn, act_tiles, lo, hi, ks=None, m0=0, m1=None, po=0):
    ps = cx.ps()
    n = hi - lo
    m1 = cw if m1 is None else m1
    if ks is None:
        ks = []
        for wb, k0, kn in pieces:
            ks += list(range(k0, k0 + kn))
    mms = []
    for i, k in enumerate(ks):
        mms.append(dict(out=ps.ap[po:po + m1 - m0, 0:n], lhsT=wslice(pieces, k, cw, m0, m1), rhs=act_fn(k, lo, hi),
                        start=(i == 0), stop=(i == len(ks) - 1)))
    cx.mm(mms, [p[0] for p in pieces] + list(act_tiles), [ps])
    return ps


def rmsnorm_fm(cx, srcs_fn, g_t, out_fn, tiles, ones_bf, nk=KC):
    for (lo, hi) in tiles:
        n = hi - lo
        ps = cx.ps()
        srcs = [srcs_fn(k, lo, hi) for k in range(nk)]
        for k in range(nk):
            sqt = cx.ring("rn_sq", 3, 512, BF16)
            sap, st = srcs[k]
            cx.act(sqt.ap[:, 0:n], sap, AF.Square, [st], [sqt])
            cx.mm([dict(out=ps.ap[:, 0:n], lhsT=ones_bf.ap, rhs=sqt.ap[:, 0:n], start=(k == 0), stop=(k == nk - 1))],
                  [sqt, ones_bf], [ps])
        rinv = cx.ring("rn_rinv", 2, 512, F32)
        cx.act(rinv.ap[:, 0:n], ps.ap[:, 0:n], AF.Ln, [ps], [rinv], scale=1.0 / (nk * 128), bias=EPS)
        cx.act(rinv.ap[:, 0:n], rinv.ap[:, 0:n], AF.Exp, [rinv], [rinv], scale=-0.5)
        for k in range(nk):
            sap, st = srcs[k]
            oap, ot = out_fn(k, lo, hi)
            cx.stt(oap, sap, g_t.ap[:, k:k + 1], rinv.ap[:, 0:n], ALU.mult, ALU.mult, [st, rinv, g_t], [ot])


def load_small(cx, dram_ap, nelem, name, parts=128):
    t = cx.tile(nelem, F32, name, parts)
    cx.dma(t.ap, dram_ap, [], [t])
    return t


def phase_T(cx, io, final, half):
    NT = NTOK_T
    cx.cast_eng = "act"
    base = cx.mark()
    ones_bf = cx.tile(128, BF16, "ones")
    cx.I("dve", "memset", [], [ones_bf], ap=ones_bf.ap, constant=1.0)
    g1 = load_small(cx, io["g_attn"], 16, "g1")
    g2 = load_small(cx, io["g_ffn"], 16, "g2")
    g3 = load_small(cx, io["g_next"], 16, "g3")
    convw = load_small(cx, io["fconv"], 88 * 3, "convw")
    hm = load_small(cx, io["hmask"], 2, "hmask")
    mg_t = cx.tile(KC * NT, BF16, "merged")
    h_t = cx.tile(KC * NT, BF16, "h")
    mv = mg_t.ap.rearrange("p (k t) -> p k t", k=KC)
    hv = h_t.ap.rearrange("p (k t) -> p k t", k=KC)
    M0 = cx.mark()
    cx.S.tag = "T/A1"
    y_t = cx.tile(20 * NT, BF16, "y")
    yv = y_t.ap.rearrange("p (k t) -> p k t", k=20)
    t0 = 0
    xdeps = io.get("x_deps", [])
    ydeps = io.get("y_deps", [])
    if io["x_mode"] == "input":
        xTT = io["xTT"].rearrange("(k p) t -> p k t", p=128)

        def load_x(dst_ap, dst_t, k, lo, hi):
            cx.dma(dst_ap, xTT[:, k, lo:hi], [], [dst_t])
    else:
        xloc_read = io["xloc_read"]
        xhalo = io["xhalo"]

        def load_x(dst_ap, dst_t, k, lo, hi):
            a = lo
            if lo < 2:
                cx.dma(dst_ap[:, 0:2 - lo], xhalo(k)[:, lo:2], xdeps, [dst_t])
                cx.ts("dve", dst_ap[:, 0:2 - lo], dst_ap[:, 0:2 - lo], hm.ap[:, 1:2], ALU.mult, [dst_t, hm], [dst_t])
                a = 2
            if hi > a:
                for src, off, n in xloc_read(k, a - 2, hi - 2):
                    cx.dma(dst_ap[:, a - lo + off:a - lo + off + n], src, xdeps, [dst_t])

    yread = io["yread"]
    kmap = [(0, b) for b in range(4)] + [(1, b) for b in range(4)] + [(0, 4), (0, 5), (1, 4), (1, 5)] + \
           [(0, b) for b in range(6, 10)] + [(1, b) for b in range(6, 10)]
    M1 = cx.mark()
    for k in range(20):
        r_, b_ = kmap[k]
        ta = cx.ring("yh0", 2, 1024, BF16)
        tb_ = cx.ring("yh1", 2, 1024, BF16)
        for q in range(2):
            cx.dma(ta.ap[:, q * 512:(q + 1) * 512], yread(r_, b_, q), ydeps, [ta])
            cx.dma(tb_.ap[:, q * 512:(q + 1) * 512], yread(r_, b_, 2 + q), ydeps, [tb_])
        cx.ts("pool", yv[:, k, 0:2], ta.ap[:, 1022:1024], hm.ap[:, 1:2], ALU.mult, [ta, hm], [y_t])
        cx.ts("pool", tb_.ap, tb_.ap, hm.ap[:, 1:2], ALU.mult, [tb_, hm], [tb_])
        cx.stt(yv[:, k, 2:NT], ta.ap, hm.ap[:, 0:1], tb_.ap, ALU.mult, ALU.add, [ta, tb_, hm], [y_t])
    for (lo, hi) in TT:
        n = hi - lo
        xt = cx.ring("xtile", 1, KC * 512, F32)
        xtv = xt.ap.rearrange("p (k t) -> p k t", k=KC)
        for k in range(KC):
            load_x(xtv[:, k, 0:n], xt, k, lo, hi)
        rmsnorm_fm(cx, lambda k, l, h_: (xtv[:, k, 0:h_ - l], xt), g1, lambda k, l, h_: (hv[:, k, l:h_], h_t),
                   [(lo, hi)], ones_bf)
    cx.release(M1)
    wg = io["wg"]
    wp = io["wp"]
    ksl = [(0, 8), (8, 12), (12, 20)]
    for j in range(16):
        gts = []
        for br in range(3):
            pcs = load_w_multi(cx, wg[br * 16 + j], KC, 128)
            gt = cx.ring("gate", 4, NT, BF16)
            for (lo, hi) in TT:
                ps = gemm_block(cx, pcs, 128, lambda k, l, h_: hv[:, k, l:h_], [h_t], lo, hi)
                cx.act(gt.ap[:, lo:hi], ps.ap[:, 0:hi - lo], AF.Sigmoid, [ps], [gt])
            gts.append(gt)
        pw = load_w_multi(cx, wp[j], 20, 128)
        for (lo, hi) in TT:
            n = hi - lo
            tmp = cx.ring("mtmp", 2, 512, F32)
            for br in range(3):
                k0, k1 = ksl[br]
                ps = gemm_block(cx, pw, 128, lambda k, l, h_: yv[:, k, l:h_], [y_t], lo, hi, ks=list(range(k0, k1)))
                gt = gts[br]
                if br == 0:
                    cx.tt("dve", tmp.ap[:, 0:n], ps.ap[:, 0:n], gt.ap[:, lo:hi], ALU.mult, [ps, gt], [tmp])
                else:
                    t2 = cx.ring("mtmp2", 2, 512, F32)
                    cx.tt("dve", t2.ap[:, 0:n], ps.ap[:, 0:n], gt.ap[:, lo:hi], ALU.mult, [ps, gt], [t2])
                    if br == 1:
                        cx.tt("pool", tmp.ap[:, 0:n], tmp.ap[:, 0:n], t2.ap[:, 0:n], ALU.add, [tmp, t2], [tmp])
                    else:
                        cx.tt("pool", mv[:, j, lo:hi], tmp.ap[:, 0:n], t2.ap[:, 0:n], ALU.add, [tmp, t2], [mg_t])
    cx.release(M0)
    cx.S.tag = "T/A2"
    x_sb = cx.tile(KC * NT, F32, "x_sb")
    xv = x_sb.ap.rearrange("p (k t) -> p k t", k=KC)
    M2 = cx.mark()
    wo = io["wo"]
    for j in range(16):
        pcs = load_w_multi(cx, wo[j], KC, 128)
        xin = cx.ring("xin", 2, NT, F32)
        load_x(xin.ap, xin, j, 0, NT)
        for (lo, hi) in TT:
            ps = gemm_block(cx, pcs, 128, lambda k, l, h_: mv[:, k, l:h_], [mg_t], lo, hi)
            cx.tt("dve", xv[:, j, lo:hi], ps.ap[:, 0:hi - lo], xin.ap[:, lo:hi], ALU.add, [ps, xin], [x_sb])
    cx.release(M2)
    cx.S.tag = "T/F"
    rmsnorm_fm(cx, lambda k, l, h_: (xv[:, k, l:h_], x_sb), g2, lambda k, l, h_: (hv[:, k, l:h_], h_t), TT, ones_bf)
    GRP = 11
    a_t = T(mg_t.ap, mg_t.d)
    av = a_t.ap[:, 0:GRP * 1024].rearrange("p (k t) -> p k t", k=GRP)
    wup = io["wup"]
    wdn = io["wdn"]
    cwv = convw.ap.rearrange("p (b t) -> p b t", t=3)
    for g in range(NFB // GRP):
        for jj in range(GRP):
            jb = g * GRP + jj
            us = []
            for part in range(2):
                blk = part * NFB + jb
                pcs = load_w_multi(cx, wup[blk], KC, 128)
                u = cx.ring("u_f32", 2, NT, F32)
                for (lo, hi) in TT:
                    ps = gemm_block(cx, pcs, 128, lambda k, l, h_: hv[:, k, l:h_], [h_t], lo, hi)
                    cx.copy("act", u.ap[:, lo:hi], ps.ap[:, 0:hi - lo], [ps], [u])
                c = cx.ring("c_f32", 2, 1024, F32)
                cx.ts("dve", c.ap, u.ap[:, 2:NT], cwv[:, blk, 2:3], ALU.mult, [u, convw], [c])
                cx.stt(c.ap, u.ap[:, 1:NT - 1], cwv[:, blk, 1:2], c.ap, ALU.mult, ALU.add, [u, convw, c], [c])
                cx.stt(c.ap, u.ap[:, 0:NT - 2], cwv[:, blk, 0:1], c.ap, ALU.mult, ALU.add, [u, convw, c], [c])
                us.append(c)
            sg = cx.ring("silu", 2, 1024, F32)
            cx.act(sg.ap, us[0].ap, AF.Silu, [us[0]], [sg])
            cx.tt("pool", av[:, jj, :], sg.ap, us[1].ap, ALU.mult, [sg, us[1]], [a_t])
        for j in range(16):
            pcs = load_w_multi(cx, wdn[j][:, g * GRP:(g + 1) * GRP, :], GRP, 128)
            for ti in range(2):
                lo, hi = ti * 512, ti * 512 + 512
                ps = gemm_block(cx, pcs, 128, lambda k, l, h_: av[:, k, l:h_], [a_t], lo, hi)
                cx.tt("dve", xv[:, j, 2 + lo:2 + hi], ps.ap[:, 0:512], xv[:, j, 2 + lo:2 + hi], ALU.add, [ps, x_sb], [x_sb])
    cx.release(M2)
    outT = io["outT"].rearrange("(k p) t -> p k t", p=128) if final else None
    odeps = io.get("out_deps", [])
    if not final:
        for k in range(KC):
            io["xwrite"](cx, k, xv[:, k, 2:NT], [x_sb])
    else:
        o_t = T(h_t.ap.bitcast(F32)[:, 0:KC * 512], h_t.d)
        ov = o_t.ap.rearrange("p (k t) -> p k t", k=KC)
        for ti in range(2):
            lo, hi = 2 + ti * 512, 2 + ti * 512 + 512
            rmsnorm_fm(cx, lambda k, l, h_: (xv[:, k, l:h_], x_sb), g3, lambda k, l, h_: (ov[:, k, 0:512], o_t),
                       [(lo, hi)], ones_bf)
            for k in range(KC):
                cx.dma(outT[:, k, t0 + lo - 2:t0 + lo - 2 + 512], ov[:, k, :], [o_t], odeps)
    cx.release(base)


import os
ATT_G = os.environ.get("ATT_G", "012")
ATT_STOP = int(os.environ.get("ATT_STOP", "99"))
ATT_SUB = int(os.environ.get("ATT_SUB", "3"))
RW_DBG = os.environ.get("RW_DBG", "")
ATT_V = os.environ.get("ATT_V", "ab")
NCH = 16
C_ID, C_MUI, C_MUS, C_MLI, C_BO, C_BS, C_S127, C_MLS = 0, 128, 256, 384, 512, 640, 642, 770
C_BD8, C_O16, C_O32, C_O64, C_O128 = 898, 1026, 1154, 1282, 1410
CST_N = 1538


def make_consts():
    c = np.zeros((128, CST_N), np.float32)
    j = np.arange(128)[:, None]
    i = np.arange(128)[None, :]
    c[:, C_ID:C_ID + 128] = (i == j)
    c[:, C_MUI:C_MUI + 128] = (i >= j)
    c[:, C_MUS:C_MUS + 128] = (i > j)
    c[:, C_MLI:C_MLI + 128] = (j >= i)
    c[:, C_BO:C_BO + 128] = ((i // 64) == (j // 64))
    c[:, C_BS:C_BS + 2] = (np.arange(2)[None, :] == (j // 64))
    c[:, C_S127:C_S127 + 128] = (j == 127)
    c[:, C_MLS:C_MLS + 128] = (j > i)
    bd = lambda s_: (i // s_) == (j // s_)
    c[:, C_BD8:C_BD8 + 128] = bd(8)
    c[:, C_O16:C_O16 + 128] = bd(16) & ~bd(8)
    c[:, C_O32:C_O32 + 128] = bd(32) & ~bd(16)
    c[:, C_O64:C_O64 + 128] = bd(64) & ~bd(32)
    c[:, C_O128:C_O128 + 128] = ~bd(64)
    return c


def make_rope():
    half = 16
    inv = 500000.0 ** (-np.arange(half, dtype=np.float32) / half)
    ang = np.arange(SEQ, dtype=np.float32)[None, :] * inv[:, None]
    cos = np.cos(ang).astype(np.float32)
    sin = np.sin(ang).astype(np.float32)
    r = np.zeros((32, 2 * SEQ), np.float32)
    r[0:16, 0:SEQ] = cos
    r[16:32, 0:SEQ] = cos
    r[0:16, SEQ:] = -sin
    r[16:32, SEQ:] = sin
    return r


class Consts:
    pass


def setup_consts(cx, io):
    K = Consts()
    cf = load_small(cx, io["cst"], CST_N, "cst_f32")
    cb = cx.tile(CST_N, BF16, "cst_bf")
    cx.copy("dve", cb.ap, cf.ap, [cf], [cb])
    K.cf, K.cb = cf, cb
    K.ones = cx.tile(128, BF16, "ones")
    cx.I("dve", "memset", [], [K.ones], ap=K.ones.ap, constant=1.0)
    K.m4 = {}
    for nm_, off in (("id", C_ID), ("bd8", C_BD8), ("o16", C_O16), ("o32", C_O32), ("o64", C_O64), ("o128", C_O128)):
        t = cx.tile(512, BF16, "m4" + nm_)
        for q in range(4):
            cx.copy("pool", t.ap[:, q * 128:(q + 1) * 128], cb.ap[:, off:off + 128], [cb], [t])
        K.m4[nm_] = t
    return K


def dplr_head(cx, K, nm, dk, dv, pb, Kg, Bg, Ag, Rg, gtiles, Wst, Win, wtiles, As, Rs, egam, Kn, Bn, V, ntiles, PC, pctiles,
              scale_all, out_cb, defer=False):
    m0 = cx.mark()
    tag0 = cx.S.tag
    cx.S.tag = tag0 + "/dplr_prep"
    idb = K.cb.ap[:, C_ID:C_ID + 128]
    neg = Bg is None
    AakT = cx.tile(NCH * 128, BF16, nm + "aak")
    ArkT = cx.tile(NCH * 128, BF16, nm + "ark")
    ArbT = cx.tile(NCH * 128, BF16, nm + "arb")
    TT_ = cx.tile(NCH * 128, BF16, nm + "TT")
    sl = lambda t, c: t.ap[:, c * 128:(c + 1) * 128]
    m1 = cx.mark()
    HC = 4
    hs = lambda t, q4: t.ap[:, q4 * 512:(q4 + 1) * 512]
    h1 = lambda t, cl: t.ap[:, cl * 128:(cl + 1) * 128]

    def mm4(dst_ps, lhs_t, rhs_t, q4):
        cx.mm([dict(out=dst_ps.ap[:, q * 128:(q + 1) * 128], lhsT=h1(lhs_t, q4 * 4 + q), rhs=h1(rhs_t, q4 * 4 + q), start=True, stop=True)
               for q in range(4)], [lhs_t, rhs_t], [dst_ps])

    def stream(half, tl_):
        U, N, Ua, Na, Nb, Ub_, Nc, Uc, P, Q, Z1b, Z2b = tl_
        for cl in range(HC):
            c = half * HC + cl
            ps = cx.ps()
            mms = [dict(out=ps.ap[:, 0:128], lhsT=Kg(c), rhs=Ag(c), start=True, stop=True),
                   dict(out=ps.ap[:, 128:256], lhsT=Kg(c), rhs=Rg(c), start=True, stop=True)]
            if not neg:
                mms += [dict(out=ps.ap[:, 256:384], lhsT=Bg(c), rhs=Ag(c), start=True, stop=True),
                        dict(out=ps.ap[:, 384:512], lhsT=Bg(c), rhs=Rg(c), start=True, stop=True)]
            cx.mm(mms, gtiles, [ps])
            cx.tt("dve", sl(AakT, c), ps.ap[:, 0:128], Wst(c), ALU.mult, [ps] + wtiles, [AakT])
            cx.tt("dve", sl(ArkT, c), ps.ap[:, 128:256], Win(c), ALU.mult, [ps] + wtiles, [ArkT])
            if neg:
                cx.stt(h1(U, cl), ps.ap[:, 0:128], -1.0, Wst(c), ALU.mult, ALU.mult, [ps] + wtiles, [U])
                cx.stt(sl(ArbT, c), ps.ap[:, 128:256], -1.0, Win(c), ALU.mult, ALU.mult, [ps] + wtiles, [ArbT])
            else:
                cx.tt("dve", h1(U, cl), ps.ap[:, 256:384], Wst(c), ALU.mult, [ps] + wtiles, [U])
                cx.tt("dve", sl(ArbT, c), ps.ap[:, 384:512], Win(c), ALU.mult, [ps] + wtiles, [ArbT])
            if cl % 2 == 1:
                yield
        q4 = 0
        ps = cx.ps()
        pv = ps.ap.bitcast(BF16)
        for q in range(4):
            cx.tr(pv[:, q * 128:(q + 1) * 128], h1(U, q), idb, [U, K.cb], [ps])
        cx.copy("act", hs(N, q4), pv[:, 0:512], [ps], [N])
        yield
        cx.tt("pool", hs(Ua, q4), hs(U, q4), K.m4["bd8"].ap, ALU.mult, [U, K.m4["bd8"]], [Ua])
        cx.tt("pool", hs(Na, q4), hs(N, q4), K.m4["bd8"].ap, ALU.mult, [N, K.m4["bd8"]], [Na])
        yield
        for (dn, du, sn, su) in ((Nb, Ub_, Na, Ua), (Nc, Uc, Nb, Ub_)):
            ps = cx.ps()
            mm4(ps, su, sn, q4)
            cx.copy("act", hs(dn, q4), ps.ap[:, 0:512], [ps], [dn])
            ps = cx.ps()
            mm4(ps, sn, su, q4)
            cx.copy("act", hs(du, q4), ps.ap[:, 0:512], [ps], [du])
            yield
        cx.tt("pool", hs(P, q4), hs(Ua, q4), K.m4["id"].ap, ALU.add, [Ua, K.m4["id"]], [P])
        cx.tt("pool", hs(Q, q4), hs(Na, q4), K.m4["id"].ap, ALU.add, [Na, K.m4["id"]], [Q])
        yield
        for (ln, lu) in ((Nb, Ub_), (Nc, Uc)):
            ps = cx.ps()
            mm4(ps, ln, P, q4)
            cx.tt("dve", hs(P, q4), ps.ap[:, 0:512], hs(P, q4), ALU.add, [ps, P], [P])
            ps = cx.ps()
            mm4(ps, lu, Q, q4)
            cx.tt("dve", hs(Q, q4), ps.ap[:, 0:512], hs(Q, q4), ALU.add, [ps, Q], [Q])
            yield
        NO, UO, P2, Q2 = Ua, Na, Nb, Ub_
        curP, curQ, nxtP, nxtQ = P, Q, P2, Q2
        for li, mk in enumerate(("o16", "o32", "o64", "o128")):
            last = (li == 3)
            cx.tt("pool", hs(NO, q4), hs(N, q4), K.m4[mk].ap, ALU.mult, [N, K.m4[mk]], [NO])
            if not last:
                cx.tt("pool", hs(UO, q4), hs(U, q4), K.m4[mk].ap, ALU.mult, [U, K.m4[mk]], [UO])
            yield
            ps = cx.ps()
            mm4(ps, NO, curP, q4)
            cx.copy("act", hs(Z1b, q4), ps.ap[:, 0:512], [ps], [Z1b])
            if not last:
                ps = cx.ps()
                mm4(ps, UO, curQ, q4)
                cx.copy("act", hs(Z2b, q4), ps.ap[:, 0:512], [ps], [Z2b])
            yield
            ps = cx.ps()
            mm4(ps, curQ, Z1b, q4)
            if last:
                c0_ = (half * HC) * 128
                cx.tt("dve", TT_.ap[:, c0_:c0_ + 512], ps.ap[:, 0:512], hs(curP, q4), ALU.add, [ps, curP], [TT_])
            else:
                cx.tt("dve", hs(nxtP, q4), ps.ap[:, 0:512], hs(curP, q4), ALU.add, [ps, curP], [nxtP])
                ps = cx.ps()
                mm4(ps, curP, Z2b, q4)
                cx.tt("dve", hs(nxtQ, q4), ps.ap[:, 0:512], hs(curQ, q4), ALU.add, [ps, curQ], [nxtQ])
            curP, curQ, nxtP, nxtQ = nxtP, nxtQ, curP, curQ
            yield

    tlA = [cx.tile(HC * 128, BF16, nm + "ivA%d" % i) for i in range(12)]
    tlB = [cx.tile(HC * 128, BF16, nm + "ivB%d" % i) for i in range(12)]
    for pair in range(0, NCH // HC, 2):
        gens = [stream(pair, tlA), stream(pair + 1, tlB)]
        alive = True
        while alive:
            alive = False
            for g_ in gens:
                try:
                    next(g_)
                    alive = True
                except StopIteration:
                    pass
    cx.release(m1)
    AV = None
    if egam is not None:
        AV = cx.tile(NCH * dv, F32, nm + "AV")
        for c in range(NCH):
            ps = cx.ps()
            cx.mm([dict(out=ps.ap[:, 0:dv], lhsT=sl(AakT, c), rhs=V(c), start=True, stop=True)], [AakT] + ntiles, [ps])
            cx.copy("act", AV.ap[:, c * dv:(c + 1) * dv], ps.ap[:, 0:dv], [ps], [AV])
    St = cx.tile(dv, F32, nm + "S")
    Sb = cx.tile(dv, BF16, nm + "Sb")
    cx.I("dve", "memset", [], [St], ap=St.ap, constant=0.0)
    cx.I("dve", "memset", [], [Sb], ap=Sb.ap, constant=0.0)
    rows = slice(pb, pb + dk)

    def step(c):
        cx.S.tag = tag0 + "/dplr_seq"
        Xb = cx.ring(nm + "Xb", 2, dv, BF16)
        Ub = cx.ring(nm + "Ub", 2, dv, BF16)
        psX = cx.ps()
        if egam is None:
            cx.mm([dict(out=psX.ap[:, 0:dv], lhsT=As(c), rhs=Sb.ap[rows, :], start=True, stop=False),
                   dict(out=psX.ap[:, 0:dv], lhsT=sl(AakT, c), rhs=V(c), start=False, stop=True)],
                  gtiles + [Sb, AakT] + ntiles, [psX])
            cx.copy("act", Xb.ap, psX.ap[:, 0:dv], [psX], [Xb])
        else:
            cx.mm([dict(out=psX.ap[:, 0:dv], lhsT=As(c), rhs=Sb.ap[rows, :], start=True, stop=True)], gtiles + [Sb], [psX])
            cx.stt(Xb.ap, psX.ap[:, 0:dv], egam(c), AV.ap[:, c * dv:(c + 1) * dv], ALU.mult, ALU.add, [psX, AV] + pctiles, [Xb])
        psU = cx.ps()
        cx.mm([dict(out=psU.ap[:, 0:dv], lhsT=sl(TT_, c), rhs=Xb.ap, start=True, stop=True)], [TT_, Xb], [psU])
        cx.copy("act", Ub.ap, psU.ap[:, 0:dv], [psU], [Ub])
        if egam is None:
            psY = cx.ps()
            cx.mm([dict(out=psY.ap[:, 0:dv], lhsT=Rs(c), rhs=Sb.ap[rows, :], start=True, stop=False),
                   dict(out=psY.ap[:, 0:dv], lhsT=sl(ArbT, c), rhs=Ub.ap, start=False, stop=False),
                   dict(out=psY.ap[:, 0:dv], lhsT=sl(ArkT, c), rhs=V(c), start=False, stop=True)],
                  gtiles + [Sb, ArbT, ArkT, Ub] + ntiles, [psY])
            out_cb(c, psY.ap[:, 0:dv], psY)
        else:
            psY1 = cx.ps()
            cx.mm([dict(out=psY1.ap[:, 0:dv], lhsT=Rs(c), rhs=Sb.ap[rows, :], start=True, stop=True)], gtiles + [Sb], [psY1])
            psY2 = cx.ps()
            cx.mm([dict(out=psY2.ap[:, 0:dv], lhsT=sl(ArbT, c), rhs=Ub.ap, start=True, stop=False),
                   dict(out=psY2.ap[:, 0:dv], lhsT=sl(ArkT, c), rhs=V(c), start=False, stop=True)],
                  [ArbT, ArkT, Ub] + ntiles, [psY2])
            y2 = cx.ring(nm + "y2", 2, dv, F32)
            cx.copy("act", y2.ap, psY2.ap[:, 0:dv], [psY2], [y2])
            yo = cx.ring(nm + "yo", 2, dv, F32)
            cx.stt(yo.ap, psY1.ap[:, 0:dv], egam(c), y2.ap, ALU.mult, ALU.add, [psY1, y2] + pctiles, [yo])
            out_cb(c, yo.ap, yo)
        psS = cx.ps()
        cx.mm([dict(out=psS.ap[rows, 0:dv], lhsT=Bn(c), rhs=Ub.ap, start=True, stop=False),
               dict(out=psS.ap[rows, 0:dv], lhsT=Kn(c), rhs=V(c), start=False, stop=True)], ntiles + [Ub], [psS])
        if scale_all:
            cx.tt("dve", St.ap[rows, :], psS.ap[rows, 0:dv], St.ap[rows, :], ALU.add, [psS, St], [St])
            cx.ts("dve", St.ap[rows, :], St.ap[rows, :], PC(c), ALU.mult, [St] + pctiles, [St])
        else:
            cx.stt(St.ap[rows, :], St.ap[rows, :], PC(c), psS.ap[rows, 0:dv], ALU.mult, ALU.add, [St, psS] + pctiles, [St])
        cx.copy("act", Sb.ap[rows, :], St.ap[rows, :], [St], [Sb])

    def finish():
        cx.release(m0)
    if defer:
        return step, finish
    for c in range(NCH):
        step(c)
    finish()


def ln_exp_rinv(cx, out_ap, in_ap, reads, wt, scale=1.0, bias=1e-24):
    cx.act(out_ap, in_ap, AF.Ln, reads, [wt], scale=scale, bias=bias)
    cx.act(out_ap, out_ap, AF.Exp, [wt], [wt], scale=-0.5)


def mixer_attention(cx, K, io, hT, hv):
    m0 = cx.mark()
    wat = io["wat"]
    wsw = io["wsw"]
    rope = load_small(cx, io["rope"], 2 * SEQ, "rope", parts=32)
    cosv = rope.ap[:, 0:SEQ]
    sinv = rope.ap[:, SEQ:2 * SEQ]
    mdiag = K.cb.ap[:, C_MUI:C_MUI + 128]
    mprev = K.cb.ap[:, C_MLI:C_MLI + 128]
    mcomb = cx.tile(256, BF16, "mcomb")
    cx.copy("dve", mcomb.ap[:, 0:128], mdiag, [K.cb], [mcomb])
    cx.copy("dve", mcomb.ap[:, 128:256], mprev, [K.cb], [mcomb])
    if ATT_STOP <= 1:
        return
    DIL = [1, 4, 16]
    scale = 128.0 ** -0.5
    for hl in range(2):
        m1 = cx.mark()
        cx.S.tag = "M/att/proj"
        qT = [cx.tile(SEQ, BF16, "qT%d" % g) for g in range(3)]
        kT = [cx.tile(SEQ, BF16, "kT%d" % g) for g in range(3)]
        Vt = [cx.tile(NCH * 128, BF16, "V%d" % g) for g in range(3)]
        for g in range(3):
            d = DIL[g]
            for which, dst in ((0, qT[g]), (1, kT[g])):
                pcs = load_w_multi(cx, wat[(hl * 3 + g) * 3 + which], KC, 128)
                pcs_sw = load_w_multi(cx, wsw[(hl * 3 + g) * 2 + which], KC, 32)
                for tb in range(4):
                    lo, hi = tb * 512, tb * 512 + 512
                    ps = gemm_block(cx, pcs, 128, lambda k, l, h_: hv[:, k, l:h_], [hT], lo, hi)
                    if ATT_SUB >= 1:
                        ps2 = gemm_block(cx, pcs_sw, 32, lambda k, l, h_: hv[:, k, l:h_], [hT], lo, hi)
                    t1 = cx.ring("rp1", 2, 512, F32)
                    t2 = cx.ring("rp2", 2, 512, F32)
                    if ATT_SUB >= 2:
                        if "a" in ATT_V:
                            cx.tt("dve", t1.ap[0:32, :], ps.ap[0:32, 0:512], cosv[:, lo:hi], ALU.mult, [ps, rope], [t1])
                        if "b" in ATT_V:
                            cx.tt("dve", t2.ap[0:32, :], ps2.ap[0:32, 0:512], sinv[:, lo:hi], ALU.mult, [ps2, rope], [t2])
                        if "c" in ATT_V:
                            cx.tt("dve", t1.ap[0:32, :], ps.ap[0:32, 0:512], K.cf.ap[0:32, 0:512], ALU.mult, [ps, K.cf], [t1])
                        if "e" in ATT_V:
                            cx.tt("dve", t1.ap[:, :], ps.ap[:, 0:512], K.cf.ap[:, 0:512], ALU.mult, [ps, K.cf], [t1])
                        if "g" in ATT_V:
                            cx.tt("dve", t1.ap[0:32, :], ps.ap[0:32, 0:512], K.cf.ap[0:32, 0:512], ALU.mult, [ps, K.cf], [t1])
                        if "d" in ATT_V:
                            cx.tt("dve", t1.ap[0:32, :], t2.ap[0:32, :], cosv[:, lo:hi], ALU.mult, [t2, rope], [t1])
                    cx.copy("act", dst.ap[:, lo:hi], ps.ap[:, 0:512], [ps, t1] if "s" in ATT_V else [ps], [dst])
                    if ATT_SUB >= 3:
                        cx.tt("pool", dst.ap[0:32, lo:hi], t1.ap[0:32, :], t2.ap[0:32, :], ALU.add, [t1, t2], [dst])
            if ATT_STOP <= 2:
                return
            pcs = load_w_multi(cx, wat[(hl * 3 + g) * 3 + 2], KC, 128)
            nb = NCH // d
            for b4 in range(4):
                ps = cx.ps()
                mms = []
                for q in range(4):
                    blk = b4 * 4 + q
                    r, b = blk // nb, blk % nb
                    t0 = r + d * 128 * b
                    for k in range(KC):
                        mms.append(dict(out=ps.ap[:, q * 128:(q + 1) * 128], lhsT=hv[:, k, t0:t0 + d * 127 + 1:d],
                                        rhs=wslice(pcs, k, 128), start=(k == 0), stop=(k == KC - 1)))
                cx.mm(mms, [p[0] for p in pcs] + [hT], [ps])
                cx.copy("act", Vt[g].ap[:, b4 * 512:(b4 + 1) * 512], ps.ap[:, 0:512], [ps], [Vt[g]])
        if ATT_STOP <= 3:
            return
        cx.S.tag = "M/att/core"
        for Tb in range(4):
            pso = cx.ps(pin=True)
            psd = cx.ps(pin=True)
            first = [True]

            def unit(g, kslices, qslice, nq, masks):
                pss = cx.ps()
                nk = len(kslices)
                cx.mm([dict(out=pss.ap[:, i * nq:(i + 1) * nq], lhsT=kT[g].ap[:, ks], rhs=qT[g].ap[:, qslice], start=True, stop=True)
                       for i, (ks, vb) in enumerate(kslices)], [kT[g], qT[g]], [pss])
                pe_ = cx.ring("pexp", 3, 256, BF16)
                pm = cx.ring("pmask", 3, 256, BF16)
                cx.act(pe_.ap[:, 0:nk * nq], pss.ap[:, 0:nk * nq], AF.Exp, [pss], [pe_], scale=scale)
                cx.tt("pool", pm.ap[:, 0:nk * nq], pe_.ap[:, 0:nk * nq], masks, ALU.mult, [pe_, mcomb, K.cb], [pm])
                mmo, mmd = [], []
                qs0 = qslice.start - Tb * 512
                st = qslice.step or 1
                ocols = slice(qs0, qs0 + (nq - 1) * st + 1, st)
                for i, (ks, vb) in enumerate(kslices):
                    f = first[0]
                    first[0] = False
                    mmo.append(dict(out=pso.ap[:, ocols], lhsT=Vt[g].ap[:, vb * 128:(vb + 1) * 128], rhs=pm.ap[:, i * nq:(i + 1) * nq],
                                    start=f, stop=False, skip_group_check=True))
                    mmd.append(dict(out=psd.ap[:, ocols], lhsT=K.ones.ap, rhs=pm.ap[:, i * nq:(i + 1) * nq],
                                    start=f, stop=False, skip_group_check=True))
                cx.mm(mmo, [Vt[g], pm], [pso])
                cx.mm(mmd, [K.ones, pm], [psd])

            for qb in range(4 * Tb, 4 * Tb + 4):
                ks = [(slice(qb * 128, qb * 128 + 128), qb)]
                if qb > 0:
                    ks.append((slice((qb - 1) * 128, qb * 128), qb - 1))
                unit(0, ks, slice(qb * 128, qb * 128 + 128), 128, mcomb.ap[:, 0:128 * len(ks)])
            for r in range(4 if "1" in ATT_G else 0):
                tq = r + 4 * 128 * Tb
                ks = [(slice(tq, tq + 4 * 127 + 1, 4), r * 4 + Tb)]
                if Tb > 0:
                    tk = r + 4 * 128 * (Tb - 1)
                    ks.append((slice(tk, tk + 4 * 127 + 1, 4), r * 4 + Tb - 1))
                unit(1, ks, slice(tq, tq + 4 * 127 + 1, 4), 128, mcomb.ap[:, 0:128 * len(ks)])
            for r in range(16 if "2" in ATT_G else 0):
                tq = r + 16 * 32 * Tb
                ks = [(slice(r, r + 16 * 127 + 1, 16), r)]
                unit(2, ks, slice(tq, tq + 16 * 31 + 1, 16), 32, mdiag[:, 32 * Tb:32 * Tb + 32])
            rd = cx.ring("rden", 2, 512, F32)
            cx.I("dve", "reciprocal", [psd], [rd], out=rd.ap, in_=psd.ap[:, 0:512])
            yo = cx.ring("ybo", 2, 512, BF16)
            cx.tt("dve", yo.ap, pso.ap[:, 0:512], rd.ap, ALU.mult, [pso, rd], [yo])
            io["ywrite"](cx, io["rb"] + hl, Tb * 512, Tb * 512 + 512, yo.ap, [yo], io["ydep"]["b"])
            cx.unpin_all()
            if ATT_STOP <= 4:
                return
        cx.release(m1)
    cx.release(m0)


def mixer_gdn(cx, K, io, hT, hv):
    m0 = cx.mark()
    wc = io["wc"]
    wba = io["wba"]
    gconv = load_small(cx, io["gconv"], 12 * 4, "gconv")
    gcv = gconv.ap.rearrange("p (b t) -> p b t", t=4)
    gpar = load_small(cx, io["gpar"], 8 + 128, "gpar")
    mui = K.cf.ap[:, C_MUI:C_MUI + 128]
    mus = K.cf.ap[:, C_MUS:C_MUS + 128]
    mls = K.cf.ap[:, C_MLS:C_MLS + 128]
    idb = K.cb.ap[:, C_ID:C_ID + 128]
    pcs = load_w_multi(cx, wba, KC, 8)
    ps = cx.ps()
    mms = []
    for c in range(NCH):
        for k in range(KC):
            mms.append(dict(out=ps.ap[:, c * 8:(c + 1) * 8], lhsT=hv[:, k, c * 128:(c + 1) * 128], rhs=wslice(pcs, k, 8),
                            start=(k == 0), stop=(k == KC - 1)))
    cx.mm(mms, [p[0] for p in pcs] + [hT], [ps])
    ba = cx.tile(NCH * 8, F32, "ba")
    cx.copy("act", ba.ap, ps.ap[:, 0:NCH * 8], [ps], [ba])
    bav = ba.ap.rearrange("p (c e) -> p c e", e=8)
    beta = cx.tile(NCH * 4, F32, "beta")
    betav = beta.ap.rearrange("p (c e) -> p c e", e=4)
    cx.act(betav, bav[:, :, 0:4], AF.Sigmoid, [ba], [beta])
    gg = cx.tile(NCH * 4, F32, "gg")
    ggv = gg.ap.rearrange("p (c e) -> p c e", e=4)
    for hh in range(4):
        cx.act(ggv[:, :, hh], bav[:, :, 4 + hh], AF.Exp, [ba, gpar], [gg], bias=gpar.ap[:, 4 + hh:5 + hh])
    cx.act(gg.ap, gg.ap, AF.Ln, [gg], [gg], bias=1.0)
    ea = cx.tile(4, F32, "expA")
    cx.act(ea.ap, gpar.ap[:, 0:4], AF.Exp, [gpar], [ea])
    for hh in range(4):
        cx.ts("dve", ggv[:, :, hh], ggv[:, :, hh], ea.ap[:, hh:hh + 1], ALU.mult, [gg, ea], [gg], s2=-1.0, op1=ALU.mult)
    ps = cx.ps()
    cx.mm([dict(out=ps.ap[:, 0:64], lhsT=mui, rhs=gg.ap, start=True, stop=True)], [K.cf, gg], [ps])
    gam = cx.tile(64, F32, "gam")
    cx.copy("act", gam.ap, ps.ap[:, 0:64], [ps], [gam])
    egam = cx.tile(64, F32, "egam")
    cx.act(egam.ap, gam.ap, AF.Exp, [gam], [egam])
    ps = cx.ps()
    cx.mm([dict(out=ps.ap[:, 0:64], lhsT=K.cf.ap[:, C_S127:C_S127 + 128], rhs=gam.ap, start=True, stop=True)], [K.cf, gam], [ps])
    pcall = cx.tile(64, F32, "pcall")
    cx.act(pcall.ap, ps.ap[:, 0:64], AF.Exp, [ps], [pcall])
    wk = cx.tile(64, F32, "wk")
    cx.tt("dve", wk.ap, ps.ap[:, 0:64], gam.ap, ALU.subtract, [ps, gam], [wk])
    cx.act(wk.ap, wk.ap, AF.Exp, [wk], [wk])
    cx.tt("dve", wk.ap, wk.ap, beta.ap, ALU.mult, [wk, beta], [wk])
    nwk = cx.tile(64, F32, "nwk")
    cx.ts("dve", nwk.ap, wk.ap, -1.0, ALU.mult, [wk], [nwk])
    for hh in range(4):
        m1 = cx.mark()
        cx.S.tag = "M/gdn/prep"
        qT = cx.tile(SEQ, BF16, "gq")
        kT = cx.tile(SEQ, BF16, "gk")
        Vn = cx.tile(NCH * 128, BF16, "gV")
        Kn = cx.tile(NCH * 128, BF16, "gKn")
        Bn = cx.tile(NCH * 128, BF16, "gBn")
        gate = cx.tile(NCH * 128, BF16, "ggate")
        Wst = cx.tile(NCH * 128, F32, "gWst")
        Win = cx.tile(NCH * 128, F32, "gWin")
        m2 = cx.mark()
        vT = cx.tile(SEQ, BF16, "gvT")
        for which, dst in ((0, qT), (1, kT), (2, vT)):
            pcs = load_w_multi(cx, wc[which * 4 + hh], KC, 128)
            zp = cx.ring("gzp", 2, 3 + SEQ, F32)
            cx.I("dve", "memset", [], [zp], ap=zp.ap[:, 0:3], constant=0.0)
            for tb in range(4):
                lo, hi = tb * 512, tb * 512 + 512
                ps = gemm_block(cx, pcs, 128, lambda k, l, h_: hv[:, k, l:h_], [hT], lo, hi)
                cx.copy("act", zp.ap[:, 3 + lo:3 + hi], ps.ap[:, 0:512], [ps], [zp])
            cv = cx.ring("gcv", 2, SEQ, F32)
            bi = which * 4 + hh
            cx.ts("dve", cv.ap, zp.ap[:, 3:3 + SEQ], gcv[:, bi, 3:4], ALU.mult, [zp, gconv], [cv])
            for tap in range(3):
                cx.stt(cv.ap, zp.ap[:, tap:tap + SEQ], gcv[:, bi, tap:tap + 1], cv.ap, ALU.mult, ALU.add, [zp, gconv, cv], [cv])
            cx.act(cv.ap, cv.ap, AF.Silu, [cv], [cv])
            if which == 2:
                cx.copy("pool", dst.ap, cv.ap, [cv], [dst])
            else:
                for tb in range(4):
                    lo, hi = tb * 512, tb * 512 + 512
                    sq = cx.ring("gsq", 2, 512, BF16)
                    cx.act(sq.ap, cv.ap[:, lo:hi], AF.Square, [cv], [sq])
                    ps = cx.ps()
                    cx.mm([dict(out=ps.ap[:, 0:512], lhsT=K.ones.ap, rhs=sq.ap, start=True, stop=True)], [K.ones, sq], [ps])
                    ri = cx.ring("gri", 2, 512, F32)
                    ln_exp_rinv(cx, ri.ap, ps.ap[:, 0:512], [ps], ri)
                    if which == 0:
                        cx.stt(dst.ap[:, lo:hi], cv.ap[:, lo:hi], 128.0 ** -0.5, ri.ap, ALU.mult, ALU.mult, [cv, ri], [dst])
                    else:
                        cx.tt("dve", dst.ap[:, lo:hi], cv.ap[:, lo:hi], ri.ap, ALU.mult, [cv, ri], [dst])
        for c4 in range(4):
            ps = cx.ps()
            pv = ps.ap.bitcast(BF16)
            for q in range(4):
                c = c4 * 4 + q
                cx.tr(pv[:, q * 128:(q + 1) * 128], vT.ap[:, c * 128:(c + 1) * 128], idb, [vT, K.cb], [ps])
            cx.copy("act", Vn.ap[:, c4 * 512:(c4 + 1) * 512], pv[:, 0:512], [ps], [Vn])
            ps = cx.ps()
            pv = ps.ap.bitcast(BF16)
            for q in range(4):
                c = c4 * 4 + q
                cx.tr(pv[:, q * 128:(q + 1) * 128], kT.ap[:, c * 128:(c + 1) * 128], idb, [kT, K.cb], [ps])
            for q in range(4):
                c = c4 * 4 + q
                col = c * 4 + hh
                cx.ts("dve", Kn.ap[:, c * 128:(c + 1) * 128], pv[:, q * 128:(q + 1) * 128], wk.ap[:, col:col + 1], ALU.mult, [ps, wk], [Kn])
                cx.ts("dve", Bn.ap[:, c * 128:(c + 1) * 128], pv[:, q * 128:(q + 1) * 128], nwk.ap[:, col:col + 1], ALU.mult, [ps, nwk], [Bn])
        pcs = load_w_multi(cx, wc[12 + hh], KC, 128)
        for c4 in range(4):
            ps = cx.ps()
            mms = []
            for q in range(4):
                c = c4 * 4 + q
                for k in range(KC):
                    mms.append(dict(out=ps.ap[:, q * 128:(q + 1) * 128], lhsT=hv[:, k, c * 128:(c + 1) * 128], rhs=wslice(pcs, k, 128),
                                    start=(k == 0), stop=(k == KC - 1)))
            cx.mm(mms, [p[0] for p in pcs] + [hT], [ps])
            cx.act(gate.ap[:, c4 * 512:(c4 + 1) * 512], ps.ap[:, 0:512], AF.Silu, [ps], [gate])
        for c in range(NCH):
            col = c * 4 + hh
            g2 = cx.ring("gG2", 2, 128, F32)
            cx.ts("dve", g2.ap, mui, gg.ap[:, col:col + 1], ALU.mult, [K.cf, gg], [g2])
            ps = cx.ps()
            cx.mm([dict(out=ps.ap[:, 0:128], lhsT=mls, rhs=g2.ap, start=True, stop=True)], [K.cf, g2], [ps])
            ex = cx.ring("gex", 2, 128, F32)
            cx.act(ex.ap, ps.ap[:, 0:128], AF.Exp, [ps], [ex])
            cx.stt(Wst.ap[:, c * 128:(c + 1) * 128], ex.ap, beta.ap[:, col:col + 1], mus, ALU.mult, ALU.mult, [ex, beta, K.cf], [Wst])
            cx.stt(Win.ap[:, c * 128:(c + 1) * 128], ex.ap, beta.ap[:, col:col + 1], mui, ALU.mult, ALU.mult, [ex, beta, K.cf], [Win])
        cx.release(m2)
        cx.S.tag = "M/gdn"
        sl = lambda t, c: t.ap[:, c * 128:(c + 1) * 128]
        ycs = cx.tile(SEQ, BF16, "ycs")

        def out_cb(c, yap, yt, hh=hh, ycs=ycs):
            junk = cx.ring("gjunk", 2, 128, F32)
            ss = cx.ring("gss", 2, 1, F32)
            cx.act(junk.ap, yap, AF.Square, [yt], [junk, ss], accum_out=ss.ap)
            ln_exp_rinv(cx, ss.ap, ss.ap, [ss], ss, scale=1.0 / 128, bias=EPS)
            o1 = cx.ring("go1", 2, 128, F32)
            cx.stt(o1.ap, yap, ss.ap[:, 0:1], gpar.ap[:, 8:136], ALU.mult, ALU.mult, [yt, ss, gpar], [o1])
            o2 = cx.ring("go2", 2, 128, BF16)
            cx.tt("pool", o2.ap, o1.ap, gate.ap[:, c * 128:(c + 1) * 128], ALU.mult, [o1, gate], [o2])
            pst = cx.ps()
            ptv = pst.ap.bitcast(BF16)
            cx.tr(ptv[:, 0:128], o2.ap, idb, [o2, K.cb], [pst])
            cx.copy("act", ycs.ap[:, c * 128:(c + 1) * 128], ptv[:, 0:128], [pst], [ycs])

        dplr_head(cx, K, "g%d" % hh, 128, 128, 0,
                  Kg=lambda c: sl(kT, c), Bg=None, Ag=lambda c: sl(kT, c), Rg=lambda c: sl(qT, c), gtiles=[kT, qT],
                  Wst=lambda c: sl(Wst, c), Win=lambda c: sl(Win, c), wtiles=[Wst, Win],
                  As=lambda c: sl(kT, c), Rs=lambda c: sl(qT, c), egam=lambda c: egam.ap[:, c * 4 + hh:c * 4 + hh + 1],
                  Kn=lambda c: sl(Kn, c), Bn=lambda c: sl(Bn, c), V=lambda c: sl(Vn, c), ntiles=[Kn, Bn, Vn],
                  PC=lambda c: pcall.ap[:, c * 4 + hh:c * 4 + hh + 1], pctiles=[pcall, egam], scale_all=False, out_cb=out_cb)
        io["ywrite"](cx, io["rc"] + hh, 0, SEQ, ycs.ap, [ycs], io["ydep"]["c"])
        cx.release(m1)
    cx.release(m0)


def mixer_rwkv(cx, K, io, hT, hv, layer1):
    m0 = cx.mark()
    wa = io["wa"]
    wal = io["wal"]
    rp = load_small(cx, io["rpar"], 44, "rpar")
    rl = load_small(cx, io["rlmu"], 4, "rlmu")
    rb = load_small(cx, io["rbc"], 1024, "rbc")
    wab = cx.tile(512, BF16, "wab")
    g2b = cx.tile(1024, BF16, "g2b")
    if layer1:
        v2b_ = cx.tile(512, BF16, "v2b", parts=64)
        v2b = T(v2b_.ap[32:64, :], v2b_.d)
    rmask = cx.tile(SEQ, BF16, "rmask")
    tl = cx.tile(SEQ, BF16, "tl")
    sgd = cx.tile(SEQ, BF16, "sgd")
    sg2h = cx.tile(SEQ, BF16, "sg2h", parts=64)
    sgd2 = T(sg2h.ap[0:32, :], sg2h.d)
    if layer1:
        hv1 = T(sg2h.ap[32:64, :], sg2h.d)
    mt = cx.mark()
    w2a2 = load_small(cx, io["rw2a2"], 512, "rw2a2")
    g2a = load_small(cx, io["rg2"], 1024, "rg2")
    cx.copy("pool", wab.ap, w2a2.ap, [w2a2], [wab])
    cx.copy("pool", g2b.ap, g2a.ap, [g2a], [g2b])
    if layer1:
        v2 = cx.tile(512, F32, "rv2", parts=64)
        cx.dma(v2.ap[32:64, :], io["rv2"], [], [v2])
        cx.copy("pool", v2b.ap, v2.ap[32:64, :], [v2], [v2b])
    idb = K.cb.ap[:, C_ID:C_ID + 128]
    msk_s = K.cb.ap[:, C_MUS:C_MUS + 128]
    msk_i = K.cb.ap[:, C_MUI:C_MUI + 128]
    cx.I("dve", "memset", [], [rmask], ap=rmask.ap, constant=1.0)
    cx.I("dve", "memset", [], [rmask], ap=rmask.ap[:, 0:SEQ:128], constant=0.0)

    def mixed_block(zp, tmp, pcs, cw, c0, m, po, mu_ap, mu_t, dst_ap, dst_t, func=None):
        pr = slice(po, po + m)
        cx.I("dve", "memset", [], [zp], ap=zp.ap[pr, 0:1], constant=0.0)
        for tb in range(4):
            lo, hi = tb * 512, tb * 512 + 512
            ps = cx.ps()
            mms = []
            for k in range(KC):
                mms.append(dict(out=ps.ap[pr, 0:512], lhsT=wslice(pcs, k, cw, c0, c0 + m), rhs=hv[:, k, lo:hi],
                                start=(k == 0), stop=(k == KC - 1)))
            cx.mm(mms, [p[0] for p in pcs] + [hT], [ps])
            cx.copy("act", zp.ap[pr, 1 + lo:1 + hi], ps.ap[pr, 0:512], [ps], [zp])
        cx.tt("dve", tmp.ap[pr, :], zp.ap[pr, 0:SEQ], zp.ap[pr, 1:1 + SEQ], ALU.subtract, [zp], [tmp])
        if func is None:
            cx.stt(dst_ap, tmp.ap[pr, :], mu_ap, zp.ap[pr, 1:1 + SEQ], ALU.mult, ALU.add, [tmp, zp, mu_t], [dst_t])
        else:
            cx.stt(tmp.ap[pr, :], tmp.ap[pr, :], mu_ap, zp.ap[pr, 1:1 + SEQ], ALU.mult, ALU.add, [tmp, zp, mu_t], [tmp])
            cx.act(dst_ap, tmp.ap[pr, :], func, [tmp], [dst_t])

    zp0 = cx.tile(SEQ + 8, F32, "zp0")
    tmp0 = cx.tile(SEQ, F32, "tmp0")
    pcl = load_w_big(cx, wal, KC, 288, "wal_b")
    mixed_block(zp0, tmp0, pcl, 288, 0, 64, 0, rl.ap[0:64, 0:1], rl, tl.ap[0:64, :], tl, func=AF.Tanh)
    mixed_block(zp0, tmp0, pcl, 288, 64, 64, 64, rl.ap[64:128, 0:1], rl, tl.ap[64:128, :], tl, func=AF.Copy)
    mixed_block(zp0, tmp0, pcl, 288, 128, 128, 0, rl.ap[:, 1:2], rl, sgd.ap, sgd, func=AF.Sigmoid)
    mixed_block(zp0, tmp0, pcl, 288, 256, 32, 0, rl.ap[0:32, 2:3], rl, sgd2.ap, sgd2, func=AF.Sigmoid)
    if layer1:
        pc1 = load_w_multi(cx, io["rv1"], KC, 32)
        for tb in range(4):
            lo, hi = tb * 512, tb * 512 + 512
            ps = gemm_block(cx, pc1, 32, lambda k, l, h_: hv[:, k, l:h_], [hT], lo, hi, po=32)
            cx.copy("act", hv1.ap[:, lo:hi], ps.ap[32:64, 0:512], [ps], [hv1])
        vfT = io["vf_in"].rearrange("(k p) t -> p k t", p=128)
    cx.release(mt)
    vout = io["v_out"].rearrange("(k p) t -> p k t", p=128)
    lnw = rb.ap[:, 0:512]
    lnb = rb.ap[:, 512:1024]
    for ct in range(4):
        m1 = cx.mark()
        par = rp.ap[:, 12 + ct * 8:12 + ct * 8 + 8]
        cx.S.tag = "M/rwkv/prep"
        As = cx.tile(SEQ, BF16, "rAs")
        Rs = cx.tile(SEQ, BF16, "rRs")
        Ks = cx.tile(SEQ, BF16, "rKs")
        Bs = cx.tile(SEQ, BF16, "rBs")
        Kn = cx.tile(NCH * 128, BF16, "rKn")
        Bn = cx.tile(NCH * 128, BF16, "rBn")
        Vn = cx.tile(NCH * 128, BF16, "rVn")
        PCt = cx.tile(NCH, F32, "rPC")
        bon = cx.tile(NCH * 2, F32, "rbon")
        m2 = cx.mark()
        zp = cx.tile(SEQ + 8, F32, "zp")
        tmp = cx.tile(SEQ, F32, "tmp")
        ld = cx.tile(SEQ, F32, "ld")
        cum = cx.tile(SEQ, F32, "cum")
        av = cx.tile(SEQ, BF16, "a_sig")
        rm = cx.tile(SEQ, BF16, "r_m")
        km = cx.tile(SEQ, BF16, "k_m")
        vm = cx.tile(SEQ, BF16, "v_m")
        rkr = cx.tile(SEQ, BF16, "rkr")
        for which, dst in ((0, rm), (1, km), (2, vm)):
            pcs = load_w_multi(cx, wa[which * 4 + ct], KC, 128)
            mixed_block(zp, tmp, pcs, 128, 0, 128, 0, rp.ap[:, which * 4 + ct:which * 4 + ct + 1], rp, dst.ap, dst)
        kx = T(zp.ap[:, 0:SEQ], zp.d)
        B1 = tmp
        for tb in range(4):
            lo, hi = tb * 512, tb * 512 + 512
            ps = cx.ps()
            cx.mm([dict(out=ps.ap[:, 0:512], lhsT=wab.ap[0:64, ct * 128:(ct + 1) * 128], rhs=tl.ap[0:64, lo:hi], start=True, stop=True)],
                  [wab, tl], [ps])
            cx.act(ld.ap[:, lo:hi], ps.ap[:, 0:512], AF.Sigmoid, [ps, rp], [ld], bias=par[:, 0:1])
            ps = cx.ps()
            cx.mm([dict(out=ps.ap[:, 0:512], lhsT=wab.ap[64:128, ct * 128:(ct + 1) * 128], rhs=tl.ap[64:128, lo:hi], start=True, stop=True)],
                  [wab, tl], [ps])
            cx.act(av.ap[:, lo:hi], ps.ap[:, 0:512], AF.Sigmoid, [ps, rp], [av], bias=par[:, 1:2])
        cx.ts("dve", ld.ap, ld.ap, -float(np.exp(-0.5)), ALU.mult, [ld], [ld])
        cx.I("dve", "tensor_tensor_scan", [rmask, ld], [cum], out=cum.ap, data0=rmask.ap, data1=ld.ap, initial=0.0,
             op0=ALU.mult, op1=ALU.add)
        if layer1:
            vf = rkr
            cx.dma(vf.ap, vfT[:, ct, :], io["vf_deps"], [vf])
            for tb in range(4):
                lo, hi = tb * 512, tb * 512 + 512
                ps = cx.ps()
                cx.mm([dict(out=ps.ap[:, 0:512], lhsT=v2b.ap[:, ct * 128:(ct + 1) * 128], rhs=hv1.ap[:, lo:hi], start=True, stop=True)],
                      [v2b, hv1], [ps])
                cx.act(B1.ap[:, lo:hi], ps.ap[:, 0:512], AF.Sigmoid, [ps, rp], [B1], bias=par[:, 5:6])
            cx.tt("dve", kx.ap, vf.ap, vm.ap, ALU.subtract, [vf, vm], [kx])
            cx.tt("dve", kx.ap, kx.ap, B1.ap, ALU.mult, [kx, B1], [kx])
            cx.tt("dve", vm.ap, vm.ap, kx.ap, ALU.add, [vm, kx], [vm])
        if io.get("write_v", True):
            cx.dma(vout[:, ct, :], vm.ap, [vm], [io["v_dep"]])
        cx.ts("dve", kx.ap, km.ap, par[:, 2:3], ALU.mult, [km, rp], [kx])
        for tb in range(4):
            lo, hi = tb * 512, tb * 512 + 512
            sq = cx.ring("rsq", 2, 512, BF16)
            cx.act(sq.ap, kx.ap[:, lo:hi], AF.Square, [kx], [sq])
            ps = cx.ps()
            cx.mm([dict(out=ps.ap[:, 0:512], lhsT=K.cb.ap[:, C_BO:C_BO + 128], rhs=sq.ap, start=True, stop=True)], [K.cb, sq], [ps])
            ri = cx.ring("rri", 2, 512, F32)
            ln_exp_rinv(cx, ri.ap, ps.ap[:, 0:512], [ps], ri)
            cx.tt("dve", kx.ap[:, lo:hi], kx.ap[:, lo:hi], ri.ap, ALU.mult, [kx, ri], [kx])
        if RW_DBG and ct == 0:
            dbg = io["dbg"].rearrange("(k p) t -> p k t", p=128)
            cx.dma(dbg[:, 0, :], kx.ap, [kx], [])
            cx.dma(dbg[:, 1, :], ld.ap, [ld], [])
            cx.dma(dbg[:, 2, :], cum.ap, [cum], [])
        cx.ts("dve", B1.ap, av.ap, -1.0, ALU.add, [av, rp], [B1], s2=par[:, 3:4], op1=ALU.mult)
        cx.ts("dve", B1.ap, B1.ap, 1.0, ALU.add, [B1], [B1])
        if RW_DBG and ct == 0:
            cx.dma(dbg[:, 3, :], B1.ap, [B1], [])
        cx.tt("dve", km.ap, km.ap, B1.ap, ALU.mult, [km, B1], [km])
        cx.stt(rkr.ap, rm.ap, par[:, 4:5], km.ap, ALU.mult, ALU.mult, [rm, km, rp], [rkr])
        ps = cx.ps()
        cx.mm([dict(out=ps.ap[:, c * 2:c * 2 + 2], lhsT=rkr.ap[:, c * 128:(c + 1) * 128], rhs=K.cb.ap[:, C_BS:C_BS + 2], start=True, stop=True)
               for c in range(NCH)], [rkr, K.cb], [ps])
        cx.copy("act", bon.ap, ps.ap[:, 0:NCH * 2], [ps], [bon])
        cx.tt("dve", B1.ap, cum.ap, ld.ap, ALU.subtract, [cum, ld], [B1])
        cx.act(B1.ap, B1.ap, AF.Exp, [B1], [B1])
        cx.stt(As.ap, kx.ap, -1.0, B1.ap, ALU.mult, ALU.mult, [kx, B1], [As])
        cx.act(B1.ap, cum.ap, AF.Exp, [cum], [B1])
        cx.tt("dve", Rs.ap, rm.ap, B1.ap, ALU.mult, [rm, B1], [Rs])
        cx.copy("dve", PCt.ap, B1.ap[:, 127:SEQ:128], [B1], [PCt])
        cx.act(B1.ap, cum.ap, AF.Exp, [cum], [B1], scale=-1.0)
        cx.tt("dve", Ks.ap, km.ap, B1.ap, ALU.mult, [km, B1], [Ks])
        cx.tt("dve", kx.ap, kx.ap, av.ap, ALU.mult, [kx, av], [kx])
        cx.tt("dve", Bs.ap, kx.ap, B1.ap, ALU.mult, [kx, B1], [Bs])
        for src, dst in ((Ks, Kn), (Bs, Bn), (vm, Vn)):
            for c4 in range(4):
                ps = cx.ps()
                pv = ps.ap.bitcast(BF16)
                for q in range(4):
                    c = c4 * 4 + q
                    cx.tr(pv[:, q * 128:(q + 1) * 128], src.ap[:, c * 128:(c + 1) * 128], idb, [src, K.cb], [ps])
                cx.copy("act", dst.ap[:, c4 * 512:(c4 + 1) * 512], pv[:, 0:512], [ps], [dst])
        cx.release(m2)
        yas = cx.tile(SEQ, BF16, "yas")
        cx.S.tag = "M/rwkv"
        sfs = []
        for hp in range(2):
            pb = hp * 64
            hl = ct * 2 + hp
            pr = slice(pb, pb + 64)

            def out_cb(c, yap, yt, hl=hl, hp=hp, ct=ct, yas=yas, pb=pb):
                st = cx.ring("rst%d" % hp, 2, 6, F32)
                cx.I("dve", "bn_stats", [yt], [st], out=st.ap, in_=yap)
                mv_ = cx.ring("rmv%d" % hp, 2, 2, F32)
                cx.I("dve", "bn_aggr", [st], [mv_], out=mv_.ap, in_=st.ap)
                rs = cx.ring("rrs%d" % hp, 2, 1, F32)
                ln_exp_rinv(cx, rs.ap, mv_.ap[:, 1:2], [mv_], rs, bias=64e-5)
                y1 = cx.ring("ry1%d" % hp, 2, 64, F32)
                cx.ts("dve", y1.ap, yap, mv_.ap[:, 0:1], ALU.subtract, [yt, mv_, rs], [y1], s2=rs.ap[:, 0:1], op1=ALU.mult)
                cx.tt("pool", y1.ap, y1.ap, lnw[:, hl * 64:(hl + 1) * 64], ALU.mult, [y1, rb], [y1])
                cx.tt("pool", y1.ap, y1.ap, lnb[:, hl * 64:(hl + 1) * 64], ALU.add, [y1, rb], [y1])
                y2 = cx.ring("ry2%d" % hp, 2, 64, F32)
                cx.stt(y2.ap, Vn.ap[:, c * 128 + hp * 64:c * 128 + hp * 64 + 64], bon.ap[:, c * 2 + hp:c * 2 + hp + 1], y1.ap,
                       ALU.mult, ALU.add, [Vn, bon, y1], [y2])
                psg = cx.ps()
                cx.mm([dict(out=psg.ap[:, 0:64], lhsT=sgd.ap[:, c * 128:(c + 1) * 128], rhs=g2b.ap[:, hl * 64:(hl + 1) * 64], start=True, stop=False),
                       dict(out=psg.ap[:, 0:64], lhsT=sgd2.ap[:, c * 128:(c + 1) * 128], rhs=g2b.ap[0:32, 512 + hl * 64:512 + (hl + 1) * 64],
                            start=False, stop=True)], [sgd, sgd2, g2b], [psg])
                y3 = cx.ring("ry3%d" % hp, 2, 64, BF16)
                if RW_DBG == "g":
                    cx.copy("dve", y3.ap, psg.ap[:, 0:64], [psg], [y3])
                elif RW_DBG == "y1":
                    cx.copy("dve", y3.ap, y1.ap, [y1, psg], [y3])
                elif RW_DBG == "y2":
                    cx.copy("dve", y3.ap, y2.ap, [y2, psg], [y3])
                elif RW_DBG == "yraw":
                    cx.copy("dve", y3.ap, yap, [yt, psg], [y3])
                else:
                    cx.tt("dve", y3.ap, psg.ap[:, 0:64], y2.ap, ALU.mult, [psg, y2], [y3])
                pst = cx.ps()
                ptv = pst.ap.bitcast(BF16)
                cx.tr(ptv[pb:pb + 64, 0:128], y3.ap, idb, [y3, K.cb], [pst])
                cx.copy("act", yas.ap[pb:pb + 64, c * 128:(c + 1) * 128], ptv[pb:pb + 64, 0:128], [pst], [yas])

            sf = dplr_head(cx, K, "r%d" % hl, 64, 64, pb, defer=True,
                      Kg=lambda c, pr=pr: Ks.ap[pr, c * 128:(c + 1) * 128], Bg=lambda c, pr=pr: Bs.ap[pr, c * 128:(c + 1) * 128],
                      Ag=lambda c, pr=pr: As.ap[pr, c * 128:(c + 1) * 128], Rg=lambda c, pr=pr: Rs.ap[pr, c * 128:(c + 1) * 128],
                      gtiles=[Ks, Bs, As, Rs],
                      Wst=lambda c: msk_s, Win=lambda c: msk_i, wtiles=[K.cb],
                      As=lambda c, pr=pr: As.ap[pr, c * 128:(c + 1) * 128], Rs=lambda c, pr=pr: Rs.ap[pr, c * 128:(c + 1) * 128], egam=None,
                      Kn=lambda c, pb=pb: Kn.ap[:, c * 128 + pb:c * 128 + pb + 64], Bn=lambda c, pb=pb: Bn.ap[:, c * 128 + pb:c * 128 + pb + 64],
                      V=lambda c, pb=pb: Vn.ap[:, c * 128 + pb:c * 128 + pb + 64], ntiles=[Kn, Bn, Vn],
                      PC=lambda c, pr=pr: PCt.ap[pr, c:c + 1], pctiles=[PCt], scale_all=True, out_cb=out_cb)
            sfs.append(sf)
        for c in range(NCH):
            for st_, fn_ in sfs:
                st_(c)
        for st_, fn_ in reversed(sfs):
            fn_()
        io["ywrite"](cx, io["ra"] + ct, 0, SEQ, yas.ap, [yas], io["ydep"]["a"])
        cx.release(m1)
    cx.release(m0)


def phase_M(cx, ios, layer1, which=("att", "gdn", "rwkv")):
    io = ios[0]
    base_ = cx.mark()
    cx.S.tag = "M/pro"
    cx.cast_eng = "pool"
    K = setup_consts(cx, io)
    g1 = load_small(cx, io["g_attn"], 16, "g1")
    hT = cx.tile(KC * SEQ, BF16, "hT")
    hv = hT.ap.rearrange("p (k t) -> p k t", k=KC)
    if "x_view" in io:
        xview = io["x_view"]
    else:
        xT = io["xT"].rearrange("(k p) t -> p k t", p=128)
        xview = lambda k, lo, hi: xT[:, k, lo:hi]
    m0 = cx.mark()
    for tb in range(4):
        lo, hi = tb * 512, tb * 512 + 512
        xt = cx.ring("xtile", 1, KC * 512, F32)
        xtv = xt.ap.rearrange("p (k t) -> p k t", k=KC)
        for k in range(KC):
            cx.dma(xtv[:, k, :], xview(k, lo, hi), io.get("x_deps", []), [xt])
        rmsnorm_fm(cx, lambda k, l, h_: (xtv[:, k, 0:h_ - l], xt), g1, lambda k, l, h_: (hv[:, k, l:h_], hT),
                   [(lo, hi)], K.ones)
    cx.release(m0)
    for io in ios:
        if "att" in which:
            cx.S.tag = "M/att"
            mixer_attention(cx, K, io, hT, hv)
        if "gdn" in which:
            cx.S.tag = "M/gdn"
            mixer_gdn(cx, K, io, hT, hv)
        if "rwkv" in which:
            cx.S.tag = "M/rwkv"
            mixer_rwkv(cx, K, io, hT, hv, layer1)
    cx.release(base_)


M_INPUTS = [("xT", (D, SEQ)), ("g_attn", (128, 16)), ("cst", (128, CST_N)), ("rope", (32, 2 * SEQ)),
            ("wat", (18, 128, 16, 128)), ("wsw", (12, 128, 16, 32)),
            ("wc", (16, 128, 16, 128)), ("wba", (128, 16, 8)), ("gconv", (128, 48)), ("gpar", (128, 136)),
            ("wa", (12, 128, 16, 128)), ("wal", (128, 16, 288)), ("rpar", (128, 44)), ("rlmu", (128, 4)), ("rbc", (128, 1024)),
            ("rw2a2", (128, 512)), ("rg2", (128, 1024))]
M_INPUTS_L1 = [("rv1", (128, 16, 32)), ("rv2", (32, 512)), ("vf_in", (512, SEQ))]
M_OUTPUTS = [("y_fm", (768, SEQ)), ("yc_tm", (SEQ, 512)), ("ya_tm", (SEQ, 512)), ("v_out", (512, SEQ))]


def prep_M_weights(inp, l, s):
    A_IN, B_IN, C_IN = 3360, 4608, 4112
    W = inp["w_in"][l]
    w = {}
    w["g_attn"] = vec_pk(inp["attn_norm"][l])
    w["cst"] = make_consts()
    w["rope"] = make_rope()

    def colblk(cols):
        sub = W[:, cols]
        n = sub.shape[1]
        return np.ascontiguousarray(sub.reshape(16, 128, n // 128, 128).transpose(2, 1, 0, 3))

    def colsmall(cols):
        sub = W[:, cols]
        return np.ascontiguousarray(sub.reshape(16, 128, len(cols)).transpose(1, 0, 2))
    b0 = A_IN
    cols = []
    cols_sw = []
    for hl in range(2):
        hi = 2 * s + hl
        for g in range(3):
            for which in range(3):
                c0 = b0 + which * 1536 + g * 512 + hi * 128
                cols += list(range(c0, c0 + 128))
                if which < 2:
                    cols_sw += list(range(c0 + 16, c0 + 32)) + list(range(c0, c0 + 16))
    w["wat"] = colblk(np.array(cols))
    sw = W[:, np.array(cols_sw)]
    w["wsw"] = np.ascontiguousarray(sw.reshape(16, 128, 12, 32).transpose(2, 1, 0, 3))
    c0 = A_IN + B_IN
    cols = []
    for which in range(3):
        for hh in range(4):
            h = 4 * s + hh
            cols += list(range(c0 + which * 1024 + h * 128, c0 + which * 1024 + (h + 1) * 128))
    for hh in range(4):
        h = 4 * s + hh
        cols += list(range(c0 + 3088 + h * 128, c0 + 3088 + (h + 1) * 128))
    w["wc"] = colblk(np.array(cols))
    w["wba"] = colsmall(np.array([c0 + 3072 + 4 * s + i for i in range(4)] + [c0 + 3080 + 4 * s + i for i in range(4)]))
    gc = inp["gdn_conv"][l]
    gcs = np.zeros((128, 12, 4), np.float32)
    for which in range(3):
        for hh in range(4):
            h = 4 * s + hh
            gcs[:, which * 4 + hh, :] = gc[:, which * 1024 + h * 128: which * 1024 + (h + 1) * 128].T
    w["gconv"] = gcs.reshape(128, 48)
    gp = np.zeros((128, 136), np.float32)
    gp[:, 0:4] = inp["gdn_A_log"][l][4 * s:4 * s + 4][None, :]
    gp[:, 4:8] = inp["gdn_dt_bias"][l][4 * s:4 * s + 4][None, :]
    gp[:, 8:136] = inp["gdn_norm"][l][None, :]
    w["gpar"] = gp
    ch0 = 512 * s
    cols = []
    for which in range(3):
        cols += list(range(which * 1024 + ch0, which * 1024 + ch0 + 512))
    w["wa"] = colblk(np.array(cols))
    w["wal"] = colsmall(np.arange(3072, 3360))
    mu = inp["rwkv_mu"][l]
    rp = np.zeros((128, 44), np.float32)
    for which in range(3):
        for ct in range(4):
            rp[:, which * 4 + ct] = mu[which * 1024 + ch0 + ct * 128: which * 1024 + ch0 + (ct + 1) * 128]
    for ct in range(4):
        sl = slice(ch0 + ct * 128, ch0 + (ct + 1) * 128)
        rp[:, 12 + ct * 8 + 0] = inp["rwkv_w0"][l][sl]
        rp[:, 12 + ct * 8 + 1] = inp["rwkv_a0"][l][sl]
        rp[:, 12 + ct * 8 + 2] = inp["rwkv_k_k"][l][sl]
        rp[:, 12 + ct * 8 + 3] = inp["rwkv_k_a"][l][sl]
        rp[:, 12 + ct * 8 + 4] = inp["rwkv_r_k"][l].reshape(-1)[sl]
        if l > 0:
            rp[:, 12 + ct * 8 + 5] = inp["rwkv_v0"][l - 1][sl]
    w["rpar"] = rp
    rl = np.zeros((128, 4), np.float32)
    rl[0:64, 0] = mu[3072:3136]
    rl[64:128, 0] = mu[3136:3200]
    rl[0:128, 1] = mu[3200:3328]
    rl[0:32, 2] = mu[3328:3360]
    w["rlmu"] = rl
    rb = np.zeros((128, 1024), np.float32)
    rb[:, 0:512] = inp["rwkv_ln_w"][l][ch0:ch0 + 512][None, :]
    rb[:, 512:1024] = inp["rwkv_ln_b"][l][ch0:ch0 + 512][None, :]
    w["rbc"] = rb
    w["rw2a2"] = np.ascontiguousarray(np.concatenate([inp["rwkv_w2"][l][:, ch0:ch0 + 512], inp["rwkv_a2"][l][:, ch0:ch0 + 512]], axis=0))
    g2 = inp["rwkv_g2"][l][:, ch0:ch0 + 512]
    rg = np.zeros((128, 1024), np.float32)
    rg[:, 0:512] = g2[0:128]
    rg[0:32, 512:1024] = g2[128:160]
    w["rg2"] = rg
    if l > 0:
        v1 = inp["rwkv_v1"][l - 1]
        w["rv1"] = np.ascontiguousarray(v1.reshape(16, 128, 32).transpose(1, 0, 2))
        w["rv2"] = np.ascontiguousarray(inp["rwkv_v2"][l - 1][:, ch0:ch0 + 512])
    return w


def tile_w(W, col0, ncols, cw=128):
    K = W.shape[0]
    sub = W[:, col0:col0 + ncols]
    return np.ascontiguousarray(sub.reshape(K // 128, 128, ncols // cw, cw).transpose(2, 1, 0, 3))


def vec_pk(v):
    return np.ascontiguousarray(v.reshape(-1, 128).T)


def prep_T_weights(inp, l):
    A_IN, B_IN, C_IN = 3360, 4608, 4112
    g0 = A_IN + B_IN + C_IN
    w = {}
    w["wg"] = tile_w(inp["w_in"][l], g0, 6144)
    pw = np.concatenate([inp["proj_a"][l], inp["proj_b"][l], inp["proj_c"][l]], axis=0)
    w["wp"] = tile_w(pw, 0, 2048)
    w["wo"] = tile_w(inp["w_out"][l], 0, 2048)
    w["wup"] = tile_w(inp["ffn_up"][l], 0, 11264)
    w["wdn"] = tile_w(inp["ffn_down"][l], 0, 2048)
    w["g_attn"] = vec_pk(inp["attn_norm"][l])
    w["g_ffn"] = vec_pk(inp["ffn_norm"][l])
    w["g_next"] = vec_pk(inp["attn_norm"][l + 1] if l + 1 < inp["attn_norm"].shape[0] else inp["final_norm"])
    fc = inp["ffn_conv"][l]
    w["fconv"] = np.ascontiguousarray(fc.T.reshape(88, 128, 3).transpose(1, 0, 2).reshape(128, 88 * 3))
    return w


class DD:
    def __init__(self, name=""):
        self.d = Dep(name)


M_SHARED = ("xT", "cst", "rope")
T_INPUTS = [("g_attn", (128, 16)), ("g_ffn", (128, 16)), ("g_next", (128, 16)), ("fconv", (128, 88 * 3)),
            ("wg", (48, 128, 16, 128)), ("wp", (16, 128, 20, 128)), ("wo", (16, 128, 16, 128)),
            ("wup", (88, 128, 16, 128)), ("wdn", (16, 128, 44, 128))]


PAIRS = [[0, 1], [2, 3], [4, 5], [6, 7]]


def build_fused(depth=2):
    nc = bass.Bass("TRN2", target_bir_lowering=False)
    ext = {}

    def din(name, shape, dt=F32):
        ext[name] = nc.dram_tensor(name, list(shape), dt, kind="ExternalInput").ap()
        return ext[name]
    xT = din("xT", (D, SEQ))
    xTT = din("xTT", (D, NTOK_T))
    hmask = din("hmask", (128, 2))
    cst = din("cst", (128, CST_N))
    rope = din("rope", (32, 2 * SEQ))
    for l in range(depth):
        for name, shape in M_INPUTS + (M_INPUTS_L1[:2] if l == 1 else []):
            if name not in M_SHARED:
                din("%s_%d" % (name, l), shape)
        for name, shape in T_INPUTS:
            din("T%s_%d" % (name, l), shape)
    out = nc.dram_tensor("outT", [D, 1024], F32, kind="ExternalOutput").ap()
    yloc = [[nc.dram_tensor("yloc%d_%d" % (l, q), [1280, 512], BF16).ap() for q in range(4)] for l in range(depth)]
    yg = [[nc.dram_tensor("yg%d_%d" % (l, q), [2560, 512], BF16).ap() for q in range(4)] for l in range(depth)]
    xloc = [[nc.dram_tensor("xloc%d_%d" % (rh, ch), [1024, 512], F32).ap() for ch in range(2)] for rh in range(2)]
    xg = [[nc.dram_tensor("xg%d_%d" % (rh, ch), [2048, 512], F32).ap() for ch in range(2)] for rh in range(2)]
    vbuf = nc.dram_tensor("vbuf", [512, SEQ], BF16).ap()
    with ExitStack() as es:
        cx = Ctx(nc, es)
        cx.wpiece = 1024
        xloc_d = [[DD("xloc") for ch in range(2)] for rh in range(2)]
        xg_d = [[DD("xg") for ch in range(2)] for rh in range(2)]
        all_xloc_d = [d for r_ in xloc_d for d in r_]
        all_xg_d = [d for r_ in xg_d for d in r_]
        v_d = DD("v")
        xloc_v = [[xloc[rh][ch].rearrange("(k p) t -> p k t", p=128) for ch in range(2)] for rh in range(2)]
        xg_v = [[xg[rh][ch].rearrange("(r k p) t -> p r k t", r=2, p=128) for ch in range(2)] for rh in range(2)]

        def ag(in_ap, out_ap, rd, wr):
            cx.S.op("pool", [("collective_compute", dict(kind="AllGather", op=ALU.bypass, replica_groups=PAIRS,
                                                        ins=[in_ap], outs=[out_ap]))], [d.d for d in rd], [d.d for d in wr], dma="cc")

        for l in range(depth):
            last = (l == depth - 1)
            yl_d = [{"a": DD("ya"), "b": DD("yb"), "c": DD("yc")} for q in range(4)]
            yg_d = [DD("yg") for q in range(4)]
            yloc_v = [yloc[l][q].rearrange("(b p) t -> p b t", p=128) for q in range(4)]
            yg_v = [yg[l][q].rearrange("(r b p) t -> p r b t", r=2, p=128) for q in range(4)]

            def ywrite(cx_, blk, t_lo, t_hi, src_ap, src_tiles, key, yl_d=yl_d, yloc_v=yloc_v):
                for q in range(t_lo // 512, t_hi // 512):
                    cx_.dma(yloc_v[q][:, blk, :], src_ap[:, q * 512 - t_lo:q * 512 - t_lo + 512], src_tiles, [yl_d[q][key]])

            io = {}
            for name, shape in M_INPUTS + (M_INPUTS_L1[:2] if l == 1 else []):
                if name not in M_SHARED:
                    io[name] = ext["%s_%d" % (name, l)]
            io.update(cst=cst, rope=rope, ydep={"a": "a", "b": "b", "c": "c"}, ywrite=ywrite, ra=0, rb=4, rc=6,
                      v_out=vbuf, v_dep=v_d, vf_in=vbuf, vf_deps=[v_d], write_v=(l == 0))
            if l == 0:
                io.update(xT=xT, x_deps=[])
            else:
                io.update(x_view=(lambda k, lo, hi: xg_v[k // 8][(lo % 1024) // 512][:, lo // 1024, k % 8, lo % 512:lo % 512 + (hi - lo)]),
                          x_deps=all_xg_d)
            phase_M(cx, [io], l == 1)
            cx.S.tag = "AG/y"
            for q in range(4):
                ag(yloc[l][q], yg[l][q], list(yl_d[q].values()), [yg_d[q]])
            io = {name: ext["T%s_%d" % (name, l)] for name, shape in T_INPUTS}
            io.update(hmask=hmask, yread=(lambda r_, b_, q, yg_v=yg_v: yg_v[q][:, r_, b_, :]), y_deps=yg_d,
                      x_deps=all_xloc_d + all_xg_d)
            if l == 0:
                io.update(x_mode="input", xTT=xTT)
            else:
                def xloc_read(k, t_lo, t_hi):
                    out_ = []
                    t = t_lo
                    while t < t_hi:
                        ch = t // 512
                        e = min(t_hi, (ch + 1) * 512)
                        out_.append((xloc_v[k // 8][ch][:, k % 8, t - ch * 512:e - ch * 512], t - t_lo, e - t))
                        t = e
                    return out_
                io.update(x_mode="exchange", xloc_read=xloc_read,
                          xhalo=(lambda k: xg_v[k // 8][1][:, 0, k % 8, 510:512]))
            if last:
                io.update(outT=out, out_deps=[])
            else:
                def xwrite(cx_, k, src_ap, src_tiles):
                    for ch in range(2):
                        cx_.dma(xloc_v[k // 8][ch][:, k % 8, :], src_ap[:, ch * 512:(ch + 1) * 512], src_tiles, [xloc_d[k // 8][ch]])
                io.update(outT=None, xwrite=xwrite)
            phase_T(cx, io, last, 0)
            if not last:
                cx.S.tag = "AG/x"
                for rh in range(2):
                    for ch in range(2):
                        ag(xloc[rh][ch], xg[rh][ch], [xloc_d[rh][ch]], [xg_d[rh][ch]])
        if cx.S.taglog is not None:
            import json
            json.dump(cx.S.taglog, open(os.environ["BASS_TAGLOG"], "w"))
        cx.S.emit()
        print("fused ops:", cx.S.nops, "insts:", cx.S.ninst, "arena peak KB:", cx.ar.peak * 4 / 1024,
              "eng counts:", cx.S.cnt, "max dma sem:", max(cx.S.dma_cnt))
    return nc


_NC_CACHE = {}


def prep_core_inputs(inp, depth, s):
    m = {"cst": make_consts(), "rope": make_rope()}
    hm = np.zeros((128, 2), np.float32)
    hm[:, s] = 1.0
    m["hmask"] = hm
    for l in range(depth):
        w = prep_M_weights(inp, l, s)
        for k, v in w.items():
            if k not in M_SHARED:
                m["%s_%d" % (k, l)] = v
    return m


def kernel(**inputs):
    inp = {k: np.asarray(v) for k, v in inputs.items()}
    x = inp["x"].astype(np.float32, copy=False)
    B, S, Dm = x.shape
    depth = inp["w_in"].shape[0]
    if "nc" not in _NC_CACHE:
        _NC_CACHE["nc"] = build_fused(depth)
    nc = _NC_CACHE["nc"]
    per_s = [prep_core_inputs(inp, depth, s) for s in range(2)]
    tw = {}
    for l in range(depth):
        for k, v in prep_T_weights(inp, l).items():
            tw["T%s_%d" % (k, l)] = v
    maps = []
    for c in range(8):
        b, s = c // 2, c % 2
        m = dict(per_s[s])
        m.update(tw)
        m["xT"] = np.ascontiguousarray(x[b].T)
        lo = 1024 * s - 2
        xs = np.zeros((NTOK_T, Dm), np.float32)
        a = max(lo, 0)
        xs[a - lo:] = x[b][a:lo + NTOK_T]
        m["xTT"] = np.ascontiguousarray(xs.T)
        maps.append(m)
    res = run_bass_kernel_spmd(nc, maps, core_ids=list(range(8))).results
    outs = [np.concatenate([np.asarray(res[2 * b + s]["outT"]).T for s in range(2)], axis=0) for b in range(B)]
    return np.ascontiguousarray(np.stack(outs, axis=0)).astype(np.float32)
```

```python
import os
import numpy as np
import concourse.bass as bass
import concourse.mybir as mybir
from concourse.bass_utils import run_bass_kernel_spmd
from contextlib import ExitStack

F32 = mybir.dt.float32
BF16 = mybir.dt.bfloat16
ALU = mybir.AluOpType
AF = mybir.ActivationFunctionType
AX = mybir.AxisListType

D = 2048
KC = 16
SEQ = 2048
NTOK_T = 1026
TT = [(0, 2), (2, 514), (514, 1026)]
D_FF = 5632
NFB = 44
EPS = 1e-6

SAME_ENG_SYNC = True


class Dep:
    __slots__ = ("w", "r", "name", "excl", "wm")

    def __init__(self, name="", excl=False):
        self.w = None
        self.wm = {}
        self.r = {}
        self.name = name
        self.excl = excl


class Sched:
    ENGS = ["pe", "act", "dve", "pool", "sp"]

    def __init__(self, nc, n_dma=24):
        self.nc = nc
        self.q = {e: [] for e in self.ENGS}
        self.cnt = {e: 0 for e in self.ENGS}
        self.n_dma = n_dma
        self.dma_cnt = [0] * n_dma
        self.dma_rr = 0
        self.waited = {e: {} for e in self.ENGS}
        self.nops = 0
        self.ninst = 0
        self.tag = ""
        self.cc_cnt = 0
        self.taglog = [] if os.environ.get("BASS_TAGLOG") else None

    def op(self, eng, calls, reads=(), writes=(), dma=False, multi=False):
        deps = {}

        def add(tok):
            if tok is None:
                return
            k, v = tok
            if deps.get(k, 0) < v:
                deps[k] = v

        for r in reads:
            add(r.w)
            for k, v in r.wm.items():
                add((k, v))
            if r.excl:
                for k, v in r.r.items():
                    add((k, v))
        for w in writes:
            if not multi:
                add(w.w)
                for k, v in w.wm.items():
                    add((k, v))
            for k, v in w.r.items():
                add((k, v))
        if dma == "cc":
            self.cc_cnt += 1
            tok = (("cc", 0), self.cc_cnt)
        elif dma:
            i = self.dma_rr
            self.dma_rr = (self.dma_rr + 1) % self.n_dma
            add((("dma", i), self.dma_cnt[i]))
            self.dma_cnt[i] += 16
            tok = (("dma", i), self.dma_cnt[i])
        else:
            self.cnt[eng] += 1
            tok = (eng, self.cnt[eng])
        waits = []
        wd = self.waited[eng]
        for k, v in deps.items():
            if v <= 0:
                continue
            if k == eng and (eng == "pe" or not SAME_ENG_SYNC):
                continue
            if wd.get(k, 0) >= v:
                continue
            wd[k] = v
            waits.append((k, v))
        for r in reads:
            if r.r.get(tok[0], 0) < tok[1]:
                r.r[tok[0]] = tok[1]
        for w in writes:
            if multi:
                if w.wm.get(tok[0], 0) < tok[1]:
                    w.wm[tok[0]] = tok[1]
            else:
                w.w = tok
                w.wm = {}
                w.r = {}
        self.q[eng].append((waits, calls, tok))
        if self.taglog is not None:
            self.taglog.append((eng, self.tag, len(calls)))
        self.nops += 1
        self.ninst += len(calls)
        return tok

    def emit(self, final_wait_eng="sp"):
        nc = self.nc
        with ExitStack() as es:
            sems = {}
            for e in self.ENGS:
                sems[e] = es.enter_context(nc.semaphore("s_" + e))
            for i in range(self.n_dma):
                sems[("dma", i)] = es.enter_context(nc.semaphore("s_dma%d" % i))
            sems[("cc", 0)] = es.enter_context(nc.semaphore("s_cc"))
            block = es.enter_context(nc.Block())
            finals = [(("dma", i), self.dma_cnt[i]) for i in range(self.n_dma) if self.dma_cnt[i] > 0]

            def run(engname, engine):
                for waits, calls, tok in self.q[engname]:
                    for k, v in waits:
                        engine.wait_ge(sems[k], v)
                    ins = None
                    for name, kw in calls:
                        ins = getattr(engine, name)(**kw)
                    ins.then_inc(sems[tok[0]], 16 if (isinstance(tok[0], tuple) and tok[0][0] == "dma") else 1)
                if engname == final_wait_eng:
                    for k, v in finals:
                        engine.wait_ge(sems[k], v)

            @block.tensor
            def _(e):
                run("pe", e)

            @block.scalar
            def _(e):
                run("act", e)

            @block.vector
            def _(e):
                run("dve", e)

            @block.gpsimd
            def _(e):
                run("pool", e)

            @block.sync
            def _(e):
                run("sp", e)


class Arena:
    def __init__(self, ap_f32, words):
        self.base = ap_f32
        self.words = words
        self.top = 0
        self.peak = 0
        self.live = []
        self.dead = []

    def mark(self):
        return self.top

    def release(self, m):
        keep = []
        for ent in self.live:
            if ent[0] >= m:
                self.dead.append(ent)
            else:
                keep.append(ent)
        self.live = keep
        self.top = m

    def alloc(self, nelem, dtype=F32, parts=128, name=""):
        w = nelem if dtype == F32 else (nelem + 1) // 2
        w = (w + 7) // 8 * 8
        assert self.top + w <= self.words, ("SBUF arena overflow", name, self.top, w, self.words)
        lo, hi = self.top, self.top + w
        a = self.base[0:parts, lo:hi]
        self.top = hi
        self.peak = max(self.peak, self.top)
        d = Dep(name)
        nd = []
        for (dlo, dhi, dd) in self.dead:
            if dlo < hi and lo < dhi:
                toks = list(dd.r.items())
                if dd.w is not None:
                    toks.append(dd.w)
                for k, v in toks:
                    if d.r.get(k, 0) < v:
                        d.r[k] = v
                if lo <= dlo and dhi <= hi:
                    continue
            nd.append((dlo, dhi, dd))
        self.dead = nd
        self.live.append((lo, hi, d))
        if dtype != F32:
            a = a.bitcast(dtype)
        return a[:, 0:nelem], d


class T:
    def __init__(self, ap, d):
        self.ap = ap
        self.d = d


class Ctx:
    ARENA_WORDS = 51 * 1024 + 512

    def __init__(self, nc, es):
        self.nc = nc
        self.S = Sched(nc)
        big = es.enter_context(nc.sbuf_tensor("arena", [128, self.ARENA_WORDS], F32))
        self.ar = Arena(big, self.ARENA_WORDS)
        self.psb = []
        for i in range(8):
            p = es.enter_context(nc.psum_tensor("psb%d" % i, [128, 512], F32))
            self.psb.append(T(p, Dep("ps%d" % i, excl=True)))
        self.ps_rr = 0
        self.rings = {}

    def ps(self, pin=False):
        pinned = getattr(self, "pinned", None)
        if pinned is None:
            pinned = self.pinned = set()
        while self.ps_rr in pinned:
            self.ps_rr = (self.ps_rr + 1) % 8
        i = self.ps_rr
        t = self.psb[i]
        self.ps_rr = (self.ps_rr + 1) % 8
        if pin:
            pinned.add(i)
        return t

    def unpin_all(self):
        self.pinned = set()

    def tile(self, nelem, dtype=F32, name="", parts=128):
        ap, d = self.ar.alloc(nelem, dtype, parts, name)
        return T(ap, d)

    def ring(self, key, n, nelem, dtype=F32):
        if key not in self.rings:
            off = self.ar.top
            self.rings[key] = [[self.tile(nelem, dtype, "%s%d" % (key, i)) for i in range(n)], 0, off]
        r = self.rings[key]
        t = r[0][r[1] % len(r[0])]
        r[1] += 1
        return t

    def mark(self):
        return self.ar.mark()

    def release(self, m):
        for k in [k for k, r in self.rings.items() if r[2] >= m]:
            del self.rings[k]
        self.ar.release(m)

    def I(self, eng, name, reads, writes, **kw):
        self.S.op(eng, [(name, kw)], [t.d for t in reads], [t.d for t in writes])

    def dma(self, out, in_, reads=(), writes=(), eng="sp", multi=False):
        self.S.op(eng, [("dma_start", dict(out=out, in_=in_))], [t.d for t in reads], [t.d for t in writes], dma=True, multi=multi)

    def act(self, out, in_, func, reads, writes, **kw):
        self.I("act", "activation", reads, writes, out=out, in_=in_, func=func, **kw)

    def tt(self, eng, out, in0, in1, op, reads, writes):
        self.I(eng, "tensor_tensor", reads, writes, out=out, in0=in0, in1=in1, op=op)

    def ts(self, eng, out, in0, s1, op0, reads, writes, s2=None, op1=None):
        kw = dict(out=out, in0=in0, scalar1=s1, scalar2=s2, op0=op0)
        if op1 is not None:
            kw["op1"] = op1
        self.I(eng, "tensor_scalar", reads, writes, **kw)

    def stt(self, out, in0, scalar, in1, op0, op1, reads, writes):
        self.I("dve", "scalar_tensor_tensor", reads, writes, out=out, in0=in0, scalar=scalar, in1=in1, op0=op0, op1=op1)

    def copy(self, eng, out, in_, reads, writes):
        if eng == "act":
            self.act(out, in_, AF.Copy, reads, writes)
        else:
            self.I(eng, "tensor_copy", reads, writes, out=out, in_=in_)

    def mm(self, mms, reads, writes):
        self.S.op("pe", [("matmul", m) for m in mms], [t.d for t in reads], [t.d for t in writes])

    def tr(self, out, in_, ident, reads, writes):
        self.S.op("pe", [("transpose", dict(out=out, in_=in_, identity=ident))], [t.d for t in reads], [t.d for t in writes])


def load_w(cx, dram_ap, kc, cw, key="w"):
    n = kc * cw
    wp = getattr(cx, "wpiece", 2048)
    assert n <= wp
    st = cx.ring(key + "_st", 2, wp, F32)
    wb = cx.ring(key + "_wb", 3, wp, BF16)
    cx.dma(st.ap[:, 0:n], dram_ap.rearrange("p k c -> p (k c)"), [], [st])
    cx.copy(getattr(cx, "cast_eng", "pool"), wb.ap[:, 0:n], st.ap[:, 0:n], [st], [wb])
    return wb


def load_w_multi(cx, dram_ap, kc, cw, key="w"):
    per = max(1, getattr(cx, "wpiece", 2048) // cw)
    out = []
    k0 = 0
    while k0 < kc:
        kn = min(per, kc - k0)
        out.append((load_w(cx, dram_ap[:, k0:k0 + kn, :], kn, cw, key), k0, kn))
        k0 += kn
    return out


def load_w_big(cx, dram_ap, kc, cw, name):
    wp = getattr(cx, "wpiece", 2048)
    big = cx.tile(kc * cw, BF16, name)
    per = max(1, wp // cw)
    k0 = 0
    while k0 < kc:
        kn = min(per, kc - k0)
        n = kn * cw
        st = cx.ring("w_st", 2, wp, F32)
        cx.dma(st.ap[:, 0:n], dram_ap[:, k0:k0 + kn, :].rearrange("p k c -> p (k c)"), [], [st])
        cx.copy("pool", big.ap[:, k0 * cw:(k0 + kn) * cw], st.ap[:, 0:n], [st], [big])
        k0 += kn
    return [(big, 0, kc)]


def wslice(pieces, k, cw, m0=0, m1=None):
    m1 = cw if m1 is None else m1
    for wb, k0, kn in pieces:
        if k0 <= k < k0 + kn:
            return wb.ap[:, (k - k0) * cw + m0:(k - k0) * cw + m1]
    raise KeyError(k)


def gemm_block(cx, pieces, cw, act_fn, act_tiles, lo, hi, ks=None, m0=0, m1=None, po=0):
    ps = cx.ps()
    n = hi - lo
    m1 = cw if m1 is None else m1
    if ks is None:
        ks = []
        for wb, k0, kn in pieces:
            ks += list(range(k0, k0 + kn))
    mms = []
    for i, k in enumerate(ks):
        mms.append(dict(out=ps.ap[po:po + m1 - m0, 0:n], lhsT=wslice(pieces, k, cw, m0, m1), rhs=act_fn(k, lo, hi),
                        start=(i == 0), stop=(i == len(ks) - 1)))
    cx.mm(mms, [p[0] for p in pieces] + list(act_tiles), [ps])
    return ps


def rmsnorm_fm(cx, srcs_fn, g_t, out_fn, tiles, ones_bf, nk=KC):
    for (lo, hi) in tiles:
        n = hi - lo
        ps = cx.ps()
        srcs = [srcs_fn(k, lo, hi) for k in range(nk)]
        for k in range(nk):
            sqt = cx.ring("rn_sq", 3, 512, BF16)
            sap, st = srcs[k]
            cx.act(sqt.ap[:, 0:n], sap, AF.Square, [st], [sqt])
            cx.mm([dict(out=ps.ap[:, 0:n], lhsT=ones_bf.ap, rhs=sqt.ap[:, 0:n], start=(k == 0), stop=(k == nk - 1))],
                  [sqt, ones_bf], [ps])
        rinv = cx.ring("rn_rinv", 2, 512, F32)
        cx.act(rinv.ap[:, 0:n], ps.ap[:, 0:n], AF.Ln, [ps], [rinv], scale=1.0 / (nk * 128), bias=EPS)
        cx.act(rinv.ap[:, 0:n], rinv.ap[:, 0:n], AF.Exp, [rinv], [rinv], scale=-0.5)
        for k in range(nk):
            sap, st = srcs[k]
            oap, ot = out_fn(k, lo, hi)
            cx.stt(oap, sap, g_t.ap[:, k:k + 1], rinv.ap[:, 0:n], ALU.mult, ALU.mult, [st, rinv, g_t], [ot])


def load_small(cx, dram_ap, nelem, name, parts=128):
    t = cx.tile(nelem, F32, name, parts)
    cx.dma(t.ap, dram_ap, [], [t])
    return t


def phase_T(cx, io, final, half):
    NT = NTOK_T
    cx.cast_eng = "act"
    base = cx.mark()
    ones_bf = cx.tile(128, BF16, "ones")
    cx.I("dve", "memset", [], [ones_bf], ap=ones_bf.ap, constant=1.0)
    g1 = load_small(cx, io["g_attn"], 16, "g1")
    g2 = load_small(cx, io["g_ffn"], 16, "g2")
    g3 = load_small(cx, io["g_next"], 16, "g3")
    convw = load_small(cx, io["fconv"], 88 * 3, "convw")
    hm = load_small(cx, io["hmask"], 2, "hmask")
    mg_t = cx.tile(KC * NT, BF16, "merged")
    h_t = cx.tile(KC * NT, BF16, "h")
    mv = mg_t.ap.rearrange("p (k t) -> p k t", k=KC)
    hv = h_t.ap.rearrange("p (k t) -> p k t", k=KC)
    M0 = cx.mark()
    cx.S.tag = "T/A1"
    y_t = cx.tile(20 * NT, BF16, "y")
    yv = y_t.ap.rearrange("p (k t) -> p k t", k=20)
    t0 = 0
    xdeps = io.get("x_deps", [])
    ydeps = io.get("y_deps", [])
    if io["x_mode"] == "input":
        xTT = io["xTT"].rearrange("(k p) t -> p k t", p=128)

        def load_x(dst_ap, dst_t, k, lo, hi, multi=False):
            cx.dma(dst_ap, xTT[:, k, lo:hi], [], [dst_t], multi=multi)
    else:
        xloc_read = io["xloc_read"]
        xhalo = io["xhalo"]

        def load_x(dst_ap, dst_t, k, lo, hi, multi=False):
            a = lo
            if lo < 2:
                cx.dma(dst_ap[:, 0:2 - lo], xhalo(k)[:, lo:2], xdeps, [dst_t])
                cx.ts("dve", dst_ap[:, 0:2 - lo], dst_ap[:, 0:2 - lo], hm.ap[:, 1:2], ALU.mult, [dst_t, hm], [dst_t])
                a = 2
            if hi > a:
                for src, off, n in xloc_read(k, a - 2, hi - 2):
                    cx.dma(dst_ap[:, a - lo + off:a - lo + off + n], src, xdeps, [dst_t], multi=(multi and lo >= 2))

    yread = io["yread"]
    kmap = [(0, b) for b in range(4)] + [(1, b) for b in range(4)] + [(0, 4), (0, 5), (1, 4), (1, 5)] + \
           [(0, b) for b in range(6, 10)] + [(1, b) for b in range(6, 10)]
    M1 = cx.mark()
    for k in range(20):
        r_, b_ = kmap[k]
        ta = cx.ring("yh0", 2, 1024, BF16)
        tb_ = cx.ring("yh1", 2, 1024, BF16)
        for q in range(2):
            cx.dma(ta.ap[:, q * 512:(q + 1) * 512], yread(r_, b_, q), ydeps, [ta], multi=True)
            cx.dma(tb_.ap[:, q * 512:(q + 1) * 512], yread(r_, b_, 2 + q), ydeps, [tb_], multi=True)
        cx.ts("pool", yv[:, k, 0:2], ta.ap[:, 1022:1024], hm.ap[:, 1:2], ALU.mult, [ta, hm], [y_t])
        cx.ts("pool", tb_.ap, tb_.ap, hm.ap[:, 1:2], ALU.mult, [tb_, hm], [tb_])
        cx.stt(yv[:, k, 2:NT], ta.ap, hm.ap[:, 0:1], tb_.ap, ALU.mult, ALU.add, [ta, tb_, hm], [y_t])
    for (lo, hi) in TT:
        n = hi - lo
        xt = cx.ring("xtile", 1, KC * 512, F32)
        xtv = xt.ap.rearrange("p (k t) -> p k t", k=KC)
        for k in range(KC):
            load_x(xtv[:, k, 0:n], xt, k, lo, hi, multi=True)
        rmsnorm_fm(cx, lambda k, l, h_: (xtv[:, k, 0:h_ - l], xt), g1, lambda k, l, h_: (hv[:, k, l:h_], h_t),
                   [(lo, hi)], ones_bf)
    cx.release(M1)
    wg = io["wg"]
    wp = io["wp"]
    ksl = [(0, 8), (8, 12), (12, 20)]
    for j in range(16):
        gts = []
        for br in range(3):
            pcs = load_w_multi(cx, wg[br * 16 + j], KC, 128)
            gt = cx.ring("gate", 4, NT, BF16)
            for (lo, hi) in TT:
                ps = gemm_block(cx, pcs, 128, lambda k, l, h_: hv[:, k, l:h_], [h_t], lo, hi)
                cx.act(gt.ap[:, lo:hi], ps.ap[:, 0:hi - lo], AF.Sigmoid, [ps], [gt])
            gts.append(gt)
        pw = load_w_multi(cx, wp[j], 20, 128)
        for (lo, hi) in TT:
            n = hi - lo
            tmp = cx.ring("mtmp", 2, 512, F32)
            for br in range(3):
                k0, k1 = ksl[br]
                ps = gemm_block(cx, pw, 128, lambda k, l, h_: yv[:, k, l:h_], [y_t], lo, hi, ks=list(range(k0, k1)))
                gt = gts[br]
                if br == 0:
                    cx.tt("dve", tmp.ap[:, 0:n], ps.ap[:, 0:n], gt.ap[:, lo:hi], ALU.mult, [ps, gt], [tmp])
                else:
                    t2 = cx.ring("mtmp2", 2, 512, F32)
                    cx.tt("dve", t2.ap[:, 0:n], ps.ap[:, 0:n], gt.ap[:, lo:hi], ALU.mult, [ps, gt], [t2])
                    if br == 1:
                        cx.tt("pool", tmp.ap[:, 0:n], tmp.ap[:, 0:n], t2.ap[:, 0:n], ALU.add, [tmp, t2], [tmp])
                    else:
                        cx.tt("pool", mv[:, j, lo:hi], tmp.ap[:, 0:n], t2.ap[:, 0:n], ALU.add, [tmp, t2], [mg_t])
    cx.release(M0)
    cx.S.tag = "T/A2"
    x_sb = cx.tile(KC * NT, F32, "x_sb")
    xv = x_sb.ap.rearrange("p (k t) -> p k t", k=KC)
    M2 = cx.mark()
    wo = io["wo"]
    for j in range(16):
        pcs = load_w_multi(cx, wo[j], KC, 128)
        xin = cx.ring("xin", 2, NT, F32)
        load_x(xin.ap, xin, j, 0, NT)
        for (lo, hi) in TT:
            ps = gemm_block(cx, pcs, 128, lambda k, l, h_: mv[:, k, l:h_], [mg_t], lo, hi)
            cx.tt("dve", xv[:, j, lo:hi], ps.ap[:, 0:hi - lo], xin.ap[:, lo:hi], ALU.add, [ps, xin], [x_sb])
    cx.release(M2)
    cx.S.tag = "T/F"
    rmsnorm_fm(cx, lambda k, l, h_: (xv[:, k, l:h_], x_sb), g2, lambda k, l, h_: (hv[:, k, l:h_], h_t), TT, ones_bf)
    GRP = 11
    a_t = T(mg_t.ap, mg_t.d)
    av = a_t.ap[:, 0:GRP * 1024].rearrange("p (k t) -> p k t", k=GRP)
    wup = io["wup"]
    wdn = io["wdn"]
    cwv = convw.ap.rearrange("p (b t) -> p b t", t=3)
    for g in range(NFB // GRP):
        for jj in range(GRP):
            jb = g * GRP + jj
            us = []
            for part in range(2):
                blk = part * NFB + jb
                pcs = load_w_multi(cx, wup[blk], KC, 128)
                u = cx.ring("u_f32", 2, NT, F32)
                for (lo, hi) in TT:
                    ps = gemm_block(cx, pcs, 128, lambda k, l, h_: hv[:, k, l:h_], [h_t], lo, hi)
                    cx.copy("act", u.ap[:, lo:hi], ps.ap[:, 0:hi - lo], [ps], [u])
                c = cx.ring("c_f32", 2, 1024, F32)
                cx.ts("dve", c.ap, u.ap[:, 2:NT], cwv[:, blk, 2:3], ALU.mult, [u, convw], [c])
                cx.stt(c.ap, u.ap[:, 1:NT - 1], cwv[:, blk, 1:2], c.ap, ALU.mult, ALU.add, [u, convw, c], [c])
                cx.stt(c.ap, u.ap[:, 0:NT - 2], cwv[:, blk, 0:1], c.ap, ALU.mult, ALU.add, [u, convw, c], [c])
                us.append(c)
            sg = cx.ring("silu", 2, 1024, F32)
            cx.act(sg.ap, us[0].ap, AF.Silu, [us[0]], [sg])
            cx.tt("pool", av[:, jj, :], sg.ap, us[1].ap, ALU.mult, [sg, us[1]], [a_t])
        for j in range(16):
            pcs = load_w_multi(cx, wdn[j][:, g * GRP:(g + 1) * GRP, :], GRP, 128)
            for ti in range(2):
                lo, hi = ti * 512, ti * 512 + 512
                ps = gemm_block(cx, pcs, 128, lambda k, l, h_: av[:, k, l:h_], [a_t], lo, hi)
                cx.tt("dve", xv[:, j, 2 + lo:2 + hi], ps.ap[:, 0:512], xv[:, j, 2 + lo:2 + hi], ALU.add, [ps, x_sb], [x_sb])
    cx.release(M2)
    outT = io["outT"].rearrange("(k p) t -> p k t", p=128) if final else None
    odeps = io.get("out_deps", [])
    if not final:
        for k in range(KC):
            io["xwrite"](cx, k, xv[:, k, 2:NT], [x_sb])
    else:
        o_t = T(h_t.ap.bitcast(F32)[:, 0:KC * 512], h_t.d)
        ov = o_t.ap.rearrange("p (k t) -> p k t", k=KC)
        for ti in range(2):
            lo, hi = 2 + ti * 512, 2 + ti * 512 + 512
            rmsnorm_fm(cx, lambda k, l, h_: (xv[:, k, l:h_], x_sb), g3, lambda k, l, h_: (ov[:, k, 0:512], o_t),
                       [(lo, hi)], ones_bf)
            for k in range(KC):
                cx.dma(outT[:, k, t0 + lo - 2:t0 + lo - 2 + 512], ov[:, k, :], [o_t], odeps)
    cx.release(base)


import os
ATT_G = os.environ.get("ATT_G", "012")
ATT_STOP = int(os.environ.get("ATT_STOP", "99"))
ATT_SUB = int(os.environ.get("ATT_SUB", "3"))
RW_DBG = os.environ.get("RW_DBG", "")
ATT_V = os.environ.get("ATT_V", "ab")
NCH = 16
C_ID, C_MUI, C_MUS, C_MLI, C_BO, C_BS, C_S127, C_MLS = 0, 128, 256, 384, 512, 640, 642, 770
C_BD8, C_O16, C_O32, C_O64, C_O128 = 898, 1026, 1154, 1282, 1410
CST_N = 1538


def make_consts():
    c = np.zeros((128, CST_N), np.float32)
    j = np.arange(128)[:, None]
    i = np.arange(128)[None, :]
    c[:, C_ID:C_ID + 128] = (i == j)
    c[:, C_MUI:C_MUI + 128] = (i >= j)
    c[:, C_MUS:C_MUS + 128] = (i > j)
    c[:, C_MLI:C_MLI + 128] = (j >= i)
    c[:, C_BO:C_BO + 128] = ((i // 64) == (j // 64))
    c[:, C_BS:C_BS + 2] = (np.arange(2)[None, :] == (j // 64))
    c[:, C_S127:C_S127 + 128] = (j == 127)
    c[:, C_MLS:C_MLS + 128] = (j > i)
    bd = lambda s_: (i // s_) == (j // s_)
    c[:, C_BD8:C_BD8 + 128] = bd(8)
    c[:, C_O16:C_O16 + 128] = bd(16) & ~bd(8)
    c[:, C_O32:C_O32 + 128] = bd(32) & ~bd(16)
    c[:, C_O64:C_O64 + 128] = bd(64) & ~bd(32)
    c[:, C_O128:C_O128 + 128] = ~bd(64)
    return c


def make_rope():
    half = 16
    inv = 500000.0 ** (-np.arange(half, dtype=np.float32) / half)
    ang = np.arange(SEQ, dtype=np.float32)[None, :] * inv[:, None]
    cos = np.cos(ang).astype(np.float32)
    sin = np.sin(ang).astype(np.float32)
    r = np.zeros((32, 2 * SEQ), np.float32)
    r[0:16, 0:SEQ] = cos
    r[16:32, 0:SEQ] = cos
    r[0:16, SEQ:] = -sin
    r[16:32, SEQ:] = sin
    return r


class Consts:
    pass


def setup_consts(cx, io):
    K = Consts()
    cf = load_small(cx, io["cst"], CST_N, "cst_f32")
    cb = cx.tile(CST_N, BF16, "cst_bf")
    cx.copy("dve", cb.ap, cf.ap, [cf], [cb])
    K.cf, K.cb = cf, cb
    K.ones = cx.tile(128, BF16, "ones")
    cx.I("dve", "memset", [], [K.ones], ap=K.ones.ap, constant=1.0)
    K.m4 = {}
    for nm_, off in (("id", C_ID), ("bd8", C_BD8), ("o16", C_O16), ("o32", C_O32), ("o64", C_O64), ("o128", C_O128)):
        t = cx.tile(512, BF16, "m4" + nm_)
        for q in range(4):
            cx.copy("pool", t.ap[:, q * 128:(q + 1) * 128], cb.ap[:, off:off + 128], [cb], [t])
        K.m4[nm_] = t
    return K


def dplr_head(cx, K, nm, dk, dv, pb, Kg, Bg, Ag, Rg, gtiles, Wst, Win, wtiles, As, Rs, egam, Kn, Bn, V, ntiles, PC, pctiles,
              scale_all, out_cb, defer=False):
    m0 = cx.mark()
    tag0 = cx.S.tag
    cx.S.tag = tag0 + "/dplr_prep"
    idb = K.cb.ap[:, C_ID:C_ID + 128]
    neg = Bg is None
    AakT = cx.tile(NCH * 128, BF16, nm + "aak")
    ArkT = cx.tile(NCH * 128, BF16, nm + "ark")
    ArbT = cx.tile(NCH * 128, BF16, nm + "arb")
    TT_ = cx.tile(NCH * 128, BF16, nm + "TT")
    sl = lambda t, c: t.ap[:, c * 128:(c + 1) * 128]
    m1 = cx.mark()
    HC = 4
    hs = lambda t, q4: t.ap[:, q4 * 512:(q4 + 1) * 512]
    h1 = lambda t, cl: t.ap[:, cl * 128:(cl + 1) * 128]

    def mm4(dst_ps, lhs_t, rhs_t, q4):
        cx.mm([dict(out=dst_ps.ap[:, q * 128:(q + 1) * 128], lhsT=h1(lhs_t, q4 * 4 + q), rhs=h1(rhs_t, q4 * 4 + q), start=True, stop=True)
               for q in range(4)], [lhs_t, rhs_t], [dst_ps])

    def stream(half, tl_):
        U, N, Ua, Na, Nb, Ub_, Nc, Uc, P, Q, Z1b, Z2b = tl_
        for cl in range(HC):
            c = half * HC + cl
            ps = cx.ps()
            mms = [dict(out=ps.ap[:, 0:128], lhsT=Kg(c), rhs=Ag(c), start=True, stop=True),
                   dict(out=ps.ap[:, 128:256], lhsT=Kg(c), rhs=Rg(c), start=True, stop=True)]
            if not neg:
                mms += [dict(out=ps.ap[:, 256:384], lhsT=Bg(c), rhs=Ag(c), start=True, stop=True),
                        dict(out=ps.ap[:, 384:512], lhsT=Bg(c), rhs=Rg(c), start=True, stop=True)]
            cx.mm(mms, gtiles, [ps])
            cx.tt("dve", sl(AakT, c), ps.ap[:, 0:128], Wst(c), ALU.mult, [ps] + wtiles, [AakT])
            cx.tt("dve", sl(ArkT, c), ps.ap[:, 128:256], Win(c), ALU.mult, [ps] + wtiles, [ArkT])
            if neg:
                cx.stt(h1(U, cl), ps.ap[:, 0:128], -1.0, Wst(c), ALU.mult, ALU.mult, [ps] + wtiles, [U])
                cx.stt(sl(ArbT, c), ps.ap[:, 128:256], -1.0, Win(c), ALU.mult, ALU.mult, [ps] + wtiles, [ArbT])
            else:
                cx.tt("dve", h1(U, cl), ps.ap[:, 256:384], Wst(c), ALU.mult, [ps] + wtiles, [U])
                cx.tt("dve", sl(ArbT, c), ps.ap[:, 384:512], Win(c), ALU.mult, [ps] + wtiles, [ArbT])
            if cl % 2 == 1:
                yield
        q4 = 0
        ps = cx.ps()
        pv = ps.ap.bitcast(BF16)
        for q in range(4):
            cx.tr(pv[:, q * 128:(q + 1) * 128], h1(U, q), idb, [U, K.cb], [ps])
        cx.copy("act", hs(N, q4), pv[:, 0:512], [ps], [N])
        yield
        cx.tt("pool", hs(Ua, q4), hs(U, q4), K.m4["bd8"].ap, ALU.mult, [U, K.m4["bd8"]], [Ua])
        cx.tt("pool", hs(Na, q4), hs(N, q4), K.m4["bd8"].ap, ALU.mult, [N, K.m4["bd8"]], [Na])
        yield
        for (dn, du, sn, su) in ((Nb, Ub_, Na, Ua), (Nc, Uc, Nb, Ub_)):
            ps = cx.ps()
            mm4(ps, su, sn, q4)
            cx.copy("act", hs(dn, q4), ps.ap[:, 0:512], [ps], [dn])
            ps = cx.ps()
            mm4(ps, sn, su, q4)
            cx.copy("act", hs(du, q4), ps.ap[:, 0:512], [ps], [du])
            yield
        cx.tt("pool", hs(P, q4), hs(Ua, q4), K.m4["id"].ap, ALU.add, [Ua, K.m4["id"]], [P])
        cx.tt("pool", hs(Q, q4), hs(Na, q4), K.m4["id"].ap, ALU.add, [Na, K.m4["id"]], [Q])
        yield
        for (ln, lu) in ((Nb, Ub_), (Nc, Uc)):
            ps = cx.ps()
            mm4(ps, ln, P, q4)
            cx.tt("dve", hs(P, q4), ps.ap[:, 0:512], hs(P, q4), ALU.add, [ps, P], [P])
            ps = cx.ps()
            mm4(ps, lu, Q, q4)
            cx.tt("dve", hs(Q, q4), ps.ap[:, 0:512], hs(Q, q4), ALU.add, [ps, Q], [Q])
            yield
        NO, UO, P2, Q2 = Ua, Na, Nb, Ub_
        curP, curQ, nxtP, nxtQ = P, Q, P2, Q2
        for li, mk in enumerate(("o16", "o32", "o64", "o128")):
            last = (li == 3)
            cx.tt("pool", hs(NO, q4), hs(N, q4), K.m4[mk].ap, ALU.mult, [N, K.m4[mk]], [NO])
            if not last:
                cx.tt("pool", hs(UO, q4), hs(U, q4), K.m4[mk].ap, ALU.mult, [U, K.m4[mk]], [UO])
            yield
            ps = cx.ps()
            mm4(ps, NO, curP, q4)
            cx.copy("act", hs(Z1b, q4), ps.ap[:, 0:512], [ps], [Z1b])
            if not last:
                ps = cx.ps()
                mm4(ps, UO, curQ, q4)
                cx.copy("act", hs(Z2b, q4), ps.ap[:, 0:512], [ps], [Z2b])
            yield
            ps = cx.ps()
            mm4(ps, curQ, Z1b, q4)
            if last:
                c0_ = (half * HC) * 128
                cx.tt("dve", TT_.ap[:, c0_:c0_ + 512], ps.ap[:, 0:512], hs(curP, q4), ALU.add, [ps, curP], [TT_])
            else:
                cx.tt("dve", hs(nxtP, q4), ps.ap[:, 0:512], hs(curP, q4), ALU.add, [ps, curP], [nxtP])
                ps = cx.ps()
                mm4(ps, curP, Z2b, q4)
                cx.tt("dve", hs(nxtQ, q4), ps.ap[:, 0:512], hs(curQ, q4), ALU.add, [ps, curQ], [nxtQ])
            curP, curQ, nxtP, nxtQ = nxtP, nxtQ, curP, curQ
            yield

    tlA = [cx.tile(HC * 128, BF16, nm + "ivA%d" % i) for i in range(12)]
    tlB = [cx.tile(HC * 128, BF16, nm + "ivB%d" % i) for i in range(12)]
    for pair in range(0, NCH // HC, 2):
        gens = [stream(pair, tlA), stream(pair + 1, tlB)]
        alive = True
        while alive:
            alive = False
            for g_ in gens:
                try:
                    next(g_)
                    alive = True
                except StopIteration:
                    pass
    cx.release(m1)
    AV = None
    if egam is not None:
        AV = cx.tile(NCH * dv, F32, nm + "AV")
        for c in range(NCH):
            ps = cx.ps()
            cx.mm([dict(out=ps.ap[:, 0:dv], lhsT=sl(AakT, c), rhs=V(c), start=True, stop=True)], [AakT] + ntiles, [ps])
            cx.copy("act", AV.ap[:, c * dv:(c + 1) * dv], ps.ap[:, 0:dv], [ps], [AV])
    St = cx.tile(dv, F32, nm + "S")
    Sb = cx.tile(dv, BF16, nm + "Sb")
    cx.I("dve", "memset", [], [St], ap=St.ap, constant=0.0)
    cx.I("dve", "memset", [], [Sb], ap=Sb.ap, constant=0.0)
    rows = slice(pb, pb + dk)

    def step(c):
        cx.S.tag = tag0 + "/dplr_seq"
        Xb = cx.ring(nm + "Xb", 2, dv, BF16)
        Ub = cx.ring(nm + "Ub", 2, dv, BF16)
        psX = cx.ps()
        if egam is None:
            cx.mm([dict(out=psX.ap[:, 0:dv], lhsT=As(c), rhs=Sb.ap[rows, :], start=True, stop=False),
                   dict(out=psX.ap[:, 0:dv], lhsT=sl(AakT, c), rhs=V(c), start=False, stop=True)],
                  gtiles + [Sb, AakT] + ntiles, [psX])
            cx.copy("act", Xb.ap, psX.ap[:, 0:dv], [psX], [Xb])
        else:
            cx.mm([dict(out=psX.ap[:, 0:dv], lhsT=As(c), rhs=Sb.ap[rows, :], start=True, stop=True)], gtiles + [Sb], [psX])
            cx.stt(Xb.ap, psX.ap[:, 0:dv], egam(c), AV.ap[:, c * dv:(c + 1) * dv], ALU.mult, ALU.add, [psX, AV] + pctiles, [Xb])
        psU = cx.ps()
        cx.mm([dict(out=psU.ap[:, 0:dv], lhsT=sl(TT_, c), rhs=Xb.ap, start=True, stop=True)], [TT_, Xb], [psU])
        cx.copy("act", Ub.ap, psU.ap[:, 0:dv], [psU], [Ub])
        if egam is None:
            psY = cx.ps()
            cx.mm([dict(out=psY.ap[:, 0:dv], lhsT=Rs(c), rhs=Sb.ap[rows, :], start=True, stop=False),
                   dict(out=psY.ap[:, 0:dv], lhsT=sl(ArbT, c), rhs=Ub.ap, start=False, stop=False),
                   dict(out=psY.ap[:, 0:dv], lhsT=sl(ArkT, c), rhs=V(c), start=False, stop=True)],
                  gtiles + [Sb, ArbT, ArkT, Ub] + ntiles, [psY])
            out_cb(c, psY.ap[:, 0:dv], psY)
        else:
            psY1 = cx.ps()
            cx.mm([dict(out=psY1.ap[:, 0:dv], lhsT=Rs(c), rhs=Sb.ap[rows, :], start=True, stop=True)], gtiles + [Sb], [psY1])
            psY2 = cx.ps()
            cx.mm([dict(out=psY2.ap[:, 0:dv], lhsT=sl(ArbT, c), rhs=Ub.ap, start=True, stop=False),
                   dict(out=psY2.ap[:, 0:dv], lhsT=sl(ArkT, c), rhs=V(c), start=False, stop=True)],
                  [ArbT, ArkT, Ub] + ntiles, [psY2])
            y2 = cx.ring(nm + "y2", 2, dv, F32)
            cx.copy("act", y2.ap, psY2.ap[:, 0:dv], [psY2], [y2])
            yo = cx.ring(nm + "yo", 2, dv, F32)
            cx.stt(yo.ap, psY1.ap[:, 0:dv], egam(c), y2.ap, ALU.mult, ALU.add, [psY1, y2] + pctiles, [yo])
            out_cb(c, yo.ap, yo)
        psS = cx.ps()
        cx.mm([dict(out=psS.ap[rows, 0:dv], lhsT=Bn(c), rhs=Ub.ap, start=True, stop=False),
               dict(out=psS.ap[rows, 0:dv], lhsT=Kn(c), rhs=V(c), start=False, stop=True)], ntiles + [Ub], [psS])
        if scale_all:
            cx.tt("dve", St.ap[rows, :], psS.ap[rows, 0:dv], St.ap[rows, :], ALU.add, [psS, St], [St])
            cx.ts("dve", St.ap[rows, :], St.ap[rows, :], PC(c), ALU.mult, [St] + pctiles, [St])
        else:
            cx.stt(St.ap[rows, :], St.ap[rows, :], PC(c), psS.ap[rows, 0:dv], ALU.mult, ALU.add, [St, psS] + pctiles, [St])
        cx.copy("act", Sb.ap[rows, :], St.ap[rows, :], [St], [Sb])

    def finish():
        cx.release(m0)
    if defer:
        return step, finish
    for c in range(NCH):
        step(c)
    finish()


def ln_exp_rinv(cx, out_ap, in_ap, reads, wt, scale=1.0, bias=1e-24):
    cx.act(out_ap, in_ap, AF.Ln, reads, [wt], scale=scale, bias=bias)
    cx.act(out_ap, out_ap, AF.Exp, [wt], [wt], scale=-0.5)


def mixer_attention(cx, K, io, hT, hv):
    m0 = cx.mark()
    wat = io["wat"]
    wsw = io["wsw"]
    rope = load_small(cx, io["rope"], 2 * SEQ, "rope", parts=32)
    cosv = rope.ap[:, 0:SEQ]
    sinv = rope.ap[:, SEQ:2 * SEQ]
    mdiag = K.cb.ap[:, C_MUI:C_MUI + 128]
    mprev = K.cb.ap[:, C_MLI:C_MLI + 128]
    mcomb = cx.tile(256, BF16, "mcomb")
    cx.copy("dve", mcomb.ap[:, 0:128], mdiag, [K.cb], [mcomb])
    cx.copy("dve", mcomb.ap[:, 128:256], mprev, [K.cb], [mcomb])
    if ATT_STOP <= 1:
        return
    DIL = [1, 4, 16]
    scale = 128.0 ** -0.5
    for hl in range(2):
        m1 = cx.mark()
        cx.S.tag = "M/att/proj"
        qT = [cx.tile(SEQ, BF16, "qT%d" % g) for g in range(3)]
        kT = [cx.tile(SEQ, BF16, "kT%d" % g) for g in range(3)]
        Vt = [cx.tile(NCH * 128, BF16, "V%d" % g) for g in range(3)]
        for g in range(3):
            d = DIL[g]
            for which, dst in ((0, qT[g]), (1, kT[g])):
                pcs = load_w_multi(cx, wat[(hl * 3 + g) * 3 + which], KC, 128)
                pcs_sw = load_w_multi(cx, wsw[(hl * 3 + g) * 2 + which], KC, 32)
                for tb in range(4):
                    lo, hi = tb * 512, tb * 512 + 512
                    ps = gemm_block(cx, pcs, 128, lambda k, l, h_: hv[:, k, l:h_], [hT], lo, hi)
                    if ATT_SUB >= 1:
                        ps2 = gemm_block(cx, pcs_sw, 32, lambda k, l, h_: hv[:, k, l:h_], [hT], lo, hi)
                    t1 = cx.ring("rp1", 2, 512, F32)
                    t2 = cx.ring("rp2", 2, 512, F32)
                    if ATT_SUB >= 2:
                        if "a" in ATT_V:
                            cx.tt("dve", t1.ap[0:32, :], ps.ap[0:32, 0:512], cosv[:, lo:hi], ALU.mult, [ps, rope], [t1])
                        if "b" in ATT_V:
                            cx.tt("dve", t2.ap[0:32, :], ps2.ap[0:32, 0:512], sinv[:, lo:hi], ALU.mult, [ps2, rope], [t2])
                        if "c" in ATT_V:
                            cx.tt("dve", t1.ap[0:32, :], ps.ap[0:32, 0:512], K.cf.ap[0:32, 0:512], ALU.mult, [ps, K.cf], [t1])
                        if "e" in ATT_V:
                            cx.tt("dve", t1.ap[:, :], ps.ap[:, 0:512], K.cf.ap[:, 0:512], ALU.mult, [ps, K.cf], [t1])
                        if "g" in ATT_V:
                            cx.tt("dve", t1.ap[0:32, :], ps.ap[0:32, 0:512], K.cf.ap[0:32, 0:512], ALU.mult, [ps, K.cf], [t1])
                        if "d" in ATT_V:
                            cx.tt("dve", t1.ap[0:32, :], t2.ap[0:32, :], cosv[:, lo:hi], ALU.mult, [t2, rope], [t1])
                    cx.copy("act", dst.ap[:, lo:hi], ps.ap[:, 0:512], [ps, t1] if "s" in ATT_V else [ps], [dst])
                    if ATT_SUB >= 3:
                        cx.tt("pool", dst.ap[0:32, lo:hi], t1.ap[0:32, :], t2.ap[0:32, :], ALU.add, [t1, t2], [dst])
            if ATT_STOP <= 2:
                return
            pcs = load_w_multi(cx, wat[(hl * 3 + g) * 3 + 2], KC, 128)
            nb = NCH // d
            for b4 in range(4):
                ps = cx.ps()
                mms = []
                for q in range(4):
                    blk = b4 * 4 + q
                    r, b = blk // nb, blk % nb
                    t0 = r + d * 128 * b
                    for k in range(KC):
                        mms.append(dict(out=ps.ap[:, q * 128:(q + 1) * 128], lhsT=hv[:, k, t0:t0 + d * 127 + 1:d],
                                        rhs=wslice(pcs, k, 128), start=(k == 0), stop=(k == KC - 1)))
                cx.mm(mms, [p[0] for p in pcs] + [hT], [ps])
                cx.copy("act", Vt[g].ap[:, b4 * 512:(b4 + 1) * 512], ps.ap[:, 0:512], [ps], [Vt[g]])
        if ATT_STOP <= 3:
            return
        cx.S.tag = "M/att/core"
        for Tb in range(4):
            pso = cx.ps(pin=True)
            psd = cx.ps(pin=True)
            first = [True]

            def unit(g, kslices, qslice, nq, masks):
                pss = cx.ps()
                nk = len(kslices)
                cx.mm([dict(out=pss.ap[:, i * nq:(i + 1) * nq], lhsT=kT[g].ap[:, ks], rhs=qT[g].ap[:, qslice], start=True, stop=True)
                       for i, (ks, vb) in enumerate(kslices)], [kT[g], qT[g]], [pss])
                pe_ = cx.ring("pexp", 3, 256, BF16)
                pm = cx.ring("pmask", 3, 256, BF16)
                cx.act(pe_.ap[:, 0:nk * nq], pss.ap[:, 0:nk * nq], AF.Exp, [pss], [pe_], scale=scale)
                cx.tt("pool", pm.ap[:, 0:nk * nq], pe_.ap[:, 0:nk * nq], masks, ALU.mult, [pe_, mcomb, K.cb], [pm])
                mmo, mmd = [], []
                qs0 = qslice.start - Tb * 512
                st = qslice.step or 1
                ocols = slice(qs0, qs0 + (nq - 1) * st + 1, st)
                for i, (ks, vb) in enumerate(kslices):
                    f = first[0]
                    first[0] = False
                    mmo.append(dict(out=pso.ap[:, ocols], lhsT=Vt[g].ap[:, vb * 128:(vb + 1) * 128], rhs=pm.ap[:, i * nq:(i + 1) * nq],
                                    start=f, stop=False, skip_group_check=True))
                    mmd.append(dict(out=psd.ap[:, ocols], lhsT=K.ones.ap, rhs=pm.ap[:, i * nq:(i + 1) * nq],
                                    start=f, stop=False, skip_group_check=True))
                cx.mm(mmo, [Vt[g], pm], [pso])
                cx.mm(mmd, [K.ones, pm], [psd])

            for qb in range(4 * Tb, 4 * Tb + 4):
                ks = [(slice(qb * 128, qb * 128 + 128), qb)]
                if qb > 0:
                    ks.append((slice((qb - 1) * 128, qb * 128), qb - 1))
                unit(0, ks, slice(qb * 128, qb * 128 + 128), 128, mcomb.ap[:, 0:128 * len(ks)])
            for r in range(4 if "1" in ATT_G else 0):
                tq = r + 4 * 128 * Tb
                ks = [(slice(tq, tq + 4 * 127 + 1, 4), r * 4 + Tb)]
                if Tb > 0:
                    tk = r + 4 * 128 * (Tb - 1)
                    ks.append((slice(tk, tk + 4 * 127 + 1, 4), r * 4 + Tb - 1))
                unit(1, ks, slice(tq, tq + 4 * 127 + 1, 4), 128, mcomb.ap[:, 0:128 * len(ks)])
            for r in range(16 if "2" in ATT_G else 0):
                tq = r + 16 * 32 * Tb
                ks = [(slice(r, r + 16 * 127 + 1, 16), r)]
                unit(2, ks, slice(tq, tq + 16 * 31 + 1, 16), 32, mdiag[:, 32 * Tb:32 * Tb + 32])
            rd = cx.ring("rden", 2, 512, F32)
            cx.I("dve", "reciprocal", [psd], [rd], out=rd.ap, in_=psd.ap[:, 0:512])
            yo = cx.ring("ybo", 2, 512, BF16)
            cx.tt("dve", yo.ap, pso.ap[:, 0:512], rd.ap, ALU.mult, [pso, rd], [yo])
            io["ywrite"](cx, io["rb"] + hl, Tb * 512, Tb * 512 + 512, yo.ap, [yo], io["ydep"]["b"])
            cx.unpin_all()
            if ATT_STOP <= 4:
                return
        cx.release(m1)
    cx.release(m0)


def mixer_gdn(cx, K, io, hT, hv):
    m0 = cx.mark()
    wc = io["wc"]
    wba = io["wba"]
    gconv = load_small(cx, io["gconv"], 12 * 4, "gconv")
    gcv = gconv.ap.rearrange("p (b t) -> p b t", t=4)
    gpar = load_small(cx, io["gpar"], 8 + 128, "gpar")
    mui = K.cf.ap[:, C_MUI:C_MUI + 128]
    mus = K.cf.ap[:, C_MUS:C_MUS + 128]
    mls = K.cf.ap[:, C_MLS:C_MLS + 128]
    idb = K.cb.ap[:, C_ID:C_ID + 128]
    pcs = load_w_multi(cx, wba, KC, 8)
    ps = cx.ps()
    mms = []
    for c in range(NCH):
        for k in range(KC):
            mms.append(dict(out=ps.ap[:, c * 8:(c + 1) * 8], lhsT=hv[:, k, c * 128:(c + 1) * 128], rhs=wslice(pcs, k, 8),
                            start=(k == 0), stop=(k == KC - 1)))
    cx.mm(mms, [p[0] for p in pcs] + [hT], [ps])
    ba = cx.tile(NCH * 8, F32, "ba")
    cx.copy("act", ba.ap, ps.ap[:, 0:NCH * 8], [ps], [ba])
    bav = ba.ap.rearrange("p (c e) -> p c e", e=8)
    beta = cx.tile(NCH * 4, F32, "beta")
    betav = beta.ap.rearrange("p (c e) -> p c e", e=4)
    cx.act(betav, bav[:, :, 0:4], AF.Sigmoid, [ba], [beta])
    gg = cx.tile(NCH * 4, F32, "gg")
    ggv = gg.ap.rearrange("p (c e) -> p c e", e=4)
    for hh in range(4):
        cx.act(ggv[:, :, hh], bav[:, :, 4 + hh], AF.Exp, [ba, gpar], [gg], bias=gpar.ap[:, 4 + hh:5 + hh])
    cx.act(gg.ap, gg.ap, AF.Ln, [gg], [gg], bias=1.0)
    ea = cx.tile(4, F32, "expA")
    cx.act(ea.ap, gpar.ap[:, 0:4], AF.Exp, [gpar], [ea])
    for hh in range(4):
        cx.ts("dve", ggv[:, :, hh], ggv[:, :, hh], ea.ap[:, hh:hh + 1], ALU.mult, [gg, ea], [gg], s2=-1.0, op1=ALU.mult)
    ps = cx.ps()
    cx.mm([dict(out=ps.ap[:, 0:64], lhsT=mui, rhs=gg.ap, start=True, stop=True)], [K.cf, gg], [ps])
    gam = cx.tile(64, F32, "gam")
    cx.copy("act", gam.ap, ps.ap[:, 0:64], [ps], [gam])
    egam = cx.tile(64, F32, "egam")
    cx.act(egam.ap, gam.ap, AF.Exp, [gam], [egam])
    ps = cx.ps()
    cx.mm([dict(out=ps.ap[:, 0:64], lhsT=K.cf.ap[:, C_S127:C_S127 + 128], rhs=gam.ap, start=True, stop=True)], [K.cf, gam], [ps])
    pcall = cx.tile(64, F32, "pcall")
    cx.act(pcall.ap, ps.ap[:, 0:64], AF.Exp, [ps], [pcall])
    wk = cx.tile(64, F32, "wk")
    cx.tt("dve", wk.ap, ps.ap[:, 0:64], gam.ap, ALU.subtract, [ps, gam], [wk])
    cx.act(wk.ap, wk.ap, AF.Exp, [wk], [wk])
    cx.tt("dve", wk.ap, wk.ap, beta.ap, ALU.mult, [wk, beta], [wk])
    nwk = cx.tile(64, F32, "nwk")
    cx.ts("dve", nwk.ap, wk.ap, -1.0, ALU.mult, [wk], [nwk])
    for hh in range(4):
        m1 = cx.mark()
        cx.S.tag = "M/gdn/prep"
        qT = cx.tile(SEQ, BF16, "gq")
        kT = cx.tile(SEQ, BF16, "gk")
        Vn = cx.tile(NCH * 128, BF16, "gV")
        Kn = cx.tile(NCH * 128, BF16, "gKn")
        Bn = cx.tile(NCH * 128, BF16, "gBn")
        gate = cx.tile(NCH * 128, BF16, "ggate")
        Wst = cx.tile(NCH * 128, F32, "gWst")
        Win = cx.tile(NCH * 128, F32, "gWin")
        m2 = cx.mark()
        vT = cx.tile(SEQ, BF16, "gvT")
        for which, dst in ((0, qT), (1, kT), (2, vT)):
            pcs = load_w_multi(cx, wc[which * 4 + hh], KC, 128)
            zp = cx.ring("gzp", 2, 3 + SEQ, F32)
            cx.I("dve", "memset", [], [zp], ap=zp.ap[:, 0:3], constant=0.0)
            for tb in range(4):
                lo, hi = tb * 512, tb * 512 + 512
                ps = gemm_block(cx, pcs, 128, lambda k, l, h_: hv[:, k, l:h_], [hT], lo, hi)
                cx.copy("act", zp.ap[:, 3 + lo:3 + hi], ps.ap[:, 0:512], [ps], [zp])
            cv = cx.ring("gcv", 2, SEQ, F32)
            bi = which * 4 + hh
            cx.ts("dve", cv.ap, zp.ap[:, 3:3 + SEQ], gcv[:, bi, 3:4], ALU.mult, [zp, gconv], [cv])
            for tap in range(3):
                cx.stt(cv.ap, zp.ap[:, tap:tap + SEQ], gcv[:, bi, tap:tap + 1], cv.ap, ALU.mult, ALU.add, [zp, gconv, cv], [cv])
            cx.act(cv.ap, cv.ap, AF.Silu, [cv], [cv])
            if which == 2:
                cx.copy("pool", dst.ap, cv.ap, [cv], [dst])
            else:
                for tb in range(4):
                    lo, hi = tb * 512, tb * 512 + 512
                    sq = cx.ring("gsq", 2, 512, BF16)
                    cx.act(sq.ap, cv.ap[:, lo:hi], AF.Square, [cv], [sq])
                    ps = cx.ps()
                    cx.mm([dict(out=ps.ap[:, 0:512], lhsT=K.ones.ap, rhs=sq.ap, start=True, stop=True)], [K.ones, sq], [ps])
                    ri = cx.ring("gri", 2, 512, F32)
                    ln_exp_rinv(cx, ri.ap, ps.ap[:, 0:512], [ps], ri)
                    if which == 0:
                        cx.stt(dst.ap[:, lo:hi], cv.ap[:, lo:hi], 128.0 ** -0.5, ri.ap, ALU.mult, ALU.mult, [cv, ri], [dst])
                    else:
                        cx.tt("dve", dst.ap[:, lo:hi], cv.ap[:, lo:hi], ri.ap, ALU.mult, [cv, ri], [dst])
        for c4 in range(4):
            ps = cx.ps()
            pv = ps.ap.bitcast(BF16)
            for q in range(4):
                c = c4 * 4 + q
                cx.tr(pv[:, q * 128:(q + 1) * 128], vT.ap[:, c * 128:(c + 1) * 128], idb, [vT, K.cb], [ps])
            cx.copy("act", Vn.ap[:, c4 * 512:(c4 + 1) * 512], pv[:, 0:512], [ps], [Vn])
            ps = cx.ps()
            pv = ps.ap.bitcast(BF16)
            for q in range(4):
                c = c4 * 4 + q
                cx.tr(pv[:, q * 128:(q + 1) * 128], kT.ap[:, c * 128:(c + 1) * 128], idb, [kT, K.cb], [ps])
            for q in range(4):
                c = c4 * 4 + q
                col = c * 4 + hh
                cx.ts("dve", Kn.ap[:, c * 128:(c + 1) * 128], pv[:, q * 128:(q + 1) * 128], wk.ap[:, col:col + 1], ALU.mult, [ps, wk], [Kn])
                cx.ts("dve", Bn.ap[:, c * 128:(c + 1) * 128], pv[:, q * 128:(q + 1) * 128], nwk.ap[:, col:col + 1], ALU.mult, [ps, nwk], [Bn])
        pcs = load_w_multi(cx, wc[12 + hh], KC, 128)
        for c4 in range(4):
            ps = cx.ps()
            mms = []
            for q in range(4):
                c = c4 * 4 + q
                for k in range(KC):
                    mms.append(dict(out=ps.ap[:, q * 128:(q + 1) * 128], lhsT=hv[:, k, c * 128:(c + 1) * 128], rhs=wslice(pcs, k, 128),
                                    start=(k == 0), stop=(k == KC - 1)))
            cx.mm(mms, [p[0] for p in pcs] + [hT], [ps])
            cx.act(gate.ap[:, c4 * 512:(c4 + 1) * 512], ps.ap[:, 0:512], AF.Silu, [ps], [gate])
        for c in range(NCH):
            col = c * 4 + hh
            g2 = cx.ring("gG2", 2, 128, F32)
            cx.ts("dve", g2.ap, mui, gg.ap[:, col:col + 1], ALU.mult, [K.cf, gg], [g2])
            ps = cx.ps()
            cx.mm([dict(out=ps.ap[:, 0:128], lhsT=mls, rhs=g2.ap, start=True, stop=True)], [K.cf, g2], [ps])
            ex = cx.ring("gex", 2, 128, F32)
            cx.act(ex.ap, ps.ap[:, 0:128], AF.Exp, [ps], [ex])
            cx.stt(Wst.ap[:, c * 128:(c + 1) * 128], ex.ap, beta.ap[:, col:col + 1], mus, ALU.mult, ALU.mult, [ex, beta, K.cf], [Wst])
            cx.stt(Win.ap[:, c * 128:(c + 1) * 128], ex.ap, beta.ap[:, col:col + 1], mui, ALU.mult, ALU.mult, [ex, beta, K.cf], [Win])
        cx.release(m2)
        cx.S.tag = "M/gdn"
        sl = lambda t, c: t.ap[:, c * 128:(c + 1) * 128]
        ycs = cx.tile(SEQ, BF16, "ycs")

        def out_cb(c, yap, yt, hh=hh, ycs=ycs):
            junk = cx.ring("gjunk", 2, 128, F32)
            ss = cx.ring("gss", 2, 1, F32)
            cx.act(junk.ap, yap, AF.Square, [yt], [junk, ss], accum_out=ss.ap)
            ln_exp_rinv(cx, ss.ap, ss.ap, [ss], ss, scale=1.0 / 128, bias=EPS)
            o1 = cx.ring("go1", 2, 128, F32)
            cx.stt(o1.ap, yap, ss.ap[:, 0:1], gpar.ap[:, 8:136], ALU.mult, ALU.mult, [yt, ss, gpar], [o1])
            o2 = cx.ring("go2", 2, 128, BF16)
            cx.tt("pool", o2.ap, o1.ap, gate.ap[:, c * 128:(c + 1) * 128], ALU.mult, [o1, gate], [o2])
            pst = cx.ps()
            ptv = pst.ap.bitcast(BF16)
            cx.tr(ptv[:, 0:128], o2.ap, idb, [o2, K.cb], [pst])
            cx.copy("act", ycs.ap[:, c * 128:(c + 1) * 128], ptv[:, 0:128], [pst], [ycs])

        dplr_head(cx, K, "g%d" % hh, 128, 128, 0,
                  Kg=lambda c: sl(kT, c), Bg=None, Ag=lambda c: sl(kT, c), Rg=lambda c: sl(qT, c), gtiles=[kT, qT],
                  Wst=lambda c: sl(Wst, c), Win=lambda c: sl(Win, c), wtiles=[Wst, Win],
                  As=lambda c: sl(kT, c), Rs=lambda c: sl(qT, c), egam=lambda c: egam.ap[:, c * 4 + hh:c * 4 + hh + 1],
                  Kn=lambda c: sl(Kn, c), Bn=lambda c: sl(Bn, c), V=lambda c: sl(Vn, c), ntiles=[Kn, Bn, Vn],
                  PC=lambda c: pcall.ap[:, c * 4 + hh:c * 4 + hh + 1], pctiles=[pcall, egam], scale_all=False, out_cb=out_cb)
        io["ywrite"](cx, io["rc"] + hh, 0, SEQ, ycs.ap, [ycs], io["ydep"]["c"])
        cx.release(m1)
    cx.release(m0)


def mixer_rwkv(cx, K, io, hT, hv, layer1):
    m0 = cx.mark()
    wa = io["wa"]
    wal = io["wal"]
    rp = load_small(cx, io["rpar"], 44, "rpar")
    rl = load_small(cx, io["rlmu"], 4, "rlmu")
    rb = load_small(cx, io["rbc"], 1024, "rbc")
    wab = cx.tile(512, BF16, "wab")
    g2b = cx.tile(1024, BF16, "g2b")
    if layer1:
        v2b_ = cx.tile(512, BF16, "v2b", parts=64)
        v2b = T(v2b_.ap[32:64, :], v2b_.d)
    rmask = cx.tile(SEQ, BF16, "rmask")
    tl = cx.tile(SEQ, BF16, "tl")
    sgd = cx.tile(SEQ, BF16, "sgd")
    sg2h = cx.tile(SEQ, BF16, "sg2h", parts=64)
    sgd2 = T(sg2h.ap[0:32, :], sg2h.d)
    if layer1:
        hv1 = T(sg2h.ap[32:64, :], sg2h.d)
    mt = cx.mark()
    w2a2 = load_small(cx, io["rw2a2"], 512, "rw2a2")
    g2a = load_small(cx, io["rg2"], 1024, "rg2")
    cx.copy("pool", wab.ap, w2a2.ap, [w2a2], [wab])
    cx.copy("pool", g2b.ap, g2a.ap, [g2a], [g2b])
    if layer1:
        v2 = cx.tile(512, F32, "rv2", parts=64)
        cx.dma(v2.ap[32:64, :], io["rv2"], [], [v2])
        cx.copy("pool", v2b.ap, v2.ap[32:64, :], [v2], [v2b])
    idb = K.cb.ap[:, C_ID:C_ID + 128]
    msk_s = K.cb.ap[:, C_MUS:C_MUS + 128]
    msk_i = K.cb.ap[:, C_MUI:C_MUI + 128]
    cx.I("dve", "memset", [], [rmask], ap=rmask.ap, constant=1.0)
    cx.I("dve", "memset", [], [rmask], ap=rmask.ap[:, 0:SEQ:128], constant=0.0)

    def mixed_block(zp, tmp, pcs, cw, c0, m, po, mu_ap, mu_t, dst_ap, dst_t, func=None):
        pr = slice(po, po + m)
        cx.I("dve", "memset", [], [zp], ap=zp.ap[pr, 0:1], constant=0.0)
        for tb in range(4):
            lo, hi = tb * 512, tb * 512 + 512
            ps = cx.ps()
            mms = []
            for k in range(KC):
                mms.append(dict(out=ps.ap[pr, 0:512], lhsT=wslice(pcs, k, cw, c0, c0 + m), rhs=hv[:, k, lo:hi],
                                start=(k == 0), stop=(k == KC - 1)))
            cx.mm(mms, [p[0] for p in pcs] + [hT], [ps])
            cx.copy("act", zp.ap[pr, 1 + lo:1 + hi], ps.ap[pr, 0:512], [ps], [zp])
        cx.tt("dve", tmp.ap[pr, :], zp.ap[pr, 0:SEQ], zp.ap[pr, 1:1 + SEQ], ALU.subtract, [zp], [tmp])
        if func is None:
            cx.stt(dst_ap, tmp.ap[pr, :], mu_ap, zp.ap[pr, 1:1 + SEQ], ALU.mult, ALU.add, [tmp, zp, mu_t], [dst_t])
        else:
            cx.stt(tmp.ap[pr, :], tmp.ap[pr, :], mu_ap, zp.ap[pr, 1:1 + SEQ], ALU.mult, ALU.add, [tmp, zp, mu_t], [tmp])
            cx.act(dst_ap, tmp.ap[pr, :], func, [tmp], [dst_t])

    zp0 = cx.tile(SEQ + 8, F32, "zp0")
    tmp0 = cx.tile(SEQ, F32, "tmp0")
    pcl = load_w_big(cx, wal, KC, 288, "wal_b")
    mixed_block(zp0, tmp0, pcl, 288, 0, 64, 0, rl.ap[0:64, 0:1], rl, tl.ap[0:64, :], tl, func=AF.Tanh)
    mixed_block(zp0, tmp0, pcl, 288, 64, 64, 64, rl.ap[64:128, 0:1], rl, tl.ap[64:128, :], tl, func=AF.Copy)
    mixed_block(zp0, tmp0, pcl, 288, 128, 128, 0, rl.ap[:, 1:2], rl, sgd.ap, sgd, func=AF.Sigmoid)
    mixed_block(zp0, tmp0, pcl, 288, 256, 32, 0, rl.ap[0:32, 2:3], rl, sgd2.ap, sgd2, func=AF.Sigmoid)
    if layer1:
        pc1 = load_w_multi(cx, io["rv1"], KC, 32)
        for tb in range(4):
            lo, hi = tb * 512, tb * 512 + 512
            ps = gemm_block(cx, pc1, 32, lambda k, l, h_: hv[:, k, l:h_], [hT], lo, hi, po=32)
            cx.copy("act", hv1.ap[:, lo:hi], ps.ap[32:64, 0:512], [ps], [hv1])
        vfT = io["vf_in"].rearrange("(k p) t -> p k t", p=128)
    cx.release(mt)
    vout = io["v_out"].rearrange("(k p) t -> p k t", p=128)
    lnw = rb.ap[:, 0:512]
    lnb = rb.ap[:, 512:1024]
    for ct in range(4):
        m1 = cx.mark()
        par = rp.ap[:, 12 + ct * 8:12 + ct * 8 + 8]
        cx.S.tag = "M/rwkv/prep"
        As = cx.tile(SEQ, BF16, "rAs")
        Rs = cx.tile(SEQ, BF16, "rRs")
        Ks = cx.tile(SEQ, BF16, "rKs")
        Bs = cx.tile(SEQ, BF16, "rBs")
        Kn = cx.tile(NCH * 128, BF16, "rKn")
        Bn = cx.tile(NCH * 128, BF16, "rBn")
        Vn = cx.tile(NCH * 128, BF16, "rVn")
        PCt = cx.tile(NCH, F32, "rPC")
        bon = cx.tile(NCH * 2, F32, "rbon")
        m2 = cx.mark()
        zp = cx.tile(SEQ + 8, F32, "zp")
        tmp = cx.tile(SEQ, F32, "tmp")
        ld = cx.tile(SEQ, F32, "ld")
        cum = cx.tile(SEQ, F32, "cum")
        av = cx.tile(SEQ, BF16, "a_sig")
        rm = cx.tile(SEQ, BF16, "r_m")
        km = cx.tile(SEQ, BF16, "k_m")
        vm = cx.tile(SEQ, BF16, "v_m")
        rkr = cx.tile(SEQ, BF16, "rkr")
        for which, dst in ((0, rm), (1, km), (2, vm)):
            pcs = load_w_multi(cx, wa[which * 4 + ct], KC, 128)
            mixed_block(zp, tmp, pcs, 128, 0, 128, 0, rp.ap[:, which * 4 + ct:which * 4 + ct + 1], rp, dst.ap, dst)
        kx = T(zp.ap[:, 0:SEQ], zp.d)
        B1 = tmp
        for tb in range(4):
            lo, hi = tb * 512, tb * 512 + 512
            ps = cx.ps()
            cx.mm([dict(out=ps.ap[:, 0:512], lhsT=wab.ap[0:64, ct * 128:(ct + 1) * 128], rhs=tl.ap[0:64, lo:hi], start=True, stop=True)],
                  [wab, tl], [ps])
            cx.act(ld.ap[:, lo:hi], ps.ap[:, 0:512], AF.Sigmoid, [ps, rp], [ld], bias=par[:, 0:1])
            ps = cx.ps()
            cx.mm([dict(out=ps.ap[:, 0:512], lhsT=wab.ap[64:128, ct * 128:(ct + 1) * 128], rhs=tl.ap[64:128, lo:hi], start=True, stop=True)],
                  [wab, tl], [ps])
            cx.act(av.ap[:, lo:hi], ps.ap[:, 0:512], AF.Sigmoid, [ps, rp], [av], bias=par[:, 1:2])
        cx.ts("dve", ld.ap, ld.ap, -float(np.exp(-0.5)), ALU.mult, [ld], [ld])
        cx.I("dve", "tensor_tensor_scan", [rmask, ld], [cum], out=cum.ap, data0=rmask.ap, data1=ld.ap, initial=0.0,
             op0=ALU.mult, op1=ALU.add)
        if layer1:
            vf = rkr
            cx.dma(vf.ap, vfT[:, ct, :], io["vf_deps"], [vf])
            for tb in range(4):
                lo, hi = tb * 512, tb * 512 + 512
                ps = cx.ps()
                cx.mm([dict(out=ps.ap[:, 0:512], lhsT=v2b.ap[:, ct * 128:(ct + 1) * 128], rhs=hv1.ap[:, lo:hi], start=True, stop=True)],
                      [v2b, hv1], [ps])
                cx.act(B1.ap[:, lo:hi], ps.ap[:, 0:512], AF.Sigmoid, [ps, rp], [B1], bias=par[:, 5:6])
            cx.tt("dve", kx.ap, vf.ap, vm.ap, ALU.subtract, [vf, vm], [kx])
            cx.tt("dve", kx.ap, kx.ap, B1.ap, ALU.mult, [kx, B1], [kx])
            cx.tt("dve", vm.ap, vm.ap, kx.ap, ALU.add, [vm, kx], [vm])
        if io.get("write_v", True):
            cx.dma(vout[:, ct, :], vm.ap, [vm], [io["v_dep"]])
        cx.ts("dve", kx.ap, km.ap, par[:, 2:3], ALU.mult, [km, rp], [kx])
        for tb in range(4):
            lo, hi = tb * 512, tb * 512 + 512
            sq = cx.ring("rsq", 2, 512, BF16)
            cx.act(sq.ap, kx.ap[:, lo:hi], AF.Square, [kx], [sq])
            ps = cx.ps()
            cx.mm([dict(out=ps.ap[:, 0:512], lhsT=K.cb.ap[:, C_BO:C_BO + 128], rhs=sq.ap, start=True, stop=True)], [K.cb, sq], [ps])
            ri = cx.ring("rri", 2, 512, F32)
            ln_exp_rinv(cx, ri.ap, ps.ap[:, 0:512], [ps], ri)
            cx.tt("dve", kx.ap[:, lo:hi], kx.ap[:, lo:hi], ri.ap, ALU.mult, [kx, ri], [kx])
        if RW_DBG and ct == 0:
            dbg = io["dbg"].rearrange("(k p) t -> p k t", p=128)
            cx.dma(dbg[:, 0, :], kx.ap, [kx], [])
            cx.dma(dbg[:, 1, :], ld.ap, [ld], [])
            cx.dma(dbg[:, 2, :], cum.ap, [cum], [])
        cx.ts("dve", B1.ap, av.ap, -1.0, ALU.add, [av, rp], [B1], s2=par[:, 3:4], op1=ALU.mult)
        cx.ts("dve", B1.ap, B1.ap, 1.0, ALU.add, [B1], [B1])
        if RW_DBG and ct == 0:
            cx.dma(dbg[:, 3, :], B1.ap, [B1], [])
        cx.tt("dve", km.ap, km.ap, B1.ap, ALU.mult, [km, B1], [km])
        cx.stt(rkr.ap, rm.ap, par[:, 4:5], km.ap, ALU.mult, ALU.mult, [rm, km, rp], [rkr])
        ps = cx.ps()
        cx.mm([dict(out=ps.ap[:, c * 2:c * 2 + 2], lhsT=rkr.ap[:, c * 128:(c + 1) * 128], rhs=K.cb.ap[:, C_BS:C_BS + 2], start=True, stop=True)
               for c in range(NCH)], [rkr, K.cb], [ps])
        cx.copy("act", bon.ap, ps.ap[:, 0:NCH * 2], [ps], [bon])
        cx.tt("dve", B1.ap, cum.ap, ld.ap, ALU.subtract, [cum, ld], [B1])
        cx.act(B1.ap, B1.ap, AF.Exp, [B1], [B1])
        cx.stt(As.ap, kx.ap, -1.0, B1.ap, ALU.mult, ALU.mult, [kx, B1], [As])
        cx.act(B1.ap, cum.ap, AF.Exp, [cum], [B1])
        cx.tt("dve", Rs.ap, rm.ap, B1.ap, ALU.mult, [rm, B1], [Rs])
        cx.copy("dve", PCt.ap, B1.ap[:, 127:SEQ:128], [B1], [PCt])
        cx.act(B1.ap, cum.ap, AF.Exp, [cum], [B1], scale=-1.0)
        cx.tt("dve", Ks.ap, km.ap, B1.ap, ALU.mult, [km, B1], [Ks])
        cx.tt("dve", kx.ap, kx.ap, av.ap, ALU.mult, [kx, av], [kx])
        cx.tt("dve", Bs.ap, kx.ap, B1.ap, ALU.mult, [kx, B1], [Bs])
        for src, dst in ((Ks, Kn), (Bs, Bn), (vm, Vn)):
            for c4 in range(4):
                ps = cx.ps()
                pv = ps.ap.bitcast(BF16)
                for q in range(4):
                    c = c4 * 4 + q
                    cx.tr(pv[:, q * 128:(q + 1) * 128], src.ap[:, c * 128:(c + 1) * 128], idb, [src, K.cb], [ps])
                cx.copy("act", dst.ap[:, c4 * 512:(c4 + 1) * 512], pv[:, 0:512], [ps], [dst])
        cx.release(m2)
        yas = cx.tile(SEQ, BF16, "yas")
        cx.S.tag = "M/rwkv"
        sfs = []
        for hp in range(2):
            pb = hp * 64
            hl = ct * 2 + hp
            pr = slice(pb, pb + 64)

            def out_cb(c, yap, yt, hl=hl, hp=hp, ct=ct, yas=yas, pb=pb):
                st = cx.ring("rst%d" % hp, 2, 6, F32)
                cx.I("dve", "bn_stats", [yt], [st], out=st.ap, in_=yap)
                mv_ = cx.ring("rmv%d" % hp, 2, 2, F32)
                cx.I("dve", "bn_aggr", [st], [mv_], out=mv_.ap, in_=st.ap)
                rs = cx.ring("rrs%d" % hp, 2, 1, F32)
                ln_exp_rinv(cx, rs.ap, mv_.ap[:, 1:2], [mv_], rs, bias=64e-5)
                y1 = cx.ring("ry1%d" % hp, 2, 64, F32)
                cx.ts("dve", y1.ap, yap, mv_.ap[:, 0:1], ALU.subtract, [yt, mv_, rs], [y1], s2=rs.ap[:, 0:1], op1=ALU.mult)
                cx.tt("pool", y1.ap, y1.ap, lnw[:, hl * 64:(hl + 1) * 64], ALU.mult, [y1, rb], [y1])
                cx.tt("pool", y1.ap, y1.ap, lnb[:, hl * 64:(hl + 1) * 64], ALU.add, [y1, rb], [y1])
                y2 = cx.ring("ry2%d" % hp, 2, 64, F32)
                cx.stt(y2.ap, Vn.ap[:, c * 128 + hp * 64:c * 128 + hp * 64 + 64], bon.ap[:, c * 2 + hp:c * 2 + hp + 1], y1.ap,
                       ALU.mult, ALU.add, [Vn, bon, y1], [y2])
                psg = cx.ps()
                cx.mm([dict(out=psg.ap[:, 0:64], lhsT=sgd.ap[:, c * 128:(c + 1) * 128], rhs=g2b.ap[:, hl * 64:(hl + 1) * 64], start=True, stop=False),
                       dict(out=psg.ap[:, 0:64], lhsT=sgd2.ap[:, c * 128:(c + 1) * 128], rhs=g2b.ap[0:32, 512 + hl * 64:512 + (hl + 1) * 64],
                            start=False, stop=True)], [sgd, sgd2, g2b], [psg])
                y3 = cx.ring("ry3%d" % hp, 2, 64, BF16)
                if RW_DBG == "g":
                    cx.copy("dve", y3.ap, psg.ap[:, 0:64], [psg], [y3])
                elif RW_DBG == "y1":
                    cx.copy("dve", y3.ap, y1.ap, [y1, psg], [y3])
                elif RW_DBG == "y2":
                    cx.copy("dve", y3.ap, y2.ap, [y2, psg], [y3])
                elif RW_DBG == "yraw":
                    cx.copy("dve", y3.ap, yap, [yt, psg], [y3])
                else:
                    cx.tt("dve", y3.ap, psg.ap[:, 0:64], y2.ap, ALU.mult, [psg, y2], [y3])
                pst = cx.ps()
                ptv = pst.ap.bitcast(BF16)
                cx.tr(ptv[pb:pb + 64, 0:128], y3.ap, idb, [y3, K.cb], [pst])
                cx.copy("act", yas.ap[pb:pb + 64, c * 128:(c + 1) * 128], ptv[pb:pb + 64, 0:128], [pst], [yas])

            sf = dplr_head(cx, K, "r%d" % hl, 64, 64, pb, defer=True,
                      Kg=lambda c, pr=pr: Ks.ap[pr, c * 128:(c + 1) * 128], Bg=lambda c, pr=pr: Bs.ap[pr, c * 128:(c + 1) * 128],
                      Ag=lambda c, pr=pr: As.ap[pr, c * 128:(c + 1) * 128], Rg=lambda c, pr=pr: Rs.ap[pr, c * 128:(c + 1) * 128],
                      gtiles=[Ks, Bs, As, Rs],
                      Wst=lambda c: msk_s, Win=lambda c: msk_i, wtiles=[K.cb],
                      As=lambda c, pr=pr: As.ap[pr, c * 128:(c + 1) * 128], Rs=lambda c, pr=pr: Rs.ap[pr, c * 128:(c + 1) * 128], egam=None,
                      Kn=lambda c, pb=pb: Kn.ap[:, c * 128 + pb:c * 128 + pb + 64], Bn=lambda c, pb=pb: Bn.ap[:, c * 128 + pb:c * 128 + pb + 64],
                      V=lambda c, pb=pb: Vn.ap[:, c * 128 + pb:c * 128 + pb + 64], ntiles=[Kn, Bn, Vn],
                      PC=lambda c, pr=pr: PCt.ap[pr, c:c + 1], pctiles=[PCt], scale_all=True, out_cb=out_cb)
            sfs.append(sf)
        for c in range(NCH):
            for st_, fn_ in sfs:
                st_(c)
        for st_, fn_ in reversed(sfs):
            fn_()
        io["ywrite"](cx, io["ra"] + ct, 0, SEQ, yas.ap, [yas], io["ydep"]["a"])
        cx.release(m1)
    cx.release(m0)


def phase_M(cx, ios, layer1, which=("att", "gdn", "rwkv")):
    io = ios[0]
    base_ = cx.mark()
    cx.S.tag = "M/pro"
    cx.cast_eng = "pool"
    K = setup_consts(cx, io)
    g1 = load_small(cx, io["g_attn"], 16, "g1")
    hT = cx.tile(KC * SEQ, BF16, "hT")
    hv = hT.ap.rearrange("p (k t) -> p k t", k=KC)
    if "x_view" in io:
        xview = io["x_view"]
    else:
        xT = io["xT"].rearrange("(k p) t -> p k t", p=128)
        xview = lambda k, lo, hi: xT[:, k, lo:hi]
    m0 = cx.mark()
    for tb in range(4):
        lo, hi = tb * 512, tb * 512 + 512
        xt = cx.ring("xtile", 1, KC * 512, F32)
        xtv = xt.ap.rearrange("p (k t) -> p k t", k=KC)
        for k in range(KC):
            cx.dma(xtv[:, k, :], xview(k, lo, hi), io.get("x_deps", []), [xt], multi=True)
        rmsnorm_fm(cx, lambda k, l, h_: (xtv[:, k, 0:h_ - l], xt), g1, lambda k, l, h_: (hv[:, k, l:h_], hT),
                   [(lo, hi)], K.ones)
    cx.release(m0)
    for io in ios:
        if "att" in which:
            cx.S.tag = "M/att"
            mixer_attention(cx, K, io, hT, hv)
        if "gdn" in which:
            cx.S.tag = "M/gdn"
            mixer_gdn(cx, K, io, hT, hv)
        if "rwkv" in which:
            cx.S.tag = "M/rwkv"
            mixer_rwkv(cx, K, io, hT, hv, layer1)
    cx.release(base_)


M_INPUTS = [("xT", (D, SEQ)), ("g_attn", (128, 16)), ("cst", (128, CST_N)), ("rope", (32, 2 * SEQ)),
            ("wat", (18, 128, 16, 128)), ("wsw", (12, 128, 16, 32)),
            ("wc", (16, 128, 16, 128)), ("wba", (128, 16, 8)), ("gconv", (128, 48)), ("gpar", (128, 136)),
            ("wa", (12, 128, 16, 128)), ("wal", (128, 16, 288)), ("rpar", (128, 44)), ("rlmu", (128, 4)), ("rbc", (128, 1024)),
            ("rw2a2", (128, 512)), ("rg2", (128, 1024))]
M_INPUTS_L1 = [("rv1", (128, 16, 32)), ("rv2", (32, 512)), ("vf_in", (512, SEQ))]
M_OUTPUTS = [("y_fm", (768, SEQ)), ("yc_tm", (SEQ, 512)), ("ya_tm", (SEQ, 512)), ("v_out", (512, SEQ))]


def prep_M_weights(inp, l, s):
    A_IN, B_IN, C_IN = 3360, 4608, 4112
    W = inp["w_in"][l]
    w = {}
    w["g_attn"] = vec_pk(inp["attn_norm"][l])
    w["cst"] = make_consts()
    w["rope"] = make_rope()

    def colblk(cols):
        sub = W[:, cols]
        n = sub.shape[1]
        return np.ascontiguousarray(sub.reshape(16, 128, n // 128, 128).transpose(2, 1, 0, 3))

    def colsmall(cols):
        sub = W[:, cols]
        return np.ascontiguousarray(sub.reshape(16, 128, len(cols)).transpose(1, 0, 2))
    b0 = A_IN
    cols = []
    cols_sw = []
    for hl in range(2):
        hi = 2 * s + hl
        for g in range(3):
            for which in range(3):
                c0 = b0 + which * 1536 + g * 512 + hi * 128
                cols += list(range(c0, c0 + 128))
                if which < 2:
                    cols_sw += list(range(c0 + 16, c0 + 32)) + list(range(c0, c0 + 16))
    w["wat"] = colblk(np.array(cols))
    sw = W[:, np.array(cols_sw)]
    w["wsw"] = np.ascontiguousarray(sw.reshape(16, 128, 12, 32).transpose(2, 1, 0, 3))
    c0 = A_IN + B_IN
    cols = []
    for which in range(3):
        for hh in range(4):
            h = 4 * s + hh
            cols += list(range(c0 + which * 1024 + h * 128, c0 + which * 1024 + (h + 1) * 128))
    for hh in range(4):
        h = 4 * s + hh
        cols += list(range(c0 + 3088 + h * 128, c0 + 3088 + (h + 1) * 128))
    w["wc"] = colblk(np.array(cols))
    w["wba"] = colsmall(np.array([c0 + 3072 + 4 * s + i for i in range(4)] + [c0 + 3080 + 4 * s + i for i in range(4)]))
    gc = inp["gdn_conv"][l]
    gcs = np.zeros((128, 12, 4), np.float32)
    for which in range(3):
        for hh in range(4):
            h = 4 * s + hh
            gcs[:, which * 4 + hh, :] = gc[:, which * 1024 + h * 128: which * 1024 + (h + 1) * 128].T
    w["gconv"] = gcs.reshape(128, 48)
    gp = np.zeros((128, 136), np.float32)
    gp[:, 0:4] = inp["gdn_A_log"][l][4 * s:4 * s + 4][None, :]
    gp[:, 4:8] = inp["gdn_dt_bias"][l][4 * s:4 * s + 4][None, :]
    gp[:, 8:136] = inp["gdn_norm"][l][None, :]
    w["gpar"] = gp
    ch0 = 512 * s
    cols = []
    for which in range(3):
        cols += list(range(which * 1024 + ch0, which * 1024 + ch0 + 512))
    w["wa"] = colblk(np.array(cols))
    w["wal"] = colsmall(np.arange(3072, 3360))
    mu = inp["rwkv_mu"][l]
    rp = np.zeros((128, 44), np.float32)
    for which in range(3):
        for ct in range(4):
            rp[:, which * 4 + ct] = mu[which * 1024 + ch0 + ct * 128: which * 1024 + ch0 + (ct + 1) * 128]
    for ct in range(4):
        sl = slice(ch0 + ct * 128, ch0 + (ct + 1) * 128)
        rp[:, 12 + ct * 8 + 0] = inp["rwkv_w0"][l][sl]
        rp[:, 12 + ct * 8 + 1] = inp["rwkv_a0"][l][sl]
        rp[:, 12 + ct * 8 + 2] = inp["rwkv_k_k"][l][sl]
        rp[:, 12 + ct * 8 + 3] = inp["rwkv_k_a"][l][sl]
        rp[:, 12 + ct * 8 + 4] = inp["rwkv_r_k"][l].reshape(-1)[sl]
        if l > 0:
            rp[:, 12 + ct * 8 + 5] = inp["rwkv_v0"][l - 1][sl]
    w["rpar"] = rp
    rl = np.zeros((128, 4), np.float32)
    rl[0:64, 0] = mu[3072:3136]
    rl[64:128, 0] = mu[3136:3200]
    rl[0:128, 1] = mu[3200:3328]
    rl[0:32, 2] = mu[3328:3360]
    w["rlmu"] = rl
    rb = np.zeros((128, 1024), np.float32)
    rb[:, 0:512] = inp["rwkv_ln_w"][l][ch0:ch0 + 512][None, :]
    rb[:, 512:1024] = inp["rwkv_ln_b"][l][ch0:ch0 + 512][None, :]
    w["rbc"] = rb
    w["rw2a2"] = np.ascontiguousarray(np.concatenate([inp["rwkv_w2"][l][:, ch0:ch0 + 512], inp["rwkv_a2"][l][:, ch0:ch0 + 512]], axis=0))
    g2 = inp["rwkv_g2"][l][:, ch0:ch0 + 512]
    rg = np.zeros((128, 1024), np.float32)
    rg[:, 0:512] = g2[0:128]
    rg[0:32, 512:1024] = g2[128:160]
    w["rg2"] = rg
    if l > 0:
        v1 = inp["rwkv_v1"][l - 1]
        w["rv1"] = np.ascontiguousarray(v1.reshape(16, 128, 32).transpose(1, 0, 2))
        w["rv2"] = np.ascontiguousarray(inp["rwkv_v2"][l - 1][:, ch0:ch0 + 512])
    return w


def tile_w(W, col0, ncols, cw=128):
    K = W.shape[0]
    sub = W[:, col0:col0 + ncols]
    return np.ascontiguousarray(sub.reshape(K // 128, 128, ncols // cw, cw).transpose(2, 1, 0, 3))


def vec_pk(v):
    return np.ascontiguousarray(v.reshape(-1, 128).T)


def prep_T_weights(inp, l):
    A_IN, B_IN, C_IN = 3360, 4608, 4112
    g0 = A_IN + B_IN + C_IN
    w = {}
    w["wg"] = tile_w(inp["w_in"][l], g0, 6144)
    pw = np.concatenate([inp["proj_a"][l], inp["proj_b"][l], inp["proj_c"][l]], axis=0)
    w["wp"] = tile_w(pw, 0, 2048)
    w["wo"] = tile_w(inp["w_out"][l], 0, 2048)
    w["wup"] = tile_w(inp["ffn_up"][l], 0, 11264)
    w["wdn"] = tile_w(inp["ffn_down"][l], 0, 2048)
    w["g_attn"] = vec_pk(inp["attn_norm"][l])
    w["g_ffn"] = vec_pk(inp["ffn_norm"][l])
    w["g_next"] = vec_pk(inp["attn_norm"][l + 1] if l + 1 < inp["attn_norm"].shape[0] else inp["final_norm"])
    fc = inp["ffn_conv"][l]
    w["fconv"] = np.ascontiguousarray(fc.T.reshape(88, 128, 3).transpose(1, 0, 2).reshape(128, 88 * 3))
    return w


class DD:
    def __init__(self, name=""):
        self.d = Dep(name)


M_SHARED = ("xT", "cst", "rope")
T_INPUTS = [("g_attn", (128, 16)), ("g_ffn", (128, 16)), ("g_next", (128, 16)), ("fconv", (128, 88 * 3)),
            ("wg", (48, 128, 16, 128)), ("wp", (16, 128, 20, 128)), ("wo", (16, 128, 16, 128)),
            ("wup", (88, 128, 16, 128)), ("wdn", (16, 128, 44, 128))]


PAIRS = [[0, 1], [2, 3], [4, 5], [6, 7]]


def build_fused(depth=2):
    nc = bass.Bass("TRN2", target_bir_lowering=False)
    ext = {}

    def din(name, shape, dt=F32):
        ext[name] = nc.dram_tensor(name, list(shape), dt, kind="ExternalInput").ap()
        return ext[name]
    xT = din("xT", (D, SEQ))
    xTT = din("xTT", (D, NTOK_T))
    hmask = din("hmask", (128, 2))
    cst = din("cst", (128, CST_N))
    rope = din("rope", (32, 2 * SEQ))
    for l in range(depth):
        for name, shape in M_INPUTS + (M_INPUTS_L1[:2] if l == 1 else []):
            if name not in M_SHARED:
                din("%s_%d" % (name, l), shape)
        for name, shape in T_INPUTS:
            din("T%s_%d" % (name, l), shape)
    out = nc.dram_tensor("outT", [D, 1024], F32, kind="ExternalOutput").ap()
    yloc = [[nc.dram_tensor("yloc%d_%d" % (l, q), [1280, 512], BF16).ap() for q in range(4)] for l in range(depth)]
    yg = [[nc.dram_tensor("yg%d_%d" % (l, q), [2560, 512], BF16).ap() for q in range(4)] for l in range(depth)]
    xloc = [[nc.dram_tensor("xloc%d_%d" % (rh, ch), [1024, 512], F32).ap() for ch in range(2)] for rh in range(2)]
    xg = [[nc.dram_tensor("xg%d_%d" % (rh, ch), [2048, 512], F32).ap() for ch in range(2)] for rh in range(2)]
    vbuf = nc.dram_tensor("vbuf", [512, SEQ], BF16).ap()
    with ExitStack() as es:
        cx = Ctx(nc, es)
        cx.wpiece = 1024
        xloc_d = [[DD("xloc") for ch in range(2)] for rh in range(2)]
        xg_d = [[DD("xg") for ch in range(2)] for rh in range(2)]
        all_xloc_d = [d for r_ in xloc_d for d in r_]
        all_xg_d = [d for r_ in xg_d for d in r_]
        v_d = DD("v")
        xloc_v = [[xloc[rh][ch].rearrange("(k p) t -> p k t", p=128) for ch in range(2)] for rh in range(2)]
        xg_v = [[xg[rh][ch].rearrange("(r k p) t -> p r k t", r=2, p=128) for ch in range(2)] for rh in range(2)]

        def ag(in_ap, out_ap, rd, wr):
            cx.S.op("pool", [("collective_compute", dict(kind="AllGather", op=ALU.bypass, replica_groups=PAIRS,
                                                        ins=[in_ap], outs=[out_ap]))], [d.d for d in rd], [d.d for d in wr], dma="cc")

        for l in range(depth):
            last = (l == depth - 1)
            yl_d = [{"a": DD("ya"), "b": DD("yb"), "c": DD("yc")} for q in range(4)]
            yg_d = [DD("yg") for q in range(4)]
            yloc_v = [yloc[l][q].rearrange("(b p) t -> p b t", p=128) for q in range(4)]
            yg_v = [yg[l][q].rearrange("(r b p) t -> p r b t", r=2, p=128) for q in range(4)]

            def ywrite(cx_, blk, t_lo, t_hi, src_ap, src_tiles, key, yl_d=yl_d, yloc_v=yloc_v):
                for q in range(t_lo // 512, t_hi // 512):
                    cx_.dma(yloc_v[q][:, blk, :], src_ap[:, q * 512 - t_lo:q * 512 - t_lo + 512], src_tiles, [yl_d[q][key]])

            io = {}
            for name, shape in M_INPUTS + (M_INPUTS_L1[:2] if l == 1 else []):
                if name not in M_SHARED:
                    io[name] = ext["%s_%d" % (name, l)]
            io.update(cst=cst, rope=rope, ydep={"a": "a", "b": "b", "c": "c"}, ywrite=ywrite, ra=0, rb=4, rc=6,
                      v_out=vbuf, v_dep=v_d, vf_in=vbuf, vf_deps=[v_d], write_v=(l == 0))
            if l == 0:
                io.update(xT=xT, x_deps=[])
            else:
                io.update(x_view=(lambda k, lo, hi: xg_v[k // 8][(lo % 1024) // 512][:, lo // 1024, k % 8, lo % 512:lo % 512 + (hi - lo)]),
                          x_deps=all_xg_d)
            phase_M(cx, [io], l == 1)
            cx.S.tag = "AG/y"
            for q in range(4):
                ag(yloc[l][q], yg[l][q], list(yl_d[q].values()), [yg_d[q]])
            io = {name: ext["T%s_%d" % (name, l)] for name, shape in T_INPUTS}
            io.update(hmask=hmask, yread=(lambda r_, b_, q, yg_v=yg_v: yg_v[q][:, r_, b_, :]), y_deps=yg_d,
                      x_deps=all_xloc_d + all_xg_d)
            if l == 0:
                io.update(x_mode="input", xTT=xTT)
            else:
                def xloc_read(k, t_lo, t_hi):
                    out_ = []
                    t = t_lo
                    while t < t_hi:
                        ch = t // 512
                        e = min(t_hi, (ch + 1) * 512)
                        out_.append((xloc_v[k // 8][ch][:, k % 8, t - ch * 512:e - ch * 512], t - t_lo, e - t))
                        t = e
                    return out_
                io.update(x_mode="exchange", xloc_read=xloc_read,
                          xhalo=(lambda k: xg_v[k // 8][1][:, 0, k % 8, 510:512]))
            if last:
                io.update(outT=out, out_deps=[])
            else:
                def xwrite(cx_, k, src_ap, src_tiles):
                    for ch in range(2):
                        cx_.dma(xloc_v[k // 8][ch][:, k % 8, :], src_ap[:, ch * 512:(ch + 1) * 512], src_tiles, [xloc_d[k // 8][ch]])
                io.update(outT=None, xwrite=xwrite)
            phase_T(cx, io, last, 0)
            if not last:
                cx.S.tag = "AG/x"
                for rh in range(2):
                    for ch in range(2):
                        ag(xloc[rh][ch], xg[rh][ch], [xloc_d[rh][ch]], [xg_d[rh][ch]])
        if cx.S.taglog is not None:
            import json
            json.dump(cx.S.taglog, open(os.environ["BASS_TAGLOG"], "w"))
        cx.S.emit()
        print("fused ops:", cx.S.nops, "insts:", cx.S.ninst, "arena peak KB:", cx.ar.peak * 4 / 1024,
              "eng counts:", cx.S.cnt, "max dma sem:", max(cx.S.dma_cnt))
    return nc


_NC_CACHE = {}


def prep_core_inputs(inp, depth, s):
    m = {"cst": make_consts(), "rope": make_rope()}
    hm = np.zeros((128, 2), np.float32)
    hm[:, s] = 1.0
    m["hmask"] = hm
    for l in range(depth):
        w = prep_M_weights(inp, l, s)
        for k, v in w.items():
            if k not in M_SHARED:
                m["%s_%d" % (k, l)] = v
    return m


def kernel(**inputs):
    inp = {k: np.asarray(v) for k, v in inputs.items()}
    x = inp["x"].astype(np.float32, copy=False)
    B, S, Dm = x.shape
    depth = inp["w_in"].shape[0]
    if "nc" not in _NC_CACHE:
        _NC_CACHE["nc"] = build_fused(depth)
    nc = _NC_CACHE["nc"]
    per_s = [prep_core_inputs(inp, depth, s) for s in range(2)]
    tw = {}
    for l in range(depth):
        for k, v in prep_T_weights(inp, l).items():
            tw["T%s_%d" % (k, l)] = v
    maps = []
    for c in range(8):
        b, s = c // 2, c % 2
        m = dict(per_s[s])
        m.update(tw)
        m["xT"] = np.ascontiguousarray(x[b].T)
        lo = 1024 * s - 2
        xs = np.zeros((NTOK_T, Dm), np.float32)
        a = max(lo, 0)
        xs[a - lo:] = x[b][a:lo + NTOK_T]
        m["xTT"] = np.ascontiguousarray(xs.T)
        maps.append(m)
    res = run_bass_kernel_spmd(nc, maps, core_ids=list(range(8))).results
    outs = [np.concatenate([np.asarray(res[2 * b + s]["outT"]).T for s in range(2)], axis=0) for b in range(B)]
    return np.ascontiguousarray(np.stack(outs, axis=0)).astype(np.float32)
```

```python
import os
import numpy as np
import concourse.bass as bass
import concourse.mybir as mybir
from concourse.bass_utils import run_bass_kernel_spmd
from contextlib import ExitStack

F32 = mybir.dt.float32
BF16 = mybir.dt.bfloat16
ALU = mybir.AluOpType
AF = mybir.ActivationFunctionType
AX = mybir.AxisListType

D = 2048
KC = 16
SEQ = 2048
NTOK_T = 1026
TT = [(0, 2), (2, 514), (514, 1026)]
D_FF = 5632
NFB = 44
EPS = 1e-6

SAME_ENG_SYNC = True


class Dep:
    __slots__ = ("w", "r", "name", "excl", "wm")

    def __init__(self, name="", excl=False):
        self.w = None
        self.wm = {}
        self.r = {}
        self.name = name
        self.excl = excl


class Sched:
    ENGS = ["pe", "act", "dve", "pool", "sp"]

    def __init__(self, nc, n_dma=24):
        self.nc = nc
        self.q = {e: [] for e in self.ENGS}
        self.cnt = {e: 0 for e in self.ENGS}
        self.n_dma = n_dma
        self.dma_cnt = [0] * n_dma
        self.dma_rr = 0
        self.waited = {e: {} for e in self.ENGS}
        self.nops = 0
        self.ninst = 0
        self.tag = ""
        self.cc_cnt = 0
        self.taglog = [] if os.environ.get("BASS_TAGLOG") else None

    def op(self, eng, calls, reads=(), writes=(), dma=False, multi=False):
        deps = {}

        def add(tok):
            if tok is None:
                return
            k, v = tok
            if deps.get(k, 0) < v:
                deps[k] = v

        for r in reads:
            add(r.w)
            for k, v in r.wm.items():
                add((k, v))
            if r.excl:
                for k, v in r.r.items():
                    add((k, v))
        for w in writes:
            if not multi:
                add(w.w)
                for k, v in w.wm.items():
                    add((k, v))
            for k, v in w.r.items():
                add((k, v))
        if dma == "cc":
            self.cc_cnt += 1
            tok = (("cc", 0), self.cc_cnt)
        elif dma:
            i = self.dma_rr
            self.dma_rr = (self.dma_rr + 1) % self.n_dma
            add((("dma", i), self.dma_cnt[i]))
            self.dma_cnt[i] += 16
            tok = (("dma", i), self.dma_cnt[i])
        else:
            self.cnt[eng] += 1
            tok = (eng, self.cnt[eng])
        waits = []
        wd = self.waited[eng]
        for k, v in deps.items():
            if v <= 0:
                continue
            if k == eng and (eng == "pe" or not SAME_ENG_SYNC):
                continue
            if wd.get(k, 0) >= v:
                continue
            wd[k] = v
            waits.append((k, v))
        for r in reads:
            if r.r.get(tok[0], 0) < tok[1]:
                r.r[tok[0]] = tok[1]
        for w in writes:
            if multi:
                if w.wm.get(tok[0], 0) < tok[1]:
                    w.wm[tok[0]] = tok[1]
            else:
                w.w = tok
                w.wm = {}
                w.r = {}
        self.q[eng].append((waits, calls, tok))
        if self.taglog is not None:
            self.taglog.append((eng, self.tag, len(calls)))
        self.nops += 1
        self.ninst += len(calls)
        return tok

    def emit(self, final_wait_eng="sp"):
        nc = self.nc
        with ExitStack() as es:
            sems = {}
            for e in self.ENGS:
                sems[e] = es.enter_context(nc.semaphore("s_" + e))
            for i in range(self.n_dma):
                sems[("dma", i)] = es.enter_context(nc.semaphore("s_dma%d" % i))
            sems[("cc", 0)] = es.enter_context(nc.semaphore("s_cc"))
            block = es.enter_context(nc.Block())
            finals = [(("dma", i), self.dma_cnt[i]) for i in range(self.n_dma) if self.dma_cnt[i] > 0]

            def run(engname, engine):
                for waits, calls, tok in self.q[engname]:
                    for k, v in waits:
                        engine.wait_ge(sems[k], v)
                    ins = None
                    for name, kw in calls:
                        ins = getattr(engine, name)(**kw)
                    ins.then_inc(sems[tok[0]], 16 if (isinstance(tok[0], tuple) and tok[0][0] == "dma") else 1)
                if engname == final_wait_eng:
                    for k, v in finals:
                        engine.wait_ge(sems[k], v)

            @block.tensor
            def _(e):
                run("pe", e)

            @block.scalar
            def _(e):
                run("act", e)

            @block.vector
            def _(e):
                run("dve", e)

            @block.gpsimd
            def _(e):
                run("pool", e)

            @block.sync
            def _(e):
                run("sp", e)


class Arena:
    def __init__(self, ap_f32, words):
        self.base = ap_f32
        self.words = words
        self.top = 0
        self.peak = 0
        self.live = []
        self.dead = []

    def mark(self):
        return self.top

    def release(self, m):
        keep = []
        for ent in self.live:
            if ent[0] >= m:
                self.dead.append(ent)
            else:
                keep.append(ent)
        self.live = keep
        self.top = m

    def alloc(self, nelem, dtype=F32, parts=128, name=""):
        w = nelem if dtype == F32 else (nelem + 1) // 2
        w = (w + 7) // 8 * 8
        assert self.top + w <= self.words, ("SBUF arena overflow", name, self.top, w, self.words)
        lo, hi = self.top, self.top + w
        a = self.base[0:parts, lo:hi]
        self.top = hi
        self.peak = max(self.peak, self.top)
        d = Dep(name)
        nd = []
        for (dlo, dhi, dd) in self.dead:
            if dlo < hi and lo < dhi:
                toks = list(dd.r.items())
                if dd.w is not None:
                    toks.append(dd.w)
                for k, v in toks:
                    if d.r.get(k, 0) < v:
                        d.r[k] = v
                if lo <= dlo and dhi <= hi:
                    continue
            nd.append((dlo, dhi, dd))
        self.dead = nd
        self.live.append((lo, hi, d))
        if dtype != F32:
            a = a.bitcast(dtype)
        return a[:, 0:nelem], d


class T:
    def __init__(self, ap, d):
        self.ap = ap
        self.d = d


class Ctx:
    ARENA_WORDS = 51 * 1024 + 512

    def __init__(self, nc, es):
        self.nc = nc
        self.S = Sched(nc)
        big = es.enter_context(nc.sbuf_tensor("arena", [128, self.ARENA_WORDS], F32))
        self.ar = Arena(big, self.ARENA_WORDS)
        self.psb = []
        for i in range(8):
            p = es.enter_context(nc.psum_tensor("psb%d" % i, [128, 512], F32))
            self.psb.append(T(p, Dep("ps%d" % i, excl=True)))
        self.ps_rr = 0
        self.rings = {}

    def ps(self, pin=False):
        pinned = getattr(self, "pinned", None)
        if pinned is None:
            pinned = self.pinned = set()
        while self.ps_rr in pinned:
            self.ps_rr = (self.ps_rr + 1) % 8
        i = self.ps_rr
        t = self.psb[i]
        self.ps_rr = (self.ps_rr + 1) % 8
        if pin:
            pinned.add(i)
        return t

    def unpin_all(self):
        self.pinned = set()

    def tile(self, nelem, dtype=F32, name="", parts=128):
        ap, d = self.ar.alloc(nelem, dtype, parts, name)
        return T(ap, d)

    def ring(self, key, n, nelem, dtype=F32):
        if key not in self.rings:
            off = self.ar.top
            self.rings[key] = [[self.tile(nelem, dtype, "%s%d" % (key, i)) for i in range(n)], 0, off]
        r = self.rings[key]
        t = r[0][r[1] % len(r[0])]
        r[1] += 1
        return t

    def mark(self):
        return self.ar.mark()

    def release(self, m):
        for k in [k for k, r in self.rings.items() if r[2] >= m]:
            del self.rings[k]
        self.ar.release(m)

    def I(self, eng, name, reads, writes, **kw):
        self.S.op(eng, [(name, kw)], [t.d for t in reads], [t.d for t in writes])

    def dma(self, out, in_, reads=(), writes=(), eng="sp", multi=False):
        self.S.op(eng, [("dma_start", dict(out=out, in_=in_))], [t.d for t in reads], [t.d for t in writes], dma=True, multi=multi)

    def act(self, out, in_, func, reads, writes, **kw):
        self.I("act", "activation", reads, writes, out=out, in_=in_, func=func, **kw)

    def tt(self, eng, out, in0, in1, op, reads, writes):
        self.I(eng, "tensor_tensor", reads, writes, out=out, in0=in0, in1=in1, op=op)

    def ts(self, eng, out, in0, s1, op0, reads, writes, s2=None, op1=None):
        kw = dict(out=out, in0=in0, scalar1=s1, scalar2=s2, op0=op0)
        if op1 is not None:
            kw["op1"] = op1
        self.I(eng, "tensor_scalar", reads, writes, **kw)

    def stt(self, out, in0, scalar, in1, op0, op1, reads, writes):
        self.I("dve", "scalar_tensor_tensor", reads, writes, out=out, in0=in0, scalar=scalar, in1=in1, op0=op0, op1=op1)

    def copy(self, eng, out, in_, reads, writes):
        if eng == "act":
            self.act(out, in_, AF.Copy, reads, writes)
        else:
            self.I(eng, "tensor_copy", reads, writes, out=out, in_=in_)

    def mm(self, mms, reads, writes):
        self.S.op("pe", [("matmul", m) for m in mms], [t.d for t in reads], [t.d for t in writes])

    def tr(self, out, in_, ident, reads, writes):
        self.S.op("pe", [("transpose", dict(out=out, in_=in_, identity=ident))], [t.d for t in reads], [t.d for t in writes])


def load_w(cx, dram_ap, kc, cw, key="w"):
    n = kc * cw
    wp = getattr(cx, "wpiece", 2048)
    assert n <= wp
    st = cx.ring(key + "_st", getattr(cx, "w_nst", 2), wp, F32)
    wb = cx.ring(key + "_wb", getattr(cx, "w_nwb", 3), wp, BF16)
    cx.dma(st.ap[:, 0:n], dram_ap.rearrange("p k c -> p (k c)"), [], [st])
    cx.copy(getattr(cx, "cast_eng", "pool"), wb.ap[:, 0:n], st.ap[:, 0:n], [st], [wb])
    return wb


def load_w_multi(cx, dram_ap, kc, cw, key="w"):
    per = max(1, getattr(cx, "wpiece", 2048) // cw)
    out = []
    k0 = 0
    while k0 < kc:
        kn = min(per, kc - k0)
        out.append((load_w(cx, dram_ap[:, k0:k0 + kn, :], kn, cw, key), k0, kn))
        k0 += kn
    return out


def load_w_big(cx, dram_ap, kc, cw, name):
    wp = getattr(cx, "wpiece", 2048)
    big = cx.tile(kc * cw, BF16, name)
    per = max(1, wp // cw)
    k0 = 0
    while k0 < kc:
        kn = min(per, kc - k0)
        n = kn * cw
        st = cx.ring("w_st", 2, wp, F32)
        cx.dma(st.ap[:, 0:n], dram_ap[:, k0:k0 + kn, :].rearrange("p k c -> p (k c)"), [], [st])
        cx.copy("pool", big.ap[:, k0 * cw:(k0 + kn) * cw], st.ap[:, 0:n], [st], [big])
        k0 += kn
    return [(big, 0, kc)]


def wslice(pieces, k, cw, m0=0, m1=None):
    m1 = cw if m1 is None else m1
    for wb, k0, kn in pieces:
        if k0 <= k < k0 + kn:
            return wb.ap[:, (k - k0) * cw + m0:(k - k0) * cw + m1]
    raise KeyError(k)


def gemm_block(cx, pieces, cw, act_fn, act_tiles, lo, hi, ks=None, m0=0, m1=None, po=0):
    ps = cx.ps()
    n = hi - lo
    m1 = cw if m1 is None else m1
    if ks is None:
        ks = []
        for wb, k0, kn in pieces:
            ks += list(range(k0, k0 + kn))
    mms = []
    for i, k in enumerate(ks):
        mms.append(dict(out=ps.ap[po:po + m1 - m0, 0:n], lhsT=wslice(pieces, k, cw, m0, m1), rhs=act_fn(k, lo, hi),
                        start=(i == 0), stop=(i == len(ks) - 1)))
    cx.mm(mms, [p[0] for p in pieces] + list(act_tiles), [ps])
    return ps


def rmsnorm_fm(cx, srcs_fn, g_t, out_fn, tiles, ones_bf, nk=KC):
    for (lo, hi) in tiles:
        n = hi - lo
        ps = cx.ps()
        srcs = [srcs_fn(k, lo, hi) for k in range(nk)]
        for k in range(nk):
            sqt = cx.ring("rn_sq", 3, 512, BF16)
            sap, st = srcs[k]
            cx.act(sqt.ap[:, 0:n], sap, AF.Square, [st], [sqt])
            cx.mm([dict(out=ps.ap[:, 0:n], lhsT=ones_bf.ap, rhs=sqt.ap[:, 0:n], start=(k == 0), stop=(k == nk - 1))],
                  [sqt, ones_bf], [ps])
        rinv = cx.ring("rn_rinv", 2, 512, F32)
        cx.act(rinv.ap[:, 0:n], ps.ap[:, 0:n], AF.Ln, [ps], [rinv], scale=1.0 / (nk * 128), bias=EPS)
        cx.act(rinv.ap[:, 0:n], rinv.ap[:, 0:n], AF.Exp, [rinv], [rinv], scale=-0.5)
        for k in range(nk):
            sap, st = srcs[k]
            oap, ot = out_fn(k, lo, hi)
            cx.stt(oap, sap, g_t.ap[:, k:k + 1], rinv.ap[:, 0:n], ALU.mult, ALU.mult, [st, rinv, g_t], [ot])


def load_small(cx, dram_ap, nelem, name, parts=128):
    t = cx.tile(nelem, F32, name, parts)
    cx.dma(t.ap, dram_ap, [], [t])
    return t


def phase_T(cx, io, final, half):
    NT = NTOK_T
    cx.cast_eng = "act"
    cx.w_nst, cx.w_nwb = 4, 6
    base = cx.mark()
    ones_bf = cx.tile(128, BF16, "ones")
    cx.I("dve", "memset", [], [ones_bf], ap=ones_bf.ap, constant=1.0)
    g1 = load_small(cx, io["g_attn"], 16, "g1")
    g2 = load_small(cx, io["g_ffn"], 16, "g2")
    g3 = load_small(cx, io["g_next"], 16, "g3")
    convw = load_small(cx, io["fconv"], 88 * 3, "convw")
    hm = load_small(cx, io["hmask"], 2, "hmask")
    mg_t = cx.tile(KC * NT, BF16, "merged")
    h_t = cx.tile(KC * NT, BF16, "h")
    mv = mg_t.ap.rearrange("p (k t) -> p k t", k=KC)
    hv = h_t.ap.rearrange("p (k t) -> p k t", k=KC)
    M0 = cx.mark()
    cx.S.tag = "T/A1"
    y_t = cx.tile(20 * NT, BF16, "y")
    yv = y_t.ap.rearrange("p (k t) -> p k t", k=20)
    t0 = 0
    xdeps = io.get("x_deps", [])
    ydeps = io.get("y_deps", [])
    if io["x_mode"] == "input":
        xTT = io["xTT"].rearrange("(k p) t -> p k t", p=128)

        def load_x(dst_ap, dst_t, k, lo, hi, multi=False):
            cx.dma(dst_ap, xTT[:, k, lo:hi], [], [dst_t], multi=multi)
    else:
        xloc_read = io["xloc_read"]
        xhalo = io["xhalo"]

        def load_x(dst_ap, dst_t, k, lo, hi, multi=False):
            a = lo
            if lo < 2:
                cx.dma(dst_ap[:, 0:2 - lo], xhalo(k)[:, lo:2], xdeps, [dst_t])
                cx.ts("dve", dst_ap[:, 0:2 - lo], dst_ap[:, 0:2 - lo], hm.ap[:, 1:2], ALU.mult, [dst_t, hm], [dst_t])
                a = 2
            if hi > a:
                for src, off, n in xloc_read(k, a - 2, hi - 2):
                    cx.dma(dst_ap[:, a - lo + off:a - lo + off + n], src, xdeps, [dst_t], multi=(multi and lo >= 2))

    yread = io["yread"]
    kmap = [(0, b) for b in range(4)] + [(1, b) for b in range(4)] + [(0, 4), (0, 5), (1, 4), (1, 5)] + \
           [(0, b) for b in range(6, 10)] + [(1, b) for b in range(6, 10)]
    M1 = cx.mark()
    for k in range(20):
        r_, b_ = kmap[k]
        ta = cx.ring("yh0", 2, 1024, BF16)
        tb_ = cx.ring("yh1", 2, 1024, BF16)
        for q in range(2):
            cx.dma(ta.ap[:, q * 512:(q + 1) * 512], yread(r_, b_, q), ydeps, [ta], multi=True)
            cx.dma(tb_.ap[:, q * 512:(q + 1) * 512], yread(r_, b_, 2 + q), ydeps, [tb_], multi=True)
        cx.ts("pool", yv[:, k, 0:2], ta.ap[:, 1022:1024], hm.ap[:, 1:2], ALU.mult, [ta, hm], [y_t])
        cx.ts("pool", tb_.ap, tb_.ap, hm.ap[:, 1:2], ALU.mult, [tb_, hm], [tb_])
        cx.stt(yv[:, k, 2:NT], ta.ap, hm.ap[:, 0:1], tb_.ap, ALU.mult, ALU.add, [ta, tb_, hm], [y_t])
    for (lo, hi) in TT:
        n = hi - lo
        xt = cx.ring("xtile", 1, KC * 512, F32)
        xtv = xt.ap.rearrange("p (k t) -> p k t", k=KC)
        for k in range(KC):
            load_x(xtv[:, k, 0:n], xt, k, lo, hi, multi=True)
        rmsnorm_fm(cx, lambda k, l, h_: (xtv[:, k, 0:h_ - l], xt), g1, lambda k, l, h_: (hv[:, k, l:h_], h_t),
                   [(lo, hi)], ones_bf)
    cx.release(M1)
    wg = io["wg"]
    wp = io["wp"]
    ksl = [(0, 8), (8, 12), (12, 20)]
    for j in range(16):
        gts = []
        for br in range(3):
            pcs = load_w_multi(cx, wg[br * 16 + j], KC, 128)
            gt = cx.ring("gate", 4, NT, BF16)
            for (lo, hi) in TT:
                ps = gemm_block(cx, pcs, 128, lambda k, l, h_: hv[:, k, l:h_], [h_t], lo, hi)
                cx.act(gt.ap[:, lo:hi], ps.ap[:, 0:hi - lo], AF.Sigmoid, [ps], [gt])
            gts.append(gt)
        pw = load_w_multi(cx, wp[j], 20, 128)
        for (lo, hi) in TT:
            n = hi - lo
            tmp = cx.ring("mtmp", 2, 512, F32)
            for br in range(3):
                k0, k1 = ksl[br]
                ps = gemm_block(cx, pw, 128, lambda k, l, h_: yv[:, k, l:h_], [y_t], lo, hi, ks=list(range(k0, k1)))
                gt = gts[br]
                if br == 0:
                    cx.tt("dve", tmp.ap[:, 0:n], ps.ap[:, 0:n], gt.ap[:, lo:hi], ALU.mult, [ps, gt], [tmp])
                else:
                    t2 = cx.ring("mtmp2", 2, 512, F32)
                    cx.tt("dve", t2.ap[:, 0:n], ps.ap[:, 0:n], gt.ap[:, lo:hi], ALU.mult, [ps, gt], [t2])
                    if br == 1:
                        cx.tt("pool", tmp.ap[:, 0:n], tmp.ap[:, 0:n], t2.ap[:, 0:n], ALU.add, [tmp, t2], [tmp])
                    else:
                        cx.tt("pool", mv[:, j, lo:hi], tmp.ap[:, 0:n], t2.ap[:, 0:n], ALU.add, [tmp, t2], [mg_t])
    cx.release(M0)
    cx.S.tag = "T/A2"
    x_sb = cx.tile(KC * NT, F32, "x_sb")
    xv = x_sb.ap.rearrange("p (k t) -> p k t", k=KC)
    M2 = cx.mark()
    wo = io["wo"]
    for j in range(16):
        pcs = load_w_multi(cx, wo[j], KC, 128)
        xin = cx.ring("xin", 2, NT, F32)
        load_x(xin.ap, xin, j, 0, NT)
        for (lo, hi) in TT:
            ps = gemm_block(cx, pcs, 128, lambda k, l, h_: mv[:, k, l:h_], [mg_t], lo, hi)
            cx.tt("dve", xv[:, j, lo:hi], ps.ap[:, 0:hi - lo], xin.ap[:, lo:hi], ALU.add, [ps, xin], [x_sb])
    cx.release(M2)
    cx.S.tag = "T/F"
    rmsnorm_fm(cx, lambda k, l, h_: (xv[:, k, l:h_], x_sb), g2, lambda k, l, h_: (hv[:, k, l:h_], h_t), TT, ones_bf)
    GRP = 11
    a_t = T(mg_t.ap, mg_t.d)
    av = a_t.ap[:, 0:GRP * 1024].rearrange("p (k t) -> p k t", k=GRP)
    wup = io["wup"]
    wdn = io["wdn"]
    cwv = convw.ap.rearrange("p (b t) -> p b t", t=3)
    for g in range(NFB // GRP):
        for jj in range(GRP):
            jb = g * GRP + jj
            us = []
            for part in range(2):
                blk = part * NFB + jb
                pcs = load_w_multi(cx, wup[blk], KC, 128)
                u = cx.ring("u_f32", 2, NT, F32)
                for (lo, hi) in TT:
                    ps = gemm_block(cx, pcs, 128, lambda k, l, h_: hv[:, k, l:h_], [h_t], lo, hi)
                    cx.copy("act", u.ap[:, lo:hi], ps.ap[:, 0:hi - lo], [ps], [u])
                c = cx.ring("c_f32", 2, 1024, F32)
                cx.ts("dve", c.ap, u.ap[:, 2:NT], cwv[:, blk, 2:3], ALU.mult, [u, convw], [c])
                cx.stt(c.ap, u.ap[:, 1:NT - 1], cwv[:, blk, 1:2], c.ap, ALU.mult, ALU.add, [u, convw, c], [c])
                cx.stt(c.ap, u.ap[:, 0:NT - 2], cwv[:, blk, 0:1], c.ap, ALU.mult, ALU.add, [u, convw, c], [c])
                us.append(c)
            sg = cx.ring("silu", 2, 1024, F32)
            cx.act(sg.ap, us[0].ap, AF.Silu, [us[0]], [sg])
            cx.tt("pool", av[:, jj, :], sg.ap, us[1].ap, ALU.mult, [sg, us[1]], [a_t])
        for j in range(16):
            pcs = load_w_multi(cx, wdn[j][:, g * GRP:(g + 1) * GRP, :], GRP, 128)
            for ti in range(2):
                lo, hi = ti * 512, ti * 512 + 512
                ps = gemm_block(cx, pcs, 128, lambda k, l, h_: av[:, k, l:h_], [a_t], lo, hi)
                cx.tt("dve", xv[:, j, 2 + lo:2 + hi], ps.ap[:, 0:512], xv[:, j, 2 + lo:2 + hi], ALU.add, [ps, x_sb], [x_sb])
    cx.release(M2)
    outT = io["outT"].rearrange("(k p) t -> p k t", p=128) if final else None
    odeps = io.get("out_deps", [])
    if not final:
        for k in range(KC):
            io["xwrite"](cx, k, xv[:, k, 2:NT], [x_sb])
    else:
        o_t = T(h_t.ap.bitcast(F32)[:, 0:KC * 512], h_t.d)
        ov = o_t.ap.rearrange("p (k t) -> p k t", k=KC)
        for ti in range(2):
            lo, hi = 2 + ti * 512, 2 + ti * 512 + 512
            rmsnorm_fm(cx, lambda k, l, h_: (xv[:, k, l:h_], x_sb), g3, lambda k, l, h_: (ov[:, k, 0:512], o_t),
                       [(lo, hi)], ones_bf)
            for k in range(KC):
                cx.dma(outT[:, k, t0 + lo - 2:t0 + lo - 2 + 512], ov[:, k, :], [o_t], odeps)
    cx.release(base)


import os
ATT_G = os.environ.get("ATT_G", "012")
ATT_STOP = int(os.environ.get("ATT_STOP", "99"))
ATT_SUB = int(os.environ.get("ATT_SUB", "3"))
RW_DBG = os.environ.get("RW_DBG", "")
ATT_V = os.environ.get("ATT_V", "ab")
NCH = 16
C_ID, C_MUI, C_MUS, C_MLI, C_BO, C_BS, C_S127, C_MLS = 0, 128, 256, 384, 512, 640, 642, 770
C_BD8, C_O16, C_O32, C_O64, C_O128 = 898, 1026, 1154, 1282, 1410
CST_N = 1538


def make_consts():
    c = np.zeros((128, CST_N), np.float32)
    j = np.arange(128)[:, None]
    i = np.arange(128)[None, :]
    c[:, C_ID:C_ID + 128] = (i == j)
    c[:, C_MUI:C_MUI + 128] = (i >= j)
    c[:, C_MUS:C_MUS + 128] = (i > j)
    c[:, C_MLI:C_MLI + 128] = (j >= i)
    c[:, C_BO:C_BO + 128] = ((i // 64) == (j // 64))
    c[:, C_BS:C_BS + 2] = (np.arange(2)[None, :] == (j // 64))
    c[:, C_S127:C_S127 + 128] = (j == 127)
    c[:, C_MLS:C_MLS + 128] = (j > i)
    bd = lambda s_: (i // s_) == (j // s_)
    c[:, C_BD8:C_BD8 + 128] = bd(8)
    c[:, C_O16:C_O16 + 128] = bd(16) & ~bd(8)
    c[:, C_O32:C_O32 + 128] = bd(32) & ~bd(16)
    c[:, C_O64:C_O64 + 128] = bd(64) & ~bd(32)
    c[:, C_O128:C_O128 + 128] = ~bd(64)
    return c


def make_rope():
    half = 16
    inv = 500000.0 ** (-np.arange(half, dtype=np.float32) / half)
    ang = np.arange(SEQ, dtype=np.float32)[None, :] * inv[:, None]
    cos = np.cos(ang).astype(np.float32)
    sin = np.sin(ang).astype(np.float32)
    r = np.zeros((32, 2 * SEQ), np.float32)
    r[0:16, 0:SEQ] = cos
    r[16:32, 0:SEQ] = cos
    r[0:16, SEQ:] = -sin
    r[16:32, SEQ:] = sin
    return r


class Consts:
    pass


def setup_consts(cx, io):
    K = Consts()
    cf = load_small(cx, io["cst"], CST_N, "cst_f32")
    cb = cx.tile(CST_N, BF16, "cst_bf")
    cx.copy("dve", cb.ap, cf.ap, [cf], [cb])
    K.cf, K.cb = cf, cb
    K.ones = cx.tile(128, BF16, "ones")
    cx.I("dve", "memset", [], [K.ones], ap=K.ones.ap, constant=1.0)
    K.m4 = {}
    for nm_, off in (("id", C_ID), ("bd8", C_BD8), ("o16", C_O16), ("o32", C_O32), ("o64", C_O64), ("o128", C_O128)):
        t = cx.tile(512, BF16, "m4" + nm_)
        for q in range(4):
            cx.copy("pool", t.ap[:, q * 128:(q + 1) * 128], cb.ap[:, off:off + 128], [cb], [t])
        K.m4[nm_] = t
    return K


def dplr_head(cx, K, nm, dk, dv, pb, Kg, Bg, Ag, Rg, gtiles, Wst, Win, wtiles, As, Rs, egam, Kn, Bn, V, ntiles, PC, pctiles,
              scale_all, out_cb, defer=False):
    m0 = cx.mark()
    tag0 = cx.S.tag
    cx.S.tag = tag0 + "/dplr_prep"
    idb = K.cb.ap[:, C_ID:C_ID + 128]
    neg = Bg is None
    AakT = cx.tile(NCH * 128, BF16, nm + "aak")
    ArkT = cx.tile(NCH * 128, BF16, nm + "ark")
    ArbT = cx.tile(NCH * 128, BF16, nm + "arb")
    TT_ = cx.tile(NCH * 128, BF16, nm + "TT")
    sl = lambda t, c: t.ap[:, c * 128:(c + 1) * 128]
    m1 = cx.mark()
    HC = 4
    hs = lambda t, q4: t.ap[:, q4 * 512:(q4 + 1) * 512]
    h1 = lambda t, cl: t.ap[:, cl * 128:(cl + 1) * 128]

    def mm4(dst_ps, lhs_t, rhs_t, q4):
        cx.mm([dict(out=dst_ps.ap[:, q * 128:(q + 1) * 128], lhsT=h1(lhs_t, q4 * 4 + q), rhs=h1(rhs_t, q4 * 4 + q), start=True, stop=True)
               for q in range(4)], [lhs_t, rhs_t], [dst_ps])

    def stream(half, tl_):
        U, N, Ua, Na, Nb, Ub_, Nc, Uc, P, Q, Z1b, Z2b = tl_
        for cl in range(HC):
            c = half * HC + cl
            ps = cx.ps()
            mms = [dict(out=ps.ap[:, 0:128], lhsT=Kg(c), rhs=Ag(c), start=True, stop=True),
                   dict(out=ps.ap[:, 128:256], lhsT=Kg(c), rhs=Rg(c), start=True, stop=True)]
            if not neg:
                mms += [dict(out=ps.ap[:, 256:384], lhsT=Bg(c), rhs=Ag(c), start=True, stop=True),
                        dict(out=ps.ap[:, 384:512], lhsT=Bg(c), rhs=Rg(c), start=True, stop=True)]
            cx.mm(mms, gtiles, [ps])
            cx.tt("dve", sl(AakT, c), ps.ap[:, 0:128], Wst(c), ALU.mult, [ps] + wtiles, [AakT])
            cx.tt("dve", sl(ArkT, c), ps.ap[:, 128:256], Win(c), ALU.mult, [ps] + wtiles, [ArkT])
            if neg:
                cx.stt(h1(U, cl), ps.ap[:, 0:128], -1.0, Wst(c), ALU.mult, ALU.mult, [ps] + wtiles, [U])
                cx.stt(sl(ArbT, c), ps.ap[:, 128:256], -1.0, Win(c), ALU.mult, ALU.mult, [ps] + wtiles, [ArbT])
            else:
                cx.tt("dve", h1(U, cl), ps.ap[:, 256:384], Wst(c), ALU.mult, [ps] + wtiles, [U])
                cx.tt("dve", sl(ArbT, c), ps.ap[:, 384:512], Win(c), ALU.mult, [ps] + wtiles, [ArbT])
            if cl % 2 == 1:
                yield
        q4 = 0
        ps = cx.ps()
        pv = ps.ap.bitcast(BF16)
        for q in range(4):
            cx.tr(pv[:, q * 128:(q + 1) * 128], h1(U, q), idb, [U, K.cb], [ps])
        cx.copy("act", hs(N, q4), pv[:, 0:512], [ps], [N])
        yield
        cx.tt("pool", hs(Ua, q4), hs(U, q4), K.m4["bd8"].ap, ALU.mult, [U, K.m4["bd8"]], [Ua])
        cx.tt("pool", hs(Na, q4), hs(N, q4), K.m4["bd8"].ap, ALU.mult, [N, K.m4["bd8"]], [Na])
        yield
        for (dn, du, sn, su) in ((Nb, Ub_, Na, Ua), (Nc, Uc, Nb, Ub_)):
            ps = cx.ps()
            mm4(ps, su, sn, q4)
            cx.copy("act", hs(dn, q4), ps.ap[:, 0:512], [ps], [dn])
            ps = cx.ps()
            mm4(ps, sn, su, q4)
            cx.copy("act", hs(du, q4), ps.ap[:, 0:512], [ps], [du])
            yield
        cx.tt("pool", hs(P, q4), hs(Ua, q4), K.m4["id"].ap, ALU.add, [Ua, K.m4["id"]], [P])
        cx.tt("pool", hs(Q, q4), hs(Na, q4), K.m4["id"].ap, ALU.add, [Na, K.m4["id"]], [Q])
        yield
        for (ln, lu) in ((Nb, Ub_), (Nc, Uc)):
            ps = cx.ps()
            mm4(ps, ln, P, q4)
            cx.tt("dve", hs(P, q4), ps.ap[:, 0:512], hs(P, q4), ALU.add, [ps, P], [P])
            ps = cx.ps()
            mm4(ps, lu, Q, q4)
            cx.tt("dve", hs(Q, q4), ps.ap[:, 0:512], hs(Q, q4), ALU.add, [ps, Q], [Q])
            yield
        NO, UO, P2, Q2 = Ua, Na, Nb, Ub_
        curP, curQ, nxtP, nxtQ = P, Q, P2, Q2
        for li, mk in enumerate(("o16", "o32", "o64", "o128")):
            last = (li == 3)
            cx.tt("pool", hs(NO, q4), hs(N, q4), K.m4[mk].ap, ALU.mult, [N, K.m4[mk]], [NO])
            if not last:
                cx.tt("pool", hs(UO, q4), hs(U, q4), K.m4[mk].ap, ALU.mult, [U, K.m4[mk]], [UO])
            yield
            ps = cx.ps()
            mm4(ps, NO, curP, q4)
            cx.copy("act", hs(Z1b, q4), ps.ap[:, 0:512], [ps], [Z1b])
            if not last:
                ps = cx.ps()
                mm4(ps, UO, curQ, q4)
                cx.copy("act", hs(Z2b, q4), ps.ap[:, 0:512], [ps], [Z2b])
            yield
            ps = cx.ps()
            mm4(ps, curQ, Z1b, q4)
            if last:
                c0_ = (half * HC) * 128
                cx.tt("dve", TT_.ap[:, c0_:c0_ + 512], ps.ap[:, 0:512], hs(curP, q4), ALU.add, [ps, curP], [TT_])
            else:
                cx.tt("dve", hs(nxtP, q4), ps.ap[:, 0:512], hs(curP, q4), ALU.add, [ps, curP], [nxtP])
                ps = cx.ps()
                mm4(ps, curP, Z2b, q4)
                cx.tt("dve", hs(nxtQ, q4), ps.ap[:, 0:512], hs(curQ, q4), ALU.add, [ps, curQ], [nxtQ])
            curP, curQ, nxtP, nxtQ = nxtP, nxtQ, curP, curQ
            yield

    tlA = [cx.tile(HC * 128, BF16, nm + "ivA%d" % i) for i in range(12)]
    tlB = [cx.tile(HC * 128, BF16, nm + "ivB%d" % i) for i in range(12)]
    for pair in range(0, NCH // HC, 2):
        gens = [stream(pair, tlA), stream(pair + 1, tlB)]
        alive = True
        while alive:
            alive = False
            for g_ in gens:
                try:
                    next(g_)
                    alive = True
                except StopIteration:
                    pass
    cx.release(m1)
    AV = None
    if egam is not None:
        AV = cx.tile(NCH * dv, F32, nm + "AV")
        for c in range(NCH):
            ps = cx.ps()
            cx.mm([dict(out=ps.ap[:, 0:dv], lhsT=sl(AakT, c), rhs=V(c), start=True, stop=True)], [AakT] + ntiles, [ps])
            cx.copy("act", AV.ap[:, c * dv:(c + 1) * dv], ps.ap[:, 0:dv], [ps], [AV])
    St = cx.tile(dv, F32, nm + "S")
    Sb = cx.tile(dv, BF16, nm + "Sb")
    cx.I("dve", "memset", [], [St], ap=St.ap, constant=0.0)
    cx.I("dve", "memset", [], [Sb], ap=Sb.ap, constant=0.0)
    rows = slice(pb, pb + dk)

    def step(c):
        cx.S.tag = tag0 + "/dplr_seq"
        Xb = cx.ring(nm + "Xb", 2, dv, BF16)
        Ub = cx.ring(nm + "Ub", 2, dv, BF16)
        psX = cx.ps()
        if egam is None:
            cx.mm([dict(out=psX.ap[:, 0:dv], lhsT=As(c), rhs=Sb.ap[rows, :], start=True, stop=False),
                   dict(out=psX.ap[:, 0:dv], lhsT=sl(AakT, c), rhs=V(c), start=False, stop=True)],
                  gtiles + [Sb, AakT] + ntiles, [psX])
            cx.copy("act", Xb.ap, psX.ap[:, 0:dv], [psX], [Xb])
        else:
            cx.mm([dict(out=psX.ap[:, 0:dv], lhsT=As(c), rhs=Sb.ap[rows, :], start=True, stop=True)], gtiles + [Sb], [psX])
            cx.stt(Xb.ap, psX.ap[:, 0:dv], egam(c), AV.ap[:, c * dv:(c + 1) * dv], ALU.mult, ALU.add, [psX, AV] + pctiles, [Xb])
        psU = cx.ps()
        cx.mm([dict(out=psU.ap[:, 0:dv], lhsT=sl(TT_, c), rhs=Xb.ap, start=True, stop=True)], [TT_, Xb], [psU])
        cx.copy("act", Ub.ap, psU.ap[:, 0:dv], [psU], [Ub])
        if egam is None:
            psY = cx.ps()
            cx.mm([dict(out=psY.ap[:, 0:dv], lhsT=Rs(c), rhs=Sb.ap[rows, :], start=True, stop=False),
                   dict(out=psY.ap[:, 0:dv], lhsT=sl(ArbT, c), rhs=Ub.ap, start=False, stop=False),
                   dict(out=psY.ap[:, 0:dv], lhsT=sl(ArkT, c), rhs=V(c), start=False, stop=True)],
                  gtiles + [Sb, ArbT, ArkT, Ub] + ntiles, [psY])
            out_cb(c, psY.ap[:, 0:dv], psY)
        else:
            psY1 = cx.ps()
            cx.mm([dict(out=psY1.ap[:, 0:dv], lhsT=Rs(c), rhs=Sb.ap[rows, :], start=True, stop=True)], gtiles + [Sb], [psY1])
            psY2 = cx.ps()
            cx.mm([dict(out=psY2.ap[:, 0:dv], lhsT=sl(ArbT, c), rhs=Ub.ap, start=True, stop=False),
                   dict(out=psY2.ap[:, 0:dv], lhsT=sl(ArkT, c), rhs=V(c), start=False, stop=True)],
                  [ArbT, ArkT, Ub] + ntiles, [psY2])
            y2 = cx.ring(nm + "y2", 2, dv, F32)
            cx.copy("act", y2.ap, psY2.ap[:, 0:dv], [psY2], [y2])
            yo = cx.ring(nm + "yo", 2, dv, F32)
            cx.stt(yo.ap, psY1.ap[:, 0:dv], egam(c), y2.ap, ALU.mult, ALU.add, [psY1, y2] + pctiles, [yo])
            out_cb(c, yo.ap, yo)
        psS = cx.ps()
        cx.mm([dict(out=psS.ap[rows, 0:dv], lhsT=Bn(c), rhs=Ub.ap, start=True, stop=False),
               dict(out=psS.ap[rows, 0:dv], lhsT=Kn(c), rhs=V(c), start=False, stop=True)], ntiles + [Ub], [psS])
        if scale_all:
            cx.tt("dve", St.ap[rows, :], psS.ap[rows, 0:dv], St.ap[rows, :], ALU.add, [psS, St], [St])
            cx.ts("dve", St.ap[rows, :], St.ap[rows, :], PC(c), ALU.mult, [St] + pctiles, [St])
        else:
            cx.stt(St.ap[rows, :], St.ap[rows, :], PC(c), psS.ap[rows, 0:dv], ALU.mult, ALU.add, [St, psS] + pctiles, [St])
        cx.copy("act", Sb.ap[rows, :], St.ap[rows, :], [St], [Sb])

    def finish():
        cx.release(m0)
    if defer:
        return step, finish
    for c in range(NCH):
        step(c)
    finish()


def ln_exp_rinv(cx, out_ap, in_ap, reads, wt, scale=1.0, bias=1e-24):
    cx.act(out_ap, in_ap, AF.Ln, reads, [wt], scale=scale, bias=bias)
    cx.act(out_ap, out_ap, AF.Exp, [wt], [wt], scale=-0.5)


def mixer_attention(cx, K, io, hT, hv):
    m0 = cx.mark()
    wat = io["wat"]
    wsw = io["wsw"]
    rope = load_small(cx, io["rope"], 2 * SEQ, "rope", parts=32)
    cosv = rope.ap[:, 0:SEQ]
    sinv = rope.ap[:, SEQ:2 * SEQ]
    mdiag = K.cb.ap[:, C_MUI:C_MUI + 128]
    mprev = K.cb.ap[:, C_MLI:C_MLI + 128]
    mcomb = cx.tile(256, BF16, "mcomb")
    cx.copy("dve", mcomb.ap[:, 0:128], mdiag, [K.cb], [mcomb])
    cx.copy("dve", mcomb.ap[:, 128:256], mprev, [K.cb], [mcomb])
    if ATT_STOP <= 1:
        return
    DIL = [1, 4, 16]
    scale = 128.0 ** -0.5
    for hl in range(2):
        m1 = cx.mark()
        cx.S.tag = "M/att/proj"
        qT = [cx.tile(SEQ, BF16, "qT%d" % g) for g in range(3)]
        kT = [cx.tile(SEQ, BF16, "kT%d" % g) for g in range(3)]
        Vt = [cx.tile(NCH * 128, BF16, "V%d" % g) for g in range(3)]
        for g in range(3):
            d = DIL[g]
            for which, dst in ((0, qT[g]), (1, kT[g])):
                pcs = load_w_multi(cx, wat[(hl * 3 + g) * 3 + which], KC, 128)
                pcs_sw = load_w_multi(cx, wsw[(hl * 3 + g) * 2 + which], KC, 32)
                for tb in range(4):
                    lo, hi = tb * 512, tb * 512 + 512
                    ps = gemm_block(cx, pcs, 128, lambda k, l, h_: hv[:, k, l:h_], [hT], lo, hi)
                    if ATT_SUB >= 1:
                        ps2 = gemm_block(cx, pcs_sw, 32, lambda k, l, h_: hv[:, k, l:h_], [hT], lo, hi)
                    t1 = cx.ring("rp1", 2, 512, F32)
                    t2 = cx.ring("rp2", 2, 512, F32)
                    if ATT_SUB >= 2:
                        if "a" in ATT_V:
                            cx.tt("dve", t1.ap[0:32, :], ps.ap[0:32, 0:512], cosv[:, lo:hi], ALU.mult, [ps, rope], [t1])
                        if "b" in ATT_V:
                            cx.tt("dve", t2.ap[0:32, :], ps2.ap[0:32, 0:512], sinv[:, lo:hi], ALU.mult, [ps2, rope], [t2])
                        if "c" in ATT_V:
                            cx.tt("dve", t1.ap[0:32, :], ps.ap[0:32, 0:512], K.cf.ap[0:32, 0:512], ALU.mult, [ps, K.cf], [t1])
                        if "e" in ATT_V:
                            cx.tt("dve", t1.ap[:, :], ps.ap[:, 0:512], K.cf.ap[:, 0:512], ALU.mult, [ps, K.cf], [t1])
                        if "g" in ATT_V:
                            cx.tt("dve", t1.ap[0:32, :], ps.ap[0:32, 0:512], K.cf.ap[0:32, 0:512], ALU.mult, [ps, K.cf], [t1])
                        if "d" in ATT_V:
                            cx.tt("dve", t1.ap[0:32, :], t2.ap[0:32, :], cosv[:, lo:hi], ALU.mult, [t2, rope], [t1])
                    cx.copy("act", dst.ap[:, lo:hi], ps.ap[:, 0:512], [ps, t1] if "s" in ATT_V else [ps], [dst])
                    if ATT_SUB >= 3:
                        cx.tt("pool", dst.ap[0:32, lo:hi], t1.ap[0:32, :], t2.ap[0:32, :], ALU.add, [t1, t2], [dst])
            if ATT_STOP <= 2:
                return
            pcs = load_w_multi(cx, wat[(hl * 3 + g) * 3 + 2], KC, 128)
            nb = NCH // d
            for b4 in range(4):
                ps = cx.ps()
                mms = []
                for q in range(4):
                    blk = b4 * 4 + q
                    r, b = blk // nb, blk % nb
                    t0 = r + d * 128 * b
                    for k in range(KC):
                        mms.append(dict(out=ps.ap[:, q * 128:(q + 1) * 128], lhsT=hv[:, k, t0:t0 + d * 127 + 1:d],
                                        rhs=wslice(pcs, k, 128), start=(k == 0), stop=(k == KC - 1)))
                cx.mm(mms, [p[0] for p in pcs] + [hT], [ps])
                cx.copy("act", Vt[g].ap[:, b4 * 512:(b4 + 1) * 512], ps.ap[:, 0:512], [ps], [Vt[g]])
        if ATT_STOP <= 3:
            return
        cx.S.tag = "M/att/core"
        for Tb in range(4):
            pso = cx.ps(pin=True)
            psd = cx.ps(pin=True)
            first = [True]

            def unit(g, kslices, qslice, nq, masks):
                pss = cx.ps()
                nk = len(kslices)
                cx.mm([dict(out=pss.ap[:, i * nq:(i + 1) * nq], lhsT=kT[g].ap[:, ks], rhs=qT[g].ap[:, qslice], start=True, stop=True)
                       for i, (ks, vb) in enumerate(kslices)], [kT[g], qT[g]], [pss])
                pe_ = cx.ring("pexp", 3, 256, BF16)
                pm = cx.ring("pmask", 3, 256, BF16)
                cx.act(pe_.ap[:, 0:nk * nq], pss.ap[:, 0:nk * nq], AF.Exp, [pss], [pe_], scale=scale)
                cx.tt("pool", pm.ap[:, 0:nk * nq], pe_.ap[:, 0:nk * nq], masks, ALU.mult, [pe_, mcomb, K.cb], [pm])
                mmo, mmd = [], []
                qs0 = qslice.start - Tb * 512
                st = qslice.step or 1
                ocols = slice(qs0, qs0 + (nq - 1) * st + 1, st)
                for i, (ks, vb) in enumerate(kslices):
                    f = first[0]
                    first[0] = False
                    mmo.append(dict(out=pso.ap[:, ocols], lhsT=Vt[g].ap[:, vb * 128:(vb + 1) * 128], rhs=pm.ap[:, i * nq:(i + 1) * nq],
                                    start=f, stop=False, skip_group_check=True))
                    mmd.append(dict(out=psd.ap[:, ocols], lhsT=K.ones.ap, rhs=pm.ap[:, i * nq:(i + 1) * nq],
                                    start=f, stop=False, skip_group_check=True))
                cx.mm(mmo, [Vt[g], pm], [pso])
                cx.mm(mmd, [K.ones, pm], [psd])

            for qb in range(4 * Tb, 4 * Tb + 4):
                ks = [(slice(qb * 128, qb * 128 + 128), qb)]
                if qb > 0:
                    ks.append((slice((qb - 1) * 128, qb * 128), qb - 1))
                unit(0, ks, slice(qb * 128, qb * 128 + 128), 128, mcomb.ap[:, 0:128 * len(ks)])
            for r in range(4 if "1" in ATT_G else 0):
                tq = r + 4 * 128 * Tb
                ks = [(slice(tq, tq + 4 * 127 + 1, 4), r * 4 + Tb)]
                if Tb > 0:
                    tk = r + 4 * 128 * (Tb - 1)
                    ks.append((slice(tk, tk + 4 * 127 + 1, 4), r * 4 + Tb - 1))
                unit(1, ks, slice(tq, tq + 4 * 127 + 1, 4), 128, mcomb.ap[:, 0:128 * len(ks)])
            for r in range(16 if "2" in ATT_G else 0):
                tq = r + 16 * 32 * Tb
                ks = [(slice(r, r + 16 * 127 + 1, 16), r)]
                unit(2, ks, slice(tq, tq + 16 * 31 + 1, 16), 32, mdiag[:, 32 * Tb:32 * Tb + 32])
            rd = cx.ring("rden", 2, 512, F32)
            cx.I("dve", "reciprocal", [psd], [rd], out=rd.ap, in_=psd.ap[:, 0:512])
            yo = cx.ring("ybo", 2, 512, BF16)
            cx.tt("dve", yo.ap, pso.ap[:, 0:512], rd.ap, ALU.mult, [pso, rd], [yo])
            io["ywrite"](cx, io["rb"] + hl, Tb * 512, Tb * 512 + 512, yo.ap, [yo], io["ydep"]["b"])
            cx.unpin_all()
            if ATT_STOP <= 4:
                return
        cx.release(m1)
    cx.release(m0)


def mixer_gdn(cx, K, io, hT, hv):
    m0 = cx.mark()
    wc = io["wc"]
    wba = io["wba"]
    gconv = load_small(cx, io["gconv"], 12 * 4, "gconv")
    gcv = gconv.ap.rearrange("p (b t) -> p b t", t=4)
    gpar = load_small(cx, io["gpar"], 8 + 128, "gpar")
    mui = K.cf.ap[:, C_MUI:C_MUI + 128]
    mus = K.cf.ap[:, C_MUS:C_MUS + 128]
    mls = K.cf.ap[:, C_MLS:C_MLS + 128]
    idb = K.cb.ap[:, C_ID:C_ID + 128]
    pcs = load_w_multi(cx, wba, KC, 8)
    ps = cx.ps()
    mms = []
    for c in range(NCH):
        for k in range(KC):
            mms.append(dict(out=ps.ap[:, c * 8:(c + 1) * 8], lhsT=hv[:, k, c * 128:(c + 1) * 128], rhs=wslice(pcs, k, 8),
                            start=(k == 0), stop=(k == KC - 1)))
    cx.mm(mms, [p[0] for p in pcs] + [hT], [ps])
    ba = cx.tile(NCH * 8, F32, "ba")
    cx.copy("act", ba.ap, ps.ap[:, 0:NCH * 8], [ps], [ba])
    bav = ba.ap.rearrange("p (c e) -> p c e", e=8)
    beta = cx.tile(NCH * 4, F32, "beta")
    betav = beta.ap.rearrange("p (c e) -> p c e", e=4)
    cx.act(betav, bav[:, :, 0:4], AF.Sigmoid, [ba], [beta])
    gg = cx.tile(NCH * 4, F32, "gg")
    ggv = gg.ap.rearrange("p (c e) -> p c e", e=4)
    for hh in range(4):
        cx.act(ggv[:, :, hh], bav[:, :, 4 + hh], AF.Exp, [ba, gpar], [gg], bias=gpar.ap[:, 4 + hh:5 + hh])
    cx.act(gg.ap, gg.ap, AF.Ln, [gg], [gg], bias=1.0)
    ea = cx.tile(4, F32, "expA")
    cx.act(ea.ap, gpar.ap[:, 0:4], AF.Exp, [gpar], [ea])
    for hh in range(4):
        cx.ts("dve", ggv[:, :, hh], ggv[:, :, hh], ea.ap[:, hh:hh + 1], ALU.mult, [gg, ea], [gg], s2=-1.0, op1=ALU.mult)
    ps = cx.ps()
    cx.mm([dict(out=ps.ap[:, 0:64], lhsT=mui, rhs=gg.ap, start=True, stop=True)], [K.cf, gg], [ps])
    gam = cx.tile(64, F32, "gam")
    cx.copy("act", gam.ap, ps.ap[:, 0:64], [ps], [gam])
    egam = cx.tile(64, F32, "egam")
    cx.act(egam.ap, gam.ap, AF.Exp, [gam], [egam])
    ps = cx.ps()
    cx.mm([dict(out=ps.ap[:, 0:64], lhsT=K.cf.ap[:, C_S127:C_S127 + 128], rhs=gam.ap, start=True, stop=True)], [K.cf, gam], [ps])
    pcall = cx.tile(64, F32, "pcall")
    cx.act(pcall.ap, ps.ap[:, 0:64], AF.Exp, [ps], [pcall])
    wk = cx.tile(64, F32, "wk")
    cx.tt("dve", wk.ap, ps.ap[:, 0:64], gam.ap, ALU.subtract, [ps, gam], [wk])
    cx.act(wk.ap, wk.ap, AF.Exp, [wk], [wk])
    cx.tt("dve", wk.ap, wk.ap, beta.ap, ALU.mult, [wk, beta], [wk])
    nwk = cx.tile(64, F32, "nwk")
    cx.ts("dve", nwk.ap, wk.ap, -1.0, ALU.mult, [wk], [nwk])
    for hh in range(4):
        m1 = cx.mark()
        cx.S.tag = "M/gdn/prep"
        qT = cx.tile(SEQ, BF16, "gq")
        kT = cx.tile(SEQ, BF16, "gk")
        Vn = cx.tile(NCH * 128, BF16, "gV")
        Kn = cx.tile(NCH * 128, BF16, "gKn")
        Bn = cx.tile(NCH * 128, BF16, "gBn")
        gate = cx.tile(NCH * 128, BF16, "ggate")
        Wst = cx.tile(NCH * 128, F32, "gWst")
        Win = cx.tile(NCH * 128, F32, "gWin")
        m2 = cx.mark()
        vT = cx.tile(SEQ, BF16, "gvT")
        for which, dst in ((0, qT), (1, kT), (2, vT)):
            pcs = load_w_multi(cx, wc[which * 4 + hh], KC, 128)
            zp = cx.ring("gzp", 2, 3 + SEQ, F32)
            cx.I("dve", "memset", [], [zp], ap=zp.ap[:, 0:3], constant=0.0)
            for tb in range(4):
                lo, hi = tb * 512, tb * 512 + 512
                ps = gemm_block(cx, pcs, 128, lambda k, l, h_: hv[:, k, l:h_], [hT], lo, hi)
                cx.copy("act", zp.ap[:, 3 + lo:3 + hi], ps.ap[:, 0:512], [ps], [zp])
            cv = cx.ring("gcv", 2, SEQ, F32)
            bi = which * 4 + hh
            cx.ts("dve", cv.ap, zp.ap[:, 3:3 + SEQ], gcv[:, bi, 3:4], ALU.mult, [zp, gconv], [cv])
            for tap in range(3):
                cx.stt(cv.ap, zp.ap[:, tap:tap + SEQ], gcv[:, bi, tap:tap + 1], cv.ap, ALU.mult, ALU.add, [zp, gconv, cv], [cv])
            cx.act(cv.ap, cv.ap, AF.Silu, [cv], [cv])
            if which == 2:
                cx.copy("pool", dst.ap, cv.ap, [cv], [dst])
            else:
                for tb in range(4):
                    lo, hi = tb * 512, tb * 512 + 512
                    sq = cx.ring("gsq", 2, 512, BF16)
                    cx.act(sq.ap, cv.ap[:, lo:hi], AF.Square, [cv], [sq])
                    ps = cx.ps()
                    cx.mm([dict(out=ps.ap[:, 0:512], lhsT=K.ones.ap, rhs=sq.ap, start=True, stop=True)], [K.ones, sq], [ps])
                    ri = cx.ring("gri", 2, 512, F32)
                    ln_exp_rinv(cx, ri.ap, ps.ap[:, 0:512], [ps], ri)
                    if which == 0:
                        cx.stt(dst.ap[:, lo:hi], cv.ap[:, lo:hi], 128.0 ** -0.5, ri.ap, ALU.mult, ALU.mult, [cv, ri], [dst])
                    else:
                        cx.tt("dve", dst.ap[:, lo:hi], cv.ap[:, lo:hi], ri.ap, ALU.mult, [cv, ri], [dst])
        for c4 in range(4):
            ps = cx.ps()
            pv = ps.ap.bitcast(BF16)
            for q in range(4):
                c = c4 * 4 + q
                cx.tr(pv[:, q * 128:(q + 1) * 128], vT.ap[:, c * 128:(c + 1) * 128], idb, [vT, K.cb], [ps])
            cx.copy("act", Vn.ap[:, c4 * 512:(c4 + 1) * 512], pv[:, 0:512], [ps], [Vn])
            ps = cx.ps()
            pv = ps.ap.bitcast(BF16)
            for q in range(4):
                c = c4 * 4 + q
                cx.tr(pv[:, q * 128:(q + 1) * 128], kT.ap[:, c * 128:(c + 1) * 128], idb, [kT, K.cb], [ps])
            for q in range(4):
                c = c4 * 4 + q
                col = c * 4 + hh
                cx.ts("dve", Kn.ap[:, c * 128:(c + 1) * 128], pv[:, q * 128:(q + 1) * 128], wk.ap[:, col:col + 1], ALU.mult, [ps, wk], [Kn])
                cx.ts("dve", Bn.ap[:, c * 128:(c + 1) * 128], pv[:, q * 128:(q + 1) * 128], nwk.ap[:, col:col + 1], ALU.mult, [ps, nwk], [Bn])
        pcs = load_w_multi(cx, wc[12 + hh], KC, 128)
        for c4 in range(4):
            ps = cx.ps()
            mms = []
            for q in range(4):
                c = c4 * 4 + q
                for k in range(KC):
                    mms.append(dict(out=ps.ap[:, q * 128:(q + 1) * 128], lhsT=hv[:, k, c * 128:(c + 1) * 128], rhs=wslice(pcs, k, 128),
                                    start=(k == 0), stop=(k == KC - 1)))
            cx.mm(mms, [p[0] for p in pcs] + [hT], [ps])
            cx.act(gate.ap[:, c4 * 512:(c4 + 1) * 512], ps.ap[:, 0:512], AF.Silu, [ps], [gate])
        for c in range(NCH):
            col = c * 4 + hh
            g2 = cx.ring("gG2", 2, 128, F32)
            cx.ts("dve", g2.ap, mui, gg.ap[:, col:col + 1], ALU.mult, [K.cf, gg], [g2])
            ps = cx.ps()
            cx.mm([dict(out=ps.ap[:, 0:128], lhsT=mls, rhs=g2.ap, start=True, stop=True)], [K.cf, g2], [ps])
            ex = cx.ring("gex", 2, 128, F32)
            cx.act(ex.ap, ps.ap[:, 0:128], AF.Exp, [ps], [ex])
            cx.stt(Wst.ap[:, c * 128:(c + 1) * 128], ex.ap, beta.ap[:, col:col + 1], mus, ALU.mult, ALU.mult, [ex, beta, K.cf], [Wst])
            cx.stt(Win.ap[:, c * 128:(c + 1) * 128], ex.ap, beta.ap[:, col:col + 1], mui, ALU.mult, ALU.mult, [ex, beta, K.cf], [Win])
        cx.release(m2)
        cx.S.tag = "M/gdn"
        sl = lambda t, c: t.ap[:, c * 128:(c + 1) * 128]
        ycs = cx.tile(SEQ, BF16, "ycs")

        def out_cb(c, yap, yt, hh=hh, ycs=ycs):
            junk = cx.ring("gjunk", 2, 128, F32)
            ss = cx.ring("gss", 2, 1, F32)
            cx.act(junk.ap, yap, AF.Square, [yt], [junk, ss], accum_out=ss.ap)
            ln_exp_rinv(cx, ss.ap, ss.ap, [ss], ss, scale=1.0 / 128, bias=EPS)
            o1 = cx.ring("go1", 2, 128, F32)
            cx.stt(o1.ap, yap, ss.ap[:, 0:1], gpar.ap[:, 8:136], ALU.mult, ALU.mult, [yt, ss, gpar], [o1])
            o2 = cx.ring("go2", 2, 128, BF16)
            cx.tt("pool", o2.ap, o1.ap, gate.ap[:, c * 128:(c + 1) * 128], ALU.mult, [o1, gate], [o2])
            pst = cx.ps()
            ptv = pst.ap.bitcast(BF16)
            cx.tr(ptv[:, 0:128], o2.ap, idb, [o2, K.cb], [pst])
            cx.copy("act", ycs.ap[:, c * 128:(c + 1) * 128], ptv[:, 0:128], [pst], [ycs])

        dplr_head(cx, K, "g%d" % hh, 128, 128, 0,
                  Kg=lambda c: sl(kT, c), Bg=None, Ag=lambda c: sl(kT, c), Rg=lambda c: sl(qT, c), gtiles=[kT, qT],
                  Wst=lambda c: sl(Wst, c), Win=lambda c: sl(Win, c), wtiles=[Wst, Win],
                  As=lambda c: sl(kT, c), Rs=lambda c: sl(qT, c), egam=lambda c: egam.ap[:, c * 4 + hh:c * 4 + hh + 1],
                  Kn=lambda c: sl(Kn, c), Bn=lambda c: sl(Bn, c), V=lambda c: sl(Vn, c), ntiles=[Kn, Bn, Vn],
                  PC=lambda c: pcall.ap[:, c * 4 + hh:c * 4 + hh + 1], pctiles=[pcall, egam], scale_all=False, out_cb=out_cb)
        io["ywrite"](cx, io["rc"] + hh, 0, SEQ, ycs.ap, [ycs], io["ydep"]["c"])
        cx.release(m1)
    cx.release(m0)


def mixer_rwkv(cx, K, io, hT, hv, layer1):
    m0 = cx.mark()
    wa = io["wa"]
    wal = io["wal"]
    rp = load_small(cx, io["rpar"], 44, "rpar")
    rl = load_small(cx, io["rlmu"], 4, "rlmu")
    rb = load_small(cx, io["rbc"], 1024, "rbc")
    wab = cx.tile(512, BF16, "wab")
    g2b = cx.tile(1024, BF16, "g2b")
    if layer1:
        v2b_ = cx.tile(512, BF16, "v2b", parts=64)
        v2b = T(v2b_.ap[32:64, :], v2b_.d)
    rmask = cx.tile(SEQ, BF16, "rmask")
    tl = cx.tile(SEQ, BF16, "tl")
    sgd = cx.tile(SEQ, BF16, "sgd")
    sg2h = cx.tile(SEQ, BF16, "sg2h", parts=64)
    sgd2 = T(sg2h.ap[0:32, :], sg2h.d)
    if layer1:
        hv1 = T(sg2h.ap[32:64, :], sg2h.d)
    mt = cx.mark()
    w2a2 = load_small(cx, io["rw2a2"], 512, "rw2a2")
    g2a = load_small(cx, io["rg2"], 1024, "rg2")
    cx.copy("pool", wab.ap, w2a2.ap, [w2a2], [wab])
    cx.copy("pool", g2b.ap, g2a.ap, [g2a], [g2b])
    if layer1:
        v2 = cx.tile(512, F32, "rv2", parts=64)
        cx.dma(v2.ap[32:64, :], io["rv2"], [], [v2])
        cx.copy("pool", v2b.ap, v2.ap[32:64, :], [v2], [v2b])
    idb = K.cb.ap[:, C_ID:C_ID + 128]
    msk_s = K.cb.ap[:, C_MUS:C_MUS + 128]
    msk_i = K.cb.ap[:, C_MUI:C_MUI + 128]
    cx.I("dve", "memset", [], [rmask], ap=rmask.ap, constant=1.0)
    cx.I("dve", "memset", [], [rmask], ap=rmask.ap[:, 0:SEQ:128], constant=0.0)

    def mixed_block(zp, tmp, pcs, cw, c0, m, po, mu_ap, mu_t, dst_ap, dst_t, func=None):
        pr = slice(po, po + m)
        cx.I("dve", "memset", [], [zp], ap=zp.ap[pr, 0:1], constant=0.0)
        for tb in range(4):
            lo, hi = tb * 512, tb * 512 + 512
            ps = cx.ps()
            mms = []
            for k in range(KC):
                mms.append(dict(out=ps.ap[pr, 0:512], lhsT=wslice(pcs, k, cw, c0, c0 + m), rhs=hv[:, k, lo:hi],
                                start=(k == 0), stop=(k == KC - 1)))
            cx.mm(mms, [p[0] for p in pcs] + [hT], [ps])
            cx.copy("act", zp.ap[pr, 1 + lo:1 + hi], ps.ap[pr, 0:512], [ps], [zp])
        cx.tt("dve", tmp.ap[pr, :], zp.ap[pr, 0:SEQ], zp.ap[pr, 1:1 + SEQ], ALU.subtract, [zp], [tmp])
        if func is None:
            cx.stt(dst_ap, tmp.ap[pr, :], mu_ap, zp.ap[pr, 1:1 + SEQ], ALU.mult, ALU.add, [tmp, zp, mu_t], [dst_t])
        else:
            cx.stt(tmp.ap[pr, :], tmp.ap[pr, :], mu_ap, zp.ap[pr, 1:1 + SEQ], ALU.mult, ALU.add, [tmp, zp, mu_t], [tmp])
            cx.act(dst_ap, tmp.ap[pr, :], func, [tmp], [dst_t])

    zp0 = cx.tile(SEQ + 8, F32, "zp0")
    tmp0 = cx.tile(SEQ, F32, "tmp0")
    pcl = load_w_big(cx, wal, KC, 288, "wal_b")
    mixed_block(zp0, tmp0, pcl, 288, 0, 64, 0, rl.ap[0:64, 0:1], rl, tl.ap[0:64, :], tl, func=AF.Tanh)
    mixed_block(zp0, tmp0, pcl, 288, 64, 64, 64, rl.ap[64:128, 0:1], rl, tl.ap[64:128, :], tl, func=AF.Copy)
    mixed_block(zp0, tmp0, pcl, 288, 128, 128, 0, rl.ap[:, 1:2], rl, sgd.ap, sgd, func=AF.Sigmoid)
    mixed_block(zp0, tmp0, pcl, 288, 256, 32, 0, rl.ap[0:32, 2:3], rl, sgd2.ap, sgd2, func=AF.Sigmoid)
    if layer1:
        pc1 = load_w_multi(cx, io["rv1"], KC, 32)
        for tb in range(4):
            lo, hi = tb * 512, tb * 512 + 512
            ps = gemm_block(cx, pc1, 32, lambda k, l, h_: hv[:, k, l:h_], [hT], lo, hi, po=32)
            cx.copy("act", hv1.ap[:, lo:hi], ps.ap[32:64, 0:512], [ps], [hv1])
        vfT = io["vf_in"].rearrange("(k p) t -> p k t", p=128)
    cx.release(mt)
    vout = io["v_out"].rearrange("(k p) t -> p k t", p=128)
    lnw = rb.ap[:, 0:512]
    lnb = rb.ap[:, 512:1024]
    for ct in range(4):
        m1 = cx.mark()
        par = rp.ap[:, 12 + ct * 8:12 + ct * 8 + 8]
        cx.S.tag = "M/rwkv/prep"
        As = cx.tile(SEQ, BF16, "rAs")
        Rs = cx.tile(SEQ, BF16, "rRs")
        Ks = cx.tile(SEQ, BF16, "rKs")
        Bs = cx.tile(SEQ, BF16, "rBs")
        Kn = cx.tile(NCH * 128, BF16, "rKn")
        Bn = cx.tile(NCH * 128, BF16, "rBn")
        Vn = cx.tile(NCH * 128, BF16, "rVn")
        PCt = cx.tile(NCH, F32, "rPC")
        bon = cx.tile(NCH * 2, F32, "rbon")
        m2 = cx.mark()
        zp = cx.tile(SEQ + 8, F32, "zp")
        tmp = cx.tile(SEQ, F32, "tmp")
        ld = cx.tile(SEQ, F32, "ld")
        cum = cx.tile(SEQ, F32, "cum")
        av = cx.tile(SEQ, BF16, "a_sig")
        rm = cx.tile(SEQ, BF16, "r_m")
        km = cx.tile(SEQ, BF16, "k_m")
        vm = cx.tile(SEQ, BF16, "v_m")
        rkr = cx.tile(SEQ, BF16, "rkr")
        for which, dst in ((0, rm), (1, km), (2, vm)):
            pcs = load_w_multi(cx, wa[which * 4 + ct], KC, 128)
            mixed_block(zp, tmp, pcs, 128, 0, 128, 0, rp.ap[:, which * 4 + ct:which * 4 + ct + 1], rp, dst.ap, dst)
        kx = T(zp.ap[:, 0:SEQ], zp.d)
        B1 = tmp
        for tb in range(4):
            lo, hi = tb * 512, tb * 512 + 512
            ps = cx.ps()
            cx.mm([dict(out=ps.ap[:, 0:512], lhsT=wab.ap[0:64, ct * 128:(ct + 1) * 128], rhs=tl.ap[0:64, lo:hi], start=True, stop=True)],
                  [wab, tl], [ps])
            cx.act(ld.ap[:, lo:hi], ps.ap[:, 0:512], AF.Sigmoid, [ps, rp], [ld], bias=par[:, 0:1])
            ps = cx.ps()
            cx.mm([dict(out=ps.ap[:, 0:512], lhsT=wab.ap[64:128, ct * 128:(ct + 1) * 128], rhs=tl.ap[64:128, lo:hi], start=True, stop=True)],
                  [wab, tl], [ps])
            cx.act(av.ap[:, lo:hi], ps.ap[:, 0:512], AF.Sigmoid, [ps, rp], [av], bias=par[:, 1:2])
        cx.ts("dve", ld.ap, ld.ap, -float(np.exp(-0.5)), ALU.mult, [ld], [ld])
        cx.I("dve", "tensor_tensor_scan", [rmask, ld], [cum], out=cum.ap, data0=rmask.ap, data1=ld.ap, initial=0.0,
             op0=ALU.mult, op1=ALU.add)
        if layer1:
            vf = rkr
            cx.dma(vf.ap, vfT[:, ct, :], io["vf_deps"], [vf])
            for tb in range(4):
                lo, hi = tb * 512, tb * 512 + 512
                ps = cx.ps()
                cx.mm([dict(out=ps.ap[:, 0:512], lhsT=v2b.ap[:, ct * 128:(ct + 1) * 128], rhs=hv1.ap[:, lo:hi], start=True, stop=True)],
                      [v2b, hv1], [ps])
                cx.act(B1.ap[:, lo:hi], ps.ap[:, 0:512], AF.Sigmoid, [ps, rp], [B1], bias=par[:, 5:6])
            cx.tt("dve", kx.ap, vf.ap, vm.ap, ALU.subtract, [vf, vm], [kx])
            cx.tt("dve", kx.ap, kx.ap, B1.ap, ALU.mult, [kx, B1], [kx])
            cx.tt("dve", vm.ap, vm.ap, kx.ap, ALU.add, [vm, kx], [vm])
        if io.get("write_v", True):
            cx.dma(vout[:, ct, :], vm.ap, [vm], [io["v_dep"]])
        cx.ts("dve", kx.ap, km.ap, par[:, 2:3], ALU.mult, [km, rp], [kx])
        for tb in range(4):
            lo, hi = tb * 512, tb * 512 + 512
            sq = cx.ring("rsq", 2, 512, BF16)
            cx.act(sq.ap, kx.ap[:, lo:hi], AF.Square, [kx], [sq])
            ps = cx.ps()
            cx.mm([dict(out=ps.ap[:, 0:512], lhsT=K.cb.ap[:, C_BO:C_BO + 128], rhs=sq.ap, start=True, stop=True)], [K.cb, sq], [ps])
            ri = cx.ring("rri", 2, 512, F32)
            ln_exp_rinv(cx, ri.ap, ps.ap[:, 0:512], [ps], ri)
            cx.tt("dve", kx.ap[:, lo:hi], kx.ap[:, lo:hi], ri.ap, ALU.mult, [kx, ri], [kx])
        if RW_DBG and ct == 0:
            dbg = io["dbg"].rearrange("(k p) t -> p k t", p=128)
            cx.dma(dbg[:, 0, :], kx.ap, [kx], [])
            cx.dma(dbg[:, 1, :], ld.ap, [ld], [])
            cx.dma(dbg[:, 2, :], cum.ap, [cum], [])
        cx.ts("dve", B1.ap, av.ap, -1.0, ALU.add, [av, rp], [B1], s2=par[:, 3:4], op1=ALU.mult)
        cx.ts("dve", B1.ap, B1.ap, 1.0, ALU.add, [B1], [B1])
        if RW_DBG and ct == 0:
            cx.dma(dbg[:, 3, :], B1.ap, [B1], [])
        cx.tt("dve", km.ap, km.ap, B1.ap, ALU.mult, [km, B1], [km])
        cx.stt(rkr.ap, rm.ap, par[:, 4:5], km.ap, ALU.mult, ALU.mult, [rm, km, rp], [rkr])
        ps = cx.ps()
        cx.mm([dict(out=ps.ap[:, c * 2:c * 2 + 2], lhsT=rkr.ap[:, c * 128:(c + 1) * 128], rhs=K.cb.ap[:, C_BS:C_BS + 2], start=True, stop=True)
               for c in range(NCH)], [rkr, K.cb], [ps])
        cx.copy("act", bon.ap, ps.ap[:, 0:NCH * 2], [ps], [bon])
        cx.tt("dve", B1.ap, cum.ap, ld.ap, ALU.subtract, [cum, ld], [B1])
        cx.act(B1.ap, B1.ap, AF.Exp, [B1], [B1])
        cx.stt(As.ap, kx.ap, -1.0, B1.ap, ALU.mult, ALU.mult, [kx, B1], [As])
        cx.act(B1.ap, cum.ap, AF.Exp, [cum], [B1])
        cx.tt("dve", Rs.ap, rm.ap, B1.ap, ALU.mult, [rm, B1], [Rs])
        cx.copy("dve", PCt.ap, B1.ap[:, 127:SEQ:128], [B1], [PCt])
        cx.act(B1.ap, cum.ap, AF.Exp, [cum], [B1], scale=-1.0)
        cx.tt("dve", Ks.ap, km.ap, B1.ap, ALU.mult, [km, B1], [Ks])
        cx.tt("dve", kx.ap, kx.ap, av.ap, ALU.mult, [kx, av], [kx])
        cx.tt("dve", Bs.ap, kx.ap, B1.ap, ALU.mult, [kx, B1], [Bs])
        for src, dst in ((Ks, Kn), (Bs, Bn), (vm, Vn)):
            for c4 in range(4):
                ps = cx.ps()
                pv = ps.ap.bitcast(BF16)
                for q in range(4):
                    c = c4 * 4 + q
                    cx.tr(pv[:, q * 128:(q + 1) * 128], src.ap[:, c * 128:(c + 1) * 128], idb, [src, K.cb], [ps])
                cx.copy("act", dst.ap[:, c4 * 512:(c4 + 1) * 512], pv[:, 0:512], [ps], [dst])
        cx.release(m2)
        yas = cx.tile(SEQ, BF16, "yas")
        cx.S.tag = "M/rwkv"
        sfs = []
        for hp in range(2):
            pb = hp * 64
            hl = ct * 2 + hp
            pr = slice(pb, pb + 64)

            def out_cb(c, yap, yt, hl=hl, hp=hp, ct=ct, yas=yas, pb=pb):
                st = cx.ring("rst%d" % hp, 2, 6, F32)
                cx.I("dve", "bn_stats", [yt], [st], out=st.ap, in_=yap)
                mv_ = cx.ring("rmv%d" % hp, 2, 2, F32)
                cx.I("dve", "bn_aggr", [st], [mv_], out=mv_.ap, in_=st.ap)
                rs = cx.ring("rrs%d" % hp, 2, 1, F32)
                ln_exp_rinv(cx, rs.ap, mv_.ap[:, 1:2], [mv_], rs, bias=64e-5)
                y1 = cx.ring("ry1%d" % hp, 2, 64, F32)
                cx.ts("dve", y1.ap, yap, mv_.ap[:, 0:1], ALU.subtract, [yt, mv_, rs], [y1], s2=rs.ap[:, 0:1], op1=ALU.mult)
                cx.tt("pool", y1.ap, y1.ap, lnw[:, hl * 64:(hl + 1) * 64], ALU.mult, [y1, rb], [y1])
                cx.tt("pool", y1.ap, y1.ap, lnb[:, hl * 64:(hl + 1) * 64], ALU.add, [y1, rb], [y1])
                y2 = cx.ring("ry2%d" % hp, 2, 64, F32)
                cx.stt(y2.ap, Vn.ap[:, c * 128 + hp * 64:c * 128 + hp * 64 + 64], bon.ap[:, c * 2 + hp:c * 2 + hp + 1], y1.ap,
                       ALU.mult, ALU.add, [Vn, bon, y1], [y2])
                psg = cx.ps()
                cx.mm([dict(out=psg.ap[:, 0:64], lhsT=sgd.ap[:, c * 128:(c + 1) * 128], rhs=g2b.ap[:, hl * 64:(hl + 1) * 64], start=True, stop=False),
                       dict(out=psg.ap[:, 0:64], lhsT=sgd2.ap[:, c * 128:(c + 1) * 128], rhs=g2b.ap[0:32, 512 + hl * 64:512 + (hl + 1) * 64],
                            start=False, stop=True)], [sgd, sgd2, g2b], [psg])
                y3 = cx.ring("ry3%d" % hp, 2, 64, BF16)
                if RW_DBG == "g":
                    cx.copy("dve", y3.ap, psg.ap[:, 0:64], [psg], [y3])
                elif RW_DBG == "y1":
                    cx.copy("dve", y3.ap, y1.ap, [y1, psg], [y3])
                elif RW_DBG == "y2":
                    cx.copy("dve", y3.ap, y2.ap, [y2, psg], [y3])
                elif RW_DBG == "yraw":
                    cx.copy("dve", y3.ap, yap, [yt, psg], [y3])
                else:
                    cx.tt("dve", y3.ap, psg.ap[:, 0:64], y2.ap, ALU.mult, [psg, y2], [y3])
                pst = cx.ps()
                ptv = pst.ap.bitcast(BF16)
                cx.tr(ptv[pb:pb + 64, 0:128], y3.ap, idb, [y3, K.cb], [pst])
                cx.copy("act", yas.ap[pb:pb + 64, c * 128:(c + 1) * 128], ptv[pb:pb + 64, 0:128], [pst], [yas])

            sf = dplr_head(cx, K, "r%d" % hl, 64, 64, pb, defer=True,
                      Kg=lambda c, pr=pr: Ks.ap[pr, c * 128:(c + 1) * 128], Bg=lambda c, pr=pr: Bs.ap[pr, c * 128:(c + 1) * 128],
                      Ag=lambda c, pr=pr: As.ap[pr, c * 128:(c + 1) * 128], Rg=lambda c, pr=pr: Rs.ap[pr, c * 128:(c + 1) * 128],
                      gtiles=[Ks, Bs, As, Rs],
                      Wst=lambda c: msk_s, Win=lambda c: msk_i, wtiles=[K.cb],
                      As=lambda c, pr=pr: As.ap[pr, c * 128:(c + 1) * 128], Rs=lambda c, pr=pr: Rs.ap[pr, c * 128:(c + 1) * 128], egam=None,
                      Kn=lambda c, pb=pb: Kn.ap[:, c * 128 + pb:c * 128 + pb + 64], Bn=lambda c, pb=pb: Bn.ap[:, c * 128 + pb:c * 128 + pb + 64],
                      V=lambda c, pb=pb: Vn.ap[:, c * 128 + pb:c * 128 + pb + 64], ntiles=[Kn, Bn, Vn],
                      PC=lambda c, pr=pr: PCt.ap[pr, c:c + 1], pctiles=[PCt], scale_all=True, out_cb=out_cb)
            sfs.append(sf)
        for c in range(NCH):
            for st_, fn_ in sfs:
                st_(c)
        for st_, fn_ in reversed(sfs):
            fn_()
        io["ywrite"](cx, io["ra"] + ct, 0, SEQ, yas.ap, [yas], io["ydep"]["a"])
        cx.release(m1)
    cx.release(m0)


def phase_M(cx, ios, layer1, which=("att", "gdn", "rwkv")):
    io = ios[0]
    base_ = cx.mark()
    cx.S.tag = "M/pro"
    cx.cast_eng = "pool"
    cx.w_nst, cx.w_nwb = 2, 3
    K = setup_consts(cx, io)
    g1 = load_small(cx, io["g_attn"], 16, "g1")
    hT = cx.tile(KC * SEQ, BF16, "hT")
    hv = hT.ap.rearrange("p (k t) -> p k t", k=KC)
    if "x_view" in io:
        xview = io["x_view"]
    else:
        xT = io["xT"].rearrange("(k p) t -> p k t", p=128)
        xview = lambda k, lo, hi: xT[:, k, lo:hi]
    m0 = cx.mark()
    for tb in range(4):
        lo, hi = tb * 512, tb * 512 + 512
        xt = cx.ring("xtile", 1, KC * 512, F32)
        xtv = xt.ap.rearrange("p (k t) -> p k t", k=KC)
        for k in range(KC):
            cx.dma(xtv[:, k, :], xview(k, lo, hi), io.get("x_deps", []), [xt], multi=True)
        rmsnorm_fm(cx, lambda k, l, h_: (xtv[:, k, 0:h_ - l], xt), g1, lambda k, l, h_: (hv[:, k, l:h_], hT),
                   [(lo, hi)], K.ones)
    cx.release(m0)
    for io in ios:
        if "att" in which:
            cx.S.tag = "M/att"
            mixer_attention(cx, K, io, hT, hv)
        if "gdn" in which:
            cx.S.tag = "M/gdn"
            mixer_gdn(cx, K, io, hT, hv)
        if "rwkv" in which:
            cx.S.tag = "M/rwkv"
            mixer_rwkv(cx, K, io, hT, hv, layer1)
    cx.release(base_)


M_INPUTS = [("xT", (D, SEQ)), ("g_attn", (128, 16)), ("cst", (128, CST_N)), ("rope", (32, 2 * SEQ)),
            ("wat", (18, 128, 16, 128)), ("wsw", (12, 128, 16, 32)),
            ("wc", (16, 128, 16, 128)), ("wba", (128, 16, 8)), ("gconv", (128, 48)), ("gpar", (128, 136)),
            ("wa", (12, 128, 16, 128)), ("wal", (128, 16, 288)), ("rpar", (128, 44)), ("rlmu", (128, 4)), ("rbc", (128, 1024)),
            ("rw2a2", (128, 512)), ("rg2", (128, 1024))]
M_INPUTS_L1 = [("rv1", (128, 16, 32)), ("rv2", (32, 512)), ("vf_in", (512, SEQ))]
M_OUTPUTS = [("y_fm", (768, SEQ)), ("yc_tm", (SEQ, 512)), ("ya_tm", (SEQ, 512)), ("v_out", (512, SEQ))]


def prep_M_weights(inp, l, s):
    A_IN, B_IN, C_IN = 3360, 4608, 4112
    W = inp["w_in"][l]
    w = {}
    w["g_attn"] = vec_pk(inp["attn_norm"][l])
    w["cst"] = make_consts()
    w["rope"] = make_rope()

    def colblk(cols):
        sub = W[:, cols]
        n = sub.shape[1]
        return np.ascontiguousarray(sub.reshape(16, 128, n // 128, 128).transpose(2, 1, 0, 3))

    def colsmall(cols):
        sub = W[:, cols]
        return np.ascontiguousarray(sub.reshape(16, 128, len(cols)).transpose(1, 0, 2))
    b0 = A_IN
    cols = []
    cols_sw = []
    for hl in range(2):
        hi = 2 * s + hl
        for g in range(3):
            for which in range(3):
                c0 = b0 + which * 1536 + g * 512 + hi * 128
                cols += list(range(c0, c0 + 128))
                if which < 2:
                    cols_sw += list(range(c0 + 16, c0 + 32)) + list(range(c0, c0 + 16))
    w["wat"] = colblk(np.array(cols))
    sw = W[:, np.array(cols_sw)]
    w["wsw"] = np.ascontiguousarray(sw.reshape(16, 128, 12, 32).transpose(2, 1, 0, 3))
    c0 = A_IN + B_IN
    cols = []
    for which in range(3):
        for hh in range(4):
            h = 4 * s + hh
            cols += list(range(c0 + which * 1024 + h * 128, c0 + which * 1024 + (h + 1) * 128))
    for hh in range(4):
        h = 4 * s + hh
        cols += list(range(c0 + 3088 + h * 128, c0 + 3088 + (h + 1) * 128))
    w["wc"] = colblk(np.array(cols))
    w["wba"] = colsmall(np.array([c0 + 3072 + 4 * s + i for i in range(4)] + [c0 + 3080 + 4 * s + i for i in range(4)]))
    gc = inp["gdn_conv"][l]
    gcs = np.zeros((128, 12, 4), np.float32)
    for which in range(3):
        for hh in range(4):
            h = 4 * s + hh
            gcs[:, which * 4 + hh, :] = gc[:, which * 1024 + h * 128: which * 1024 + (h + 1) * 128].T
    w["gconv"] = gcs.reshape(128, 48)
    gp = np.zeros((128, 136), np.float32)
    gp[:, 0:4] = inp["gdn_A_log"][l][4 * s:4 * s + 4][None, :]
    gp[:, 4:8] = inp["gdn_dt_bias"][l][4 * s:4 * s + 4][None, :]
    gp[:, 8:136] = inp["gdn_norm"][l][None, :]
    w["gpar"] = gp
    ch0 = 512 * s
    cols = []
    for which in range(3):
        cols += list(range(which * 1024 + ch0, which * 1024 + ch0 + 512))
    w["wa"] = colblk(np.array(cols))
    w["wal"] = colsmall(np.arange(3072, 3360))
    mu = inp["rwkv_mu"][l]
    rp = np.zeros((128, 44), np.float32)
    for which in range(3):
        for ct in range(4):
            rp[:, which * 4 + ct] = mu[which * 1024 + ch0 + ct * 128: which * 1024 + ch0 + (ct + 1) * 128]
    for ct in range(4):
        sl = slice(ch0 + ct * 128, ch0 + (ct + 1) * 128)
        rp[:, 12 + ct * 8 + 0] = inp["rwkv_w0"][l][sl]
        rp[:, 12 + ct * 8 + 1] = inp["rwkv_a0"][l][sl]
        rp[:, 12 + ct * 8 + 2] = inp["rwkv_k_k"][l][sl]
        rp[:, 12 + ct * 8 + 3] = inp["rwkv_k_a"][l][sl]
        rp[:, 12 + ct * 8 + 4] = inp["rwkv_r_k"][l].reshape(-1)[sl]
        if l > 0:
            rp[:, 12 + ct * 8 + 5] = inp["rwkv_v0"][l - 1][sl]
    w["rpar"] = rp
    rl = np.zeros((128, 4), np.float32)
    rl[0:64, 0] = mu[3072:3136]
    rl[64:128, 0] = mu[3136:3200]
    rl[0:128, 1] = mu[3200:3328]
    rl[0:32, 2] = mu[3328:3360]
    w["rlmu"] = rl
    rb = np.zeros((128, 1024), np.float32)
    rb[:, 0:512] = inp["rwkv_ln_w"][l][ch0:ch0 + 512][None, :]
    rb[:, 512:1024] = inp["rwkv_ln_b"][l][ch0:ch0 + 512][None, :]
    w["rbc"] = rb
    w["rw2a2"] = np.ascontiguousarray(np.concatenate([inp["rwkv_w2"][l][:, ch0:ch0 + 512], inp["rwkv_a2"][l][:, ch0:ch0 + 512]], axis=0))
    g2 = inp["rwkv_g2"][l][:, ch0:ch0 + 512]
    rg = np.zeros((128, 1024), np.float32)
    rg[:, 0:512] = g2[0:128]
    rg[0:32, 512:1024] = g2[128:160]
    w["rg2"] = rg
    if l > 0:
        v1 = inp["rwkv_v1"][l - 1]
        w["rv1"] = np.ascontiguousarray(v1.reshape(16, 128, 32).transpose(1, 0, 2))
        w["rv2"] = np.ascontiguousarray(inp["rwkv_v2"][l - 1][:, ch0:ch0 + 512])
    return w


def tile_w(W, col0, ncols, cw=128):
    K = W.shape[0]
    sub = W[:, col0:col0 + ncols]
    return np.ascontiguousarray(sub.reshape(K // 128, 128, ncols // cw, cw).transpose(2, 1, 0, 3))


def vec_pk(v):
    return np.ascontiguousarray(v.reshape(-1, 128).T)


def prep_T_weights(inp, l):
    A_IN, B_IN, C_IN = 3360, 4608, 4112
    g0 = A_IN + B_IN + C_IN
    w = {}
    w["wg"] = tile_w(inp["w_in"][l], g0, 6144)
    pw = np.concatenate([inp["proj_a"][l], inp["proj_b"][l], inp["proj_c"][l]], axis=0)
    w["wp"] = tile_w(pw, 0, 2048)
    w["wo"] = tile_w(inp["w_out"][l], 0, 2048)
    w["wup"] = tile_w(inp["ffn_up"][l], 0, 11264)
    w["wdn"] = tile_w(inp["ffn_down"][l], 0, 2048)
    w["g_attn"] = vec_pk(inp["attn_norm"][l])
    w["g_ffn"] = vec_pk(inp["ffn_norm"][l])
    w["g_next"] = vec_pk(inp["attn_norm"][l + 1] if l + 1 < inp["attn_norm"].shape[0] else inp["final_norm"])
    fc = inp["ffn_conv"][l]
    w["fconv"] = np.ascontiguousarray(fc.T.reshape(88, 128, 3).transpose(1, 0, 2).reshape(128, 88 * 3))
    return w


class DD:
    def __init__(self, name=""):
        self.d = Dep(name)


M_SHARED = ("xT", "cst", "rope")
T_INPUTS = [("g_attn", (128, 16)), ("g_ffn", (128, 16)), ("g_next", (128, 16)), ("fconv", (128, 88 * 3)),
            ("wg", (48, 128, 16, 128)), ("wp", (16, 128, 20, 128)), ("wo", (16, 128, 16, 128)),
            ("wup", (88, 128, 16, 128)), ("wdn", (16, 128, 44, 128))]


PAIRS = [[0, 1], [2, 3], [4, 5], [6, 7]]


def build_fused(depth=2):
    nc = bass.Bass("TRN2", target_bir_lowering=False)
    ext = {}

    def din(name, shape, dt=F32):
        ext[name] = nc.dram_tensor(name, list(shape), dt, kind="ExternalInput").ap()
        return ext[name]
    xT = din("xT", (D, SEQ))
    xTT = din("xTT", (D, NTOK_T))
    hmask = din("hmask", (128, 2))
    cst = din("cst", (128, CST_N))
    rope = din("rope", (32, 2 * SEQ))
    for l in range(depth):
        for name, shape in M_INPUTS + (M_INPUTS_L1[:2] if l == 1 else []):
            if name not in M_SHARED:
                din("%s_%d" % (name, l), shape)
        for name, shape in T_INPUTS:
            din("T%s_%d" % (name, l), shape)
    out = nc.dram_tensor("outT", [D, 1024], F32, kind="ExternalOutput").ap()
    yloc = [[nc.dram_tensor("yloc%d_%d" % (l, q), [1280, 512], BF16).ap() for q in range(4)] for l in range(depth)]
    yg = [[nc.dram_tensor("yg%d_%d" % (l, q), [2560, 512], BF16).ap() for q in range(4)] for l in range(depth)]
    xloc = [[nc.dram_tensor("xloc%d_%d" % (rh, ch), [1024, 512], F32).ap() for ch in range(2)] for rh in range(2)]
    xg = [[nc.dram_tensor("xg%d_%d" % (rh, ch), [2048, 512], F32).ap() for ch in range(2)] for rh in range(2)]
    vbuf = nc.dram_tensor("vbuf", [512, SEQ], BF16).ap()
    with ExitStack() as es:
        cx = Ctx(nc, es)
        cx.wpiece = 1024
        xloc_d = [[DD("xloc") for ch in range(2)] for rh in range(2)]
        xg_d = [[DD("xg") for ch in range(2)] for rh in range(2)]
        all_xloc_d = [d for r_ in xloc_d for d in r_]
        all_xg_d = [d for r_ in xg_d for d in r_]
        v_d = DD("v")
        xloc_v = [[xloc[rh][ch].rearrange("(k p) t -> p k t", p=128) for ch in range(2)] for rh in range(2)]
        xg_v = [[xg[rh][ch].rearrange("(r k p) t -> p r k t", r=2, p=128) for ch in range(2)] for rh in range(2)]

        def ag(in_ap, out_ap, rd, wr):
            cx.S.op("pool", [("collective_compute", dict(kind="AllGather", op=ALU.bypass, replica_groups=PAIRS,
                                                        ins=[in_ap], outs=[out_ap]))], [d.d for d in rd], [d.d for d in wr], dma="cc")

        for l in range(depth):
            last = (l == depth - 1)
            yl_d = [{"a": DD("ya"), "b": DD("yb"), "c": DD("yc")} for q in range(4)]
            yg_d = [DD("yg") for q in range(4)]
            yloc_v = [yloc[l][q].rearrange("(b p) t -> p b t", p=128) for q in range(4)]
            yg_v = [yg[l][q].rearrange("(r b p) t -> p r b t", r=2, p=128) for q in range(4)]

            def ywrite(cx_, blk, t_lo, t_hi, src_ap, src_tiles, key, yl_d=yl_d, yloc_v=yloc_v):
                for q in range(t_lo // 512, t_hi // 512):
                    cx_.dma(yloc_v[q][:, blk, :], src_ap[:, q * 512 - t_lo:q * 512 - t_lo + 512], src_tiles, [yl_d[q][key]])

            io = {}
            for name, shape in M_INPUTS + (M_INPUTS_L1[:2] if l == 1 else []):
                if name not in M_SHARED:
                    io[name] = ext["%s_%d" % (name, l)]
            io.update(cst=cst, rope=rope, ydep={"a": "a", "b": "b", "c": "c"}, ywrite=ywrite, ra=0, rb=4, rc=6,
                      v_out=vbuf, v_dep=v_d, vf_in=vbuf, vf_deps=[v_d], write_v=(l == 0))
            if l == 0:
                io.update(xT=xT, x_deps=[])
            else:
                io.update(x_view=(lambda k, lo, hi: xg_v[k // 8][(lo % 1024) // 512][:, lo // 1024, k % 8, lo % 512:lo % 512 + (hi - lo)]),
                          x_deps=all_xg_d)
            phase_M(cx, [io], l == 1)
            cx.S.tag = "AG/y"
            for q in range(4):
                ag(yloc[l][q], yg[l][q], list(yl_d[q].values()), [yg_d[q]])
            io = {name: ext["T%s_%d" % (name, l)] for name, shape in T_INPUTS}
            io.update(hmask=hmask, yread=(lambda r_, b_, q, yg_v=yg_v: yg_v[q][:, r_, b_, :]), y_deps=yg_d,
                      x_deps=all_xloc_d + all_xg_d)
            if l == 0:
                io.update(x_mode="input", xTT=xTT)
            else:
                def xloc_read(k, t_lo, t_hi):
                    out_ = []
                    t = t_lo
                    while t < t_hi:
                        ch = t // 512
                        e = min(t_hi, (ch + 1) * 512)
                        out_.append((xloc_v[k // 8][ch][:, k % 8, t - ch * 512:e - ch * 512], t - t_lo, e - t))
                        t = e
                    return out_
                io.update(x_mode="exchange", xloc_read=xloc_read,
                          xhalo=(lambda k: xg_v[k // 8][1][:, 0, k % 8, 510:512]))
            if last:
                io.update(outT=out, out_deps=[])
            else:
                def xwrite(cx_, k, src_ap, src_tiles):
                    for ch in range(2):
                        cx_.dma(xloc_v[k // 8][ch][:, k % 8, :], src_ap[:, ch * 512:(ch + 1) * 512], src_tiles, [xloc_d[k // 8][ch]])
                io.update(outT=None, xwrite=xwrite)
            phase_T(cx, io, last, 0)
            if not last:
                cx.S.tag = "AG/x"
                for rh in range(2):
                    for ch in range(2):
                        ag(xloc[rh][ch], xg[rh][ch], [xloc_d[rh][ch]], [xg_d[rh][ch]])
        if cx.S.taglog is not None:
            import json
            json.dump(cx.S.taglog, open(os.environ["BASS_TAGLOG"], "w"))
        cx.S.emit()
        print("fused ops:", cx.S.nops, "insts:", cx.S.ninst, "arena peak KB:", cx.ar.peak * 4 / 1024,
              "eng counts:", cx.S.cnt, "max dma sem:", max(cx.S.dma_cnt))
    return nc


_NC_CACHE = {}


def prep_core_inputs(inp, depth, s):
    m = {"cst": make_consts(), "rope": make_rope()}
    hm = np.zeros((128, 2), np.float32)
    hm[:, s] = 1.0
    m["hmask"] = hm
    for l in range(depth):
        w = prep_M_weights(inp, l, s)
        for k, v in w.items():
            if k not in M_SHARED:
                m["%s_%d" % (k, l)] = v
    return m


def kernel(**inputs):
    inp = {k: np.asarray(v) for k, v in inputs.items()}
    x = inp["x"].astype(np.float32, copy=False)
    B, S, Dm = x.shape
    depth = inp["w_in"].shape[0]
    if "nc" not in _NC_CACHE:
        _NC_CACHE["nc"] = build_fused(depth)
    nc = _NC_CACHE["nc"]
    per_s = [prep_core_inputs(inp, depth, s) for s in range(2)]
    tw = {}
    for l in range(depth):
        for k, v in prep_T_weights(inp, l).items():
            tw["T%s_%d" % (k, l)] = v
    maps = []
    for c in range(8):
        b, s = c // 2, c % 2
        m = dict(per_s[s])
        m.update(tw)
        m["xT"] = np.ascontiguousarray(x[b].T)
        lo = 1024 * s - 2
        xs = np.zeros((NTOK_T, Dm), np.float32)
        a = max(lo, 0)
        xs[a - lo:] = x[b][a:lo + NTOK_T]
        m["xTT"] = np.ascontiguousarray(xs.T)
        maps.append(m)
    res = run_bass_kernel_spmd(nc, maps, core_ids=list(range(8))).results
    outs = [np.concatenate([np.asarray(res[2 * b + s]["outT"]).T for s in range(2)], axis=0) for b in range(B)]
    return np.ascontiguousarray(np.stack(outs, axis=0)).astype(np.float32)
```
